# Optimizing a Trainium2 kernel written in Bass

```python
import math
import jax, jax.numpy as jnp
from jax import lax
import numpy as np

D_MODEL = 1024
BATCH = 8
SEQ = 4096
DEPTH = 4

MLA_HEADS = 8
MLA_Q_LORA = 256
MLA_KV_LORA = 128
MLA_NOPE_DIM = 64
MLA_ROPE_DIM = 32
MLA_V_DIM = 64
ROPE_THETA = 10000.0
Q_BLOCK = 128
SC_DIM = 256
SC_WIDTH = 3
SSD_HEADS = 4
SSD_HEAD_DIM = 64
SSD_GROUPS = 2
SSD_STATE = 128
SSD_CONV_WIDTH = 4
SSD_CHUNK = 128
FFN_DIM = 2816
FFN_CONV_WIDTH = 3
NORM_EPS = 1e-6

MLA_QK_DIM = MLA_NOPE_DIM + MLA_ROPE_DIM
MLA_OUT = MLA_HEADS * MLA_V_DIM
SSD_DIM = SSD_HEADS * SSD_HEAD_DIM
SSD_CONV_DIM = SSD_DIM + 2 * SSD_GROUPS * SSD_STATE
SSD_IN = SSD_DIM + SSD_CONV_DIM + SSD_HEADS
IN_WIDTHS = (MLA_Q_LORA, MLA_KV_LORA, MLA_ROPE_DIM, SC_DIM, SC_DIM, SC_DIM, SSD_IN)
IN_SPLITS = tuple(int(v) for v in np.cumsum(IN_WIDTHS)[:-1])
D_IN = sum(IN_WIDTHS)
D_MIX = MLA_OUT + SC_DIM + SSD_DIM

kernel_name = "hybrid_mla_shortconv_ssd_convffn"


def rms_norm(x, w):
    xf = x.astype(jnp.float32)
    y = xf * lax.rsqrt(jnp.mean(xf * xf, axis=-1, keepdims=True) + NORM_EPS)
    return (y * w.astype(jnp.float32)).astype(x.dtype)


def causal_dwconv(u, w):
    width = w.shape[0]
    s = u.shape[1]
    up = jnp.pad(u, ((0, 0), (width - 1, 0), (0, 0)))
    out = up[:, 0:s] * w[0]
    for i in range(1, width):
        out = out + up[:, i:i + s] * w[i]
    return out


def rope(x, cos, sin):
    x1, x2 = jnp.split(x, 2, axis=-1)
    return jnp.concatenate([x1 * cos - x2 * sin, x2 * cos + x1 * sin], axis=-1).astype(x.dtype)


def mla_mixer(c_q, c_kv, k_rope, cos, sin, q_norm, w_q_up, kv_norm, w_kv_up):
    b, s, _ = c_q.shape
    q = (rms_norm(c_q, q_norm) @ w_q_up).reshape(b, s, MLA_HEADS, MLA_QK_DIM)
    q_nope = q[..., :MLA_NOPE_DIM]
    q_rope = rope(q[..., MLA_NOPE_DIM:], cos[:, :, None, :], sin[:, :, None, :])
    kv = (rms_norm(c_kv, kv_norm) @ w_kv_up).reshape(b, s, MLA_HEADS, MLA_NOPE_DIM + MLA_V_DIM)
    k_nope = kv[..., :MLA_NOPE_DIM]
    v = kv[..., MLA_NOPE_DIM:]
    k_rope = rope(k_rope, cos, sin)
    scale = MLA_QK_DIM ** -0.5
    key_idx = jnp.arange(s)

    def block(i):
        start = i * Q_BLOCK
        qn = lax.dynamic_slice_in_dim(q_nope, start, Q_BLOCK, axis=1)
        qr = lax.dynamic_slice_in_dim(q_rope, start, Q_BLOCK, axis=1)
        sc = jnp.einsum("bqhd,bkhd->bhqk", qn, k_nope) + jnp.einsum("bqhd,bkd->bhqk", qr, k_rope)
        sc = sc.astype(jnp.float32) * scale
        causal = (start + jnp.arange(Q_BLOCK))[:, None] >= key_idx[None, :]
        p = jax.nn.softmax(jnp.where(causal, sc, -jnp.inf), axis=-1)
        return jnp.einsum("bhqk,bkhd->bqhd", p.astype(v.dtype), v)

    out = lax.map(block, jnp.arange(s // Q_BLOCK))
    return out.transpose(1, 0, 2, 3, 4).reshape(b, s, MLA_OUT)


def short_conv_mixer(gate_b, gate_c, h, conv_w):
    return gate_b * causal_dwconv(gate_c * h, conv_w)


def ssd_scan(xdt, a_dt, bh, ch):
    b, s, h, p = xdt.shape
    n = bh.shape[-1]
    L = SSD_CHUNK
    c = s // L
    xdt = xdt.reshape(b, c, L, h, p)
    bh = bh.reshape(b, c, L, h, n)
    ch = ch.reshape(b, c, L, h, n)
    a_cs = jnp.cumsum(a_dt.reshape(b, c, L, h).transpose(0, 3, 1, 2), axis=-1)
    diff = a_cs[..., :, None] - a_cs[..., None, :]
    tri = jnp.tril(jnp.ones((L, L), dtype=bool))
    decay_in = jnp.exp(jnp.where(tri, diff, -jnp.inf))
    scores = jnp.einsum("bclhn,bcshn->bhcls", ch, bh) * decay_in
    y_diag = jnp.einsum("bhcls,bcshp->bclhp", scores, xdt)
    decay_to_end = jnp.exp(a_cs[..., -1:] - a_cs).transpose(0, 2, 3, 1)
    chunk_states = jnp.einsum("bclhn,bclhp->bchpn", bh * decay_to_end[..., None], xdt)
    chunk_decay = jnp.exp(a_cs[..., -1]).transpose(2, 0, 1)

    def step(state, inp):
        st, dec = inp
        return state * dec[..., None, None] + st, state

    init = jnp.zeros((b, h, p, n), chunk_states.dtype)
    _, prev = lax.scan(step, init, (chunk_states.transpose(1, 0, 2, 3, 4), chunk_decay))
    prev = prev.transpose(1, 0, 2, 3, 4)
    decay_from_start = jnp.exp(a_cs).transpose(0, 2, 3, 1)
    y_off = jnp.einsum("bclhn,bchpn->bclhp", ch, prev) * decay_from_start[..., None]
    return (y_diag + y_off).reshape(b, s, h, p)


def ssd_mixer(zxbcdt, conv_w, conv_b, dt_bias, a_log, d_skip, norm_w):
    b, s, _ = zxbcdt.shape
    z = zxbcdt[..., :SSD_DIM]
    xbc = zxbcdt[..., SSD_DIM:SSD_DIM + SSD_CONV_DIM]
    dt = zxbcdt[..., SSD_DIM + SSD_CONV_DIM:]
    xbc = jax.nn.silu(causal_dwconv(xbc, conv_w) + conv_b)
    xs = xbc[..., :SSD_DIM].reshape(b, s, SSD_HEADS, SSD_HEAD_DIM)
    heads_per_group = SSD_HEADS // SSD_GROUPS
    gn = SSD_GROUPS * SSD_STATE
    bm = jnp.repeat(xbc[..., SSD_DIM:SSD_DIM + gn].reshape(b, s, SSD_GROUPS, SSD_STATE), heads_per_group, axis=2)
    cm = jnp.repeat(xbc[..., SSD_DIM + gn:].reshape(b, s, SSD_GROUPS, SSD_STATE), heads_per_group, axis=2)
    dt = jax.nn.softplus(dt.astype(jnp.float32) + dt_bias.astype(jnp.float32))
    a = -jnp.exp(a_log.astype(jnp.float32))
    y = ssd_scan(xs * dt[..., None], dt * a, bm, cm)
    y = y + xs * d_skip[:, None]
    y = y.reshape(b, s, SSD_DIM).astype(zxbcdt.dtype)
    return rms_norm(y * jax.nn.silu(z), norm_w)


def setup_inputs(seed: int = 0) -> dict:
    key = jax.random.key(seed)
    ks = jax.random.split(key, 24)
    L = DEPTH

    def normal(k, shape, scale):
        return scale * jax.random.normal(k, shape, jnp.float32)

    def gain(k, n):
        return 1.0 + normal(k, (L, n), 0.02)

    dt0 = jnp.exp(jax.random.uniform(ks[15], (L, SSD_HEADS), jnp.float32, math.log(1e-3), math.log(1e-1)))
    return {
        "x": normal(ks[0], (BATCH, SEQ, D_MODEL), 1.0),
        "positions": jnp.broadcast_to(jnp.arange(SEQ, dtype=jnp.int32), (BATCH, SEQ)),
        "norm_mix_pre": gain(ks[1], D_MODEL),
        "norm_mix_post": gain(ks[2], D_MODEL),
        "norm_ffn_pre": gain(ks[3], D_MODEL),
        "norm_ffn_post": gain(ks[4], D_MODEL),
        "w_in": normal(ks[5], (L, D_MODEL, D_IN), D_MODEL ** -0.5),
        "mla_q_norm": gain(ks[6], MLA_Q_LORA),
        "mla_w_q_up": normal(ks[7], (L, MLA_Q_LORA, MLA_HEADS * MLA_QK_DIM), MLA_Q_LORA ** -0.5),
        "mla_kv_norm": gain(ks[8], MLA_KV_LORA),
        "mla_w_kv_up": normal(ks[9], (L, MLA_KV_LORA, MLA_HEADS * (MLA_NOPE_DIM + MLA_V_DIM)), MLA_KV_LORA ** -0.5),
        "sc_conv_w": normal(ks[10], (L, SC_WIDTH, SC_DIM), SC_WIDTH ** -0.5),
        "ssd_conv_w": normal(ks[11], (L, SSD_CONV_WIDTH, SSD_CONV_DIM), SSD_CONV_WIDTH ** -0.5),
        "ssd_conv_b": normal(ks[12], (L, SSD_CONV_DIM), 0.02),
        "ssd_dt_bias": dt0 + jnp.log(-jnp.expm1(-dt0)),
        "ssd_a_log": jnp.log(jax.random.uniform(ks[13], (L, SSD_HEADS), jnp.float32, 1.0, 16.0)),
        "ssd_d": 1.0 + normal(ks[14], (L, SSD_HEADS), 0.1),
        "ssd_norm": gain(ks[16], SSD_DIM),
        "w_out": normal(ks[17], (L, D_MIX, D_MODEL), D_MIX ** -0.5),
        "ffn_w_up": normal(ks[18], (L, D_MODEL, 2 * FFN_DIM), D_MODEL ** -0.5),
        "ffn_conv_w": normal(ks[19], (L, FFN_CONV_WIDTH, 2 * FFN_DIM), FFN_CONV_WIDTH ** -0.5),
        "ffn_conv_b": normal(ks[20], (L, 2 * FFN_DIM), 0.02),
        "ffn_w_down": normal(ks[21], (L, FFN_DIM, D_MODEL), FFN_DIM ** -0.5),
    }


def reference(x, positions, norm_mix_pre, norm_mix_post, norm_ffn_pre, norm_ffn_post, w_in,
              mla_q_norm, mla_w_q_up, mla_kv_norm, mla_w_kv_up, sc_conv_w, ssd_conv_w, ssd_conv_b,
              ssd_dt_bias, ssd_a_log, ssd_d, ssd_norm, w_out, ffn_w_up, ffn_conv_w, ffn_conv_b,
              ffn_w_down):
    inv_freq = 1.0 / (ROPE_THETA ** (jnp.arange(0, MLA_ROPE_DIM, 2, dtype=jnp.float32) / MLA_ROPE_DIM))
    ang = positions.astype(jnp.float32)[..., None] * inv_freq
    cos = jnp.cos(ang).astype(x.dtype)
    sin = jnp.sin(ang).astype(x.dtype)
    for l in range(DEPTH):
        h = rms_norm(x, norm_mix_pre[l])
        c_q, c_kv, k_rope, sc_b, sc_c, sc_h, ssd_in = jnp.split(h @ w_in[l], IN_SPLITS, axis=-1)
        y_att = mla_mixer(c_q, c_kv, k_rope, cos, sin, mla_q_norm[l], mla_w_q_up[l], mla_kv_norm[l], mla_w_kv_up[l])
        y_conv = short_conv_mixer(sc_b, sc_c, sc_h, sc_conv_w[l])
        y_ssd = ssd_mixer(ssd_in, ssd_conv_w[l], ssd_conv_b[l], ssd_dt_bias[l], ssd_a_log[l], ssd_d[l], ssd_norm[l])
        mixed = jnp.concatenate([y_att, y_conv, y_ssd], axis=-1) @ w_out[l]
        x = x + rms_norm(mixed, norm_mix_post[l])
        h = rms_norm(x, norm_ffn_pre[l])
        u = causal_dwconv(h @ ffn_w_up[l], ffn_conv_w[l]) + ffn_conv_b[l]
        gate, up = jnp.split(u, 2, axis=-1)
        x = x + rms_norm((jax.nn.silu(gate) * up) @ ffn_w_down[l], norm_ffn_post[l])
    return x
```

```python
import math
from contextlib import ExitStack
import numpy as np
import concourse.bass as bass
import concourse.mybir as mybir
from concourse.bass_utils import run_bass_kernel_spmd

F32 = mybir.dt.float32
BF16 = mybir.dt.bfloat16
I32 = mybir.dt.int32
AF = mybir.ActivationFunctionType
ALU = mybir.AluOpType

D = 1024
NH = 8
FF = 2816
NFC = 22
EPS = 1e-6
SCALE = 96 ** -0.5
NCOLW = 2240
PI = float(np.pi)

C_GPRE1 = 0
C_GPRE2 = 8
C_QN = 16
C_KVN = 18
C_SCW = 19
C_SSW = 25
C_SSB = 49
C_DSK = 55
C_SSN = 57
C_FW = 59
C_FB = 191
NCOLS = 235


class _Rec:
    def __getattr__(self, name):
        def f(*args, **kwargs):
            return (name, args, kwargs)
        return f


REC = _Rec()


def _run(fn, e):
    if callable(fn):
        return fn(e)
    name, args, kwargs = fn
    return getattr(e, name)(*args, **kwargs)


class Src:
    def __init__(self, sem, step):
        self.sem, self.step, self.count = sem, step, 0


class Buf:
    def __init__(self, name):
        self.name, self.writer, self.readers = name, None, []


class Eng:
    def __init__(self, name, src, inorder=False):
        self.name, self.src, self.q, self.waited, self.inorder = name, src, [], {}, inorder


class Tile:
    def __init__(self, t, b):
        self.t, self.b = t, b

    def __getitem__(self, k):
        return self.t[k]


class Op:
    __slots__ = ("eng", "insts", "reads", "writes", "idx", "preds", "succs", "cost", "is_dma", "lat",
                 "npred", "ready", "finish", "src", "val", "prev_val", "seg")

    def __init__(self, eng):
        self.eng, self.insts, self.reads, self.writes = eng, [], [], []
        self.preds, self.succs = set(), []
        self.cost, self.is_dma, self.lat = 0.0, False, 0.0
        self.ready, self.finish = 0.0, 0.0
        self.src = self.val = self.prev_val = None


def _free_size(ap):
    n = 1
    for d in ap.shape[1:]:
        n *= int(d)
    return n


def _est_cost(eng, fn):
    name, args, kwargs = fn
    try:
        if name == "matmul":
            rhs = args[2]
            n = _free_size(rhs)
            return (max(32.0, n / 1.95) + 12.0) * (4.0 if rhs.dtype == F32 else 1.0)
        if name == "transpose":
            return 75.0
        if name == "dma_start":
            return 120.0
        out = args[0] if args else kwargs.get("out")
        n = _free_size(out)
        if eng.name == "act":
            return 150.0 + 0.85 * n
        if name == "reciprocal":
            return 160.0 + 2.6 * n
        if name == "memset":
            return 100.0 + 0.3 * n
        if eng.name == "pool":
            return 250.0 + 2.1 * n
        return 150.0 + 1.04 * n
    except Exception:
        return 300.0


class Prog:
    LOOK_W = 24
    LOOK_IDX = 6000

    def __init__(self, nc, es, n_dma_sems=24):
        self.nc, self.es = nc, es
        def sem(n):
            return es.enter_context(nc.semaphore(n))
        self.pe = Eng("pe", Src(sem("s_pe"), 1), inorder=True)
        self.act = Eng("act", Src(sem("s_act"), 1))
        self.dve = Eng("dve", Src(sem("s_dve"), 1))
        self.pool = Eng("pool", Src(sem("s_pool"), 1))
        self.sp = Eng("sp", Src(sem("s_sp"), 1))
        self.engs = [self.pe, self.act, self.dve, self.pool, self.sp]
        self.dma_srcs = [Src(sem("s_dma%d" % i), 16) for i in range(n_dma_sems)]
        self.dma_rr = {"sp": 0, "pool": 0}
        self.dma_part = {"sp": self.dma_srcs[:n_dma_sems - 8], "pool": self.dma_srcs[n_dma_sems - 8:]}
        self.ops = []
        self.cur = {}
        self.seg_start = [0]
        self.ninst = 0

    def tile(self, name, shape, dt):
        t = self.es.enter_context(self.nc.sbuf_tensor("sb_" + name, shape, dt))
        return Tile(t, Buf(name))

    def emit(self, eng, fn, reads=(), writes=(), sig=True, dma_bytes=None):
        op = self.cur.get(eng.name)
        if op is None:
            op = Op(eng)
            self.cur[eng.name] = op
        op.insts.append(fn)
        op.reads.extend(reads)
        op.writes.extend(writes)
        op.cost += _est_cost(eng, fn)
        self.ninst += 1
        if dma_bytes is not None:
            op.is_dma = True
            op.lat = float(dma_bytes)
        if sig:
            self.cur[eng.name] = None
            self._close(op)

    def _close(self, op):
        seg0 = self.seg_start[-1]
        op.idx = len(self.ops)
        op.seg = len(self.seg_start) - 1
        preds = set()
        for b in op.reads:
            if b.writer is not None:
                preds.add(b.writer)
        for b in op.writes:
            if b.writer is not None:
                preds.add(b.writer)
            preds.update(b.readers)
        preds.discard(op)
        op.preds = {p for p in preds if p.idx >= seg0}
        wset = set(id(b) for b in op.writes)
        for b in op.writes:
            b.writer = op
            b.readers = []
        for b in op.reads:
            if id(b) not in wset:
                b.readers.append(op)
        self.ops.append(op)

    def dma(self, eng, out, in_, reads=(), writes=()):
        nbytes = 1
        for d in out.shape:
            nbytes *= int(d)
        nbytes *= 2 if out.dtype == BF16 else 4
        nb2 = 1
        for d in in_.shape:
            nb2 *= int(d)
        nb2 *= 2 if in_.dtype == BF16 else 4
        self.emit(eng, REC.dma_start(out=out, in_=in_), reads, writes, dma_bytes=max(nbytes, nb2))

    def barrier(self):
        assert all(v is None for v in self.cur.values())
        if len(self.ops) > self.seg_start[-1]:
            self.seg_start.append(len(self.ops))

    def _schedule(self, ops):
        import bisect
        for op in ops:
            op.succs = []
            op.ready = 0.0
        for op in ops:
            op.npred = len(op.preds)
            for p in op.preds:
                p.succs.append(op)
        avail = {e.name: [] for e in self.engs}
        t_free = {e.name: 0.0 for e in self.engs}
        order = {e.name: [] for e in self.engs}
        for op in ops:
            if op.npred == 0:
                avail[op.eng.name].append((op.idx, op))
        scheduled = 0
        dma_free = 0.0
        n = len(ops)
        base = ops[0].idx
        done = [False] * n
        lo = 0
        while scheduled < n:
            while lo < n and done[lo]:
                lo += 1
            lim = base + lo + self.LOOK_IDX
            best = None
            for e in self.engs:
                lst = avail[e.name]
                if not lst:
                    continue
                tf = t_free[e.name]
                for (idx, op) in lst[:self.LOOK_W]:
                    if idx > lim and best is not None:
                        break
                    st = op.ready if op.ready > tf else tf
                    key = (st, idx)
                    if best is None or key < best[0]:
                        best = (key, e, op)
            (st, idx), e, op = best
            avail[e.name].remove((idx, op))
            if op.is_dma:
                t_free[e.name] = st + op.cost
                xs = max(st + op.cost, dma_free)
                dma_free = xs + op.lat / 170.0
                op.finish = dma_free + 2000.0
            else:
                op.finish = st + op.cost
                t_free[e.name] = op.finish
            order[e.name].append(op)
            done[op.idx - base] = True
            scheduled += 1
            for sc in op.succs:
                if sc.ready < op.finish:
                    sc.ready = op.finish
                sc.npred -= 1
                if sc.npred == 0:
                    bisect.insort(avail[sc.eng.name], (sc.idx, sc))
        return order, max(op.finish for op in ops)

    def _wait(self, eng, src, val):
        if val <= 0 or eng.waited.get(src, 0) >= val:
            return
        eng.waited[src] = val
        eng.q.append(REC.wait_ge(src.sem, val))

    def _barrier_waits(self):
        srcs = [e.src for e in self.engs] + self.dma_srcs
        for e in self.engs:
            for sr in srcs:
                if sr is not e.src:
                    self._wait(e, sr, sr.count)

    def finalize(self):
        assert all(v is None for v in self.cur.values())
        bounds = self.seg_start + [len(self.ops)]
        total = 0.0
        for si in range(len(bounds) - 1):
            ops = self.ops[bounds[si]:bounds[si + 1]]
            if not ops:
                continue
            if si > 0:
                self._barrier_waits()
            order, span = self._schedule(ops)
            total += span
            sb = {}
            for op in ops:
                sb[op.eng.name] = sb.get(op.eng.name, 0.0) + op.cost
            print("seg", si, "span us %.1f" % (span / 1e3), {k: round(v / 1e3, 1) for k, v in sb.items()}, flush=True)
            for e in self.engs:
                for op in order[e.name]:
                    if op.is_dma:
                        part = self.dma_part[e.name]
                        src = part[self.dma_rr[e.name] % len(part)]
                        self.dma_rr[e.name] += 1
                        op.prev_val = src.count
                    else:
                        src = e.src
                    src.count += src.step
                    op.src, op.val = src, src.count
            for e in self.engs:
                for op in order[e.name]:
                    need = {}
                    for p in op.preds:
                        if e.inorder and p.src is e.src:
                            continue
                        if need.get(p.src, 0) < p.val:
                            need[p.src] = p.val
                    for sr, v in need.items():
                        self._wait(e, sr, v)
                    if op.is_dma:
                        self._wait(e, op.src, op.prev_val)
                    last = len(op.insts) - 1
                    for i, fn in enumerate(op.insts):
                        if i == last:
                            e.q.append(("__sig__", fn, op.src.sem, op.src.step))
                        else:
                            e.q.append(fn)
        for e in (self.sp, self.pool):
            for s_ in self.dma_srcs:
                self._wait(e, s_, s_.count)
        busy = {}
        for op in self.ops:
            busy[op.eng.name] = busy.get(op.eng.name, 0.0) + op.cost
        print("busy ms:", {k: round(v / 1e6, 3) for k, v in busy.items()}, flush=True)
        print("ops:", len(self.ops), "insts:", self.ninst, "est span ms: %.3f" % (total / 1e6),
              {e.name: len(e.q) for e in self.engs}, flush=True)


def _replay(q, e):
    for item in q:
        if item[0] == "__sig__":
            _, fn, sem, step = item
            _run(fn, e).then_inc(sem, step)
        else:
            _run(item, e)


def build(S, NL, dbg=None):
    NB = S // 512
    NKB = S // 128
    nc = bass.Bass("TRN2", target_bir_lowering=False)

    def din(name, shape, dt=F32):
        return nc.dram_tensor(name, shape, dt, kind="ExternalInput").ap()

    x_d = din("x", [S, D])
    pos_d = din("posrep", [32, S], I32)
    rc_d = din("ropec", [32, 2])
    cst_d = din("consts", [128, 384])
    cols_d = din("cols", [128, NL, NCOLS])
    rows_d = din("rows", [NL, 128, 2056])
    w_in_d = din("w_in_g", [NL, 128, 8 * NCOLW])
    w_dt_d = din("w_dt", [NL, D, 4])
    w_q_d = din("w_q", [NL, 256, 768])
    w_qsw_d = din("w_qsw", [NL, 256, 256])
    w_kn_d = din("w_kn", [NL, 128, 512])
    w_v_d = din("w_v", [NL, 128, 512])
    w_out_d = din("w_out", [NL, D, D])
    w_up_d = din("w_up_g", [NL, 44, 128, 1024])
    w_down_d = din("w_down", [NL, FF, D])
    y_d = nc.dram_tensor("y", [S, D], F32, kind="ExternalOutput").ap()
    xa_d = nc.dram_tensor("xa", [S, D], F32, kind="Internal").ap()
    xb_d = [nc.dram_tensor("xb%d" % i, [S, D], F32, kind="Internal").ap() for i in range(2)]
    kd_d = nc.dram_tensor("kd", [NH, 96, S], BF16, kind="Internal").ap()
    dbg_d = {}
    if dbg:
        for name, shape in dbg.items():
            dbg_d[name] = nc.dram_tensor("dbg_" + name, shape, F32, kind="ExternalOutput").ap()

    es = ExitStack()
    with es:
        P = Prog(nc, es)
        pe, act, dve, pool, sp = P.pe, P.act, P.dve, P.pool, P.sp
        T = P.tile
        b_xa, b_kd = Buf("xa"), Buf("kd")
        b_xb = [Buf("xb0"), Buf("xb1")]
        b_y = Buf("y")
        psb = []
        for i in range(7):
            t = es.enter_context(nc.psum_tensor("ps%d" % i, [128, 512], F32))
            psb.append(Tile(t, Buf("ps%d" % i)))
        pst = Tile(es.enter_context(nc.psum_tensor("pst", [128, 1024], BF16)), Buf("pst"))
        pools = {"mm": [0, 1, 2], "acc": [3, 4], "aux": [5, 6]}
        prr = {"mm": 0, "acc": 0, "aux": 0}

        def PS(pool_name):
            lst = pools[pool_name]
            i = lst[prr[pool_name] % len(lst)]
            prr[pool_name] += 1
            return psb[i]

        cst32 = T("cst32", [128, 384], F32)
        cstb = T("cstb", [128, 384], BF16)
        cols = T("cols", [128, NCOLS], F32)
        ropec = T("ropec", [32, 2], F32)
        epsc = T("epsc", [128, 1], F32)
        P.dma(sp, cst32[:], cst_d, writes=[cst32.b])
        P.dma(sp, ropec[:], rc_d, writes=[ropec.b])
        P.emit(dve, REC.tensor_copy(cstb[:], cst32[:]), [cst32.b], [cstb.b])
        P.emit(dve, REC.memset(epsc[:], EPS), [], [epsc.b])
        ident_b = cstb[:, 0:128]
        tri_b = cstb[:, 128:256]
        ones_b = cstb[:, 256:384]
        tri_f = cst32[:, 128:256]
        ones_f = cst32[:, 256:384]

        xin = [T("xin%d" % i, [128, D], F32) for i in range(2)]
        mixs = [T("mix%d" % i, [128, D], F32) for i in range(2)]
        xr = T("xr", [128, D], F32)
        junkP = T("junkP", [128, 512], BF16)
        smP = T("smP", [128, 8], F32)
        hb = T("hb", [128, D], BF16)
        hT = T("hT", [128, 8, 512], BF16)
        junk = T("junk", [128, 512], BF16)
        sm = T("sm", [128, 16], F32)
        rows = T("rows", [128, 1032], F32)
        tmpA = T("tmpA", [128, 512], F32)
        WREG = 135168
        wreg = T("wreg", [128, WREG // 2], BF16)
        REG2 = 35200
        reg2 = T("reg2", [128, REG2 // 2], BF16)

        def col(l, c, n=1):
            return cols[:, c:c + n]

        def rstd_from_ss(ss_ap, ss_bufs, n, out_ap, out_buf):
            P.emit(act, REC.activation(out_ap, ss_ap, AF.Ln, scale=1.0 / n, bias=epsc[:, 0:1]),
                   list(ss_bufs) + [epsc.b], [out_buf])
            P.emit(act, REC.activation(out_ap, out_ap, AF.Exp, scale=-0.5), [out_buf], [out_buf])

        def dump(name, ap, bufs, npart=128, idx=None):
            if not dbg or name not in dbg:
                return
            dst = dbg_d[name] if idx is None else dbg_d[name][idx]
            P.emit(dve, REC.tensor_copy(tmpA[0:npart, :], ap), bufs, [tmpA.b])
            P.dma(sp, dst[0:npart, :], tmpA[0:npart, :], [tmpA.b], [])

        def norm_transpose(src_d, src_buf, row0, l, gcol):
            for tt in range(4):
                xt = xin[tt % 2]
                r0 = row0 + tt * 128
                P.dma(sp, xt[:], src_d[r0:r0 + 128, :], [src_buf], [xt.b])
                P.emit(act, REC.activation(junk[:, :], xt[:, 0:512], AF.Square, accum_out=sm[:, 0:1]),
                       [xt.b], [junk.b, sm.b])
                P.emit(act, REC.activation(junk[:, :], xt[:, 512:1024], AF.Square, accum_out=sm[:, 1:2]),
                       [xt.b], [junk.b, sm.b])
                P.emit(dve, REC.tensor_tensor(sm[:, 2:3], sm[:, 0:1], sm[:, 1:2], ALU.add), [sm.b], [sm.b])
                rstd_from_ss(sm[:, 2:3], [sm.b], D, sm[:, 3:4], sm.b)
                P.emit(dve, REC.tensor_scalar(hb[:], xt[:], sm[:, 3:4], None, ALU.mult),
                       [xt.b, sm.b], [hb.b])
                for kc in range(8):
                    P.emit(pe, REC.transpose(pst[:, kc * 128:(kc + 1) * 128], hb[:, kc * 128:(kc + 1) * 128], ident_b),
                           [hb.b, cstb.b], [pst.b], sig=(kc == 7))
                P.emit(dve, REC.tensor_tensor(
                    hT[:, :, tt * 128:(tt + 1) * 128],
                    pst[:, :].rearrange("p (k t) -> p k t", k=8),
                    col(l, gcol, 8).unsqueeze(2).to_broadcast([128, 8, 128]), ALU.mult),
                    [pst.b, cols.b], [hT.b])

        def post_norm_residual(ps_list, src_d, src_buf, dst_d, dst_buf, r0, k):
            xt = xr
            mx = mixs[k % 2]
            P.dma(sp, xt[:], src_d[r0:r0 + 128, :], [src_buf], [xt.b])
            for half in range(2):
                ps = ps_list[half]
                P.emit(act, REC.activation(mx[:, half * 512:(half + 1) * 512], ps[:, :], AF.Copy),
                       [ps.b], [mx.b])
                P.emit(act, REC.activation(junkP[:, :], mx[:, half * 512:(half + 1) * 512], AF.Square,
                                                              accum_out=smP[:, 4 + half:5 + half]),
                       [mx.b], [junkP.b, smP.b])
            P.emit(dve, REC.tensor_tensor(smP[:, 6:7], smP[:, 4:5], smP[:, 5:6], ALU.add), [smP.b], [smP.b])
            rstd_from_ss(smP[:, 6:7], [smP.b], D, smP[:, 7:8], smP.b)
            P.emit(dve, REC.scalar_tensor_tensor(mx[:], mx[:], smP[:, 7:8], rows[:, 0:1024], ALU.mult, ALU.mult),
                   [mx.b, smP.b, rows.b], [mx.b])
            P.emit(dve, REC.tensor_tensor(mx[:], mx[:], xt[:], ALU.add), [mx.b, xt.b], [mx.b])
            P.dma(sp, dst_d[r0:r0 + 128, :], mx[:], [mx.b], [dst_buf])

        for l in range(NL):
            src_d, src_buf = (x_d, Buf("xsrc")) if l == 0 else (xb_d[(l - 1) % 2], b_xb[(l - 1) % 2])
            dst_d, dst_buf = (y_d, b_y) if l == NL - 1 else (xb_d[l % 2], b_xb[l % 2])

            o = 0
            def carve(n_el):
                nonlocal o
                a = wreg.t[:, o:o + n_el]
                o += n_el
                return a
            w_in_sb = carve(8 * NCOLW)
            w_dt_sb = carve(8 * 4).rearrange("p (k n) -> p k n", k=8)
            w_q_sb = carve(2 * 768).rearrange("p (k n) -> p k n", k=2)
            w_qsw_sb = carve(2 * 256).rearrange("p (k n) -> p k n", k=2)
            w_kn_sb = carve(512)
            w_v_sb = carve(512)
            def carve_t(shape, dt):
                nonlocal o
                n = int(np.prod(shape))
                if dt == F32:
                    o += (o % 2)
                    a = wreg.t[:, o:o + 2 * n].bitcast(F32)
                    o += 2 * n
                else:
                    a = wreg.t[:, o:o + n]
                    o += n
                if len(shape) == 2:
                    return a.rearrange("p (a b) -> p a b", a=shape[0])
                if len(shape) == 3:
                    return a.rearrange("p (a b c) -> p a b c", a=shape[0], b=shape[1])
                return a
            bW = Buf("wM%d" % l)
            GROUPS = [(0, 128), (128, 128), (256, 128), (384, 32), (416, 32)] + [(448 + g * 128, 128) for g in range(14)]
            bWg = {}
            bWo = Buf("wo%d" % l)
            P.barrier()
            for (dst, srcap) in [
                (w_dt_sb, w_dt_d[l].rearrange("(k p) n -> p k n", p=128)),
                (w_q_sb, w_q_d[l].rearrange("(k p) n -> p k n", p=128)),
                (w_qsw_sb, w_qsw_d[l].rearrange("(k p) n -> p k n", p=128)),
                (w_kn_sb, w_kn_d[l]),
                (w_v_sb, w_v_d[l]),
            ]:
                P.dma(pool, dst, srcap, [], [bW])
            for (c0g, Mg) in GROUPS:
                bWg[c0g] = Buf("wg%d_%d" % (l, c0g))
                P.dma(pool, w_in_sb[:, 8 * c0g:8 * (c0g + Mg)], w_in_d[l, :, 8 * c0g:8 * (c0g + Mg)], [], [bWg[c0g]])
            P.dma(sp, cols[:, :], cols_d[:, l, :], [], [cols.b])
            P.dma(sp, rows[:, 0:1024], rows_d[l, :, 0:1024], [], [rows.b])
            P.dma(sp, rows[:, 1024:1032], rows_d[l, :, 2048:2056], [], [rows.b])

            def MT(name, shape, dt):
                return Tile(carve_t(shape, dt), Buf(name))
            cq_sb = MT("cq", [3, 512], BF16)
            cq_sq = MT("cqsq", [3, 512], BF16)
            cqn = MT("cqn", [3, 512], BF16)
            conv_sb = MT("conv", [4, 512], F32)
            szs = MT("szs", [2, 512], F32)
            ubs = [MT("ub%d" % i, [1, 515], F32) for i in range(2)]
            uh = MT("uh", [6, 3], F32)
            vbuf = MT("vbuf", [2, 514], F32)
            ycv = MT("ycv", [2, 512], BF16)
            xs_f = MT("xsf", [2, 512], F32)
            xs_b = MT("xsb", [2, 512], BF16)
            Bt = MT("Bt", [2, 512], BF16)
            Ct = MT("Ct", [2, 512], BF16)
            o_wout = o
            Kcur = MT("Kcur", [8, 512], BF16)
            kbuf = [MT("kbuf0", [1, 4096], BF16)]
            w_out_sb = wreg.t[:, o_wout:o_wout + 8 * D].rearrange("p (k n) -> p k n", k=8)
            assert o - o_wout == 8 * D
            Qh = [MT("Qh%d" % i, [1, 512], BF16) for i in range(2)]
            Pt = [MT("Pt%d" % i, [1, 512], BF16) for i in range(3)]
            tabs = MT("tabs", [2, 512], F32)
            rtmp = MT("rtmp", [2, 512], F32)
            rint = Tile(carve_t([1, 512], F32).bitcast(I32), Buf("rint"))
            krot = MT("krot", [1, 512], F32)
            rd = MT("rd", [1, 512], F32)
            osb = MT("osb", [1, 512], F32)
            gat = Tile(conv_sb.t[:, 0:2, :], conv_sb.b)
            gsq = Tile(cq_sq.t[:, 0:2, :], cq_sq.b)
            dtT = MT("dtT", [4, 4], F32)
            adt = MT("adt", [4, 4], F32)
            arow = MT("arow", [1, 4], F32)
            state = MT("state", [4, 64], F32)
            prevp = MT("prevp", [4, 128], BF16)
            xdtp = MT("xdtp", [4, 128], BF16)
            xdtd = MT("xdtd", [4, 64], BF16)
            Btok = MT("Btok", [2, 128], BF16)
            adtri = MT("adtri", [4, 128], F32)
            dec = MT("dec", [4, 128], F32)
            drow = MT("drow", [4, 128], F32)
            Cs = MT("Cs", [4, 128], BF16)
            scT = MT("scT", [4, 128], BF16)
            s4 = MT("s4", [8, 4], F32)
            assert o * 2 <= WREG, o * 2
            Vc = Tile(reg2.t[:, 0:NKB * 8 * 65].rearrange("p (j h d) -> p j h d", j=NKB, h=8), Buf("Vc%d" % l))
            yT = hT

            P.emit(dve, REC.memset(Vc[:, :, :, 64:65], 1.0), [], [Vc.b])
            P.emit(dve, REC.memset(vbuf[:, :, 0:2], 0.0), [], [vbuf.b])
            P.emit(dve, REC.memset(uh[:, :, :], 0.0), [], [uh.b])
            P.emit(dve, REC.memset(state[:], 0.0), [], [state.b])
            P.emit(dve, REC.memset(prevp[:], 0.0), [], [prevp.b])
            P.emit(dve, REC.memset(xdtp[:], 0.0), [], [xdtp.b])
            P.emit(act, REC.activation(arow[:, 0, :], rows[:, 1028:1032], AF.Exp), [rows.b], [arow.b])
            P.emit(dve, REC.tensor_scalar(arow[:, 0, :], arow[:, 0, :], -1.0, None, ALU.mult), [arow.b], [arow.b])

            for blk in range(NB):
                t0 = blk * 512
                P.dma(sp, rint[0:32, 0, :], pos_d[:, t0:t0 + 512], [], [rint.b])
                P.emit(dve, REC.tensor_copy(rtmp[0:32, 0, :], rint[0:32, 0, :]), [rint.b], [rtmp.b])
                P.emit(dve, REC.tensor_scalar(rtmp[0:32, 0, :], rtmp[0:32, 0, :], ropec[:, 0:1], None, ALU.mult),
                       [rtmp.b, ropec.b], [rtmp.b])
                for ti, phase in ((0, PI / 2), (1, 0.0)):
                    P.emit(dve, REC.tensor_scalar(rtmp[0:32, 1, :], rtmp[0:32, 0, :], 1.0 / (2 * PI), phase / (2 * PI) + 0.5, ALU.mult, ALU.add),
                           [rtmp.b], [rtmp.b])
                    P.emit(dve, REC.tensor_copy(rint[0:32, 0, :], rtmp[0:32, 1, :]), [rtmp.b], [rint.b])
                    P.emit(dve, REC.tensor_copy(rtmp[0:32, 1, :], rint[0:32, 0, :]), [rint.b], [rtmp.b])
                    P.emit(dve, REC.tensor_scalar(rtmp[0:32, 1, :], rtmp[0:32, 1, :], -2 * PI, None, ALU.mult), [rtmp.b], [rtmp.b])
                    P.emit(dve, REC.scalar_tensor_tensor(tabs[0:32, ti, :], rtmp[0:32, 0, :], phase, rtmp[0:32, 1, :], ALU.add, ALU.add),
                           [rtmp.b], [tabs.b])
                    P.emit(dve, REC.tensor_scalar(rtmp[0:32, 1, :], tabs[0:32, ti, :], -PI, 2 * PI, ALU.is_lt, ALU.mult), [tabs.b], [rtmp.b])
                    P.emit(dve, REC.tensor_tensor(tabs[0:32, ti, :], tabs[0:32, ti, :], rtmp[0:32, 1, :], ALU.add), [tabs.b, rtmp.b], [tabs.b])
                    P.emit(dve, REC.tensor_scalar(rtmp[0:32, 1, :], tabs[0:32, ti, :], PI, -2 * PI, ALU.is_gt, ALU.mult), [tabs.b], [rtmp.b])
                    P.emit(dve, REC.tensor_tensor(tabs[0:32, ti, :], tabs[0:32, ti, :], rtmp[0:32, 1, :], ALU.add), [tabs.b, rtmp.b], [tabs.b])
                    if ti == 0:
                        P.emit(act, REC.activation(tabs[0:32, 0, :], tabs[0:32, 0, :], AF.Sin), [tabs.b], [tabs.b])
                    else:
                        P.emit(act, REC.activation(tabs[0:32, 1, :], tabs[0:32, 1, :], AF.Sin, scale=ropec[:, 1:2]), [tabs.b, ropec.b], [tabs.b])
                Ctab = tabs[0:32, 0, :]
                Stab = tabs[0:32, 1, :]

                norm_transpose(src_d, src_buf, t0, l, C_GPRE1)
                if l == 0 and blk == NB - 1:
                    dump("hT0", hT[:, 0, :], [hT.b])

                def inproj(c0, M):
                    ps = PS("mm")
                    for kc in range(8):
                        P.emit(pe, REC.matmul(ps[0:M, :], w_in_sb[:, 8 * c0 + kc * M:8 * c0 + (kc + 1) * M], hT[:, kc, :], start=(kc == 0), stop=(kc == 7)),
                               [bWg[c0], hT.b], [ps.b], sig=(kc == 7))
                    return ps
                for g in range(3):
                    ps = inproj(g * 128, 128)
                    P.emit(act, REC.activation(cq_sb[:, g, :], ps[:, :], AF.Copy), [ps.b], [cq_sb.b])
                    P.emit(act, REC.activation(cq_sq[:, g, :], ps[:, :], AF.Square), [ps.b], [cq_sq.b])
                ps_kr = inproj(384, 32)
                P.emit(dve, REC.tensor_tensor(rtmp[0:32, 0, :], ps_kr[0:32, :], Ctab, ALU.mult), [ps_kr.b, tabs.b], [rtmp.b])
                ps_ks = inproj(416, 32)
                P.emit(dve, REC.tensor_tensor(rtmp[0:32, 1, :], ps_ks[0:32, :], Stab, ALU.mult), [ps_ks.b, tabs.b], [rtmp.b])
                P.emit(dve, REC.tensor_tensor(krot[0:32, 0, :], rtmp[0:32, 0, :], rtmp[0:32, 1, :], ALU.add), [rtmp.b], [krot.b])
                P.emit(dve, REC.tensor_copy(Kcur[64:96, :, :], krot[0:32, 0:1, :].to_broadcast([32, 8, 512])), [krot.b], [Kcur.b])
                for g in range(4):
                    ps = inproj(448 + g * 128, 128)
                    P.emit(act, REC.activation(conv_sb[:, g, :], ps[:, :], AF.Copy), [ps.b], [conv_sb.b])
                for c in range(2):
                    ps = inproj(448 + (4 + c) * 128, 128)
                    P.emit(dve, REC.tensor_tensor(vbuf[:, c, 2:514], conv_sb[:, 2 + c, :], ps[:, :], ALU.mult), [conv_sb.b, ps.b], [vbuf.b])
                    P.emit(dve, REC.tensor_scalar(tmpA[:, :], vbuf[:, c, 0:512], col(l, C_SCW + c * 3 + 0), None, ALU.mult), [vbuf.b, cols.b], [tmpA.b])
                    P.emit(dve, REC.scalar_tensor_tensor(tmpA[:, :], vbuf[:, c, 1:513], col(l, C_SCW + c * 3 + 1), tmpA[:, :], ALU.mult, ALU.add), [vbuf.b, cols.b, tmpA.b], [tmpA.b])
                    P.emit(dve, REC.scalar_tensor_tensor(tmpA[:, :], vbuf[:, c, 2:514], col(l, C_SCW + c * 3 + 2), tmpA[:, :], ALU.mult, ALU.add), [vbuf.b, cols.b, tmpA.b], [tmpA.b])
                    P.emit(dve, REC.tensor_tensor(ycv[:, c, :], tmpA[:, :], conv_sb[:, c, :], ALU.mult), [tmpA.b, conv_sb.b], [ycv.b])
                    P.emit(dve, REC.tensor_copy(vbuf[:, c, 0:2], vbuf[:, c, 512:514]), [vbuf.b], [vbuf.b])
                for g in range(2):
                    ps = inproj(448 + 768 + g * 128, 128)
                    P.emit(act, REC.activation(szs[:, g, :], ps[:, :], AF.Silu), [ps.b], [szs.b])
                for c in range(6):
                    ps = inproj(448 + 1024 + c * 128, 128)
                    ub = ubs[c % 2]
                    P.emit(act, REC.activation(ub[:, 0, 3:515], ps[:, :], AF.Copy), [ps.b], [ub.b])
                    P.emit(dve, REC.tensor_copy(ub[:, 0, 0:3], uh[:, c, :]), [uh.b], [ub.b])
                    P.emit(dve, REC.tensor_scalar(tmpA[:, :], ub[:, 0, 0:512], col(l, C_SSW + c * 4 + 0), None, ALU.mult), [ub.b, cols.b], [tmpA.b])
                    for k in range(1, 4):
                        P.emit(dve, REC.scalar_tensor_tensor(tmpA[:, :], ub[:, 0, k:k + 512], col(l, C_SSW + c * 4 + k), tmpA[:, :], ALU.mult, ALU.add),
                               [ub.b, cols.b, tmpA.b], [tmpA.b])
                    if c < 2:
                        P.emit(act, REC.activation(xs_f[:, c, :], tmpA[:, :], AF.Silu, bias=col(l, C_SSB + c)), [tmpA.b, cols.b], [xs_f.b])
                        P.emit(dve, REC.tensor_copy(xs_b[:, c, :], xs_f[:, c, :]), [xs_f.b], [xs_b.b])
                    elif c < 4:
                        P.emit(act, REC.activation(Bt[:, c - 2, :], tmpA[:, :], AF.Silu, bias=col(l, C_SSB + c)), [tmpA.b, cols.b], [Bt.b])
                    else:
                        P.emit(act, REC.activation(Ct[:, c - 4, :], tmpA[:, :], AF.Silu, bias=col(l, C_SSB + c)), [tmpA.b, cols.b], [Ct.b])
                    P.emit(dve, REC.tensor_copy(uh[:, c, :], ub[:, 0, 512:515]), [ub.b], [uh.b])
                ps = PS("aux")
                for tt in range(4):
                    for kc in range(8):
                        P.emit(pe, REC.matmul(ps[:, tt * 4:tt * 4 + 4], hT[:, kc, tt * 128:(tt + 1) * 128], w_dt_sb[:, kc, :],
                                                                          start=(kc == 0), stop=(kc == 7)),
                               [bW, hT.b], [ps.b], sig=(kc == 7))
                dt4 = dtT[:, :, :]
                dtb = rows[:, 1024:1028].unsqueeze(1).to_broadcast([128, 4, 4])
                P.emit(dve, REC.tensor_tensor(dt4, ps[:, 0:16].rearrange("p (a b) -> p a b", a=4), dtb, ALU.add), [ps.b, rows.b], [dtT.b])
                P.emit(act, REC.activation(adt[:, :, :], dt4, AF.Abs), [dtT.b], [adt.b])
                P.emit(act, REC.activation(adt[:, :, :], adt[:, :, :], AF.Exp, scale=-1.0), [adt.b], [adt.b])
                P.emit(act, REC.activation(adt[:, :, :], adt[:, :, :], AF.Ln, bias=1.0), [adt.b], [adt.b])
                P.emit(dve, REC.scalar_tensor_tensor(dt4, dt4, 0.0, adt[:, :, :], ALU.max, ALU.add), [dtT.b, adt.b], [dtT.b])
                P.emit(dve, REC.tensor_tensor(adt[:, :, :], dt4, arow[:, 0:1, :].to_broadcast([128, 4, 4]), ALU.mult), [dtT.b, arow.b], [adt.b])

                for (chs, n, qc) in (((0, 1), 256, C_QN), ((2,), 128, C_KVN)):
                    ps = PS("aux")
                    for i, c in enumerate(chs):
                        P.emit(pe, REC.matmul(ps[:, :], ones_b, cq_sq[:, c, :], start=(i == 0), stop=(i == len(chs) - 1)),
                               [cstb.b, cq_sq.b], [ps.b], sig=(i == len(chs) - 1))
                    P.emit(act, REC.activation(tmpA[:, :], ps[:, :], AF.Ln, scale=1.0 / n, bias=epsc[:, 0:1]), [ps.b, epsc.b], [tmpA.b])
                    P.emit(act, REC.activation(tmpA[:, :], tmpA[:, :], AF.Exp, scale=-0.5), [tmpA.b], [tmpA.b])
                    for i, c in enumerate(chs):
                        P.emit(dve, REC.scalar_tensor_tensor(cqn[:, c, :], cq_sb[:, c, :], col(l, qc + i), tmpA[:, :], ALU.mult, ALU.mult),
                               [cq_sb.b, cols.b, tmpA.b], [cqn.b])
                ckvn = cqn[:, 2, :]
                if l == 0 and blk == NB - 1:
                    for c in range(3):
                        dump("cqn", cqn[:, c, :], [cqn.b], idx=c)
                    dump("dtT", dtT[:, :, :].rearrange("p a b -> p (a b)"), [dtT.b]) if False else None

                for hp in range(4):
                    ps = PS("mm")
                    P.emit(pe, REC.matmul(ps[:, :], w_kn_sb[:, hp * 128:(hp + 1) * 128], ckvn, start=True, stop=True), [bW, cqn.b], [ps.b])
                    P.emit(act, REC.activation(Kcur[0:64, 2 * hp, :], ps[0:64, :], AF.Copy), [ps.b], [Kcur.b])
                    P.emit(dve, REC.tensor_copy(Kcur[0:64, 2 * hp + 1, :], ps[64:128, :]), [ps.b], [Kcur.b])
                for tt in range(4):
                    ps = PS("mm")
                    P.emit(pe, REC.matmul(ps[:, :], cqn[:, 2, tt * 128:(tt + 1) * 128], w_v_sb, start=True, stop=True), [bW, cqn.b], [ps.b])
                    P.emit(act, REC.activation(Vc[:, blk * 4 + tt, :, 0:64], ps[:, :].rearrange("p (h d) -> p h d", h=8), AF.Copy), [ps.b], [Vc.b])
                if blk < NB - 1:
                    P.dma(sp, kd_d.rearrange("h d s -> d h s")[:, :, t0:t0 + 512], Kcur[0:96, :, :], [Kcur.b], [b_kd])

                for h in range(NH):
                    q = Qh[h % 2]
                    psQ = PS("mm")
                    for c in range(2):
                        P.emit(pe, REC.matmul(psQ[0:96, :], w_q_sb[:, c, h * 96:(h + 1) * 96], cqn[:, c, :], start=(c == 0), stop=(c == 1)),
                               [bW, cqn.b], [psQ.b], sig=(c == 1))
                    psS = PS("mm")
                    for c in range(2):
                        P.emit(pe, REC.matmul(psS[0:32, :], w_qsw_sb[:, c, h * 32:(h + 1) * 32], cqn[:, c, :], start=(c == 0), stop=(c == 1)),
                               [bW, cqn.b], [psS.b], sig=(c == 1))
                    P.emit(act, REC.activation(q[0:64, 0, :], psQ[0:64, :], AF.Copy), [psQ.b], [q.b])
                    P.emit(dve, REC.tensor_tensor(rtmp[0:32, 0, :], psS[0:32, :], Stab, ALU.mult), [psS.b, tabs.b], [rtmp.b])
                    P.emit(dve, REC.tensor_tensor(rtmp[0:32, 1, :], psQ[64:96, :], Ctab, ALU.mult), [psQ.b, tabs.b], [rtmp.b])
                    P.emit(dve, REC.tensor_tensor(q[64:96, 0, :], rtmp[0:32, 0, :], rtmp[0:32, 1, :], ALU.add), [rtmp.b], [q.b])
                    kb = kbuf[0]
                    if l == 0 and blk == NB - 1 and h == 0:
                        dump("Q0", q[0:96, 0, :], [q.b], npart=96)
                        dump("K0", Kcur[0:96, 0, :], [Kcur.b], npart=96)
                    if blk > 0:
                        P.dma(sp, kb[0:96, 0, 0:t0], kd_d[h, :, 0:t0], [b_kd], [kb.b])
                    psO = PS("acc")
                    nfull = 4 * blk
                    for j in range(nfull + 4):
                        pss = PS("mm")
                        pt = Pt[j % 3]
                        if j < nfull:
                            c0 = 0
                            P.emit(pe, REC.matmul(pss[:, :], kb[0:96, 0, j * 128:(j + 1) * 128], q[0:96, 0, :], start=True, stop=True),
                                   [kb.b, q.b], [pss.b])
                        else:
                            jj = j - nfull
                            c0 = jj * 128
                            P.emit(pe, REC.matmul(pss[:, c0:512], Kcur[0:96, h, c0:c0 + 128], q[0:96, 0, c0:512], start=True, stop=True),
                                   [Kcur.b, q.b], [pss.b])
                        P.emit(act, REC.activation(pt[:, 0, c0:512], pss[:, c0:512], AF.Exp, scale=SCALE), [pss.b], [pt.b])
                        if j >= nfull:
                            P.emit(dve, REC.tensor_tensor(pt[:, 0, c0:c0 + 128], pt[:, 0, c0:c0 + 128], tri_b, ALU.mult), [pt.b, cstb.b], [pt.b])
                        P.emit(pe, REC.matmul(psO[0:65, c0:512], Vc[:, j, h, :], pt[:, 0, c0:512], start=(j == 0), stop=(j == nfull + 3)),
                               [Vc.b, pt.b], [psO.b])
                    P.emit(act, REC.activation(rd[64:65, 0, :], psO[64:65, :], AF.Ln), [psO.b], [rd.b])
                    P.emit(act, REC.activation(rd[64:65, 0, :], rd[64:65, 0, :], AF.Exp, scale=-1.0), [rd.b], [rd.b])
                    psB = PS("mm")
                    P.emit(pe, REC.matmul(psB[0:64, :], ones_f[64:65, 0:64], rd[64:65, 0, :], start=True, stop=True), [cst32.b, rd.b], [psB.b])
                    P.emit(act, REC.activation(osb[0:64, 0, :], psO[0:64, :], AF.Copy), [psO.b], [osb.b])
                    P.emit(dve, REC.tensor_tensor(yT[(h % 2) * 64:(h % 2) * 64 + 64, h // 2, :], osb[0:64, 0, :], psB[0:64, :], ALU.mult),
                           [osb.b, psB.b], [yT.b])

                P.dma(pool, w_out_sb, w_out_d[l].rearrange("(k p) n -> p k n", p=128), [], [Kcur.b, kbuf[0].b, bWo])
                for c in range(2):
                    P.emit(dve, REC.tensor_copy(yT[:, 4 + c, :], ycv[:, c, :]), [ycv.b], [yT.b])
                psY = [PS("acc"), PS("acc")]
                for tt in range(4):
                    ts = slice(tt * 128, (tt + 1) * 128)
                    P.emit(dve, REC.tensor_copy(prevp[:, 0:4:2, 0:64], state[:, 0:4:2, :]), [state.b], [prevp.b])
                    P.emit(dve, REC.tensor_copy(prevp[:, 1:4:2, 64:128], state[:, 1:4:2, :]), [state.b], [prevp.b])
                    for c in range(2):
                        P.emit(pe, REC.transpose(pst[:, c * 128:(c + 1) * 128], xs_b[:, c, ts], ident_b), [xs_b.b, cstb.b], [pst.b], sig=False)
                    for g in range(2):
                        P.emit(pe, REC.transpose(pst[:, 256 + g * 128:256 + (g + 1) * 128], Bt[:, g, ts], ident_b), [Bt.b, cstb.b], [pst.b], sig=(g == 1))
                    xtok = pst[:, 0:256].rearrange("p (h d) -> p h d", h=4)
                    P.emit(dve, REC.tensor_tensor(xdtp[:, 0:4:2, 0:64], xtok[:, 0:4:2, :], dtT[:, tt, 0:4:2].unsqueeze(2).to_broadcast([128, 2, 64]), ALU.mult),
                           [pst.b, dtT.b], [xdtp.b])
                    P.emit(dve, REC.tensor_tensor(xdtp[:, 1:4:2, 64:128], xtok[:, 1:4:2, :], dtT[:, tt, 1:4:2].unsqueeze(2).to_broadcast([128, 2, 64]), ALU.mult),
                           [pst.b, dtT.b], [xdtp.b])
                    P.emit(act, REC.activation(Btok[:, :, :], pst[:, 256:512].rearrange("p (g n) -> p g n", g=2), AF.Copy), [pst.b], [Btok.b])
                    P.emit(dve, REC.tensor_tensor(adtri[:, :, :], tri_f.unsqueeze(1).to_broadcast([128, 4, 128]),
                                                                 adt[:, tt, :].unsqueeze(2).to_broadcast([128, 4, 128]), ALU.mult), [cst32.b, adt.b], [adtri.b])
                    psR = PS("aux")
                    P.emit(pe, REC.matmul(psR[:, :], ones_f, adtri[:, :, :].rearrange("p h l -> p (h l)"), start=True, stop=True), [cst32.b, adtri.b], [psR.b])
                    psA = PS("aux")
                    P.emit(pe, REC.matmul(psA[:, 0:4], tri_f, adt[:, tt, :], start=True, stop=True), [cst32.b, adt.b], [psA.b])
                    acol = s4[:, 0, :]
                    P.emit(dve, REC.tensor_copy(acol, psA[:, 0:4]), [psA.b], [s4.b])
                    psR3 = psR[:, :].rearrange("p (h l) -> p h l", h=4)
                    P.emit(dve, REC.tensor_tensor(dec[:, :, :], psR3, acol.unsqueeze(2).to_broadcast([128, 4, 128]), ALU.subtract), [psR.b, s4.b], [dec.b])
                    P.emit(dve, REC.tensor_scalar(dec[:, :, :], dec[:, :, :], 0.0, None, ALU.min), [dec.b], [dec.b])
                    P.emit(act, REC.activation(dec[:, :, :], dec[:, :, :], AF.Exp), [dec.b], [dec.b])
                    P.emit(dve, REC.tensor_tensor(dec[:, :, :], dec[:, :, :], tri_f.unsqueeze(1).to_broadcast([128, 4, 128]), ALU.mult), [dec.b, cst32.b], [dec.b])
                    P.emit(act, REC.activation(drow[:, :, :], psR3, AF.Exp), [psR.b], [drow.b])
                    P.emit(dve, REC.tensor_tensor(s4[:, 1, :], psR3[:, :, 127], acol, ALU.subtract), [psR.b, s4.b], [s4.b])
                    P.emit(act, REC.activation(s4[:, 2, :], s4[:, 1, :], AF.Exp), [s4.b], [s4.b])
                    P.emit(act, REC.activation(s4[:, 3, :], psR3[:, :, 127], AF.Exp), [psR.b], [s4.b])
                    P.emit(dve, REC.tensor_tensor(s4[:, 4, :], s4[:, 2, :], dtT[:, tt, :], ALU.mult), [s4.b, dtT.b], [s4.b])
                    P.emit(dve, REC.tensor_tensor(xdtd[:, :, :], xtok, s4[:, 4, :].unsqueeze(2).to_broadcast([128, 4, 64]), ALU.mult), [pst.b, s4.b], [xdtd.b])
                    for g in range(2):
                        P.emit(dve, REC.tensor_tensor(Cs[:, 2 * g:2 * g + 2, :], drow[:, 2 * g:2 * g + 2, :],
                                                                          Ct[:, g:g + 1, ts].to_broadcast([128, 2, 128]), ALU.mult), [drow.b, Ct.b], [Cs.b])
                    psG = PS("mm")
                    for g in range(2):
                        P.emit(pe, REC.matmul(psG[:, g * 128:(g + 1) * 128], Bt[:, g, ts], Ct[:, g, ts], start=True, stop=True), [Bt.b, Ct.b], [psG.b], sig=(g == 1))
                    for g in range(2):
                        P.emit(dve, REC.tensor_tensor(scT[:, 2 * g:2 * g + 2, :], dec[:, 2 * g:2 * g + 2, :],
                                                                             psG[:, g * 128:(g + 1) * 128].unsqueeze(1).to_broadcast([128, 2, 128]), ALU.mult), [dec.b, psG.b], [scT.b])
                    for k in range(2):
                        ops = [(xdtp[:, 2 * k, :], scT[:, 2 * k, :], [xdtp.b, scT.b]), (xdtp[:, 2 * k + 1, :], scT[:, 2 * k + 1, :], [xdtp.b, scT.b]),
                               (prevp[:, 2 * k, :], Cs[:, 2 * k, :], [prevp.b, Cs.b]), (prevp[:, 2 * k + 1, :], Cs[:, 2 * k + 1, :], [prevp.b, Cs.b])]
                        for i, (lh, rh, rb) in enumerate(ops):
                            P.emit(pe, REC.matmul(psY[k][:, ts], lh, rh, start=(i == 0), stop=(i == 3)), rb, [psY[k].b], sig=(i == 3))
                    psSt = PS("aux")
                    for g in range(2):
                        P.emit(pe, REC.matmul(psSt[:, g * 128:(g + 1) * 128], Btok[:, g, :], xdtd[:, 2 * g:2 * g + 2, :].rearrange("p h d -> p (h d)"), start=True, stop=True),
                               [Btok.b, xdtd.b], [psSt.b], sig=(g == 1))
                    P.emit(dve, REC.tensor_tensor(state[:, :, :], state[:, :, :], s4[:, 3, :].unsqueeze(2).to_broadcast([128, 4, 64]), ALU.mult), [state.b, s4.b], [state.b])
                    P.emit(dve, REC.tensor_tensor(state[:, :, :], state[:, :, :], psSt[:, 0:256].rearrange("p (h d) -> p h d", h=4), ALU.add), [state.b, psSt.b], [state.b])
                for k in range(2):
                    P.emit(dve, REC.scalar_tensor_tensor(gat[:, k, :], xs_f[:, k, :], col(l, C_DSK + k), psY[k][:, :], ALU.mult, ALU.add), [xs_f.b, cols.b, psY[k].b], [gat.b])
                    P.emit(dve, REC.tensor_tensor(gat[:, k, :], gat[:, k, :], szs[:, k, :], ALU.mult), [gat.b, szs.b], [gat.b])
                    P.emit(act, REC.activation(gsq[:, k, :], gat[:, k, :], AF.Square), [gat.b], [gsq.b])
                ps = PS("aux")
                for k in range(2):
                    P.emit(pe, REC.matmul(ps[:, :], ones_b, gsq[:, k, :], start=(k == 0), stop=(k == 1)), [cstb.b, gsq.b], [ps.b], sig=(k == 1))
                P.emit(act, REC.activation(tmpA[:, :], ps[:, :], AF.Ln, scale=1.0 / 256, bias=epsc[:, 0:1]), [ps.b, epsc.b], [tmpA.b])
                P.emit(act, REC.activation(tmpA[:, :], tmpA[:, :], AF.Exp, scale=-0.5), [tmpA.b], [tmpA.b])
                for k in range(2):
                    P.emit(dve, REC.scalar_tensor_tensor(yT[:, 6 + k, :], gat[:, k, :], col(l, C_SSN + k), tmpA[:, :], ALU.mult, ALU.mult), [gat.b, cols.b, tmpA.b], [yT.b])

                if l == 0 and blk == NB - 1:
                    for kc in range(8):
                        dump("yT", yT[:, kc, :], [yT.b], idx=kc)
                for tt in range(4):
                    pl = []
                    for half in range(2):
                        ps = PS("mm")
                        for kc in range(8):
                            P.emit(pe, REC.matmul(ps[:, :], yT[:, kc, tt * 128:(tt + 1) * 128], w_out_sb[:, kc, half * 512:(half + 1) * 512],
                                                                                        start=(kc == 0), stop=(kc == 7)), [yT.b, bWo, Kcur.b, kbuf[0].b], [ps.b], sig=(kc == 7))
                        pl.append(ps)
                    post_norm_residual(pl, src_d, src_buf, xa_d, b_xa, t0 + tt * 128, tt)

            bWF = Buf("wF%d" % l)
            P.barrier()
            w_up_sb = wreg.t[:, 0:8 * 2 * FF].rearrange("p (j k n) -> p j k n", j=44, k=8)
            w_down_sb = wreg.t[:, 8 * 2 * FF:8 * 2 * FF + NFC * D].rearrange("p (k n) -> p k n", k=NFC)
            assert (8 * 2 * FF + NFC * D) * 2 <= WREG
            bWU = [Buf("wu%d_%d" % (l, j)) for j in range(44)]
            for i in range(NFC):
                for j in (i, NFC + i):
                    P.dma(pool, w_up_sb[:, j, :, :], w_up_d[l, j].rearrange("p (k n) -> p k n", k=8), [], [bWU[j]])
            wdv = w_down_d[l].rearrange("(k p) n -> p k n", p=128)
            P.dma(pool, w_down_sb[:, 0:11, :], wdv[:, 0:11, :], [], [bWF])
            P.dma(pool, w_down_sb[:, 11:22, :], wdv[:, 11:22, :], [], [bWF])
            P.dma(sp, rows[:, 0:1024], rows_d[l, :, 1024:2048], [], [rows.b])
            gT = Tile(reg2.t[:, 0:NFC * 512].rearrange("p (k n) -> p k n", k=NFC), Buf("gT%d" % l))
            o2 = NFC * 512
            def carve2(shape):
                nonlocal o2
                n = int(np.prod(shape))
                a = reg2.t[:, o2:o2 + 2 * n].bitcast(F32)
                o2 += 2 * n
                return Tile(a.rearrange("p (a b) -> p a b", a=shape[0]), Buf("r2_%d" % o2))
            ug = [carve2([1, 514]) for _ in range(2)]
            uu = [carve2([1, 514]) for _ in range(2)]
            fh = carve2([44, 2])
            tB = carve2([1, 512])
            tC = carve2([1, 512])
            tmpB = Tile(tB.t[:, 0, :], tB.b)
            tmpC = Tile(tC.t[:, 0, :], tC.b)
            assert o2 * 2 <= REG2, o2 * 2
            P.emit(dve, REC.memset(fh[:, :, :], 0.0), [], [fh.b])

            for blk in range(NB):
                t0 = blk * 512
                norm_transpose(xa_d, b_xa, t0, l, C_GPRE2)
                for i in range(NFC):
                    br = []
                    for bi, (ub, cbase) in enumerate(((ug[i % 2], i * 128), (uu[i % 2], FF + i * 128))):
                        ps = PS("mm")
                        for kc in range(8):
                            P.emit(pe, REC.matmul(ps[:, :], w_up_sb[:, i + bi * NFC, kc, :], hT[:, kc, :], start=(kc == 0), stop=(kc == 7)),
                                   [bWU[i + bi * NFC], hT.b], [ps.b], sig=(kc == 7))
                        j = i + bi * NFC
                        veng = dve
                        acc = tmpA if bi == 0 else tmpB
                        P.emit(act, REC.activation(ub[:, 0, 2:514], ps[:, :], AF.Copy), [ps.b], [ub.b])
                        P.emit(act, REC.activation(acc[:, :], ps[:, :], AF.Identity, scale=col(l, C_FW + j * 3 + 2), bias=col(l, C_FB + j)), [ps.b, cols.b], [acc.b])
                        P.emit(dve, REC.tensor_copy(ub[:, 0, 0:2], fh[:, j, :]), [fh.b], [ub.b])
                        for k in (0, 1):
                            P.emit(veng, REC.scalar_tensor_tensor(acc[:, :], ub[:, 0, k:k + 512], col(l, C_FW + j * 3 + k), acc[:, :], ALU.mult, ALU.add),
                                   [ub.b, cols.b, acc.b], [acc.b])
                        P.emit(dve, REC.tensor_copy(fh[:, j, :], ub[:, 0, 512:514]), [ub.b], [fh.b])
                    P.emit(act, REC.activation(tmpC[:, :], tmpA[:, :], AF.Silu), [tmpA.b], [tmpC.b])
                    P.emit(dve, REC.tensor_tensor(gT[:, i, :], tmpC[:, :], tmpB[:, :], ALU.mult), [tmpC.b, tmpB.b], [gT.b])
                for tt in range(4):
                    pl = []
                    for half in range(2):
                        ps = PS("mm")
                        for i in range(NFC):
                            P.emit(pe, REC.matmul(ps[:, :], gT[:, i, tt * 128:(tt + 1) * 128], w_down_sb[:, i, half * 512:(half + 1) * 512],
                                                                                       start=(i == 0), stop=(i == NFC - 1)), [gT.b, bWF], [ps.b], sig=(i == NFC - 1))
                        pl.append(ps)
                    post_norm_residual(pl, xa_d, b_xa, dst_d, dst_buf, t0 + tt * 128, tt)

        P.finalize()
        block = es.enter_context(nc.Block())

        @block.tensor
        def _(e):
            _replay(pe.q, e)

        @block.scalar
        def _(e):
            _replay(act.q, e)

        @block.vector
        def _(e):
            _replay(dve.q, e)

        @block.gpsimd
        def _(e):
            _replay(pool.q, e)

        @block.sync
        def _(e):
            _replay(sp.q, e)
    return nc


def _cols_layer(p, l):
    c = np.zeros((128, NCOLS), np.float32)
    def chunks(v, n):
        return np.asarray(v, np.float32).reshape(n, 128).T
    c[:, C_GPRE1:C_GPRE1 + 8] = chunks(p["norm_mix_pre"][l], 8)
    c[:, C_GPRE2:C_GPRE2 + 8] = chunks(p["norm_ffn_pre"][l], 8)
    c[:, C_QN:C_QN + 2] = chunks(p["mla_q_norm"][l], 2)
    c[:, C_KVN:C_KVN + 1] = chunks(p["mla_kv_norm"][l], 1)
    scw = np.asarray(p["sc_conv_w"][l], np.float32)
    for ch in range(2):
        for k in range(3):
            c[:, C_SCW + ch * 3 + k] = scw[k, ch * 128:(ch + 1) * 128]
    ssw = np.asarray(p["ssd_conv_w"][l], np.float32)
    ssb = np.asarray(p["ssd_conv_b"][l], np.float32)
    for ch in range(6):
        for k in range(4):
            c[:, C_SSW + ch * 4 + k] = ssw[k, ch * 128:(ch + 1) * 128]
        c[:, C_SSB + ch] = ssb[ch * 128:(ch + 1) * 128]
    dsk = np.repeat(np.asarray(p["ssd_d"][l], np.float32), 64)
    c[:, C_DSK:C_DSK + 2] = chunks(dsk, 2)
    c[:, C_SSN:C_SSN + 2] = chunks(p["ssd_norm"][l], 2)
    fw = np.asarray(p["ffn_conv_w"][l], np.float32)
    fb = np.asarray(p["ffn_conv_b"][l], np.float32)
    for j in range(44):
        for k in range(3):
            c[:, C_FW + j * 3 + k] = fw[k, j * 128:(j + 1) * 128]
        c[:, C_FB + j] = fb[j * 128:(j + 1) * 128]
    return c


def prep_shared(p, NL):
    f = lambda a: np.ascontiguousarray(np.asarray(a, np.float32))
    w_in = f(p["w_in"])[:NL]
    sw = np.concatenate([np.arange(16, 32), np.arange(0, 16)])
    kr = w_in[:, :, 384:416]
    w_in_r = np.concatenate([w_in[:, :, 0:384], kr, kr[:, :, sw], w_in[:, :, 416:2208]], axis=2)
    assert w_in_r.shape[2] == NCOLW
    groups = [(0, 128), (128, 128), (256, 128), (384, 32), (416, 32)] + [(448 + g * 128, 128) for g in range(14)]
    w4 = w_in_r.reshape(NL, 8, 128, NCOLW)
    w_in_g = np.concatenate([np.transpose(w4[:, :, :, c0:c0 + M], (0, 2, 1, 3)).reshape(NL, 128, 8 * M) for (c0, M) in groups], axis=2)
    assert w_in_g.shape[2] == 8 * NCOLW
    w_dt = w_in[:, :, 2208:2212]
    wq = f(p["mla_w_q_up"])[:NL]
    wq4 = wq.reshape(NL, 256, 8, 96)
    w_qsw = wq4[:, :, :, 64:96][:, :, :, sw].reshape(NL, 256, 256)
    wkv = f(p["mla_w_kv_up"])[:NL].reshape(NL, 128, 8, 128)
    w_kn = wkv[:, :, :, 0:64].reshape(NL, 128, 512)
    w_v = wkv[:, :, :, 64:128].reshape(NL, 128, 512)
    wu = f(p["ffn_w_up"])[:NL].reshape(NL, 8, 128, 44, 128)
    w_up_g = np.ascontiguousarray(np.transpose(wu, (0, 3, 2, 1, 4)).reshape(NL, 44, 128, 1024))
    cols = np.stack([_cols_layer(p, l) for l in range(NL)], axis=1)
    rows = np.zeros((NL, 128, 2056), np.float32)
    for l in range(NL):
        rows[l, :, 0:1024] = np.asarray(p["norm_mix_post"][l], np.float32)[None, :]
        rows[l, :, 1024:2048] = np.asarray(p["norm_ffn_post"][l], np.float32)[None, :]
        rows[l, :, 2048:2052] = np.asarray(p["ssd_dt_bias"][l], np.float32)[None, :]
        rows[l, :, 2052:2056] = np.asarray(p["ssd_a_log"][l], np.float32)[None, :]
    inv_freq = (1.0 / (10000.0 ** (np.arange(0, 32, 2, dtype=np.float32) / np.float32(32)))).astype(np.float32)
    ropec = np.zeros((32, 2), np.float32)
    ropec[:, 0] = np.concatenate([inv_freq, inv_freq])
    ropec[:, 1] = np.concatenate([-np.ones(16), np.ones(16)])
    consts = np.concatenate([np.eye(128), np.triu(np.ones((128, 128))), np.ones((128, 128))], axis=1).astype(np.float32)
    return {
        "ropec": ropec, "consts": consts, "cols": np.ascontiguousarray(cols), "rows": rows,
        "w_in_g": np.ascontiguousarray(w_in_g), "w_dt": np.ascontiguousarray(w_dt),
        "w_q": np.ascontiguousarray(wq), "w_qsw": np.ascontiguousarray(w_qsw),
        "w_kn": np.ascontiguousarray(w_kn), "w_v": np.ascontiguousarray(w_v),
        "w_out": f(p["w_out"])[:NL], "w_up_g": w_up_g, "w_down": f(p["ffn_w_down"])[:NL],
    }


def run(inputs, S, NL, ncores, dbg=None):
    shared = prep_shared(inputs, NL)
    x = np.asarray(inputs["x"], np.float32)
    pos = np.asarray(inputs["positions"], np.int32)
    in_maps = []
    for c in range(ncores):
        m = dict(shared)
        m["x"] = np.ascontiguousarray(x[c, :S])
        m["posrep"] = np.ascontiguousarray(np.broadcast_to(pos[c, :S][None, :], (32, S)))
        in_maps.append(m)
    nc = build(S, NL, dbg)
    res = run_bass_kernel_spmd(nc, in_maps, core_ids=list(range(ncores)))
    return res


def kernel(**inputs):
    res = run(inputs, 4096, 4, 8)
    return np.stack([r["y"] for r in res.results], axis=0).astype(np.float32)
```

```python
import math
from contextlib import ExitStack
import numpy as np
import concourse.bass as bass
import concourse.mybir as mybir
from concourse.bass_utils import run_bass_kernel_spmd

F32 = mybir.dt.float32
BF16 = mybir.dt.bfloat16
I32 = mybir.dt.int32
AF = mybir.ActivationFunctionType
ALU = mybir.AluOpType

D = 1024
NH = 8
FF = 2816
NFC = 22
EPS = 1e-6
SCALE = 96 ** -0.5
NCOLW = 2240
PI = float(np.pi)
F_ROPECACHE = True

C_GPRE1 = 0
C_GPRE2 = 8
C_QN = 16
C_KVN = 18
C_SCW = 19
C_SSW = 25
C_SSB = 49
C_DSK = 55
C_SSN = 57
C_FW = 59
C_FB = 191
NCOLS = 235


class _Rec:
    def __getattr__(self, name):
        def f(*args, **kwargs):
            return (name, args, kwargs)
        return f


REC = _Rec()


def _run(fn, e):
    if callable(fn):
        return fn(e)
    name, args, kwargs = fn
    return getattr(e, name)(*args, **kwargs)


class Src:
    def __init__(self, sem, step):
        self.sem, self.step, self.count = sem, step, 0


class Buf:
    def __init__(self, name):
        self.name, self.writer, self.readers = name, None, []


class Eng:
    def __init__(self, name, src, inorder=False):
        self.name, self.src, self.q, self.waited, self.inorder = name, src, [], {}, inorder


class Tile:
    def __init__(self, t, b):
        self.t, self.b = t, b

    def __getitem__(self, k):
        return self.t[k]


class Op:
    __slots__ = ("eng", "insts", "reads", "writes", "idx", "preds", "succs", "cost", "is_dma", "lat",
                 "npred", "ready", "finish", "src", "val", "prev_val", "seg")

    def __init__(self, eng):
        self.eng, self.insts, self.reads, self.writes = eng, [], [], []
        self.preds, self.succs = set(), []
        self.cost, self.is_dma, self.lat = 0.0, False, 0.0
        self.ready, self.finish = 0.0, 0.0
        self.src = self.val = self.prev_val = None


def _free_size(ap):
    n = 1
    for d in ap.shape[1:]:
        n *= int(d)
    return n


def _est_cost(eng, fn):
    name, args, kwargs = fn
    try:
        if name == "matmul":
            rhs = args[2]
            n = _free_size(rhs)
            return (max(32.0, n / 1.95) + 12.0) * (4.0 if rhs.dtype == F32 else 1.0)
        if name == "transpose":
            return 75.0
        if name == "dma_start":
            return 120.0
        out = args[0] if args else kwargs.get("out")
        n = _free_size(out)
        if eng.name == "act":
            return 150.0 + 0.85 * n
        if name == "reciprocal":
            return 160.0 + 2.6 * n
        if name == "memset":
            return 100.0 + 0.3 * n
        if eng.name == "pool":
            return 250.0 + 2.1 * n
        return 150.0 + 1.04 * n
    except Exception:
        return 300.0


class Prog:
    LOOK_W = 24
    LOOK_IDX = 6000

    def __init__(self, nc, es, n_dma_sems=24):
        self.nc, self.es = nc, es
        def sem(n):
            return es.enter_context(nc.semaphore(n))
        self.pe = Eng("pe", Src(sem("s_pe"), 1), inorder=True)
        self.act = Eng("act", Src(sem("s_act"), 1))
        self.dve = Eng("dve", Src(sem("s_dve"), 1))
        self.pool = Eng("pool", Src(sem("s_pool"), 1))
        self.sp = Eng("sp", Src(sem("s_sp"), 1))
        self.engs = [self.pe, self.act, self.dve, self.pool, self.sp]
        self.dma_srcs = [Src(sem("s_dma%d" % i), 16) for i in range(n_dma_sems)]
        self.dma_rr = {"sp": 0, "pool": 0}
        self.dma_part = {"sp": self.dma_srcs[:n_dma_sems - 8], "pool": self.dma_srcs[n_dma_sems - 8:]}
        self.ops = []
        self.cur = {}
        self.seg_start = [0]
        self.ninst = 0

    def tile(self, name, shape, dt):
        t = self.es.enter_context(self.nc.sbuf_tensor("sb_" + name, shape, dt))
        return Tile(t, Buf(name))

    def emit(self, eng, fn, reads=(), writes=(), sig=True, dma_bytes=None):
        op = self.cur.get(eng.name)
        if op is None:
            op = Op(eng)
            self.cur[eng.name] = op
        op.insts.append(fn)
        op.reads.extend(reads)
        op.writes.extend(writes)
        op.cost += _est_cost(eng, fn)
        self.ninst += 1
        if dma_bytes is not None:
            op.is_dma = True
            op.lat = float(dma_bytes)
        if sig:
            self.cur[eng.name] = None
            self._close(op)

    def _close(self, op):
        seg0 = self.seg_start[-1]
        op.idx = len(self.ops)
        op.seg = len(self.seg_start) - 1
        preds = set()
        for b in op.reads:
            if b.writer is not None:
                preds.add(b.writer)
        for b in op.writes:
            if b.writer is not None:
                preds.add(b.writer)
            preds.update(b.readers)
        preds.discard(op)
        op.preds = {p for p in preds if p.idx >= seg0}
        wset = set(id(b) for b in op.writes)
        for b in op.writes:
            b.writer = op
            b.readers = []
        for b in op.reads:
            if id(b) not in wset:
                b.readers.append(op)
        self.ops.append(op)

    def dma(self, eng, out, in_, reads=(), writes=()):
        nbytes = 1
        for d in out.shape:
            nbytes *= int(d)
        nbytes *= 2 if out.dtype == BF16 else 4
        nb2 = 1
        for d in in_.shape:
            nb2 *= int(d)
        nb2 *= 2 if in_.dtype == BF16 else 4
        self.emit(eng, REC.dma_start(out=out, in_=in_), reads, writes, dma_bytes=max(nbytes, nb2))

    def barrier(self):
        assert all(v is None for v in self.cur.values())
        if len(self.ops) > self.seg_start[-1]:
            self.seg_start.append(len(self.ops))

    def _schedule(self, ops):
        import bisect
        for op in ops:
            op.succs = []
            op.ready = 0.0
        for op in ops:
            op.npred = len(op.preds)
            for p in op.preds:
                p.succs.append(op)
        avail = {e.name: [] for e in self.engs}
        t_free = {e.name: 0.0 for e in self.engs}
        order = {e.name: [] for e in self.engs}
        for op in ops:
            if op.npred == 0:
                avail[op.eng.name].append((op.idx, op))
        scheduled = 0
        dma_free = 0.0
        n = len(ops)
        base = ops[0].idx
        done = [False] * n
        lo = 0
        while scheduled < n:
            while lo < n and done[lo]:
                lo += 1
            lim = base + lo + self.LOOK_IDX
            best = None
            for e in self.engs:
                lst = avail[e.name]
                if not lst:
                    continue
                tf = t_free[e.name]
                for (idx, op) in lst[:self.LOOK_W]:
                    if idx > lim and best is not None:
                        break
                    st = op.ready if op.ready > tf else tf
                    key = (st, idx)
                    if best is None or key < best[0]:
                        best = (key, e, op)
            (st, idx), e, op = best
            avail[e.name].remove((idx, op))
            if op.is_dma:
                t_free[e.name] = st + op.cost
                xs = max(st + op.cost, dma_free)
                dma_free = xs + op.lat / 170.0
                op.finish = dma_free + 2000.0
            else:
                op.finish = st + op.cost
                t_free[e.name] = op.finish
            order[e.name].append(op)
            done[op.idx - base] = True
            scheduled += 1
            for sc in op.succs:
                if sc.ready < op.finish:
                    sc.ready = op.finish
                sc.npred -= 1
                if sc.npred == 0:
                    bisect.insort(avail[sc.eng.name], (sc.idx, sc))
        return order, max(op.finish for op in ops)

    def _wait(self, eng, src, val):
        if val <= 0 or eng.waited.get(src, 0) >= val:
            return
        eng.waited[src] = val
        eng.q.append(REC.wait_ge(src.sem, val))

    def _barrier_waits(self):
        srcs = [e.src for e in self.engs] + self.dma_srcs
        for e in self.engs:
            for sr in srcs:
                if sr is not e.src:
                    self._wait(e, sr, sr.count)

    def finalize(self):
        assert all(v is None for v in self.cur.values())
        bounds = self.seg_start + [len(self.ops)]
        total = 0.0
        for si in range(len(bounds) - 1):
            ops = self.ops[bounds[si]:bounds[si + 1]]
            if not ops:
                continue
            if si > 0:
                self._barrier_waits()
            order, span = self._schedule(ops)
            total += span
            sb = {}
            for op in ops:
                sb[op.eng.name] = sb.get(op.eng.name, 0.0) + op.cost
            print("seg", si, "span us %.1f" % (span / 1e3), {k: round(v / 1e3, 1) for k, v in sb.items()}, flush=True)
            for e in self.engs:
                for op in order[e.name]:
                    if op.is_dma:
                        part = self.dma_part[e.name]
                        src = part[self.dma_rr[e.name] % len(part)]
                        self.dma_rr[e.name] += 1
                        op.prev_val = src.count
                    else:
                        src = e.src
                    src.count += src.step
                    op.src, op.val = src, src.count
            for e in self.engs:
                for op in order[e.name]:
                    need = {}
                    for p in op.preds:
                        if e.inorder and p.src is e.src:
                            continue
                        if need.get(p.src, 0) < p.val:
                            need[p.src] = p.val
                    for sr, v in need.items():
                        self._wait(e, sr, v)
                    if op.is_dma:
                        self._wait(e, op.src, op.prev_val)
                    last = len(op.insts) - 1
                    for i, fn in enumerate(op.insts):
                        if i == last:
                            e.q.append(("__sig__", fn, op.src.sem, op.src.step))
                        else:
                            e.q.append(fn)
        for e in (self.sp, self.pool):
            for s_ in self.dma_srcs:
                self._wait(e, s_, s_.count)
        busy = {}
        for op in self.ops:
            busy[op.eng.name] = busy.get(op.eng.name, 0.0) + op.cost
        print("busy ms:", {k: round(v / 1e6, 3) for k, v in busy.items()}, flush=True)
        print("ops:", len(self.ops), "insts:", self.ninst, "est span ms: %.3f" % (total / 1e6),
              {e.name: len(e.q) for e in self.engs}, flush=True)


def _replay(q, e):
    for item in q:
        if item[0] == "__sig__":
            _, fn, sem, step = item
            _run(fn, e).then_inc(sem, step)
        else:
            _run(item, e)


def build(S, NL, dbg=None):
    NB = S // 512
    NKB = S // 128
    nc = bass.Bass("TRN2", target_bir_lowering=False)

    def din(name, shape, dt=F32):
        return nc.dram_tensor(name, shape, dt, kind="ExternalInput").ap()

    x_d = din("x", [S, D])
    pos_d = din("posrep", [32, S], I32)
    rc_d = din("ropec", [32, 2])
    cst_d = din("consts", [128, 384])
    cols_d = din("cols", [128, NL, NCOLS])
    rows_d = din("rows", [NL, 128, 2056])
    w_in_d = din("w_in_g", [NL, 128, 8 * NCOLW])
    w_dt_d = din("w_dt", [NL, D, 4])
    w_q_d = din("w_q", [NL, 256, 768])
    w_qsw_d = din("w_qsw", [NL, 256, 256])
    w_kn_d = din("w_kn", [NL, 128, 512])
    w_v_d = din("w_v", [NL, 128, 512])
    w_out_d = din("w_out", [NL, D, D])
    w_up_d = din("w_up_g", [NL, 44, 128, 1024])
    w_down_d = din("w_down", [NL, FF, D])
    y_d = nc.dram_tensor("y", [S, D], F32, kind="ExternalOutput").ap()
    xa_d = nc.dram_tensor("xa", [S, D], F32, kind="Internal").ap()
    xb_d = [nc.dram_tensor("xb%d" % i, [S, D], F32, kind="Internal").ap() for i in range(2)]
    kd_d = nc.dram_tensor("kd", [NH, 96, S], BF16, kind="Internal").ap()
    tab_d = nc.dram_tensor("tabd", [NB, 32, 1024], F32, kind="Internal").ap() if F_ROPECACHE else None
    dbg_d = {}
    if dbg:
        for name, shape in dbg.items():
            dbg_d[name] = nc.dram_tensor("dbg_" + name, shape, F32, kind="ExternalOutput").ap()

    es = ExitStack()
    with es:
        P = Prog(nc, es)
        pe, act, dve, pool, sp = P.pe, P.act, P.dve, P.pool, P.sp
        T = P.tile
        b_xa, b_kd = Buf("xa"), Buf("kd")
        b_tab = [Buf("tab%d" % i) for i in range(NB)]
        b_xb = [Buf("xb0"), Buf("xb1")]
        b_y = Buf("y")
        psb = []
        for i in range(7):
            t = es.enter_context(nc.psum_tensor("ps%d" % i, [128, 512], F32))
            psb.append(Tile(t, Buf("ps%d" % i)))
        pst = Tile(es.enter_context(nc.psum_tensor("pst", [128, 1024], BF16)), Buf("pst"))
        pools = {"mm": [0, 1, 2], "acc": [3, 4], "aux": [5, 6]}
        prr = {"mm": 0, "acc": 0, "aux": 0}

        pst_list = [pst]

        def set_pools(cfg, psts):
            pools.clear()
            pools.update(cfg)
            pst_list[:] = psts

        def PS(pool_name):
            lst = pools[pool_name]
            i = lst[prr[pool_name] % len(lst)]
            prr[pool_name] += 1
            return psb[i]

        cst32 = T("cst32", [128, 384], F32)
        cstb = T("cstb", [128, 384], BF16)
        cols = T("cols", [128, NCOLS], F32)
        ropec = T("ropec", [32, 2], F32)
        epsc = T("epsc", [128, 1], F32)
        P.dma(sp, cst32[:], cst_d, writes=[cst32.b])
        P.dma(sp, ropec[:], rc_d, writes=[ropec.b])
        P.emit(dve, REC.tensor_copy(cstb[:], cst32[:]), [cst32.b], [cstb.b])
        P.emit(dve, REC.memset(epsc[:], EPS), [], [epsc.b])
        ident_b = cstb[:, 0:128]
        tri_b = cstb[:, 128:256]
        ones_b = cstb[:, 256:384]
        tri_f = cst32[:, 128:256]
        ones_f = cst32[:, 256:384]

        xin = [T("xin%d" % i, [128, D], F32) for i in range(2)]
        mixs = [T("mix%d" % i, [128, D], F32) for i in range(2)]
        xr = T("xr", [128, D], F32)
        junkP = T("junkP", [128, 512], BF16)
        smP = T("smP", [128, 8], F32)
        hbs = [T("hb0", [128, D], BF16)]
        hT = T("hT", [128, 8, 512], BF16)
        junk = T("junk", [128, 512], BF16)
        sms = [T("sm%d" % i, [128, 16], F32) for i in range(2)]
        rows = T("rows", [128, 1032], F32)
        tmpA = T("tmpA", [128, 512], F32)
        WREG = 135168
        wreg = T("wreg", [128, WREG // 2], BF16)
        REG2 = 35200
        reg2 = T("reg2", [128, REG2 // 2], BF16)

        def col(l, c, n=1):
            return cols[:, c:c + n]

        def rstd_from_ss(ss_ap, ss_bufs, n, out_ap, out_buf):
            P.emit(act, REC.activation(out_ap, ss_ap, AF.Ln, scale=1.0 / n, bias=epsc[:, 0:1]),
                   list(ss_bufs) + [epsc.b], [out_buf])
            P.emit(act, REC.activation(out_ap, out_ap, AF.Exp, scale=-0.5), [out_buf], [out_buf])

        def dump(name, ap, bufs, npart=128, idx=None):
            if not dbg or name not in dbg:
                return
            dst = dbg_d[name] if idx is None else dbg_d[name][idx]
            P.emit(dve, REC.tensor_copy(tmpA[0:npart, :], ap), bufs, [tmpA.b])
            P.dma(sp, dst[0:npart, :], tmpA[0:npart, :], [tmpA.b], [])

        def norm_transpose(src_d, src_buf, row0, l, gcol):
            for tt in range(4):
                xt = xin[tt % 2]
                sm = sms[0]
                hb = hbs[0]
                pt_ = pst_list[tt % len(pst_list)]
                r0 = row0 + tt * 128
                P.dma(sp, xt[:], src_d[r0:r0 + 128, :], [src_buf], [xt.b])
                P.emit(act, REC.activation(junk[:, :], xt[:, 0:512], AF.Square, accum_out=sm[:, 0:1]),
                       [xt.b], [junk.b, sm.b])
                P.emit(act, REC.activation(junk[:, :], xt[:, 512:1024], AF.Square, accum_out=sm[:, 1:2]),
                       [xt.b], [junk.b, sm.b])
                P.emit(dve, REC.tensor_tensor(sm[:, 2:3], sm[:, 0:1], sm[:, 1:2], ALU.add), [sm.b], [sm.b])
                rstd_from_ss(sm[:, 2:3], [sm.b], D, sm[:, 3:4], sm.b)
                P.emit(dve, REC.tensor_scalar(hb[:], xt[:], sm[:, 3:4], None, ALU.mult),
                       [xt.b, sm.b], [hb.b])
                for kc in range(8):
                    P.emit(pe, REC.transpose(pt_[:, kc * 128:(kc + 1) * 128], hb[:, kc * 128:(kc + 1) * 128], ident_b),
                           [hb.b, cstb.b], [pt_.b], sig=(kc == 7))
                P.emit(dve, REC.tensor_tensor(
                    hT[:, :, tt * 128:(tt + 1) * 128],
                    pt_[:, :].rearrange("p (k t) -> p k t", k=8),
                    col(l, gcol, 8).unsqueeze(2).to_broadcast([128, 8, 128]), ALU.mult),
                    [pt_.b, cols.b], [hT.b])

        def post_norm_residual(ps_list, src_d, src_buf, dst_d, dst_buf, r0, k):
            xt = xr
            mx = mixs[k % 2]
            P.dma(sp, xt[:], src_d[r0:r0 + 128, :], [src_buf], [xt.b])
            for half in range(2):
                ps = ps_list[half]
                P.emit(act, REC.activation(mx[:, half * 512:(half + 1) * 512], ps[:, :], AF.Copy),
                       [ps.b], [mx.b])
                P.emit(act, REC.activation(junkP[:, :], mx[:, half * 512:(half + 1) * 512], AF.Square,
                                                              accum_out=smP[:, 4 + half:5 + half]),
                       [mx.b], [junkP.b, smP.b])
            P.emit(dve, REC.tensor_tensor(smP[:, 6:7], smP[:, 4:5], smP[:, 5:6], ALU.add), [smP.b], [smP.b])
            rstd_from_ss(smP[:, 6:7], [smP.b], D, smP[:, 7:8], smP.b)
            P.emit(dve, REC.scalar_tensor_tensor(mx[:], mx[:], smP[:, 7:8], rows[:, 0:1024], ALU.mult, ALU.mult),
                   [mx.b, smP.b, rows.b], [mx.b])
            P.emit(dve, REC.tensor_tensor(mx[:], mx[:], xt[:], ALU.add), [mx.b, xt.b], [mx.b])
            P.dma(sp, dst_d[r0:r0 + 128, :], mx[:], [mx.b], [dst_buf])

        for l in range(NL):
            src_d, src_buf = (x_d, Buf("xsrc")) if l == 0 else (xb_d[(l - 1) % 2], b_xb[(l - 1) % 2])
            dst_d, dst_buf = (y_d, b_y) if l == NL - 1 else (xb_d[l % 2], b_xb[l % 2])

            o = 0
            def carve(n_el):
                nonlocal o
                a = wreg.t[:, o:o + n_el]
                o += n_el
                return a
            w_in_sb = carve(8 * NCOLW)
            w_dt_sb = carve(8 * 4).rearrange("p (k n) -> p k n", k=8)
            w_q_sb = carve(2 * 768).rearrange("p (k n) -> p k n", k=2)
            w_qsw_sb = carve(2 * 256).rearrange("p (k n) -> p k n", k=2)
            w_kn_sb = carve(512)
            w_v_sb = carve(512)
            def carve_t(shape, dt):
                nonlocal o
                n = int(np.prod(shape))
                if dt == F32:
                    o += (o % 2)
                    a = wreg.t[:, o:o + 2 * n].bitcast(F32)
                    o += 2 * n
                else:
                    a = wreg.t[:, o:o + n]
                    o += n
                if len(shape) == 2:
                    return a.rearrange("p (a b) -> p a b", a=shape[0])
                if len(shape) == 3:
                    return a.rearrange("p (a b c) -> p a b c", a=shape[0], b=shape[1])
                return a
            bW = Buf("wM%d" % l)
            GROUPS = [(0, 128), (128, 128), (256, 128), (384, 32), (416, 32)] + [(448 + g * 128, 128) for g in range(14)]
            bWg = {}
            bWo = Buf("wo%d" % l)
            P.barrier()
            set_pools({"mm": [0, 1, 2], "acc": [3, 4], "aux": [5, 6]}, [pst])
            for (dst, srcap) in [
                (w_dt_sb, w_dt_d[l].rearrange("(k p) n -> p k n", p=128)),
                (w_q_sb, w_q_d[l].rearrange("(k p) n -> p k n", p=128)),
                (w_qsw_sb, w_qsw_d[l].rearrange("(k p) n -> p k n", p=128)),
                (w_kn_sb, w_kn_d[l]),
                (w_v_sb, w_v_d[l]),
            ]:
                P.dma(pool, dst, srcap, [], [bW])
            for (c0g, Mg) in GROUPS:
                bWg[c0g] = Buf("wg%d_%d" % (l, c0g))
                P.dma(pool, w_in_sb[:, 8 * c0g:8 * (c0g + Mg)], w_in_d[l, :, 8 * c0g:8 * (c0g + Mg)], [], [bWg[c0g]])
            P.dma(sp, cols[:, :], cols_d[:, l, :], [], [cols.b])
            P.dma(sp, rows[:, 0:1024], rows_d[l, :, 0:1024], [], [rows.b])
            P.dma(sp, rows[:, 1024:1032], rows_d[l, :, 2048:2056], [], [rows.b])

            def MT(name, shape, dt):
                return Tile(carve_t(shape, dt), Buf(name))
            cq_sb = MT("cq", [3, 512], BF16)
            cq_sq = MT("cqsq", [3, 512], BF16)
            cqn = MT("cqn", [3, 512], BF16)
            conv_sb = MT("conv", [4, 512], F32)
            szs = MT("szs", [2, 512], F32)
            ubs = [MT("ub%d" % i, [1, 515], F32) for i in range(2)]
            uh = MT("uh", [6, 3], F32)
            vbuf = MT("vbuf", [2, 514], F32)
            ycv = MT("ycv", [2, 512], BF16)
            xs_f = MT("xsf", [2, 512], F32)
            xs_b = MT("xsb", [2, 512], BF16)
            Bt = MT("Bt", [2, 512], BF16)
            Ct = MT("Ct", [2, 512], BF16)
            o_wout = o
            Kcur = MT("Kcur", [8, 512], BF16)
            kbuf = [MT("kbuf0", [1, 4096], BF16)]
            kbq = [Buf("kbq%d" % i) for i in range(4)]
            w_out_sb = wreg.t[:, o_wout:o_wout + 8 * D].rearrange("p (k n) -> p k n", k=8)
            assert o - o_wout == 8 * D
            Qh = [MT("Qh%d" % i, [1, 512], BF16) for i in range(2)]
            Pt = [MT("Pt%d" % i, [1, 512], BF16) for i in range(3)]
            tabs = MT("tabs", [2, 512], F32)
            rtmp = MT("rtmp", [2, 512], F32)
            rint = Tile(carve_t([1, 512], F32).bitcast(I32), Buf("rint"))
            krot = MT("krot", [1, 512], F32)
            rd = MT("rd", [1, 512], F32)
            osb = MT("osb", [1, 512], F32)
            gat = Tile(conv_sb.t[:, 0:2, :], conv_sb.b)
            gsq = Tile(cq_sq.t[:, 0:2, :], cq_sq.b)
            dtT = MT("dtT", [4, 4], F32)
            adt = MT("adt", [4, 4], F32)
            arow = MT("arow", [1, 4], F32)
            state = MT("state", [4, 64], F32)
            prevp = MT("prevp", [4, 128], BF16)
            xdtp = MT("xdtp", [4, 128], BF16)
            xdtd = MT("xdtd", [4, 64], BF16)
            Btok = MT("Btok", [2, 128], BF16)
            adtri = MT("adtri", [4, 128], F32)
            dec = MT("dec", [4, 128], F32)
            drow = MT("drow", [4, 128], F32)
            Cs = MT("Cs", [4, 128], BF16)
            scT = MT("scT", [4, 128], BF16)
            s4 = MT("s4", [8, 4], F32)
            assert o * 2 <= WREG, o * 2
            Vc = Tile(reg2.t[:, 0:NKB * 8 * 65].rearrange("p (j h d) -> p j h d", j=NKB, h=8), Buf("Vc%d" % l))
            yT = hT

            P.emit(dve, REC.memset(Vc[:, :, :, 64:65], 1.0), [], [Vc.b])
            P.emit(dve, REC.memset(vbuf[:, :, 0:2], 0.0), [], [vbuf.b])
            P.emit(dve, REC.memset(uh[:, :, :], 0.0), [], [uh.b])
            P.emit(dve, REC.memset(state[:], 0.0), [], [state.b])
            P.emit(dve, REC.memset(prevp[:], 0.0), [], [prevp.b])
            P.emit(dve, REC.memset(xdtp[:], 0.0), [], [xdtp.b])
            P.emit(act, REC.activation(arow[:, 0, :], rows[:, 1028:1032], AF.Exp), [rows.b], [arow.b])
            P.emit(dve, REC.tensor_scalar(arow[:, 0, :], arow[:, 0, :], -1.0, None, ALU.mult), [arow.b], [arow.b])

            for blk in range(NB):
                t0 = blk * 512
                if l == 0 or not F_ROPECACHE:
                    P.dma(sp, rint[0:32, 0, :], pos_d[:, t0:t0 + 512], [], [rint.b])
                    P.emit(dve, REC.tensor_copy(rtmp[0:32, 0, :], rint[0:32, 0, :]), [rint.b], [rtmp.b])
                    P.emit(dve, REC.tensor_scalar(rtmp[0:32, 0, :], rtmp[0:32, 0, :], ropec[:, 0:1], None, ALU.mult),
                           [rtmp.b, ropec.b], [rtmp.b])
                    for ti, phase in ((0, PI / 2), (1, 0.0)):
                        P.emit(dve, REC.tensor_scalar(rtmp[0:32, 1, :], rtmp[0:32, 0, :], 1.0 / (2 * PI), phase / (2 * PI) + 0.5, ALU.mult, ALU.add),
                               [rtmp.b], [rtmp.b])
                        P.emit(dve, REC.tensor_copy(rint[0:32, 0, :], rtmp[0:32, 1, :]), [rtmp.b], [rint.b])
                        P.emit(dve, REC.tensor_copy(rtmp[0:32, 1, :], rint[0:32, 0, :]), [rint.b], [rtmp.b])
                        P.emit(dve, REC.tensor_scalar(rtmp[0:32, 1, :], rtmp[0:32, 1, :], -2 * PI, None, ALU.mult), [rtmp.b], [rtmp.b])
                        P.emit(dve, REC.scalar_tensor_tensor(tabs[0:32, ti, :], rtmp[0:32, 0, :], phase, rtmp[0:32, 1, :], ALU.add, ALU.add),
                               [rtmp.b], [tabs.b])
                        P.emit(dve, REC.tensor_scalar(rtmp[0:32, 1, :], tabs[0:32, ti, :], -PI, 2 * PI, ALU.is_lt, ALU.mult), [tabs.b], [rtmp.b])
                        P.emit(dve, REC.tensor_tensor(tabs[0:32, ti, :], tabs[0:32, ti, :], rtmp[0:32, 1, :], ALU.add), [tabs.b, rtmp.b], [tabs.b])
                        P.emit(dve, REC.tensor_scalar(rtmp[0:32, 1, :], tabs[0:32, ti, :], PI, -2 * PI, ALU.is_gt, ALU.mult), [tabs.b], [rtmp.b])
                        P.emit(dve, REC.tensor_tensor(tabs[0:32, ti, :], tabs[0:32, ti, :], rtmp[0:32, 1, :], ALU.add), [tabs.b, rtmp.b], [tabs.b])
                        if ti == 0:
                            P.emit(act, REC.activation(tabs[0:32, 0, :], tabs[0:32, 0, :], AF.Sin), [tabs.b], [tabs.b])
                        else:
                            P.emit(act, REC.activation(tabs[0:32, 1, :], tabs[0:32, 1, :], AF.Sin, scale=ropec[:, 1:2]), [tabs.b, ropec.b], [tabs.b])
                    if F_ROPECACHE:
                        P.dma(sp, tab_d[blk], tabs[0:32, :, :].rearrange("p a b -> p (a b)"), [tabs.b], [b_tab[blk]])
                else:
                    P.dma(sp, tabs[0:32, :, :].rearrange("p a b -> p (a b)"), tab_d[blk], [b_tab[blk]], [tabs.b])
                Ctab = tabs[0:32, 0, :]
                Stab = tabs[0:32, 1, :]

                norm_transpose(src_d, src_buf, t0, l, C_GPRE1)
                if l == 0 and blk == NB - 1:
                    dump("hT0", hT[:, 0, :], [hT.b])

                def inproj(c0, M):
                    ps = PS("mm")
                    for kc in range(8):
                        P.emit(pe, REC.matmul(ps[0:M, :], w_in_sb[:, 8 * c0 + kc * M:8 * c0 + (kc + 1) * M], hT[:, kc, :], start=(kc == 0), stop=(kc == 7)),
                               [bWg[c0], hT.b], [ps.b], sig=(kc == 7))
                    return ps
                for g in range(3):
                    ps = inproj(g * 128, 128)
                    P.emit(act, REC.activation(cq_sb[:, g, :], ps[:, :], AF.Copy), [ps.b], [cq_sb.b])
                    P.emit(act, REC.activation(cq_sq[:, g, :], ps[:, :], AF.Square), [ps.b], [cq_sq.b])
                ps_kr = inproj(384, 32)
                P.emit(dve, REC.tensor_tensor(rtmp[0:32, 0, :], ps_kr[0:32, :], Ctab, ALU.mult), [ps_kr.b, tabs.b], [rtmp.b])
                ps_ks = inproj(416, 32)
                P.emit(dve, REC.tensor_tensor(rtmp[0:32, 1, :], ps_ks[0:32, :], Stab, ALU.mult), [ps_ks.b, tabs.b], [rtmp.b])
                P.emit(dve, REC.tensor_tensor(krot[0:32, 0, :], rtmp[0:32, 0, :], rtmp[0:32, 1, :], ALU.add), [rtmp.b], [krot.b])
                P.emit(dve, REC.tensor_copy(Kcur[64:96, :, :], krot[0:32, 0:1, :].to_broadcast([32, 8, 512])), [krot.b], [Kcur.b])
                for g in range(4):
                    ps = inproj(448 + g * 128, 128)
                    P.emit(act, REC.activation(conv_sb[:, g, :], ps[:, :], AF.Copy), [ps.b], [conv_sb.b])
                for c in range(2):
                    ps = inproj(448 + (4 + c) * 128, 128)
                    P.emit(dve, REC.tensor_tensor(vbuf[:, c, 2:514], conv_sb[:, 2 + c, :], ps[:, :], ALU.mult), [conv_sb.b, ps.b], [vbuf.b])
                    P.emit(dve, REC.tensor_scalar(tmpA[:, :], vbuf[:, c, 0:512], col(l, C_SCW + c * 3 + 0), None, ALU.mult), [vbuf.b, cols.b], [tmpA.b])
                    P.emit(dve, REC.scalar_tensor_tensor(tmpA[:, :], vbuf[:, c, 1:513], col(l, C_SCW + c * 3 + 1), tmpA[:, :], ALU.mult, ALU.add), [vbuf.b, cols.b, tmpA.b], [tmpA.b])
                    P.emit(dve, REC.scalar_tensor_tensor(tmpA[:, :], vbuf[:, c, 2:514], col(l, C_SCW + c * 3 + 2), tmpA[:, :], ALU.mult, ALU.add), [vbuf.b, cols.b, tmpA.b], [tmpA.b])
                    P.emit(dve, REC.tensor_tensor(ycv[:, c, :], tmpA[:, :], conv_sb[:, c, :], ALU.mult), [tmpA.b, conv_sb.b], [ycv.b])
                    P.emit(dve, REC.tensor_copy(vbuf[:, c, 0:2], vbuf[:, c, 512:514]), [vbuf.b], [vbuf.b])
                for g in range(2):
                    ps = inproj(448 + 768 + g * 128, 128)
                    P.emit(act, REC.activation(szs[:, g, :], ps[:, :], AF.Silu), [ps.b], [szs.b])
                for c in range(6):
                    ps = inproj(448 + 1024 + c * 128, 128)
                    ub = ubs[c % 2]
                    P.emit(act, REC.activation(ub[:, 0, 3:515], ps[:, :], AF.Copy), [ps.b], [ub.b])
                    P.emit(dve, REC.tensor_copy(ub[:, 0, 0:3], uh[:, c, :]), [uh.b], [ub.b])
                    P.emit(dve, REC.tensor_scalar(tmpA[:, :], ub[:, 0, 0:512], col(l, C_SSW + c * 4 + 0), None, ALU.mult), [ub.b, cols.b], [tmpA.b])
                    for k in range(1, 4):
                        P.emit(dve, REC.scalar_tensor_tensor(tmpA[:, :], ub[:, 0, k:k + 512], col(l, C_SSW + c * 4 + k), tmpA[:, :], ALU.mult, ALU.add),
                               [ub.b, cols.b, tmpA.b], [tmpA.b])
                    if c < 2:
                        P.emit(act, REC.activation(xs_f[:, c, :], tmpA[:, :], AF.Silu, bias=col(l, C_SSB + c)), [tmpA.b, cols.b], [xs_f.b])
                        P.emit(dve, REC.tensor_copy(xs_b[:, c, :], xs_f[:, c, :]), [xs_f.b], [xs_b.b])
                    elif c < 4:
                        P.emit(act, REC.activation(Bt[:, c - 2, :], tmpA[:, :], AF.Silu, bias=col(l, C_SSB + c)), [tmpA.b, cols.b], [Bt.b])
                    else:
                        P.emit(act, REC.activation(Ct[:, c - 4, :], tmpA[:, :], AF.Silu, bias=col(l, C_SSB + c)), [tmpA.b, cols.b], [Ct.b])
                    P.emit(dve, REC.tensor_copy(uh[:, c, :], ub[:, 0, 512:515]), [ub.b], [uh.b])
                ps = PS("aux")
                for tt in range(4):
                    for kc in range(8):
                        P.emit(pe, REC.matmul(ps[:, tt * 4:tt * 4 + 4], hT[:, kc, tt * 128:(tt + 1) * 128], w_dt_sb[:, kc, :],
                                                                          start=(kc == 0), stop=(kc == 7)),
                               [bW, hT.b], [ps.b], sig=(kc == 7))
                dt4 = dtT[:, :, :]
                dtb = rows[:, 1024:1028].unsqueeze(1).to_broadcast([128, 4, 4])
                P.emit(dve, REC.tensor_tensor(dt4, ps[:, 0:16].rearrange("p (a b) -> p a b", a=4), dtb, ALU.add), [ps.b, rows.b], [dtT.b])
                P.emit(act, REC.activation(adt[:, :, :], dt4, AF.Abs), [dtT.b], [adt.b])
                P.emit(act, REC.activation(adt[:, :, :], adt[:, :, :], AF.Exp, scale=-1.0), [adt.b], [adt.b])
                P.emit(act, REC.activation(adt[:, :, :], adt[:, :, :], AF.Ln, bias=1.0), [adt.b], [adt.b])
                P.emit(dve, REC.scalar_tensor_tensor(dt4, dt4, 0.0, adt[:, :, :], ALU.max, ALU.add), [dtT.b, adt.b], [dtT.b])
                P.emit(dve, REC.tensor_tensor(adt[:, :, :], dt4, arow[:, 0:1, :].to_broadcast([128, 4, 4]), ALU.mult), [dtT.b, arow.b], [adt.b])

                for (chs, n, qc) in (((0, 1), 256, C_QN), ((2,), 128, C_KVN)):
                    ps = PS("aux")
                    for i, c in enumerate(chs):
                        P.emit(pe, REC.matmul(ps[:, :], ones_b, cq_sq[:, c, :], start=(i == 0), stop=(i == len(chs) - 1)),
                               [cstb.b, cq_sq.b], [ps.b], sig=(i == len(chs) - 1))
                    P.emit(act, REC.activation(tmpA[:, :], ps[:, :], AF.Ln, scale=1.0 / n, bias=epsc[:, 0:1]), [ps.b, epsc.b], [tmpA.b])
                    P.emit(act, REC.activation(tmpA[:, :], tmpA[:, :], AF.Exp, scale=-0.5), [tmpA.b], [tmpA.b])
                    for i, c in enumerate(chs):
                        P.emit(dve, REC.scalar_tensor_tensor(cqn[:, c, :], cq_sb[:, c, :], col(l, qc + i), tmpA[:, :], ALU.mult, ALU.mult),
                               [cq_sb.b, cols.b, tmpA.b], [cqn.b])
                ckvn = cqn[:, 2, :]
                if l == 0 and blk == NB - 1:
                    for c in range(3):
                        dump("cqn", cqn[:, c, :], [cqn.b], idx=c)
                    dump("dtT", dtT[:, :, :].rearrange("p a b -> p (a b)"), [dtT.b]) if False else None

                for hp in range(4):
                    ps = PS("mm")
                    P.emit(pe, REC.matmul(ps[:, :], w_kn_sb[:, hp * 128:(hp + 1) * 128], ckvn, start=True, stop=True), [bW, cqn.b], [ps.b])
                    P.emit(act, REC.activation(Kcur[0:64, 2 * hp, :], ps[0:64, :], AF.Copy), [ps.b], [Kcur.b])
                    P.emit(dve, REC.tensor_copy(Kcur[0:64, 2 * hp + 1, :], ps[64:128, :]), [ps.b], [Kcur.b])
                for tt in range(4):
                    ps = PS("mm")
                    P.emit(pe, REC.matmul(ps[:, :], cqn[:, 2, tt * 128:(tt + 1) * 128], w_v_sb, start=True, stop=True), [bW, cqn.b], [ps.b])
                    P.emit(act, REC.activation(Vc[:, blk * 4 + tt, :, 0:64], ps[:, :].rearrange("p (h d) -> p h d", h=8), AF.Copy), [ps.b], [Vc.b])
                if blk < NB - 1:
                    P.dma(sp, kd_d.rearrange("h d s -> d h s")[:, :, t0:t0 + 512], Kcur[0:96, :, :], [Kcur.b], [b_kd])

                for h in range(NH):
                    q = Qh[h % 2]
                    psQ = PS("mm")
                    for c in range(2):
                        P.emit(pe, REC.matmul(psQ[0:96, :], w_q_sb[:, c, h * 96:(h + 1) * 96], cqn[:, c, :], start=(c == 0), stop=(c == 1)),
                               [bW, cqn.b], [psQ.b], sig=(c == 1))
                    psS = PS("mm")
                    for c in range(2):
                        P.emit(pe, REC.matmul(psS[0:32, :], w_qsw_sb[:, c, h * 32:(h + 1) * 32], cqn[:, c, :], start=(c == 0), stop=(c == 1)),
                               [bW, cqn.b], [psS.b], sig=(c == 1))
                    P.emit(act, REC.activation(q[0:64, 0, :], psQ[0:64, :], AF.Copy), [psQ.b], [q.b])
                    P.emit(dve, REC.tensor_tensor(rtmp[0:32, 0, :], psS[0:32, :], Stab, ALU.mult), [psS.b, tabs.b], [rtmp.b])
                    P.emit(dve, REC.tensor_tensor(rtmp[0:32, 1, :], psQ[64:96, :], Ctab, ALU.mult), [psQ.b, tabs.b], [rtmp.b])
                    P.emit(dve, REC.tensor_tensor(q[64:96, 0, :], rtmp[0:32, 0, :], rtmp[0:32, 1, :], ALU.add), [rtmp.b], [q.b])
                    kb = kbuf[0]
                    if l == 0 and blk == NB - 1 and h == 0:
                        dump("Q0", q[0:96, 0, :], [q.b], npart=96)
                        dump("K0", Kcur[0:96, 0, :], [Kcur.b], npart=96)
                    if blk > 0:
                        for qi in range(4):
                            lo_, hi_ = qi * 1024, min(t0, (qi + 1) * 1024)
                            if hi_ > lo_:
                                P.dma(sp, kb[0:96, 0, lo_:hi_], kd_d[h, :, lo_:hi_], [b_kd], [kbq[qi]])
                    psO = PS("acc")
                    nfull = 4 * blk
                    for j in range(nfull + 4):
                        pss = PS("mm")
                        pt = Pt[j % 3]
                        if j < nfull:
                            c0 = 0
                            P.emit(pe, REC.matmul(pss[:, :], kb[0:96, 0, j * 128:(j + 1) * 128], q[0:96, 0, :], start=True, stop=True),
                                   [kbq[(j * 128) // 1024], q.b], [pss.b])
                        else:
                            jj = j - nfull
                            c0 = jj * 128
                            P.emit(pe, REC.matmul(pss[:, c0:512], Kcur[0:96, h, c0:c0 + 128], q[0:96, 0, c0:512], start=True, stop=True),
                                   [Kcur.b, q.b], [pss.b])
                        P.emit(act, REC.activation(pt[:, 0, c0:512], pss[:, c0:512], AF.Exp, scale=SCALE), [pss.b], [pt.b])
                        if j >= nfull:
                            P.emit(dve, REC.tensor_tensor(pt[:, 0, c0:c0 + 128], pt[:, 0, c0:c0 + 128], tri_b, ALU.mult), [pt.b, cstb.b], [pt.b])
                        P.emit(pe, REC.matmul(psO[0:65, c0:512], Vc[:, j, h, :], pt[:, 0, c0:512], start=(j == 0), stop=(j == nfull + 3)),
                               [Vc.b, pt.b], [psO.b])
                    P.emit(act, REC.activation(rd[64:65, 0, :], psO[64:65, :], AF.Ln), [psO.b], [rd.b])
                    P.emit(act, REC.activation(rd[64:65, 0, :], rd[64:65, 0, :], AF.Exp, scale=-1.0), [rd.b], [rd.b])
                    psB = PS("mm")
                    P.emit(pe, REC.matmul(psB[0:64, :], ones_f[64:65, 0:64], rd[64:65, 0, :], start=True, stop=True), [cst32.b, rd.b], [psB.b])
                    P.emit(act, REC.activation(osb[0:64, 0, :], psO[0:64, :], AF.Copy), [psO.b], [osb.b])
                    P.emit(dve, REC.tensor_tensor(yT[(h % 2) * 64:(h % 2) * 64 + 64, h // 2, :], osb[0:64, 0, :], psB[0:64, :], ALU.mult),
                           [osb.b, psB.b], [yT.b])

                P.dma(pool, w_out_sb, w_out_d[l].rearrange("(k p) n -> p k n", p=128), [], [Kcur.b, bWo] + kbq)
                for c in range(2):
                    P.emit(dve, REC.tensor_copy(yT[:, 4 + c, :], ycv[:, c, :]), [ycv.b], [yT.b])
                psY = [PS("acc"), PS("acc")]
                for tt in range(4):
                    ts = slice(tt * 128, (tt + 1) * 128)
                    P.emit(dve, REC.tensor_copy(prevp[:, 0:4:2, 0:64], state[:, 0:4:2, :]), [state.b], [prevp.b])
                    P.emit(dve, REC.tensor_copy(prevp[:, 1:4:2, 64:128], state[:, 1:4:2, :]), [state.b], [prevp.b])
                    for c in range(2):
                        P.emit(pe, REC.transpose(pst[:, c * 128:(c + 1) * 128], xs_b[:, c, ts], ident_b), [xs_b.b, cstb.b], [pst.b], sig=False)
                    for g in range(2):
                        P.emit(pe, REC.transpose(pst[:, 256 + g * 128:256 + (g + 1) * 128], Bt[:, g, ts], ident_b), [Bt.b, cstb.b], [pst.b], sig=(g == 1))
                    xtok = pst[:, 0:256].rearrange("p (h d) -> p h d", h=4)
                    P.emit(dve, REC.tensor_tensor(xdtp[:, 0:4:2, 0:64], xtok[:, 0:4:2, :], dtT[:, tt, 0:4:2].unsqueeze(2).to_broadcast([128, 2, 64]), ALU.mult),
                           [pst.b, dtT.b], [xdtp.b])
                    P.emit(dve, REC.tensor_tensor(xdtp[:, 1:4:2, 64:128], xtok[:, 1:4:2, :], dtT[:, tt, 1:4:2].unsqueeze(2).to_broadcast([128, 2, 64]), ALU.mult),
                           [pst.b, dtT.b], [xdtp.b])
                    P.emit(act, REC.activation(Btok[:, :, :], pst[:, 256:512].rearrange("p (g n) -> p g n", g=2), AF.Copy), [pst.b], [Btok.b])
                    P.emit(dve, REC.tensor_tensor(adtri[:, :, :], tri_f.unsqueeze(1).to_broadcast([128, 4, 128]),
                                                                 adt[:, tt, :].unsqueeze(2).to_broadcast([128, 4, 128]), ALU.mult), [cst32.b, adt.b], [adtri.b])
                    psR = PS("aux")
                    P.emit(pe, REC.matmul(psR[:, :], ones_f, adtri[:, :, :].rearrange("p h l -> p (h l)"), start=True, stop=True), [cst32.b, adtri.b], [psR.b])
                    psA = PS("aux")
                    P.emit(pe, REC.matmul(psA[:, 0:4], tri_f, adt[:, tt, :], start=True, stop=True), [cst32.b, adt.b], [psA.b])
                    acol = s4[:, 0, :]
                    P.emit(dve, REC.tensor_copy(acol, psA[:, 0:4]), [psA.b], [s4.b])
                    psR3 = psR[:, :].rearrange("p (h l) -> p h l", h=4)
                    P.emit(dve, REC.tensor_tensor(dec[:, :, :], psR3, acol.unsqueeze(2).to_broadcast([128, 4, 128]), ALU.subtract), [psR.b, s4.b], [dec.b])
                    P.emit(dve, REC.tensor_scalar(dec[:, :, :], dec[:, :, :], 0.0, None, ALU.min), [dec.b], [dec.b])
                    P.emit(act, REC.activation(dec[:, :, :], dec[:, :, :], AF.Exp), [dec.b], [dec.b])
                    P.emit(dve, REC.tensor_tensor(dec[:, :, :], dec[:, :, :], tri_f.unsqueeze(1).to_broadcast([128, 4, 128]), ALU.mult), [dec.b, cst32.b], [dec.b])
                    P.emit(act, REC.activation(drow[:, :, :], psR3, AF.Exp), [psR.b], [drow.b])
                    P.emit(dve, REC.tensor_tensor(s4[:, 1, :], psR3[:, :, 127], acol, ALU.subtract), [psR.b, s4.b], [s4.b])
                    P.emit(act, REC.activation(s4[:, 2, :], s4[:, 1, :], AF.Exp), [s4.b], [s4.b])
                    P.emit(act, REC.activation(s4[:, 3, :], psR3[:, :, 127], AF.Exp), [psR.b], [s4.b])
                    P.emit(dve, REC.tensor_tensor(s4[:, 4, :], s4[:, 2, :], dtT[:, tt, :], ALU.mult), [s4.b, dtT.b], [s4.b])
                    P.emit(dve, REC.tensor_tensor(xdtd[:, :, :], xtok, s4[:, 4, :].unsqueeze(2).to_broadcast([128, 4, 64]), ALU.mult), [pst.b, s4.b], [xdtd.b])
                    for g in range(2):
                        P.emit(dve, REC.tensor_tensor(Cs[:, 2 * g:2 * g + 2, :], drow[:, 2 * g:2 * g + 2, :],
                                                                          Ct[:, g:g + 1, ts].to_broadcast([128, 2, 128]), ALU.mult), [drow.b, Ct.b], [Cs.b])
                    psG = PS("mm")
                    for g in range(2):
                        P.emit(pe, REC.matmul(psG[:, g * 128:(g + 1) * 128], Bt[:, g, ts], Ct[:, g, ts], start=True, stop=True), [Bt.b, Ct.b], [psG.b], sig=(g == 1))
                    for g in range(2):
                        P.emit(dve, REC.tensor_tensor(scT[:, 2 * g:2 * g + 2, :], dec[:, 2 * g:2 * g + 2, :],
                                                                             psG[:, g * 128:(g + 1) * 128].unsqueeze(1).to_broadcast([128, 2, 128]), ALU.mult), [dec.b, psG.b], [scT.b])
                    for k in range(2):
                        ops = [(xdtp[:, 2 * k, :], scT[:, 2 * k, :], [xdtp.b, scT.b]), (xdtp[:, 2 * k + 1, :], scT[:, 2 * k + 1, :], [xdtp.b, scT.b]),
                               (prevp[:, 2 * k, :], Cs[:, 2 * k, :], [prevp.b, Cs.b]), (prevp[:, 2 * k + 1, :], Cs[:, 2 * k + 1, :], [prevp.b, Cs.b])]
                        for i, (lh, rh, rb) in enumerate(ops):
                            P.emit(pe, REC.matmul(psY[k][:, ts], lh, rh, start=(i == 0), stop=(i == 3)), rb, [psY[k].b], sig=(i == 3))
                    psSt = PS("aux")
                    for g in range(2):
                        P.emit(pe, REC.matmul(psSt[:, g * 128:(g + 1) * 128], Btok[:, g, :], xdtd[:, 2 * g:2 * g + 2, :].rearrange("p h d -> p (h d)"), start=True, stop=True),
                               [Btok.b, xdtd.b], [psSt.b], sig=(g == 1))
                    P.emit(dve, REC.tensor_tensor(state[:, :, :], state[:, :, :], s4[:, 3, :].unsqueeze(2).to_broadcast([128, 4, 64]), ALU.mult), [state.b, s4.b], [state.b])
                    P.emit(dve, REC.tensor_tensor(state[:, :, :], state[:, :, :], psSt[:, 0:256].rearrange("p (h d) -> p h d", h=4), ALU.add), [state.b, psSt.b], [state.b])
                for k in range(2):
                    P.emit(dve, REC.scalar_tensor_tensor(gat[:, k, :], xs_f[:, k, :], col(l, C_DSK + k), psY[k][:, :], ALU.mult, ALU.add), [xs_f.b, cols.b, psY[k].b], [gat.b])
                    P.emit(dve, REC.tensor_tensor(gat[:, k, :], gat[:, k, :], szs[:, k, :], ALU.mult), [gat.b, szs.b], [gat.b])
                    P.emit(act, REC.activation(gsq[:, k, :], gat[:, k, :], AF.Square), [gat.b], [gsq.b])
                ps = PS("aux")
                for k in range(2):
                    P.emit(pe, REC.matmul(ps[:, :], ones_b, gsq[:, k, :], start=(k == 0), stop=(k == 1)), [cstb.b, gsq.b], [ps.b], sig=(k == 1))
                P.emit(act, REC.activation(tmpA[:, :], ps[:, :], AF.Ln, scale=1.0 / 256, bias=epsc[:, 0:1]), [ps.b, epsc.b], [tmpA.b])
                P.emit(act, REC.activation(tmpA[:, :], tmpA[:, :], AF.Exp, scale=-0.5), [tmpA.b], [tmpA.b])
                for k in range(2):
                    P.emit(dve, REC.scalar_tensor_tensor(yT[:, 6 + k, :], gat[:, k, :], col(l, C_SSN + k), tmpA[:, :], ALU.mult, ALU.mult), [gat.b, cols.b, tmpA.b], [yT.b])

                if l == 0 and blk == NB - 1:
                    for kc in range(8):
                        dump("yT", yT[:, kc, :], [yT.b], idx=kc)
                for tt in range(4):
                    pl = []
                    for half in range(2):
                        ps = PS("mm")
                        for kc in range(8):
                            P.emit(pe, REC.matmul(ps[:, :], yT[:, kc, tt * 128:(tt + 1) * 128], w_out_sb[:, kc, half * 512:(half + 1) * 512],
                                                                                        start=(kc == 0), stop=(kc == 7)), [yT.b, bWo, Kcur.b] + kbq, [ps.b], sig=(kc == 7))
                        pl.append(ps)
                    post_norm_residual(pl, src_d, src_buf, xa_d, b_xa, t0 + tt * 128, tt)

            bWF = Buf("wF%d" % l)
            P.barrier()
            set_pools({"mm": [0, 1, 2, 3, 4, 5, 6]}, [pst])
            w_up_sb = wreg.t[:, 0:8 * 2 * FF].rearrange("p (j k n) -> p j k n", j=44, k=8)
            w_down_sb = wreg.t[:, 8 * 2 * FF:8 * 2 * FF + NFC * D].rearrange("p (k n) -> p k n", k=NFC)
            assert (8 * 2 * FF + NFC * D) * 2 <= WREG
            bWU = [Buf("wu%d_%d" % (l, j)) for j in range(44)]
            for i in range(NFC):
                for j in (i, NFC + i):
                    P.dma(pool, w_up_sb[:, j, :, :], w_up_d[l, j].rearrange("p (k n) -> p k n", k=8), [], [bWU[j]])
            wdv = w_down_d[l].rearrange("(k p) n -> p k n", p=128)
            P.dma(pool, w_down_sb[:, 0:11, :], wdv[:, 0:11, :], [], [bWF])
            P.dma(pool, w_down_sb[:, 11:22, :], wdv[:, 11:22, :], [], [bWF])
            P.dma(sp, rows[:, 0:1024], rows_d[l, :, 1024:2048], [], [rows.b])
            gT = Tile(reg2.t[:, 0:NFC * 512].rearrange("p (k n) -> p k n", k=NFC), Buf("gT%d" % l))
            o2 = NFC * 512
            def carve2(shape):
                nonlocal o2
                n = int(np.prod(shape))
                a = reg2.t[:, o2:o2 + 2 * n].bitcast(F32)
                o2 += 2 * n
                return Tile(a.rearrange("p (a b) -> p a b", a=shape[0]), Buf("r2_%d" % o2))
            ug = [carve2([1, 514]) for _ in range(2)]
            uu = [carve2([1, 514]) for _ in range(2)]
            fh = carve2([44, 2])
            tB = carve2([1, 512])
            tC = carve2([1, 512])
            tmpB = Tile(tB.t[:, 0, :], tB.b)
            tmpC = Tile(tC.t[:, 0, :], tC.b)
            assert o2 * 2 <= REG2, o2 * 2
            P.emit(dve, REC.memset(fh[:, :, :], 0.0), [], [fh.b])

            for blk in range(NB):
                t0 = blk * 512
                norm_transpose(xa_d, b_xa, t0, l, C_GPRE2)
                for i in range(NFC):
                    br = []
                    for bi, (ub, cbase) in enumerate(((ug[i % 2], i * 128), (uu[i % 2], FF + i * 128))):
                        ps = PS("mm")
                        for kc in range(8):
                            P.emit(pe, REC.matmul(ps[:, :], w_up_sb[:, i + bi * NFC, kc, :], hT[:, kc, :], start=(kc == 0), stop=(kc == 7)),
                                   [bWU[i + bi * NFC], hT.b], [ps.b], sig=(kc == 7))
                        j = i + bi * NFC
                        veng = dve
                        acc = tmpA if bi == 0 else tmpB
                        P.emit(act, REC.activation(ub[:, 0, 2:514], ps[:, :], AF.Copy), [ps.b], [ub.b])
                        P.emit(act, REC.activation(acc[:, :], ps[:, :], AF.Identity, scale=col(l, C_FW + j * 3 + 2), bias=col(l, C_FB + j)), [ps.b, cols.b], [acc.b])
                        P.emit(dve, REC.tensor_copy(ub[:, 0, 0:2], fh[:, j, :]), [fh.b], [ub.b])
                        for k in (0, 1):
                            P.emit(veng, REC.scalar_tensor_tensor(acc[:, :], ub[:, 0, k:k + 512], col(l, C_FW + j * 3 + k), acc[:, :], ALU.mult, ALU.add),
                                   [ub.b, cols.b, acc.b], [acc.b])
                        P.emit(dve, REC.tensor_copy(fh[:, j, :], ub[:, 0, 512:514]), [ub.b], [fh.b])
                    P.emit(act, REC.activation(tmpC[:, :], tmpA[:, :], AF.Silu), [tmpA.b], [tmpC.b])
                    P.emit(dve, REC.tensor_tensor(gT[:, i, :], tmpC[:, :], tmpB[:, :], ALU.mult), [tmpC.b, tmpB.b], [gT.b])
                for tt in range(4):
                    pl = []
                    for half in range(2):
                        ps = PS("mm")
                        for i in range(NFC):
                            P.emit(pe, REC.matmul(ps[:, :], gT[:, i, tt * 128:(tt + 1) * 128], w_down_sb[:, i, half * 512:(half + 1) * 512],
                                                                                       start=(i == 0), stop=(i == NFC - 1)), [gT.b, bWF], [ps.b], sig=(i == NFC - 1))
                        pl.append(ps)
                    post_norm_residual(pl, xa_d, b_xa, dst_d, dst_buf, t0 + tt * 128, tt)

        P.finalize()
        block = es.enter_context(nc.Block())

        @block.tensor
        def _(e):
            _replay(pe.q, e)

        @block.scalar
        def _(e):
            _replay(act.q, e)

        @block.vector
        def _(e):
            _replay(dve.q, e)

        @block.gpsimd
        def _(e):
            _replay(pool.q, e)

        @block.sync
        def _(e):
            _replay(sp.q, e)
    return nc


def _cols_layer(p, l):
    c = np.zeros((128, NCOLS), np.float32)
    def chunks(v, n):
        return np.asarray(v, np.float32).reshape(n, 128).T
    c[:, C_GPRE1:C_GPRE1 + 8] = chunks(p["norm_mix_pre"][l], 8)
    c[:, C_GPRE2:C_GPRE2 + 8] = chunks(p["norm_ffn_pre"][l], 8)
    c[:, C_QN:C_QN + 2] = chunks(p["mla_q_norm"][l], 2)
    c[:, C_KVN:C_KVN + 1] = chunks(p["mla_kv_norm"][l], 1)
    scw = np.asarray(p["sc_conv_w"][l], np.float32)
    for ch in range(2):
        for k in range(3):
            c[:, C_SCW + ch * 3 + k] = scw[k, ch * 128:(ch + 1) * 128]
    ssw = np.asarray(p["ssd_conv_w"][l], np.float32)
    ssb = np.asarray(p["ssd_conv_b"][l], np.float32)
    for ch in range(6):
        for k in range(4):
            c[:, C_SSW + ch * 4 + k] = ssw[k, ch * 128:(ch + 1) * 128]
        c[:, C_SSB + ch] = ssb[ch * 128:(ch + 1) * 128]
    dsk = np.repeat(np.asarray(p["ssd_d"][l], np.float32), 64)
    c[:, C_DSK:C_DSK + 2] = chunks(dsk, 2)
    c[:, C_SSN:C_SSN + 2] = chunks(p["ssd_norm"][l], 2)
    fw = np.asarray(p["ffn_conv_w"][l], np.float32)
    fb = np.asarray(p["ffn_conv_b"][l], np.float32)
    for j in range(44):
        for k in range(3):
            c[:, C_FW + j * 3 + k] = fw[k, j * 128:(j + 1) * 128]
        c[:, C_FB + j] = fb[j * 128:(j + 1) * 128]
    return c


def prep_shared(p, NL):
    f = lambda a: np.ascontiguousarray(np.asarray(a, np.float32))
    w_in = f(p["w_in"])[:NL]
    sw = np.concatenate([np.arange(16, 32), np.arange(0, 16)])
    kr = w_in[:, :, 384:416]
    w_in_r = np.concatenate([w_in[:, :, 0:384], kr, kr[:, :, sw], w_in[:, :, 416:2208]], axis=2)
    assert w_in_r.shape[2] == NCOLW
    groups = [(0, 128), (128, 128), (256, 128), (384, 32), (416, 32)] + [(448 + g * 128, 128) for g in range(14)]
    w4 = w_in_r.reshape(NL, 8, 128, NCOLW)
    w_in_g = np.concatenate([np.transpose(w4[:, :, :, c0:c0 + M], (0, 2, 1, 3)).reshape(NL, 128, 8 * M) for (c0, M) in groups], axis=2)
    assert w_in_g.shape[2] == 8 * NCOLW
    w_dt = w_in[:, :, 2208:2212]
    wq = f(p["mla_w_q_up"])[:NL]
    wq4 = wq.reshape(NL, 256, 8, 96)
    w_qsw = wq4[:, :, :, 64:96][:, :, :, sw].reshape(NL, 256, 256)
    wkv = f(p["mla_w_kv_up"])[:NL].reshape(NL, 128, 8, 128)
    w_kn = wkv[:, :, :, 0:64].reshape(NL, 128, 512)
    w_v = wkv[:, :, :, 64:128].reshape(NL, 128, 512)
    wu = f(p["ffn_w_up"])[:NL].reshape(NL, 8, 128, 44, 128)
    w_up_g = np.ascontiguousarray(np.transpose(wu, (0, 3, 2, 1, 4)).reshape(NL, 44, 128, 1024))
    cols = np.stack([_cols_layer(p, l) for l in range(NL)], axis=1)
    rows = np.zeros((NL, 128, 2056), np.float32)
    for l in range(NL):
        rows[l, :, 0:1024] = np.asarray(p["norm_mix_post"][l], np.float32)[None, :]
        rows[l, :, 1024:2048] = np.asarray(p["norm_ffn_post"][l], np.float32)[None, :]
        rows[l, :, 2048:2052] = np.asarray(p["ssd_dt_bias"][l], np.float32)[None, :]
        rows[l, :, 2052:2056] = np.asarray(p["ssd_a_log"][l], np.float32)[None, :]
    inv_freq = (1.0 / (10000.0 ** (np.arange(0, 32, 2, dtype=np.float32) / np.float32(32)))).astype(np.float32)
    ropec = np.zeros((32, 2), np.float32)
    ropec[:, 0] = np.concatenate([inv_freq, inv_freq])
    ropec[:, 1] = np.concatenate([-np.ones(16), np.ones(16)])
    consts = np.concatenate([np.eye(128), np.triu(np.ones((128, 128))), np.ones((128, 128))], axis=1).astype(np.float32)
    return {
        "ropec": ropec, "consts": consts, "cols": np.ascontiguousarray(cols), "rows": rows,
        "w_in_g": np.ascontiguousarray(w_in_g), "w_dt": np.ascontiguousarray(w_dt),
        "w_q": np.ascontiguousarray(wq), "w_qsw": np.ascontiguousarray(w_qsw),
        "w_kn": np.ascontiguousarray(w_kn), "w_v": np.ascontiguousarray(w_v),
        "w_out": f(p["w_out"])[:NL], "w_up_g": w_up_g, "w_down": f(p["ffn_w_down"])[:NL],
    }


def run(inputs, S, NL, ncores, dbg=None):
    shared = prep_shared(inputs, NL)
    x = np.asarray(inputs["x"], np.float32)
    pos = np.asarray(inputs["positions"], np.int32)
    in_maps = []
    for c in range(ncores):
        m = dict(shared)
        m["x"] = np.ascontiguousarray(x[c, :S])
        m["posrep"] = np.ascontiguousarray(np.broadcast_to(pos[c, :S][None, :], (32, S)))
        in_maps.append(m)
    nc = build(S, NL, dbg)
    res = run_bass_kernel_spmd(nc, in_maps, core_ids=list(range(ncores)))
    return res


def kernel(**inputs):
    res = run(inputs, 4096, 4, 8)
    return np.stack([r["y"] for r in res.results], axis=0).astype(np.float32)
```

```python
import math
from contextlib import ExitStack
import numpy as np
import concourse.bass as bass
import concourse.mybir as mybir
from concourse.bass_utils import run_bass_kernel_spmd

F32 = mybir.dt.float32
BF16 = mybir.dt.bfloat16
I32 = mybir.dt.int32
AF = mybir.ActivationFunctionType
ALU = mybir.AluOpType

D = 1024
NH = 8
FF = 2816
NFC = 22
EPS = 1e-6
SCALE = 96 ** -0.5
NCOLW = 2240
PI = float(np.pi)
F_ROPECACHE = True
import os
SHIFT_M = 0
PRIO_RANK = 0

C_GPRE1 = 0
C_GPRE2 = 8
C_QN = 16
C_KVN = 18
C_SCW = 19
C_SSW = 25
C_SSB = 49
C_DSK = 55
C_SSN = 57
C_FW = 59
C_FB = 191
NCOLS = 235


class _Rec:
    def __getattr__(self, name):
        def f(*args, **kwargs):
            return (name, args, kwargs)
        return f


REC = _Rec()


def _run(fn, e):
    if callable(fn):
        return fn(e)
    name, args, kwargs = fn
    return getattr(e, name)(*args, **kwargs)


class Src:
    def __init__(self, sem, step):
        self.sem, self.step, self.count = sem, step, 0


class Buf:
    def __init__(self, name):
        self.name, self.writer, self.readers = name, None, []


class Eng:
    def __init__(self, name, src, inorder=False):
        self.name, self.src, self.q, self.waited, self.inorder = name, src, [], {}, inorder


class Tile:
    def __init__(self, t, b):
        self.t, self.b = t, b

    def __getitem__(self, k):
        return self.t[k]


class Op:
    __slots__ = ("eng", "insts", "reads", "writes", "idx", "preds", "succs", "cost", "is_dma", "lat",
                 "npred", "ready", "finish", "src", "val", "prev_val", "seg", "key", "rank")

    def __init__(self, eng):
        self.eng, self.insts, self.reads, self.writes = eng, [], [], []
        self.preds, self.succs = set(), []
        self.cost, self.is_dma, self.lat = 0.0, False, 0.0
        self.ready, self.finish = 0.0, 0.0
        self.src = self.val = self.prev_val = None


def _free_size(ap):
    n = 1
    for d in ap.shape[1:]:
        n *= int(d)
    return n


def _est_cost(eng, fn):
    name, args, kwargs = fn
    try:
        if name == "matmul":
            rhs = args[2]
            n = _free_size(rhs)
            return (max(28.0, n / 2.4) + 6.0) * (4.0 if rhs.dtype == F32 else 1.0)
        if name == "transpose":
            return 75.0
        if name == "dma_start":
            return 120.0
        out = args[0] if args else kwargs.get("out")
        n = _free_size(out)
        if eng.name == "act":
            return 150.0 + 0.85 * n
        if name == "reciprocal":
            return 160.0 + 2.6 * n
        if name == "memset":
            return 100.0 + 0.3 * n
        if eng.name == "pool":
            return 250.0 + 2.1 * n
        return 150.0 + 1.04 * n
    except Exception:
        return 300.0


class Prog:
    LOOK_W = 24
    LOOK_IDX = 6000

    def __init__(self, nc, es, n_dma_sems=24):
        self.nc, self.es = nc, es
        def sem(n):
            return es.enter_context(nc.semaphore(n))
        self.pe = Eng("pe", Src(sem("s_pe"), 1), inorder=True)
        self.act = Eng("act", Src(sem("s_act"), 1))
        self.dve = Eng("dve", Src(sem("s_dve"), 1))
        self.pool = Eng("pool", Src(sem("s_pool"), 1))
        self.sp = Eng("sp", Src(sem("s_sp"), 1))
        self.engs = [self.pe, self.act, self.dve, self.pool, self.sp]
        self.dma_srcs = [Src(sem("s_dma%d" % i), 16) for i in range(n_dma_sems)]
        self.dma_rr = {"sp": 0, "pool": 0}
        self.dma_part = {"sp": self.dma_srcs[:n_dma_sems - 8], "pool": self.dma_srcs[n_dma_sems - 8:]}
        self.ops = []
        self.cur = {}
        self.seg_start = [0]
        self.ninst = 0
        self.shift = 0

    def tile(self, name, shape, dt):
        t = self.es.enter_context(self.nc.sbuf_tensor("sb_" + name, shape, dt))
        return Tile(t, Buf(name))

    def emit(self, eng, fn, reads=(), writes=(), sig=True, dma_bytes=None):
        op = self.cur.get(eng.name)
        if op is None:
            op = Op(eng)
            self.cur[eng.name] = op
        op.insts.append(fn)
        op.reads.extend(reads)
        op.writes.extend(writes)
        op.cost += _est_cost(eng, fn)
        self.ninst += 1
        if dma_bytes is not None:
            op.is_dma = True
            op.lat = float(dma_bytes)
        if sig:
            self.cur[eng.name] = None
            self._close(op)

    def _close(self, op):
        seg0 = self.seg_start[-1]
        op.idx = len(self.ops)
        op.key = op.idx + self.shift
        op.seg = len(self.seg_start) - 1
        preds = set()
        for b in op.reads:
            if b.writer is not None:
                preds.add(b.writer)
        for b in op.writes:
            if b.writer is not None:
                preds.add(b.writer)
            preds.update(b.readers)
        preds.discard(op)
        op.preds = {p for p in preds if p.idx >= seg0}
        wset = set(id(b) for b in op.writes)
        for b in op.writes:
            b.writer = op
            b.readers = []
        for b in op.reads:
            if id(b) not in wset:
                b.readers.append(op)
        self.ops.append(op)

    def dma(self, eng, out, in_, reads=(), writes=()):
        nbytes = 1
        for d in out.shape:
            nbytes *= int(d)
        nbytes *= 2 if out.dtype == BF16 else 4
        nb2 = 1
        for d in in_.shape:
            nb2 *= int(d)
        nb2 *= 2 if in_.dtype == BF16 else 4
        self.emit(eng, REC.dma_start(out=out, in_=in_), reads, writes, dma_bytes=max(nbytes, nb2))

    def barrier(self):
        assert all(v is None for v in self.cur.values())
        if len(self.ops) > self.seg_start[-1]:
            self.seg_start.append(len(self.ops))

    def _schedule(self, ops):
        import bisect
        for op in ops:
            op.succs = []
            op.ready = 0.0
        for op in ops:
            op.npred = len(op.preds)
            for p in op.preds:
                p.succs.append(op)
        if PRIO_RANK:
            for op in reversed(ops):
                m = 0.0
                for sc in op.succs:
                    if sc.rank > m:
                        m = sc.rank
                op.rank = m + op.cost + (op.lat / 170.0 + 2000.0 if op.is_dma else 0.0)
                op.key = -op.rank
        avail = {e.name: [] for e in self.engs}
        t_free = {e.name: 0.0 for e in self.engs}
        order = {e.name: [] for e in self.engs}
        for op in ops:
            if op.npred == 0:
                avail[op.eng.name].append((op.key, op.idx, op))
        for e in self.engs:
            avail[e.name].sort(key=lambda x: (x[0], x[1]))
        scheduled = 0
        dma_free = 0.0
        n = len(ops)
        base = ops[0].idx
        done = [False] * n
        lo = 0
        while scheduled < n:
            while lo < n and done[lo]:
                lo += 1
            lim = base + lo + self.LOOK_IDX
            best = None
            for e in self.engs:
                lst = avail[e.name]
                if not lst:
                    continue
                tf = t_free[e.name]
                for (k_, idx, op) in lst[:self.LOOK_W]:
                    if idx > lim and best is not None:
                        continue
                    st = op.ready if op.ready > tf else tf
                    key = (st, k_, idx)
                    if best is None or key < best[0]:
                        best = (key, e, op)
            (st, k_, idx), e, op = best
            avail[e.name].remove((k_, idx, op))
            if op.is_dma:
                t_free[e.name] = st + op.cost
                xs = max(st + op.cost, dma_free)
                dma_free = xs + op.lat / 170.0
                op.finish = dma_free + 2000.0
            else:
                op.finish = st + op.cost
                t_free[e.name] = op.finish
            order[e.name].append(op)
            done[op.idx - base] = True
            scheduled += 1
            for sc in op.succs:
                if sc.ready < op.finish:
                    sc.ready = op.finish
                sc.npred -= 1
                if sc.npred == 0:
                    bisect.insort(avail[sc.eng.name], (sc.key, sc.idx, sc))
        return order, max(op.finish for op in ops)

    def _wait(self, eng, src, val):
        if val <= 0 or eng.waited.get(src, 0) >= val:
            return
        eng.waited[src] = val
        eng.q.append(REC.wait_ge(src.sem, val))

    def _barrier_waits(self):
        srcs = [e.src for e in self.engs] + self.dma_srcs
        for e in self.engs:
            for sr in srcs:
                if sr is not e.src:
                    self._wait(e, sr, sr.count)

    def finalize(self):
        assert all(v is None for v in self.cur.values())
        bounds = self.seg_start + [len(self.ops)]
        total = 0.0
        for si in range(len(bounds) - 1):
            ops = self.ops[bounds[si]:bounds[si + 1]]
            if not ops:
                continue
            if si > 0:
                self._barrier_waits()
            order, span = self._schedule(ops)
            total += span
            sb = {}
            for op in ops:
                sb[op.eng.name] = sb.get(op.eng.name, 0.0) + op.cost
            print("seg", si, "span us %.1f" % (span / 1e3), {k: round(v / 1e3, 1) for k, v in sb.items()}, flush=True)
            for e in self.engs:
                for op in order[e.name]:
                    if op.is_dma:
                        part = self.dma_part[e.name]
                        src = part[self.dma_rr[e.name] % len(part)]
                        self.dma_rr[e.name] += 1
                        op.prev_val = src.count
                    else:
                        src = e.src
                    src.count += src.step
                    op.src, op.val = src, src.count
            for e in self.engs:
                for op in order[e.name]:
                    need = {}
                    for p in op.preds:
                        if e.inorder and p.src is e.src:
                            continue
                        if need.get(p.src, 0) < p.val:
                            need[p.src] = p.val
                    for sr, v in need.items():
                        self._wait(e, sr, v)
                    if op.is_dma:
                        self._wait(e, op.src, op.prev_val)
                    last = len(op.insts) - 1
                    for i, fn in enumerate(op.insts):
                        if i == last:
                            e.q.append(("__sig__", fn, op.src.sem, op.src.step))
                        else:
                            e.q.append(fn)
        for e in (self.sp, self.pool):
            for s_ in self.dma_srcs:
                self._wait(e, s_, s_.count)
        busy = {}
        for op in self.ops:
            busy[op.eng.name] = busy.get(op.eng.name, 0.0) + op.cost
        print("busy ms:", {k: round(v / 1e6, 3) for k, v in busy.items()}, flush=True)
        print("ops:", len(self.ops), "insts:", self.ninst, "est span ms: %.3f" % (total / 1e6),
              {e.name: len(e.q) for e in self.engs}, flush=True)


def _replay(q, e):
    for item in q:
        if item[0] == "__sig__":
            _, fn, sem, step = item
            _run(fn, e).then_inc(sem, step)
        else:
            _run(item, e)


def build(S, NL, dbg=None):
    NB = S // 512
    NKB = S // 128
    nc = bass.Bass("TRN2", target_bir_lowering=False)

    def din(name, shape, dt=F32):
        return nc.dram_tensor(name, shape, dt, kind="ExternalInput").ap()

    x_d = din("x", [S, D])
    pos_d = din("posrep", [32, S], I32)
    rc_d = din("ropec", [32, 2])
    cst_d = din("consts", [128, 512])
    cols_d = din("cols", [128, NL, NCOLS])
    rows_d = din("rows", [NL, 128, 2056])
    w_in_d = din("w_in_g", [NL, 128, 8 * NCOLW])
    w_dt_d = din("w_dt", [NL, D, 4])
    w_q_d = din("w_q", [NL, 256, 768])
    w_qsw_d = din("w_qsw", [NL, 256, 256])
    w_kn_d = din("w_kn", [NL, 128, 512])
    w_v_d = din("w_v", [NL, 128, 512])
    w_out_d = din("w_out", [NL, D, D])
    w_up_d = din("w_up_g", [NL, 44, 128, 1024])
    w_down_d = din("w_down", [NL, FF, D])
    y_d = nc.dram_tensor("y", [S, D], F32, kind="ExternalOutput").ap()
    xa_d = nc.dram_tensor("xa", [S, D], F32, kind="Internal").ap()
    xb_d = [nc.dram_tensor("xb%d" % i, [S, D], F32, kind="Internal").ap() for i in range(2)]
    kd_d = nc.dram_tensor("kd", [NH, 96, S], BF16, kind="Internal").ap()
    vd_d = nc.dram_tensor("vd", [NH, 128, NKB, 65], BF16, kind="Internal").ap()
    tab_d = nc.dram_tensor("tabd", [NB, 32, 1024], F32, kind="Internal").ap() if F_ROPECACHE else None
    dbg_d = {}
    if dbg:
        for name, shape in dbg.items():
            dbg_d[name] = nc.dram_tensor("dbg_" + name, shape, F32, kind="ExternalOutput").ap()

    es = ExitStack()
    with es:
        P = Prog(nc, es)
        pe, act, dve, pool, sp = P.pe, P.act, P.dve, P.pool, P.sp
        T = P.tile
        b_xa, b_kd = Buf("xa"), Buf("kd")
        b_vd = Buf("vd")
        b_tab = [Buf("tab%d" % i) for i in range(NB)]
        b_xb = [Buf("xb0"), Buf("xb1")]
        b_y = Buf("y")
        psb = []
        for i in range(7):
            t = es.enter_context(nc.psum_tensor("ps%d" % i, [128, 512], F32))
            psb.append(Tile(t, Buf("ps%d" % i)))
        pst = Tile(es.enter_context(nc.psum_tensor("pst", [128, 1024], BF16)), Buf("pst"))
        pools = {"mm": [0, 1, 2], "acc": [3, 4], "aux": [5, 6]}
        prr = {"mm": 0, "acc": 0, "aux": 0}

        pst_list = [pst]

        def set_pools(cfg, psts):
            pools.clear()
            pools.update(cfg)
            pst_list[:] = psts

        def PS(pool_name):
            lst = pools[pool_name]
            i = lst[prr[pool_name] % len(lst)]
            prr[pool_name] += 1
            return psb[i]

        cst32 = T("cst32", [128, 256], F32)
        cstb = T("cstb", [128, 512], BF16)
        cols = T("cols", [128, NCOLS], F32)
        ropec = T("ropec", [32, 2], F32)
        epsc = T("epsc", [128, 1], F32)
        P.dma(sp, cst32[:], cst_d[:, 128:384], writes=[cst32.b])
        P.dma(pool, cstb[:], cst_d, writes=[cstb.b])
        P.dma(sp, ropec[:], rc_d, writes=[ropec.b])
        P.emit(dve, REC.memset(epsc[:], EPS), [], [epsc.b])
        ident_b = cstb[:, 0:128]
        tri_b = cstb[:, 128:256]
        ones_b = cstb[:, 256:384]
        mneg_b = cstb[:, 384:512]
        tri_f = cst32[:, 0:128]
        ones_f = cst32[:, 128:256]

        xin = [T("xin%d" % i, [128, D], F32) for i in range(2)]
        mixs = [T("mix%d" % i, [128, D], F32) for i in range(2)]
        xr = T("xr", [128, D], F32)
        junkP = T("junkP", [128, 512], BF16)
        smP = T("smP", [128, 8], F32)
        hbs = [T("hb0", [128, D], BF16)]
        hT = T("hT", [128, 8, 512], BF16)
        junk = T("junk", [128, 512], BF16)
        sms = [T("sm%d" % i, [128, 16], F32) for i in range(2)]
        rows = T("rows", [128, 1032], F32)
        tmpA = T("tmpA", [128, 512], F32)
        WREG = 135168
        wreg = T("wreg", [128, WREG // 2], BF16)
        REG2 = 35200
        reg2 = T("reg2", [128, REG2 // 2], BF16)

        def col(l, c, n=1):
            return cols[:, c:c + n]

        def rstd_from_ss(ss_ap, ss_bufs, n, out_ap, out_buf):
            P.emit(act, REC.activation(out_ap, ss_ap, AF.Ln, scale=1.0 / n, bias=epsc[:, 0:1]),
                   list(ss_bufs) + [epsc.b], [out_buf])
            P.emit(act, REC.activation(out_ap, out_ap, AF.Exp, scale=-0.5), [out_buf], [out_buf])

        def dump(name, ap, bufs, npart=128, idx=None):
            if not dbg or name not in dbg:
                return
            dst = dbg_d[name] if idx is None else dbg_d[name][idx]
            P.emit(dve, REC.tensor_copy(tmpA[0:npart, :], ap), bufs, [tmpA.b])
            P.dma(sp, dst[0:npart, :], tmpA[0:npart, :], [tmpA.b], [])

        def norm_transpose(src_d, src_buf, row0, l, gcol):
            for tt in range(4):
                xt = xin[tt % 2]
                sm = sms[0]
                hb = hbs[0]
                pt_ = pst_list[tt % len(pst_list)]
                r0 = row0 + tt * 128
                P.dma(sp, xt[:], src_d[r0:r0 + 128, :], [src_buf], [xt.b])
                P.emit(act, REC.activation(junk[:, :], xt[:, 0:512], AF.Square, accum_out=sm[:, 0:1]),
                       [xt.b], [junk.b, sm.b])
                P.emit(act, REC.activation(junk[:, :], xt[:, 512:1024], AF.Square, accum_out=sm[:, 1:2]),
                       [xt.b], [junk.b, sm.b])
                P.emit(dve, REC.tensor_tensor(sm[:, 2:3], sm[:, 0:1], sm[:, 1:2], ALU.add), [sm.b], [sm.b])
                rstd_from_ss(sm[:, 2:3], [sm.b], D, sm[:, 3:4], sm.b)
                P.emit(dve, REC.tensor_scalar(hb[:], xt[:], sm[:, 3:4], None, ALU.mult),
                       [xt.b, sm.b], [hb.b])
                for kc in range(8):
                    P.emit(pe, REC.transpose(pt_[:, kc * 128:(kc + 1) * 128], hb[:, kc * 128:(kc + 1) * 128], ident_b),
                           [hb.b, cstb.b], [pt_.b], sig=(kc == 7))
                P.emit(dve, REC.tensor_tensor(
                    hT[:, :, tt * 128:(tt + 1) * 128],
                    pt_[:, :].rearrange("p (k t) -> p k t", k=8),
                    col(l, gcol, 8).unsqueeze(2).to_broadcast([128, 8, 128]), ALU.mult),
                    [pt_.b, cols.b], [hT.b])

        def post_norm_residual(ps_list, src_d, src_buf, dst_d, dst_buf, r0, k, nocopy=False):
            xt = xr
            mx = mixs[k % 2]
            P.dma(sp, xt[:], src_d[r0:r0 + 128, :], [src_buf], [xt.b])
            for half in range(2):
                ps = ps_list[half]
                if nocopy:
                    P.emit(act, REC.activation(junkP[:, :], ps[:, :], AF.Square, accum_out=smP[:, 4 + half:5 + half]),
                           [ps.b], [junkP.b, smP.b])
                else:
                    P.emit(act, REC.activation(mx[:, half * 512:(half + 1) * 512], ps[:, :], AF.Copy), [ps.b], [mx.b])
                    P.emit(act, REC.activation(junkP[:, :], mx[:, half * 512:(half + 1) * 512], AF.Square,
                                               accum_out=smP[:, 4 + half:5 + half]), [mx.b], [junkP.b, smP.b])
            P.emit(dve, REC.tensor_tensor(smP[:, 6:7], smP[:, 4:5], smP[:, 5:6], ALU.add), [smP.b], [smP.b])
            rstd_from_ss(smP[:, 6:7], [smP.b], D, smP[:, 7:8], smP.b)
            if nocopy:
                for half in range(2):
                    ps = ps_list[half]
                    P.emit(dve, REC.scalar_tensor_tensor(mx[:, half * 512:(half + 1) * 512], ps[:, :], smP[:, 7:8], rows[:, half * 512:(half + 1) * 512], ALU.mult, ALU.mult),
                           [ps.b, smP.b, rows.b], [mx.b])
            else:
                P.emit(dve, REC.scalar_tensor_tensor(mx[:], mx[:], smP[:, 7:8], rows[:, 0:1024], ALU.mult, ALU.mult),
                       [mx.b, smP.b, rows.b], [mx.b])
            P.emit(dve, REC.tensor_tensor(mx[:], mx[:], xt[:], ALU.add), [mx.b, xt.b], [mx.b])
            P.dma(sp, dst_d[r0:r0 + 128, :], mx[:], [mx.b], [dst_buf])

        for l in range(NL):
            src_d, src_buf = (x_d, Buf("xsrc")) if l == 0 else (xb_d[(l - 1) % 2], b_xb[(l - 1) % 2])
            dst_d, dst_buf = (y_d, b_y) if l == NL - 1 else (xb_d[l % 2], b_xb[l % 2])

            o = 0
            def carve(n_el):
                nonlocal o
                a = wreg.t[:, o:o + n_el]
                o += n_el
                return a
            w_in_sb = carve(8 * NCOLW)
            w_dt_sb = carve(8 * 4).rearrange("p (k n) -> p k n", k=8)
            w_q_sb = carve(2 * 768).rearrange("p (k n) -> p k n", k=2)
            w_qsw_sb = carve(2 * 256).rearrange("p (k n) -> p k n", k=2)
            w_kn_sb = carve(512)
            w_v_sb = carve(512)
            def carve_t(shape, dt):
                nonlocal o
                n = int(np.prod(shape))
                if dt == F32:
                    o += (o % 2)
                    a = wreg.t[:, o:o + 2 * n].bitcast(F32)
                    o += 2 * n
                else:
                    a = wreg.t[:, o:o + n]
                    o += n
                if len(shape) == 2:
                    return a.rearrange("p (a b) -> p a b", a=shape[0])
                if len(shape) == 3:
                    return a.rearrange("p (a b c) -> p a b c", a=shape[0], b=shape[1])
                return a
            bW = Buf("wM%d" % l)
            GROUPS = [(0, 128), (128, 128), (256, 128), (384, 32), (416, 32)] + [(448 + g * 128, 128) for g in range(14)]
            bWg = {}
            bWo = Buf("wo%d" % l)
            P.barrier()
            set_pools({"mm": [0, 1, 2], "acc": [3, 4], "aux": [5, 6]}, [pst])
            for (dst, srcap) in [
                (w_dt_sb, w_dt_d[l].rearrange("(k p) n -> p k n", p=128)),
                (w_q_sb, w_q_d[l].rearrange("(k p) n -> p k n", p=128)),
                (w_qsw_sb, w_qsw_d[l].rearrange("(k p) n -> p k n", p=128)),
                (w_kn_sb, w_kn_d[l]),
                (w_v_sb, w_v_d[l]),
            ]:
                P.dma(pool, dst, srcap, [], [bW])
            for (c0g, Mg) in GROUPS:
                bWg[c0g] = Buf("wg%d_%d" % (l, c0g))
                P.dma(pool, w_in_sb[:, 8 * c0g:8 * (c0g + Mg)], w_in_d[l, :, 8 * c0g:8 * (c0g + Mg)], [], [bWg[c0g]])
            P.dma(sp, cols[:, :], cols_d[:, l, :], [], [cols.b])
            P.dma(sp, rows[:, 0:1024], rows_d[l, :, 0:1024], [], [rows.b])
            P.dma(sp, rows[:, 1024:1032], rows_d[l, :, 2048:2056], [], [rows.b])

            def MT(name, shape, dt):
                return Tile(carve_t(shape, dt), Buf(name))
            cq_sb = MT("cq", [3, 512], BF16)
            cq_sq = MT("cqsq", [3, 512], BF16)
            cqns = [MT("cqn%d" % i, [3, 512], BF16) for i in range(2)]
            conv_sb = MT("conv", [4, 512], F32)
            szs = MT("szs", [2, 512], F32)
            ubs = [MT("ub%d" % i, [1, 515], F32) for i in range(2)]
            uh = MT("uh", [6, 3], F32)
            vbuf = MT("vbuf", [2, 514], F32)
            xs_f = MT("xsf", [2, 512], F32)
            xs_b = MT("xsb", [2, 512], BF16)
            Bt = MT("Bt", [2, 512], BF16)
            Ct = MT("Ct", [2, 512], BF16)
            o_wout = o
            Kcur = MT("Kcur", [8, 512], BF16)
            kbuf = [MT("kbuf0", [1, 4096], BF16)]
            kbq = [Buf("kbq%d" % i) for i in range(4)]
            w_out_sb = wreg.t[:, o_wout:o_wout + 8 * D].rearrange("p (k n) -> p k n", k=8)
            assert o - o_wout == 8 * D
            Qh = [MT("Qh%d" % i, [1, 512], BF16) for i in range(2)]
            Pt = [MT("Pt%d" % i, [1, 512], BF16) for i in range(3)]
            tabs = MT("tabs", [2, 512], F32)
            rtmp = MT("rtmp", [2, 512], F32)
            rint = Tile(carve_t([1, 512], F32).bitcast(I32), Buf("rint"))
            krot = MT("krot", [1, 512], F32)
            rd = Tile(krot.t, Buf("rd"))
            osb = MT("osb", [1, 512], F32)
            dtT = MT("dtT", [4, 4], F32)
            adt = MT("adt", [4, 4], F32)
            arow = MT("arow", [1, 4], F32)
            state = MT("state", [4, 64], F32)
            prevp = MT("prevp", [4, 128], BF16)
            xdtp = MT("xdtp", [4, 128], BF16)
            xdtd = MT("xdtd", [4, 64], BF16)
            Btok = MT("Btok", [2, 128], BF16)
            adtri = MT("adtri", [4, 128], F32)
            dec = MT("dec", [4, 128], F32)
            drow = MT("drow", [4, 128], F32)
            Cs = MT("Cs", [4, 128], BF16)
            scT = MT("scT", [4, 128], BF16)
            s4 = MT("s4", [8, 4], F32)
            assert o * 2 <= WREG, o * 2
            o3 = 0
            def R2T(name, shape, dt):
                nonlocal o3
                n = int(np.prod(shape))
                if dt == F32:
                    a_ = reg2.t[:, o3:o3 + 2 * n].bitcast(F32)
                    o3 += 2 * n
                else:
                    a_ = reg2.t[:, o3:o3 + n]
                    o3 += n
                if len(shape) == 2:
                    a_ = a_.rearrange("p (a b) -> p a b", a=shape[0])
                elif len(shape) == 3:
                    a_ = a_.rearrange("p (a b c) -> p a b c", a=shape[0], b=shape[1])
                return Tile(a_, Buf(name))
            gat = R2T("gat", [2, 512], F32)
            yT = R2T("yT", [8, 512], BF16)
            gsq = R2T("gsq", [2, 512], BF16)
            Vcur = R2T("Vcur", [8, 4, 65], BF16)
            vbufs = [R2T("vbuf%d" % i, [28, 65], BF16) for i in range(2)]
            kbuf.append(R2T("kbuf1", [1, 4096], BF16))
            kbqs = [kbq, [Buf("kbq1_%d" % i) for i in range(4)]]
            assert o3 * 2 <= REG2, o3 * 2

            P.emit(dve, REC.memset(Vcur[:, :, :, 64:65], 1.0), [], [Vcur.b])
            P.emit(dve, REC.memset(vbuf[:, :, 0:2], 0.0), [], [vbuf.b])
            P.emit(dve, REC.memset(uh[:, :, :], 0.0), [], [uh.b])
            P.emit(dve, REC.memset(state[:], 0.0), [], [state.b])
            P.emit(dve, REC.memset(prevp[:], 0.0), [], [prevp.b])
            P.emit(dve, REC.memset(xdtp[:], 0.0), [], [xdtp.b])
            P.emit(act, REC.activation(arow[:, 0, :], rows[:, 1028:1032], AF.Exp), [rows.b], [arow.b])
            P.emit(dve, REC.tensor_scalar(arow[:, 0, :], arow[:, 0, :], -1.0, None, ALU.mult), [arow.b], [arow.b])

            for blk in range(NB):
                t0 = blk * 512
                cqn = cqns[blk % 2]
                P.shift = -SHIFT_M
                if l == 0 or not F_ROPECACHE:
                    P.dma(sp, rint[0:32, 0, :], pos_d[:, t0:t0 + 512], [], [rint.b])
                    P.emit(dve, REC.tensor_copy(rtmp[0:32, 0, :], rint[0:32, 0, :]), [rint.b], [rtmp.b])
                    P.emit(dve, REC.tensor_scalar(rtmp[0:32, 0, :], rtmp[0:32, 0, :], ropec[:, 0:1], None, ALU.mult),
                           [rtmp.b, ropec.b], [rtmp.b])
                    for ti, phase in ((0, PI / 2), (1, 0.0)):
                        P.emit(dve, REC.tensor_scalar(rtmp[0:32, 1, :], rtmp[0:32, 0, :], 1.0 / (2 * PI), phase / (2 * PI) + 0.5, ALU.mult, ALU.add),
                               [rtmp.b], [rtmp.b])
                        P.emit(dve, REC.tensor_copy(rint[0:32, 0, :], rtmp[0:32, 1, :]), [rtmp.b], [rint.b])
                        P.emit(dve, REC.tensor_copy(rtmp[0:32, 1, :], rint[0:32, 0, :]), [rint.b], [rtmp.b])
                        P.emit(dve, REC.tensor_scalar(rtmp[0:32, 1, :], rtmp[0:32, 1, :], -2 * PI, None, ALU.mult), [rtmp.b], [rtmp.b])
                        P.emit(dve, REC.scalar_tensor_tensor(tabs[0:32, ti, :], rtmp[0:32, 0, :], phase, rtmp[0:32, 1, :], ALU.add, ALU.add),
                               [rtmp.b], [tabs.b])
                        P.emit(dve, REC.tensor_scalar(rtmp[0:32, 1, :], tabs[0:32, ti, :], -PI, 2 * PI, ALU.is_lt, ALU.mult), [tabs.b], [rtmp.b])
                        P.emit(dve, REC.tensor_tensor(tabs[0:32, ti, :], tabs[0:32, ti, :], rtmp[0:32, 1, :], ALU.add), [tabs.b, rtmp.b], [tabs.b])
                        P.emit(dve, REC.tensor_scalar(rtmp[0:32, 1, :], tabs[0:32, ti, :], PI, -2 * PI, ALU.is_gt, ALU.mult), [tabs.b], [rtmp.b])
                        P.emit(dve, REC.tensor_tensor(tabs[0:32, ti, :], tabs[0:32, ti, :], rtmp[0:32, 1, :], ALU.add), [tabs.b, rtmp.b], [tabs.b])
                        if ti == 0:
                            P.emit(act, REC.activation(tabs[0:32, 0, :], tabs[0:32, 0, :], AF.Sin), [tabs.b], [tabs.b])
                        else:
                            P.emit(act, REC.activation(tabs[0:32, 1, :], tabs[0:32, 1, :], AF.Sin, scale=ropec[:, 1:2]), [tabs.b, ropec.b], [tabs.b])
                    if F_ROPECACHE:
                        P.dma(sp, tab_d[blk], tabs[0:32, :, :].rearrange("p a b -> p (a b)"), [tabs.b], [b_tab[blk]])
                else:
                    P.dma(sp, tabs[0:32, :, :].rearrange("p a b -> p (a b)"), tab_d[blk], [b_tab[blk]], [tabs.b])
                Ctab = tabs[0:32, 0, :]
                Stab = tabs[0:32, 1, :]

                norm_transpose(src_d, src_buf, t0, l, C_GPRE1)
                if l == 0 and blk == NB - 1:
                    dump("hT0", hT[:, 0, :], [hT.b])

                def inproj(c0, M):
                    ps = PS("mm")
                    for kc in range(8):
                        P.emit(pe, REC.matmul(ps[0:M, :], w_in_sb[:, 8 * c0 + kc * M:8 * c0 + (kc + 1) * M], hT[:, kc, :], start=(kc == 0), stop=(kc == 7)),
                               [bWg[c0], hT.b], [ps.b], sig=(kc == 7))
                    return ps
                for g in range(3):
                    ps = inproj(g * 128, 128)
                    P.emit(act, REC.activation(cq_sb[:, g, :], ps[:, :], AF.Copy), [ps.b], [cq_sb.b])
                    P.emit(act, REC.activation(cq_sq[:, g, :], ps[:, :], AF.Square), [ps.b], [cq_sq.b])
                ps_kr = inproj(384, 32)
                P.emit(dve, REC.tensor_tensor(rtmp[0:32, 0, :], ps_kr[0:32, :], Ctab, ALU.mult), [ps_kr.b, tabs.b], [rtmp.b])
                ps_ks = inproj(416, 32)
                P.emit(dve, REC.tensor_tensor(rtmp[0:32, 1, :], ps_ks[0:32, :], Stab, ALU.mult), [ps_ks.b, tabs.b], [rtmp.b])
                P.emit(dve, REC.tensor_tensor(krot[0:32, 0, :], rtmp[0:32, 0, :], rtmp[0:32, 1, :], ALU.add), [rtmp.b], [krot.b])
                P.emit(dve, REC.tensor_copy(Kcur[64:96, :, :], krot[0:32, 0:1, :].to_broadcast([32, 8, 512])), [krot.b], [Kcur.b])
                for g in range(4):
                    ps = inproj(448 + g * 128, 128)
                    P.emit(act, REC.activation(conv_sb[:, g, :], ps[:, :], AF.Copy), [ps.b], [conv_sb.b])
                for c in range(2):
                    ps = inproj(448 + (4 + c) * 128, 128)
                    P.emit(dve, REC.tensor_tensor(vbuf[:, c, 2:514], conv_sb[:, 2 + c, :], ps[:, :], ALU.mult), [conv_sb.b, ps.b], [vbuf.b])
                    P.emit(dve, REC.tensor_scalar(tmpA[:, :], vbuf[:, c, 0:512], col(l, C_SCW + c * 3 + 0), None, ALU.mult), [vbuf.b, cols.b], [tmpA.b])
                    P.emit(dve, REC.scalar_tensor_tensor(tmpA[:, :], vbuf[:, c, 1:513], col(l, C_SCW + c * 3 + 1), tmpA[:, :], ALU.mult, ALU.add), [vbuf.b, cols.b, tmpA.b], [tmpA.b])
                    P.emit(dve, REC.scalar_tensor_tensor(tmpA[:, :], vbuf[:, c, 2:514], col(l, C_SCW + c * 3 + 2), tmpA[:, :], ALU.mult, ALU.add), [vbuf.b, cols.b, tmpA.b], [tmpA.b])
                    P.emit(dve, REC.tensor_tensor(yT[:, 4 + c, :], tmpA[:, :], conv_sb[:, c, :], ALU.mult), [tmpA.b, conv_sb.b], [yT.b])
                    P.emit(dve, REC.tensor_copy(vbuf[:, c, 0:2], vbuf[:, c, 512:514]), [vbuf.b], [vbuf.b])
                for g in range(2):
                    ps = inproj(448 + 768 + g * 128, 128)
                    P.emit(act, REC.activation(szs[:, g, :], ps[:, :], AF.Silu), [ps.b], [szs.b])
                for c in range(6):
                    ps = inproj(448 + 1024 + c * 128, 128)
                    ub = ubs[c % 2]
                    P.emit(act, REC.activation(ub[:, 0, 3:515], ps[:, :], AF.Copy), [ps.b], [ub.b])
                    P.emit(dve, REC.tensor_copy(ub[:, 0, 0:3], uh[:, c, :]), [uh.b], [ub.b])
                    P.emit(dve, REC.tensor_scalar(tmpA[:, :], ub[:, 0, 0:512], col(l, C_SSW + c * 4 + 0), None, ALU.mult), [ub.b, cols.b], [tmpA.b])
                    for k in range(1, 4):
                        P.emit(dve, REC.scalar_tensor_tensor(tmpA[:, :], ub[:, 0, k:k + 512], col(l, C_SSW + c * 4 + k), tmpA[:, :], ALU.mult, ALU.add),
                               [ub.b, cols.b, tmpA.b], [tmpA.b])
                    if c < 2:
                        P.emit(act, REC.activation(xs_f[:, c, :], tmpA[:, :], AF.Silu, bias=col(l, C_SSB + c)), [tmpA.b, cols.b], [xs_f.b])
                        P.emit(dve, REC.tensor_copy(xs_b[:, c, :], xs_f[:, c, :]), [xs_f.b], [xs_b.b])
                    elif c < 4:
                        P.emit(act, REC.activation(Bt[:, c - 2, :], tmpA[:, :], AF.Silu, bias=col(l, C_SSB + c)), [tmpA.b, cols.b], [Bt.b])
                    else:
                        P.emit(act, REC.activation(Ct[:, c - 4, :], tmpA[:, :], AF.Silu, bias=col(l, C_SSB + c)), [tmpA.b, cols.b], [Ct.b])
                    P.emit(dve, REC.tensor_copy(uh[:, c, :], ub[:, 0, 512:515]), [ub.b], [uh.b])
                ps = PS("aux")
                for tt in range(4):
                    for kc in range(8):
                        P.emit(pe, REC.matmul(ps[:, tt * 4:tt * 4 + 4], hT[:, kc, tt * 128:(tt + 1) * 128], w_dt_sb[:, kc, :],
                                                                          start=(kc == 0), stop=(kc == 7)),
                               [bW, hT.b], [ps.b], sig=(kc == 7))
                dt4 = dtT[:, :, :]
                dtb = rows[:, 1024:1028].unsqueeze(1).to_broadcast([128, 4, 4])
                P.emit(dve, REC.tensor_tensor(dt4, ps[:, 0:16].rearrange("p (a b) -> p a b", a=4), dtb, ALU.add), [ps.b, rows.b], [dtT.b])
                P.emit(act, REC.activation(adt[:, :, :], dt4, AF.Abs), [dtT.b], [adt.b])
                P.emit(act, REC.activation(adt[:, :, :], adt[:, :, :], AF.Exp, scale=-1.0), [adt.b], [adt.b])
                P.emit(act, REC.activation(adt[:, :, :], adt[:, :, :], AF.Ln, bias=1.0), [adt.b], [adt.b])
                P.emit(dve, REC.scalar_tensor_tensor(dt4, dt4, 0.0, adt[:, :, :], ALU.max, ALU.add), [dtT.b, adt.b], [dtT.b])
                P.emit(dve, REC.tensor_tensor(adt[:, :, :], dt4, arow[:, 0:1, :].to_broadcast([128, 4, 4]), ALU.mult), [dtT.b, arow.b], [adt.b])

                for (chs, n, qc) in (((0, 1), 256, C_QN), ((2,), 128, C_KVN)):
                    ps = PS("aux")
                    for i, c in enumerate(chs):
                        P.emit(pe, REC.matmul(ps[:, :], ones_b, cq_sq[:, c, :], start=(i == 0), stop=(i == len(chs) - 1)),
                               [cstb.b, cq_sq.b], [ps.b], sig=(i == len(chs) - 1))
                    P.emit(act, REC.activation(tmpA[:, :], ps[:, :], AF.Ln, scale=1.0 / n, bias=epsc[:, 0:1]), [ps.b, epsc.b], [tmpA.b])
                    P.emit(act, REC.activation(tmpA[:, :], tmpA[:, :], AF.Exp, scale=-0.5), [tmpA.b], [tmpA.b])
                    for i, c in enumerate(chs):
                        P.emit(dve, REC.scalar_tensor_tensor(cqn[:, c, :], cq_sb[:, c, :], col(l, qc + i), tmpA[:, :], ALU.mult, ALU.mult),
                               [cq_sb.b, cols.b, tmpA.b], [cqn.b])
                ckvn = cqn[:, 2, :]
                if l == 0 and blk == NB - 1:
                    for c in range(3):
                        dump("cqn", cqn[:, c, :], [cqn.b], idx=c)
                    dump("dtT", dtT[:, :, :].rearrange("p a b -> p (a b)"), [dtT.b]) if False else None

                for hp in range(4):
                    ps = PS("mm")
                    P.emit(pe, REC.matmul(ps[:, :], w_kn_sb[:, hp * 128:(hp + 1) * 128], ckvn, start=True, stop=True), [bW, cqn.b], [ps.b])
                    P.emit(act, REC.activation(Kcur[0:64, 2 * hp, :], ps[0:64, :], AF.Copy), [ps.b], [Kcur.b])
                    P.emit(dve, REC.tensor_copy(Kcur[0:64, 2 * hp + 1, :], ps[64:128, :]), [ps.b], [Kcur.b])
                for tt in range(4):
                    ps = PS("mm")
                    P.emit(pe, REC.matmul(ps[:, :], cqn[:, 2, tt * 128:(tt + 1) * 128], w_v_sb, start=True, stop=True), [bW, cqn.b], [ps.b])
                    P.emit(act, REC.activation(Vcur[:, :, tt, 0:64], ps[:, :].rearrange("p (h d) -> p h d", h=8), AF.Copy), [ps.b], [Vcur.b])
                if blk < NB - 1:
                    P.dma(sp, kd_d.rearrange("h d s -> d h s")[:, :, t0:t0 + 512], Kcur[0:96, :, :], [Kcur.b], [b_kd])
                    P.dma(sp, vd_d.rearrange("h p j d -> p h (j d)")[:, :, blk * 260:(blk + 1) * 260],
                          Vcur[:, :, :, :].rearrange("p h j d -> p h (j d)"), [Vcur.b], [b_vd])

                P.shift = 0
                for h in range(NH):
                    q = Qh[h % 2]
                    psQ = PS("mm")
                    for c in range(2):
                        P.emit(pe, REC.matmul(psQ[0:96, :], w_q_sb[:, c, h * 96:(h + 1) * 96], cqn[:, c, :], start=(c == 0), stop=(c == 1)),
                               [bW, cqn.b], [psQ.b], sig=(c == 1))
                    psS = PS("mm")
                    for c in range(2):
                        P.emit(pe, REC.matmul(psS[0:32, :], w_qsw_sb[:, c, h * 32:(h + 1) * 32], cqn[:, c, :], start=(c == 0), stop=(c == 1)),
                               [bW, cqn.b], [psS.b], sig=(c == 1))
                    P.emit(act, REC.activation(q[0:64, 0, :], psQ[0:64, :], AF.Copy), [psQ.b], [q.b])
                    P.emit(dve, REC.tensor_tensor(rtmp[0:32, 0, :], psS[0:32, :], Stab, ALU.mult), [psS.b, tabs.b], [rtmp.b])
                    P.emit(dve, REC.tensor_tensor(rtmp[0:32, 1, :], psQ[64:96, :], Ctab, ALU.mult), [psQ.b, tabs.b], [rtmp.b])
                    P.emit(dve, REC.tensor_tensor(q[64:96, 0, :], rtmp[0:32, 0, :], rtmp[0:32, 1, :], ALU.add), [rtmp.b], [q.b])
                    kb = kbuf[h % 2]
                    kbq_h = kbqs[h % 2]
                    vb = vbufs[h % 2]
                    if l == 0 and blk == NB - 1 and h == 0:
                        dump("Q0", q[0:96, 0, :], [q.b], npart=96)
                        dump("K0", Kcur[0:96, 0, :], [Kcur.b], npart=96)
                    if blk > 0:
                        for qi in range(4):
                            lo_, hi_ = qi * 1024, min(t0, (qi + 1) * 1024)
                            if hi_ > lo_:
                                P.dma(sp, kb[0:96, 0, lo_:hi_], kd_d[h, :, lo_:hi_], [b_kd], [kbq_h[qi]])
                        P.dma(sp, vb[:, 0:4 * blk, :], vd_d[h, :, 0:4 * blk, :], [b_vd], [vb.b])
                    psO = PS("acc")
                    nfull = 4 * blk
                    for j in range(nfull + 4):
                        pss = PS("mm")
                        pt = Pt[j % 3]
                        if j < nfull:
                            c0 = 0
                            P.emit(pe, REC.matmul(pss[:, :], kb[0:96, 0, j * 128:(j + 1) * 128], q[0:96, 0, :], start=True, stop=True),
                                   [kbq_h[(j * 128) // 1024], q.b], [pss.b])
                        else:
                            jj = j - nfull
                            c0 = jj * 128
                            P.emit(pe, REC.matmul(pss[:, c0:512], Kcur[0:96, h, c0:c0 + 128], q[0:96, 0, c0:512], start=True, stop=False),
                                   [Kcur.b, q.b], [pss.b], sig=False)
                            P.emit(pe, REC.matmul(pss[:, c0:c0 + 128], ident_b, mneg_b, start=False, stop=True),
                                   [cstb.b], [pss.b])
                        P.emit(act, REC.activation(pt[:, 0, c0:512], pss[:, c0:512], AF.Exp, scale=SCALE), [pss.b], [pt.b])
                        if j < nfull:
                            vl, vlb = vb[:, j, :], vb.b
                        else:
                            vl, vlb = Vcur[:, h, j - nfull, :], Vcur.b
                        P.emit(pe, REC.matmul(psO[0:65, c0:512], vl, pt[:, 0, c0:512], start=(j == 0), stop=(j == nfull + 3)),
                               [vlb, pt.b], [psO.b])
                    P.emit(act, REC.activation(rd[64:65, 0, :], psO[64:65, :], AF.Ln), [psO.b], [rd.b])
                    P.emit(act, REC.activation(rd[64:65, 0, :], rd[64:65, 0, :], AF.Exp, scale=-1.0), [rd.b], [rd.b])
                    psB = PS("mm")
                    P.emit(pe, REC.matmul(psB[0:64, :], ones_f[64:65, 0:64], rd[64:65, 0, :], start=True, stop=True), [cst32.b, rd.b], [psB.b])
                    P.emit(act, REC.activation(osb[0:64, 0, :], psO[0:64, :], AF.Copy), [psO.b], [osb.b])
                    P.emit(dve, REC.tensor_tensor(yT[(h % 2) * 64:(h % 2) * 64 + 64, h // 2, :], osb[0:64, 0, :], psB[0:64, :], ALU.mult),
                           [osb.b, psB.b], [yT.b])

                P.dma(pool, w_out_sb, w_out_d[l].rearrange("(k p) n -> p k n", p=128), [], [Kcur.b, bWo] + kbq)
                psY = [PS("acc"), PS("acc")]
                for tt in range(4):
                    ts = slice(tt * 128, (tt + 1) * 128)
                    P.emit(dve, REC.tensor_copy(prevp[:, 0:4:2, 0:64], state[:, 0:4:2, :]), [state.b], [prevp.b])
                    P.emit(dve, REC.tensor_copy(prevp[:, 1:4:2, 64:128], state[:, 1:4:2, :]), [state.b], [prevp.b])
                    for c in range(2):
                        P.emit(pe, REC.transpose(pst[:, c * 128:(c + 1) * 128], xs_b[:, c, ts], ident_b), [xs_b.b, cstb.b], [pst.b], sig=False)
                    for g in range(2):
                        P.emit(pe, REC.transpose(pst[:, 256 + g * 128:256 + (g + 1) * 128], Bt[:, g, ts], ident_b), [Bt.b, cstb.b], [pst.b], sig=(g == 1))
                    xtok = pst[:, 0:256].rearrange("p (h d) -> p h d", h=4)
                    P.emit(dve, REC.tensor_tensor(xdtp[:, 0:4:2, 0:64], xtok[:, 0:4:2, :], dtT[:, tt, 0:4:2].unsqueeze(2).to_broadcast([128, 2, 64]), ALU.mult),
                           [pst.b, dtT.b], [xdtp.b])
                    P.emit(dve, REC.tensor_tensor(xdtp[:, 1:4:2, 64:128], xtok[:, 1:4:2, :], dtT[:, tt, 1:4:2].unsqueeze(2).to_broadcast([128, 2, 64]), ALU.mult),
                           [pst.b, dtT.b], [xdtp.b])
                    P.emit(act, REC.activation(Btok[:, :, :], pst[:, 256:512].rearrange("p (g n) -> p g n", g=2), AF.Copy), [pst.b], [Btok.b])
                    P.emit(dve, REC.tensor_tensor(adtri[:, :, :], tri_f.unsqueeze(1).to_broadcast([128, 4, 128]),
                                                                 adt[:, tt, :].unsqueeze(2).to_broadcast([128, 4, 128]), ALU.mult), [cst32.b, adt.b], [adtri.b])
                    psR = PS("aux")
                    P.emit(pe, REC.matmul(psR[:, :], ones_f, adtri[:, :, :].rearrange("p h l -> p (h l)"), start=True, stop=True), [cst32.b, adtri.b], [psR.b])
                    psA = PS("aux")
                    P.emit(pe, REC.matmul(psA[:, 0:4], tri_f, adt[:, tt, :], start=True, stop=True), [cst32.b, adt.b], [psA.b])
                    acol = s4[:, 0, :]
                    P.emit(dve, REC.tensor_copy(acol, psA[:, 0:4]), [psA.b], [s4.b])
                    psR3 = psR[:, :].rearrange("p (h l) -> p h l", h=4)
                    P.emit(dve, REC.tensor_tensor(dec[:, :, :], psR3, acol.unsqueeze(2).to_broadcast([128, 4, 128]), ALU.subtract), [psR.b, s4.b], [dec.b])
                    P.emit(dve, REC.tensor_scalar(dec[:, :, :], dec[:, :, :], 0.0, None, ALU.min), [dec.b], [dec.b])
                    P.emit(act, REC.activation(dec[:, :, :], dec[:, :, :], AF.Exp), [dec.b], [dec.b])
                    P.emit(dve, REC.tensor_tensor(dec[:, :, :], dec[:, :, :], tri_f.unsqueeze(1).to_broadcast([128, 4, 128]), ALU.mult), [dec.b, cst32.b], [dec.b])
                    P.emit(act, REC.activation(drow[:, :, :], psR3, AF.Exp), [psR.b], [drow.b])
                    P.emit(dve, REC.tensor_tensor(s4[:, 1, :], psR3[:, :, 127], acol, ALU.subtract), [psR.b, s4.b], [s4.b])
                    P.emit(act, REC.activation(s4[:, 2, :], s4[:, 1, :], AF.Exp), [s4.b], [s4.b])
                    P.emit(act, REC.activation(s4[:, 3, :], psR3[:, :, 127], AF.Exp), [psR.b], [s4.b])
                    P.emit(dve, REC.tensor_tensor(s4[:, 4, :], s4[:, 2, :], dtT[:, tt, :], ALU.mult), [s4.b, dtT.b], [s4.b])
                    P.emit(dve, REC.tensor_tensor(xdtd[:, :, :], xtok, s4[:, 4, :].unsqueeze(2).to_broadcast([128, 4, 64]), ALU.mult), [pst.b, s4.b], [xdtd.b])
                    for g in range(2):
                        P.emit(dve, REC.tensor_tensor(Cs[:, 2 * g:2 * g + 2, :], drow[:, 2 * g:2 * g + 2, :],
                                                                          Ct[:, g:g + 1, ts].to_broadcast([128, 2, 128]), ALU.mult), [drow.b, Ct.b], [Cs.b])
                    psG = PS("mm")
                    for g in range(2):
                        P.emit(pe, REC.matmul(psG[:, g * 128:(g + 1) * 128], Bt[:, g, ts], Ct[:, g, ts], start=True, stop=True), [Bt.b, Ct.b], [psG.b], sig=(g == 1))
                    for g in range(2):
                        P.emit(dve, REC.tensor_tensor(scT[:, 2 * g:2 * g + 2, :], dec[:, 2 * g:2 * g + 2, :],
                                                                             psG[:, g * 128:(g + 1) * 128].unsqueeze(1).to_broadcast([128, 2, 128]), ALU.mult), [dec.b, psG.b], [scT.b])
                    for k in range(2):
                        ops = [(xdtp[:, 2 * k, :], scT[:, 2 * k, :], [xdtp.b, scT.b]), (xdtp[:, 2 * k + 1, :], scT[:, 2 * k + 1, :], [xdtp.b, scT.b]),
                               (prevp[:, 2 * k, :], Cs[:, 2 * k, :], [prevp.b, Cs.b]), (prevp[:, 2 * k + 1, :], Cs[:, 2 * k + 1, :], [prevp.b, Cs.b])]
                        for i, (lh, rh, rb) in enumerate(ops):
                            P.emit(pe, REC.matmul(psY[k][:, ts], lh, rh, start=(i == 0), stop=(i == 3)), rb, [psY[k].b], sig=(i == 3))
                    psSt = PS("aux")
                    for g in range(2):
                        P.emit(pe, REC.matmul(psSt[:, g * 128:(g + 1) * 128], Btok[:, g, :], xdtd[:, 2 * g:2 * g + 2, :].rearrange("p h d -> p (h d)"), start=True, stop=True),
                               [Btok.b, xdtd.b], [psSt.b], sig=(g == 1))
                    P.emit(dve, REC.tensor_tensor(state[:, :, :], state[:, :, :], s4[:, 3, :].unsqueeze(2).to_broadcast([128, 4, 64]), ALU.mult), [state.b, s4.b], [state.b])
                    P.emit(dve, REC.tensor_tensor(state[:, :, :], state[:, :, :], psSt[:, 0:256].rearrange("p (h d) -> p h d", h=4), ALU.add), [state.b, psSt.b], [state.b])
                for k in range(2):
                    P.emit(dve, REC.scalar_tensor_tensor(gat[:, k, :], xs_f[:, k, :], col(l, C_DSK + k), psY[k][:, :], ALU.mult, ALU.add), [xs_f.b, cols.b, psY[k].b], [gat.b])
                    P.emit(dve, REC.tensor_tensor(gat[:, k, :], gat[:, k, :], szs[:, k, :], ALU.mult), [gat.b, szs.b], [gat.b])
                    P.emit(act, REC.activation(gsq[:, k, :], gat[:, k, :], AF.Square), [gat.b], [gsq.b])
                ps = PS("aux")
                for k in range(2):
                    P.emit(pe, REC.matmul(ps[:, :], ones_b, gsq[:, k, :], start=(k == 0), stop=(k == 1)), [cstb.b, gsq.b], [ps.b], sig=(k == 1))
                P.emit(act, REC.activation(tmpA[:, :], ps[:, :], AF.Ln, scale=1.0 / 256, bias=epsc[:, 0:1]), [ps.b, epsc.b], [tmpA.b])
                P.emit(act, REC.activation(tmpA[:, :], tmpA[:, :], AF.Exp, scale=-0.5), [tmpA.b], [tmpA.b])
                for k in range(2):
                    P.emit(dve, REC.scalar_tensor_tensor(yT[:, 6 + k, :], gat[:, k, :], col(l, C_SSN + k), tmpA[:, :], ALU.mult, ALU.mult), [gat.b, cols.b, tmpA.b], [yT.b])

                if l == 0 and blk == NB - 1:
                    for kc in range(8):
                        dump("yT", yT[:, kc, :], [yT.b], idx=kc)
                for tt in range(4):
                    pl = []
                    for half in range(2):
                        ps = PS("mm")
                        for kc in range(8):
                            P.emit(pe, REC.matmul(ps[:, :], yT[:, kc, tt * 128:(tt + 1) * 128], w_out_sb[:, kc, half * 512:(half + 1) * 512],
                                                                                        start=(kc == 0), stop=(kc == 7)), [yT.b, bWo, Kcur.b] + kbq, [ps.b], sig=(kc == 7))
                        pl.append(ps)
                    post_norm_residual(pl, src_d, src_buf, xa_d, b_xa, t0 + tt * 128, tt, nocopy=True)

            bWF = Buf("wF%d" % l)
            P.barrier()
            set_pools({"mm": [0, 1, 2, 3, 4, 5, 6]}, [pst])
            w_up_sb = wreg.t[:, 0:8 * 2 * FF].rearrange("p (j k n) -> p j k n", j=44, k=8)
            w_down_sb = wreg.t[:, 8 * 2 * FF:8 * 2 * FF + NFC * D].rearrange("p (k n) -> p k n", k=NFC)
            assert (8 * 2 * FF + NFC * D) * 2 <= WREG
            bWU = [Buf("wu%d_%d" % (l, j)) for j in range(44)]
            for i in range(NFC):
                for j in (i, NFC + i):
                    P.dma(pool, w_up_sb[:, j, :, :], w_up_d[l, j].rearrange("p (k n) -> p k n", k=8), [], [bWU[j]])
            wdv = w_down_d[l].rearrange("(k p) n -> p k n", p=128)
            P.dma(pool, w_down_sb[:, 0:11, :], wdv[:, 0:11, :], [], [bWF])
            P.dma(pool, w_down_sb[:, 11:22, :], wdv[:, 11:22, :], [], [bWF])
            P.dma(sp, rows[:, 0:1024], rows_d[l, :, 1024:2048], [], [rows.b])
            gT = Tile(reg2.t[:, 0:NFC * 512].rearrange("p (k n) -> p k n", k=NFC), Buf("gT%d" % l))
            o2 = NFC * 512
            def carve2(shape):
                nonlocal o2
                n = int(np.prod(shape))
                a = reg2.t[:, o2:o2 + 2 * n].bitcast(F32)
                o2 += 2 * n
                return Tile(a.rearrange("p (a b) -> p a b", a=shape[0]), Buf("r2_%d" % o2))
            ug = [carve2([1, 514]) for _ in range(2)]
            uu = [carve2([1, 514]) for _ in range(2)]
            fh = carve2([44, 2])
            tB = carve2([1, 512])
            tC = carve2([1, 512])
            tmpB = Tile(tB.t[:, 0, :], tB.b)
            tmpC = Tile(tC.t[:, 0, :], tC.b)
            assert o2 * 2 <= REG2, o2 * 2
            P.emit(dve, REC.memset(fh[:, :, :], 0.0), [], [fh.b])

            for blk in range(NB):
                t0 = blk * 512
                norm_transpose(xa_d, b_xa, t0, l, C_GPRE2)
                for i in range(NFC):
                    br = []
                    for bi, (ub, cbase) in enumerate(((ug[i % 2], i * 128), (uu[i % 2], FF + i * 128))):
                        ps = PS("mm")
                        for kc in range(8):
                            P.emit(pe, REC.matmul(ps[:, :], w_up_sb[:, i + bi * NFC, kc, :], hT[:, kc, :], start=(kc == 0), stop=(kc == 7)),
                                   [bWU[i + bi * NFC], hT.b], [ps.b], sig=(kc == 7))
                        j = i + bi * NFC
                        veng = dve
                        acc = tmpA if bi == 0 else tmpB
                        P.emit(act, REC.activation(ub[:, 0, 2:514], ps[:, :], AF.Copy), [ps.b], [ub.b])
                        P.emit(act, REC.activation(acc[:, :], ps[:, :], AF.Identity, scale=col(l, C_FW + j * 3 + 2), bias=col(l, C_FB + j)), [ps.b, cols.b], [acc.b])
                        P.emit(dve, REC.tensor_copy(ub[:, 0, 0:2], fh[:, j, :]), [fh.b], [ub.b])
                        for k in (0, 1):
                            P.emit(veng, REC.scalar_tensor_tensor(acc[:, :], ub[:, 0, k:k + 512], col(l, C_FW + j * 3 + k), acc[:, :], ALU.mult, ALU.add),
                                   [ub.b, cols.b, acc.b], [acc.b])
                        P.emit(dve, REC.tensor_copy(fh[:, j, :], ub[:, 0, 512:514]), [ub.b], [fh.b])
                    P.emit(act, REC.activation(tmpC[:, :], tmpA[:, :], AF.Silu), [tmpA.b], [tmpC.b])
                    P.emit(dve, REC.tensor_tensor(gT[:, i, :], tmpC[:, :], tmpB[:, :], ALU.mult), [tmpC.b, tmpB.b], [gT.b])
                for tt in range(4):
                    pl = []
                    for half in range(2):
                        ps = PS("mm")
                        for i in range(NFC):
                            P.emit(pe, REC.matmul(ps[:, :], gT[:, i, tt * 128:(tt + 1) * 128], w_down_sb[:, i, half * 512:(half + 1) * 512],
                                                                                       start=(i == 0), stop=(i == NFC - 1)), [gT.b, bWF], [ps.b], sig=(i == NFC - 1))
                        pl.append(ps)
                    post_norm_residual(pl, xa_d, b_xa, dst_d, dst_buf, t0 + tt * 128, tt)

        P.finalize()
        block = es.enter_context(nc.Block())

        @block.tensor
        def _(e):
            _replay(pe.q, e)

        @block.scalar
        def _(e):
            _replay(act.q, e)

        @block.vector
        def _(e):
            _replay(dve.q, e)

        @block.gpsimd
        def _(e):
            _replay(pool.q, e)

        @block.sync
        def _(e):
            _replay(sp.q, e)
    return nc


def _cols_layer(p, l):
    c = np.zeros((128, NCOLS), np.float32)
    def chunks(v, n):
        return np.asarray(v, np.float32).reshape(n, 128).T
    c[:, C_GPRE1:C_GPRE1 + 8] = chunks(p["norm_mix_pre"][l], 8)
    c[:, C_GPRE2:C_GPRE2 + 8] = chunks(p["norm_ffn_pre"][l], 8)
    c[:, C_QN:C_QN + 2] = chunks(p["mla_q_norm"][l], 2)
    c[:, C_KVN:C_KVN + 1] = chunks(p["mla_kv_norm"][l], 1)
    scw = np.asarray(p["sc_conv_w"][l], np.float32)
    for ch in range(2):
        for k in range(3):
            c[:, C_SCW + ch * 3 + k] = scw[k, ch * 128:(ch + 1) * 128]
    ssw = np.asarray(p["ssd_conv_w"][l], np.float32)
    ssb = np.asarray(p["ssd_conv_b"][l], np.float32)
    for ch in range(6):
        for k in range(4):
            c[:, C_SSW + ch * 4 + k] = ssw[k, ch * 128:(ch + 1) * 128]
        c[:, C_SSB + ch] = ssb[ch * 128:(ch + 1) * 128]
    dsk = np.repeat(np.asarray(p["ssd_d"][l], np.float32), 64)
    c[:, C_DSK:C_DSK + 2] = chunks(dsk, 2)
    c[:, C_SSN:C_SSN + 2] = chunks(p["ssd_norm"][l], 2)
    fw = np.asarray(p["ffn_conv_w"][l], np.float32)
    fb = np.asarray(p["ffn_conv_b"][l], np.float32)
    for j in range(44):
        for k in range(3):
            c[:, C_FW + j * 3 + k] = fw[k, j * 128:(j + 1) * 128]
        c[:, C_FB + j] = fb[j * 128:(j + 1) * 128]
    return c


def prep_shared(p, NL):
    f = lambda a: np.ascontiguousarray(np.asarray(a, np.float32))
    w_in = f(p["w_in"])[:NL]
    sw = np.concatenate([np.arange(16, 32), np.arange(0, 16)])
    kr = w_in[:, :, 384:416]
    w_in_r = np.concatenate([w_in[:, :, 0:384], kr, kr[:, :, sw], w_in[:, :, 416:2208]], axis=2)
    assert w_in_r.shape[2] == NCOLW
    groups = [(0, 128), (128, 128), (256, 128), (384, 32), (416, 32)] + [(448 + g * 128, 128) for g in range(14)]
    w4 = w_in_r.reshape(NL, 8, 128, NCOLW)
    w_in_g = np.concatenate([np.transpose(w4[:, :, :, c0:c0 + M], (0, 2, 1, 3)).reshape(NL, 128, 8 * M) for (c0, M) in groups], axis=2)
    assert w_in_g.shape[2] == 8 * NCOLW
    w_dt = w_in[:, :, 2208:2212]
    wq = f(p["mla_w_q_up"])[:NL]
    wq4 = wq.reshape(NL, 256, 8, 96)
    w_qsw = wq4[:, :, :, 64:96][:, :, :, sw].reshape(NL, 256, 256)
    wkv = f(p["mla_w_kv_up"])[:NL].reshape(NL, 128, 8, 128)
    w_kn = wkv[:, :, :, 0:64].reshape(NL, 128, 512)
    w_v = wkv[:, :, :, 64:128].reshape(NL, 128, 512)
    wu = f(p["ffn_w_up"])[:NL].reshape(NL, 8, 128, 44, 128)
    w_up_g = np.ascontiguousarray(np.transpose(wu, (0, 3, 2, 1, 4)).reshape(NL, 44, 128, 1024))
    cols = np.stack([_cols_layer(p, l) for l in range(NL)], axis=1)
    rows = np.zeros((NL, 128, 2056), np.float32)
    for l in range(NL):
        rows[l, :, 0:1024] = np.asarray(p["norm_mix_post"][l], np.float32)[None, :]
        rows[l, :, 1024:2048] = np.asarray(p["norm_ffn_post"][l], np.float32)[None, :]
        rows[l, :, 2048:2052] = np.asarray(p["ssd_dt_bias"][l], np.float32)[None, :]
        rows[l, :, 2052:2056] = np.asarray(p["ssd_a_log"][l], np.float32)[None, :]
    inv_freq = (1.0 / (10000.0 ** (np.arange(0, 32, 2, dtype=np.float32) / np.float32(32)))).astype(np.float32)
    ropec = np.zeros((32, 2), np.float32)
    ropec[:, 0] = np.concatenate([inv_freq, inv_freq])
    ropec[:, 1] = np.concatenate([-np.ones(16), np.ones(16)])
    tri_ = np.triu(np.ones((128, 128)))
    consts = np.concatenate([np.eye(128), tri_, np.ones((128, 128)), (1.0 - tri_) * -30000.0], axis=1).astype(np.float32)
    return {
        "ropec": ropec, "consts": consts, "cols": np.ascontiguousarray(cols), "rows": rows,
        "w_in_g": np.ascontiguousarray(w_in_g), "w_dt": np.ascontiguousarray(w_dt),
        "w_q": np.ascontiguousarray(wq), "w_qsw": np.ascontiguousarray(w_qsw),
        "w_kn": np.ascontiguousarray(w_kn), "w_v": np.ascontiguousarray(w_v),
        "w_out": f(p["w_out"])[:NL], "w_up_g": w_up_g, "w_down": f(p["ffn_w_down"])[:NL],
    }


def run(inputs, S, NL, ncores, dbg=None):
    shared = prep_shared(inputs, NL)
    x = np.asarray(inputs["x"], np.float32)
    pos = np.asarray(inputs["positions"], np.int32)
    in_maps = []
    for c in range(ncores):
        m = dict(shared)
        m["x"] = np.ascontiguousarray(x[c, :S])
        m["posrep"] = np.ascontiguousarray(np.broadcast_to(pos[c, :S][None, :], (32, S)))
        in_maps.append(m)
    nc = build(S, NL, dbg)
    res = run_bass_kernel_spmd(nc, in_maps, core_ids=list(range(ncores)))
    return res


def kernel(**inputs):
    res = run(inputs, 4096, 4, 8)
    return np.stack([r["y"] for r in res.results], axis=0).astype(np.float32)
```

```python
import math
from contextlib import ExitStack
import numpy as np
import concourse.bass as bass
import concourse.mybir as mybir
from concourse.bass_utils import run_bass_kernel_spmd

F32 = mybir.dt.float32
BF16 = mybir.dt.bfloat16
I32 = mybir.dt.int32
AF = mybir.ActivationFunctionType
ALU = mybir.AluOpType

D = 1024
NH = 8
FF = 2816
NFC = 22
EPS = 1e-6
SCALE = 96 ** -0.5
NCOLW = 2240
PI = float(np.pi)
F_ROPECACHE = True
import os
SHIFT_M = 0
PRIO_RANK = 0
NOBARRIER = 1

C_GPRE1 = 0
C_GPRE2 = 8
C_QN = 16
C_KVN = 18
C_SCW = 19
C_SSW = 25
C_SSB = 49
C_DSK = 55
C_SSN = 57
C_FW = 59
C_FB = 191
NCOLS = 235


class _Rec:
    def __getattr__(self, name):
        def f(*args, **kwargs):
            return (name, args, kwargs)
        return f


REC = _Rec()


def _run(fn, e):
    if callable(fn):
        return fn(e)
    name, args, kwargs = fn
    return getattr(e, name)(*args, **kwargs)


class Src:
    def __init__(self, sem, step):
        self.sem, self.step, self.count = sem, step, 0


class Buf:
    def __init__(self, name):
        self.name, self.writer, self.readers = name, None, []


class Eng:
    def __init__(self, name, src, inorder=False):
        self.name, self.src, self.q, self.waited, self.inorder = name, src, [], {}, inorder


class Tile:
    def __init__(self, t, b):
        self.t, self.b = t, b

    def __getitem__(self, k):
        return self.t[k]


class Op:
    __slots__ = ("eng", "insts", "reads", "writes", "idx", "preds", "succs", "cost", "is_dma", "lat",
                 "npred", "ready", "finish", "src", "val", "prev_val", "seg", "key", "rank")

    def __init__(self, eng):
        self.eng, self.insts, self.reads, self.writes = eng, [], [], []
        self.preds, self.succs = set(), []
        self.cost, self.is_dma, self.lat = 0.0, False, 0.0
        self.ready, self.finish = 0.0, 0.0
        self.src = self.val = self.prev_val = None


def _free_size(ap):
    n = 1
    for d in ap.shape[1:]:
        n *= int(d)
    return n


def _est_cost(eng, fn):
    name, args, kwargs = fn
    try:
        if name == "matmul":
            rhs = args[2]
            n = _free_size(rhs)
            return (max(28.0, n / 2.4) + 6.0) * (4.0 if rhs.dtype == F32 else 1.0)
        if name == "transpose":
            return 75.0
        if name == "dma_start":
            return 120.0
        out = args[0] if args else kwargs.get("out")
        n = _free_size(out)
        if eng.name == "act":
            return 150.0 + 0.85 * n
        if name == "reciprocal":
            return 160.0 + 2.6 * n
        if name == "memset":
            return 100.0 + 0.3 * n
        if eng.name == "pool":
            return 250.0 + 2.1 * n
        return 150.0 + 1.04 * n
    except Exception:
        return 300.0


class Prog:
    LOOK_W = 24
    LOOK_IDX = 6000

    def __init__(self, nc, es, n_dma_sems=24):
        self.nc, self.es = nc, es
        def sem(n):
            return es.enter_context(nc.semaphore(n))
        self.pe = Eng("pe", Src(sem("s_pe"), 1), inorder=True)
        self.act = Eng("act", Src(sem("s_act"), 1))
        self.dve = Eng("dve", Src(sem("s_dve"), 1))
        self.pool = Eng("pool", Src(sem("s_pool"), 1))
        self.sp = Eng("sp", Src(sem("s_sp"), 1))
        self.engs = [self.pe, self.act, self.dve, self.pool, self.sp]
        self.dma_srcs = [Src(sem("s_dma%d" % i), 16) for i in range(n_dma_sems)]
        self.dma_rr = {"sp": 0, "pool": 0}
        self.dma_part = {"sp": self.dma_srcs[:n_dma_sems - 8], "pool": self.dma_srcs[n_dma_sems - 8:]}
        self.ops = []
        self.cur = {}
        self.seg_start = [0]
        self.ninst = 0
        self.shift = 0

    def tile(self, name, shape, dt):
        t = self.es.enter_context(self.nc.sbuf_tensor("sb_" + name, shape, dt))
        return Tile(t, Buf(name))

    def emit(self, eng, fn, reads=(), writes=(), sig=True, dma_bytes=None):
        op = self.cur.get(eng.name)
        if op is None:
            op = Op(eng)
            self.cur[eng.name] = op
        op.insts.append(fn)
        op.reads.extend(reads)
        op.writes.extend(writes)
        op.cost += _est_cost(eng, fn)
        self.ninst += 1
        if dma_bytes is not None:
            op.is_dma = True
            op.lat = float(dma_bytes)
        if sig:
            self.cur[eng.name] = None
            self._close(op)

    def _close(self, op):
        seg0 = self.seg_start[-1]
        op.idx = len(self.ops)
        op.key = op.idx + self.shift
        op.seg = len(self.seg_start) - 1
        preds = set()
        for b in op.reads:
            if b.writer is not None:
                preds.add(b.writer)
        for b in op.writes:
            if b.writer is not None:
                preds.add(b.writer)
            preds.update(b.readers)
        preds.discard(op)
        op.preds = {p for p in preds if p.idx >= seg0}
        wset = set(id(b) for b in op.writes)
        for b in op.writes:
            b.writer = op
            b.readers = []
        for b in op.reads:
            if id(b) not in wset:
                b.readers.append(op)
        self.ops.append(op)

    def dma(self, eng, out, in_, reads=(), writes=()):
        nbytes = 1
        for d in out.shape:
            nbytes *= int(d)
        nbytes *= 2 if out.dtype == BF16 else 4
        nb2 = 1
        for d in in_.shape:
            nb2 *= int(d)
        nb2 *= 2 if in_.dtype == BF16 else 4
        self.emit(eng, REC.dma_start(out=out, in_=in_), reads, writes, dma_bytes=max(nbytes, nb2))

    def barrier(self):
        if NOBARRIER:
            return
        assert all(v is None for v in self.cur.values())
        if len(self.ops) > self.seg_start[-1]:
            self.seg_start.append(len(self.ops))

    def _schedule(self, ops):
        import bisect
        for op in ops:
            op.succs = []
            op.ready = 0.0
        for op in ops:
            op.npred = len(op.preds)
            for p in op.preds:
                p.succs.append(op)
        if PRIO_RANK:
            for op in reversed(ops):
                m = 0.0
                for sc in op.succs:
                    if sc.rank > m:
                        m = sc.rank
                op.rank = m + op.cost + (op.lat / 170.0 + 2000.0 if op.is_dma else 0.0)
                op.key = -op.rank
        avail = {e.name: [] for e in self.engs}
        t_free = {e.name: 0.0 for e in self.engs}
        order = {e.name: [] for e in self.engs}
        for op in ops:
            if op.npred == 0:
                avail[op.eng.name].append((op.key, op.idx, op))
        for e in self.engs:
            avail[e.name].sort(key=lambda x: (x[0], x[1]))
        scheduled = 0
        dma_free = 0.0
        n = len(ops)
        base = ops[0].idx
        done = [False] * n
        lo = 0
        while scheduled < n:
            while lo < n and done[lo]:
                lo += 1
            lim = base + lo + self.LOOK_IDX
            best = None
            for e in self.engs:
                lst = avail[e.name]
                if not lst:
                    continue
                tf = t_free[e.name]
                for (k_, idx, op) in lst[:self.LOOK_W]:
                    if idx > lim and best is not None:
                        continue
                    st = op.ready if op.ready > tf else tf
                    key = (st, k_, idx)
                    if best is None or key < best[0]:
                        best = (key, e, op)
            (st, k_, idx), e, op = best
            avail[e.name].remove((k_, idx, op))
            if op.is_dma:
                t_free[e.name] = st + op.cost
                xs = max(st + op.cost, dma_free)
                dma_free = xs + op.lat / 170.0
                op.finish = dma_free + 2000.0
            else:
                op.finish = st + op.cost
                t_free[e.name] = op.finish
            order[e.name].append(op)
            done[op.idx - base] = True
            scheduled += 1
            for sc in op.succs:
                if sc.ready < op.finish:
                    sc.ready = op.finish
                sc.npred -= 1
                if sc.npred == 0:
                    bisect.insort(avail[sc.eng.name], (sc.key, sc.idx, sc))
        return order, max(op.finish for op in ops)

    def _wait(self, eng, src, val):
        if val <= 0 or eng.waited.get(src, 0) >= val:
            return
        eng.waited[src] = val
        eng.q.append(REC.wait_ge(src.sem, val))

    def _barrier_waits(self):
        srcs = [e.src for e in self.engs] + self.dma_srcs
        for e in self.engs:
            for sr in srcs:
                if sr is not e.src:
                    self._wait(e, sr, sr.count)

    def finalize(self):
        assert all(v is None for v in self.cur.values())
        bounds = self.seg_start + [len(self.ops)]
        total = 0.0
        for si in range(len(bounds) - 1):
            ops = self.ops[bounds[si]:bounds[si + 1]]
            if not ops:
                continue
            if si > 0:
                self._barrier_waits()
            order, span = self._schedule(ops)
            total += span
            sb = {}
            for op in ops:
                sb[op.eng.name] = sb.get(op.eng.name, 0.0) + op.cost
            print("seg", si, "span us %.1f" % (span / 1e3), {k: round(v / 1e3, 1) for k, v in sb.items()}, flush=True)
            for e in self.engs:
                for op in order[e.name]:
                    if op.is_dma:
                        part = self.dma_part[e.name]
                        src = part[self.dma_rr[e.name] % len(part)]
                        self.dma_rr[e.name] += 1
                        op.prev_val = src.count
                    else:
                        src = e.src
                    src.count += src.step
                    op.src, op.val = src, src.count
            for e in self.engs:
                for op in order[e.name]:
                    need = {}
                    for p in op.preds:
                        if e.inorder and p.src is e.src:
                            continue
                        if need.get(p.src, 0) < p.val:
                            need[p.src] = p.val
                    for sr, v in need.items():
                        self._wait(e, sr, v)
                    if op.is_dma:
                        self._wait(e, op.src, op.prev_val)
                    last = len(op.insts) - 1
                    for i, fn in enumerate(op.insts):
                        if i == last:
                            e.q.append(("__sig__", fn, op.src.sem, op.src.step))
                        else:
                            e.q.append(fn)
        for e in (self.sp, self.pool):
            for s_ in self.dma_srcs:
                self._wait(e, s_, s_.count)
        busy = {}
        for op in self.ops:
            busy[op.eng.name] = busy.get(op.eng.name, 0.0) + op.cost
        print("busy ms:", {k: round(v / 1e6, 3) for k, v in busy.items()}, flush=True)
        print("ops:", len(self.ops), "insts:", self.ninst, "est span ms: %.3f" % (total / 1e6),
              {e.name: len(e.q) for e in self.engs}, flush=True)


def _replay(q, e):
    for item in q:
        if item[0] == "__sig__":
            _, fn, sem, step = item
            _run(fn, e).then_inc(sem, step)
        else:
            _run(item, e)


def build(S, NL, dbg=None):
    NB = S // 512
    NKB = S // 128
    nc = bass.Bass("TRN2", target_bir_lowering=False)

    def din(name, shape, dt=F32):
        return nc.dram_tensor(name, shape, dt, kind="ExternalInput").ap()

    x_d = din("x", [S, D])
    pos_d = din("posrep", [32, S], I32)
    rc_d = din("ropec", [32, 2])
    cst_d = din("consts", [128, 512])
    cols_d = din("cols", [128, NL, NCOLS])
    rows_d = din("rows", [NL, 128, 2056])
    w_in_d = din("w_in_g", [NL, 128, 8 * NCOLW])
    w_dt_d = din("w_dt", [NL, D, 4])
    w_q_d = din("w_q", [NL, 256, 768])
    w_qsw_d = din("w_qsw", [NL, 256, 256])
    w_kn_d = din("w_kn", [NL, 128, 512])
    w_v_d = din("w_v", [NL, 128, 512])
    w_out_d = din("w_out", [NL, D, D])
    w_up_d = din("w_up_g", [NL, 44, 128, 1024])
    w_down_d = din("w_down", [NL, FF, D])
    y_d = nc.dram_tensor("y", [S, D], F32, kind="ExternalOutput").ap()
    xa_d = nc.dram_tensor("xa", [S, D], F32, kind="Internal").ap()
    xb_d = [nc.dram_tensor("xb%d" % i, [S, D], F32, kind="Internal").ap() for i in range(2)]
    kd_d = nc.dram_tensor("kd", [NH, 96, S], BF16, kind="Internal").ap()
    vd_d = nc.dram_tensor("vd", [NH, 128, NKB, 65], BF16, kind="Internal").ap()
    tab_d = nc.dram_tensor("tabd", [NB, 32, 1024], F32, kind="Internal").ap() if F_ROPECACHE else None
    dbg_d = {}
    if dbg:
        for name, shape in dbg.items():
            dbg_d[name] = nc.dram_tensor("dbg_" + name, shape, F32, kind="ExternalOutput").ap()

    es = ExitStack()
    with es:
        P = Prog(nc, es)
        pe, act, dve, pool, sp = P.pe, P.act, P.dve, P.pool, P.sp
        T = P.tile
        b_xa, b_kd = Buf("xa"), Buf("kd")
        b_vd = Buf("vd")
        b_tab = [Buf("tab%d" % i) for i in range(NB)]
        b_xb = [Buf("xb0"), Buf("xb1")]
        b_y = Buf("y")
        psb = []
        for i in range(7):
            t = es.enter_context(nc.psum_tensor("ps%d" % i, [128, 512], F32))
            psb.append(Tile(t, Buf("ps%d" % i)))
        pst = Tile(es.enter_context(nc.psum_tensor("pst", [128, 1024], BF16)), Buf("pst"))
        pools = {"mm": [0, 1, 2], "acc": [3, 4], "aux": [5, 6]}
        prr = {"mm": 0, "acc": 0, "aux": 0}

        pst_list = [pst]

        def set_pools(cfg, psts):
            pools.clear()
            pools.update(cfg)
            pst_list[:] = psts

        def PS(pool_name):
            lst = pools[pool_name]
            i = lst[prr[pool_name] % len(lst)]
            prr[pool_name] += 1
            return psb[i]

        cst32 = T("cst32", [128, 256], F32)
        cstb = T("cstb", [128, 512], BF16)
        cols = T("cols", [128, NCOLS], F32)
        ropec = T("ropec", [32, 2], F32)
        epsc = T("epsc", [128, 1], F32)
        P.dma(sp, cst32[:], cst_d[:, 128:384], writes=[cst32.b])
        P.dma(pool, cstb[:], cst_d, writes=[cstb.b])
        P.dma(sp, ropec[:], rc_d, writes=[ropec.b])
        P.emit(dve, REC.memset(epsc[:], EPS), [], [epsc.b])
        ident_b = cstb[:, 0:128]
        tri_b = cstb[:, 128:256]
        ones_b = cstb[:, 256:384]
        mneg_b = cstb[:, 384:512]
        tri_f = cst32[:, 0:128]
        ones_f = cst32[:, 128:256]

        xin = [T("xin%d" % i, [128, D], F32) for i in range(2)]
        mixs = [T("mix%d" % i, [128, D], F32) for i in range(2)]
        xr = T("xr", [128, D], F32)
        junkP = T("junkP", [128, 512], BF16)
        smP = T("smP", [128, 8], F32)
        hbs = [T("hb0", [128, D], BF16)]
        hT = T("hT", [128, 8, 512], BF16)
        junk = T("junk", [128, 512], BF16)
        sms = [T("sm%d" % i, [128, 16], F32) for i in range(2)]
        rows = T("rows", [128, 1032], F32)
        tmpA = T("tmpA", [128, 512], F32)
        WREG = 135168
        wreg = T("wreg", [128, WREG // 2], BF16)
        REG2 = 35200
        reg2 = T("reg2", [128, REG2 // 2], BF16)

        def col(l, c, n=1):
            return cols[:, c:c + n]

        def rstd_from_ss(ss_ap, ss_bufs, n, out_ap, out_buf):
            P.emit(act, REC.activation(out_ap, ss_ap, AF.Ln, scale=1.0 / n, bias=epsc[:, 0:1]),
                   list(ss_bufs) + [epsc.b], [out_buf])
            P.emit(act, REC.activation(out_ap, out_ap, AF.Exp, scale=-0.5), [out_buf], [out_buf])

        def dump(name, ap, bufs, npart=128, idx=None):
            if not dbg or name not in dbg:
                return
            dst = dbg_d[name] if idx is None else dbg_d[name][idx]
            P.emit(dve, REC.tensor_copy(tmpA[0:npart, :], ap), bufs, [tmpA.b])
            P.dma(sp, dst[0:npart, :], tmpA[0:npart, :], [tmpA.b], [])

        def norm_transpose(src_d, src_buf, row0, l, gcol):
            for tt in range(4):
                xt = xin[tt % 2]
                sm = sms[0]
                hb = hbs[0]
                pt_ = pst_list[tt % len(pst_list)]
                r0 = row0 + tt * 128
                P.dma(sp, xt[:], src_d[r0:r0 + 128, :], [src_buf], [xt.b])
                P.emit(act, REC.activation(junk[:, :], xt[:, 0:512], AF.Square, accum_out=sm[:, 0:1]),
                       [xt.b], [junk.b, sm.b])
                P.emit(act, REC.activation(junk[:, :], xt[:, 512:1024], AF.Square, accum_out=sm[:, 1:2]),
                       [xt.b], [junk.b, sm.b])
                P.emit(dve, REC.tensor_tensor(sm[:, 2:3], sm[:, 0:1], sm[:, 1:2], ALU.add), [sm.b], [sm.b])
                rstd_from_ss(sm[:, 2:3], [sm.b], D, sm[:, 3:4], sm.b)
                P.emit(dve, REC.tensor_scalar(hb[:], xt[:], sm[:, 3:4], None, ALU.mult),
                       [xt.b, sm.b], [hb.b])
                for kc in range(8):
                    P.emit(pe, REC.transpose(pt_[:, kc * 128:(kc + 1) * 128], hb[:, kc * 128:(kc + 1) * 128], ident_b),
                           [hb.b, cstb.b], [pt_.b], sig=(kc == 7))
                P.emit(dve, REC.tensor_tensor(
                    hT[:, :, tt * 128:(tt + 1) * 128],
                    pt_[:, :].rearrange("p (k t) -> p k t", k=8),
                    col(l, gcol, 8).unsqueeze(2).to_broadcast([128, 8, 128]), ALU.mult),
                    [pt_.b, cols.b], [hT.b])

        def post_norm_residual(ps_list, src_d, src_buf, dst_d, dst_buf, r0, k, nocopy=False):
            xt = xr
            mx = mixs[k % 2]
            P.dma(sp, xt[:], src_d[r0:r0 + 128, :], [src_buf], [xt.b])
            for half in range(2):
                ps = ps_list[half]
                if nocopy:
                    P.emit(act, REC.activation(junkP[:, :], ps[:, :], AF.Square, accum_out=smP[:, 4 + half:5 + half]),
                           [ps.b], [junkP.b, smP.b])
                else:
                    P.emit(act, REC.activation(mx[:, half * 512:(half + 1) * 512], ps[:, :], AF.Copy), [ps.b], [mx.b])
                    P.emit(act, REC.activation(junkP[:, :], mx[:, half * 512:(half + 1) * 512], AF.Square,
                                               accum_out=smP[:, 4 + half:5 + half]), [mx.b], [junkP.b, smP.b])
            P.emit(dve, REC.tensor_tensor(smP[:, 6:7], smP[:, 4:5], smP[:, 5:6], ALU.add), [smP.b], [smP.b])
            rstd_from_ss(smP[:, 6:7], [smP.b], D, smP[:, 7:8], smP.b)
            if nocopy:
                for half in range(2):
                    ps = ps_list[half]
                    P.emit(dve, REC.scalar_tensor_tensor(mx[:, half * 512:(half + 1) * 512], ps[:, :], smP[:, 7:8], rows[:, half * 512:(half + 1) * 512], ALU.mult, ALU.mult),
                           [ps.b, smP.b, rows.b], [mx.b])
            else:
                P.emit(dve, REC.scalar_tensor_tensor(mx[:], mx[:], smP[:, 7:8], rows[:, 0:1024], ALU.mult, ALU.mult),
                       [mx.b, smP.b, rows.b], [mx.b])
            P.emit(dve, REC.tensor_tensor(mx[:], mx[:], xt[:], ALU.add), [mx.b, xt.b], [mx.b])
            P.dma(sp, dst_d[r0:r0 + 128, :], mx[:], [mx.b], [dst_buf])

        REG = {"wreg": [], "reg2": []}

        def reg_buf(region, sb_, eb_, buf):
            for (s0, e0, ob) in REG[region]:
                if s0 < eb_ and sb_ < e0 and ob is not buf:
                    if ob.writer is not None and ob.writer not in buf.readers:
                        buf.readers.append(ob.writer)
                    for r_ in ob.readers:
                        if r_ not in buf.readers:
                            buf.readers.append(r_)
            REG[region].append((sb_, eb_, buf))

        for l in range(NL):
            src_d, src_buf = (x_d, Buf("xsrc")) if l == 0 else (xb_d[(l - 1) % 2], b_xb[(l - 1) % 2])
            dst_d, dst_buf = (y_d, b_y) if l == NL - 1 else (xb_d[l % 2], b_xb[l % 2])

            o = 0
            def carve(n_el):
                nonlocal o
                a = wreg.t[:, o:o + n_el]
                o += n_el
                return a
            w_in_sb = carve(8 * NCOLW)
            w_dt_sb = carve(8 * 4).rearrange("p (k n) -> p k n", k=8)
            w_q_sb = carve(2 * 768).rearrange("p (k n) -> p k n", k=2)
            w_qsw_sb = carve(2 * 256).rearrange("p (k n) -> p k n", k=2)
            w_kn_sb = carve(512)
            w_v_sb = carve(512)
            def carve_t(shape, dt):
                nonlocal o
                n = int(np.prod(shape))
                if dt == F32:
                    o += (o % 2)
                    a = wreg.t[:, o:o + 2 * n].bitcast(F32)
                    o += 2 * n
                else:
                    a = wreg.t[:, o:o + n]
                    o += n
                carve_t.last = (2 * (o - (2 * n if dt == F32 else n)), 2 * o)
                if len(shape) == 2:
                    return a.rearrange("p (a b) -> p a b", a=shape[0])
                if len(shape) == 3:
                    return a.rearrange("p (a b c) -> p a b c", a=shape[0], b=shape[1])
                return a
            bW = Buf("wM%d" % l)
            GROUPS = [(0, 128), (128, 128), (256, 128), (384, 32), (416, 32)] + [(448 + g * 128, 128) for g in range(14)]
            bWg = {}
            bWo = Buf("wo%d" % l)
            P.barrier()
            reg_buf("wreg", 2 * 8 * NCOLW, 2 * o, bW)
            set_pools({"mm": [0, 1, 2], "acc": [3, 4], "aux": [5, 6]}, [pst])
            for (dst, srcap) in [
                (w_dt_sb, w_dt_d[l].rearrange("(k p) n -> p k n", p=128)),
                (w_q_sb, w_q_d[l].rearrange("(k p) n -> p k n", p=128)),
                (w_qsw_sb, w_qsw_d[l].rearrange("(k p) n -> p k n", p=128)),
                (w_kn_sb, w_kn_d[l]),
                (w_v_sb, w_v_d[l]),
            ]:
                P.dma(pool, dst, srcap, [], [bW])
            for (c0g, Mg) in GROUPS:
                bWg[c0g] = Buf("wg%d_%d" % (l, c0g))
                reg_buf("wreg", 2 * 8 * c0g, 2 * 8 * (c0g + Mg), bWg[c0g])
                P.dma(pool, w_in_sb[:, 8 * c0g:8 * (c0g + Mg)], w_in_d[l, :, 8 * c0g:8 * (c0g + Mg)], [], [bWg[c0g]])
            P.dma(sp, cols[:, :], cols_d[:, l, :], [], [cols.b])
            P.dma(sp, rows[:, 0:1024], rows_d[l, :, 0:1024], [], [rows.b])
            P.dma(sp, rows[:, 1024:1032], rows_d[l, :, 2048:2056], [], [rows.b])

            def MT(name, shape, dt):
                t_ = Tile(carve_t(shape, dt), Buf(name))
                reg_buf("wreg", carve_t.last[0], carve_t.last[1], t_.b)
                return t_
            cq_sb = MT("cq", [3, 512], BF16)
            cq_sq = MT("cqsq", [3, 512], BF16)
            cqns = [MT("cqn%d" % i, [3, 512], BF16) for i in range(2)]
            conv_sb = MT("conv", [4, 512], F32)
            szs = MT("szs", [2, 512], F32)
            ubs = [MT("ub%d" % i, [1, 515], F32) for i in range(2)]
            uh = MT("uh", [6, 3], F32)
            vbuf = MT("vbuf", [2, 514], F32)
            xs_f = MT("xsf", [2, 512], F32)
            xs_b = MT("xsb", [2, 512], BF16)
            Bt = MT("Bt", [2, 512], BF16)
            Ct = MT("Ct", [2, 512], BF16)
            o_wout = o
            Kcur = MT("Kcur", [8, 512], BF16)
            kbuf = [MT("kbuf0", [1, 4096], BF16)]
            kbq = [Buf("kbq%d" % i) for i in range(4)]
            for i_ in range(4):
                reg_buf("wreg", carve_t.last[0] + 2048 * i_, carve_t.last[0] + 2048 * (i_ + 1), kbq[i_])
            reg_buf("wreg", 2 * o_wout, 2 * (o_wout + 8 * D), bWo)
            w_out_sb = wreg.t[:, o_wout:o_wout + 8 * D].rearrange("p (k n) -> p k n", k=8)
            assert o - o_wout == 8 * D
            Qh = [MT("Qh%d" % i, [1, 512], BF16) for i in range(2)]
            Pt = [MT("Pt%d" % i, [1, 512], BF16) for i in range(3)]
            tabs = MT("tabs", [2, 512], F32)
            rtmp = MT("rtmp", [2, 512], F32)
            rint = Tile(carve_t([1, 512], F32).bitcast(I32), Buf("rint"))
            krot = MT("krot", [1, 512], F32)
            rd = Tile(krot.t, Buf("rd"))
            osb = MT("osb", [1, 512], F32)
            dtT = MT("dtT", [4, 4], F32)
            adt = MT("adt", [4, 4], F32)
            arow = MT("arow", [1, 4], F32)
            state = MT("state", [4, 64], F32)
            prevp = MT("prevp", [4, 128], BF16)
            xdtp = MT("xdtp", [4, 128], BF16)
            xdtd = MT("xdtd", [4, 64], BF16)
            Btok = MT("Btok", [2, 128], BF16)
            adtri = MT("adtri", [4, 128], F32)
            dec = MT("dec", [4, 128], F32)
            drow = MT("drow", [4, 128], F32)
            Cs = MT("Cs", [4, 128], BF16)
            scT = MT("scT", [4, 128], BF16)
            s4 = MT("s4", [8, 4], F32)
            assert o * 2 <= WREG, o * 2
            o3 = 0
            def R2T(name, shape, dt):
                nonlocal o3
                n = int(np.prod(shape))
                if dt == F32:
                    a_ = reg2.t[:, o3:o3 + 2 * n].bitcast(F32)
                    o3 += 2 * n
                else:
                    a_ = reg2.t[:, o3:o3 + n]
                    o3 += n
                if len(shape) == 2:
                    a_ = a_.rearrange("p (a b) -> p a b", a=shape[0])
                elif len(shape) == 3:
                    a_ = a_.rearrange("p (a b c) -> p a b c", a=shape[0], b=shape[1])
                t_ = Tile(a_, Buf(name))
                R2T.last = (2 * (o3 - (2 * n if dt == F32 else n)), 2 * o3)
                reg_buf("reg2", R2T.last[0], R2T.last[1], t_.b)
                return t_
            gat = R2T("gat", [2, 512], F32)
            yT = R2T("yT", [8, 512], BF16)
            gsq = R2T("gsq", [2, 512], BF16)
            Vcur = R2T("Vcur", [8, 4, 65], BF16)
            vbufs = [R2T("vbuf%d" % i, [28, 65], BF16) for i in range(2)]
            kbuf.append(R2T("kbuf1", [1, 4096], BF16))
            kbqs = [kbq, [Buf("kbq1_%d" % i) for i in range(4)]]
            for i_ in range(4):
                reg_buf("reg2", R2T.last[0] + 2048 * i_, R2T.last[0] + 2048 * (i_ + 1), kbqs[1][i_])
            assert o3 * 2 <= REG2, o3 * 2

            P.emit(dve, REC.memset(Vcur[:, :, :, 64:65], 1.0), [], [Vcur.b])
            P.emit(dve, REC.memset(vbuf[:, :, 0:2], 0.0), [], [vbuf.b])
            P.emit(dve, REC.memset(uh[:, :, :], 0.0), [], [uh.b])
            P.emit(dve, REC.memset(state[:], 0.0), [], [state.b])
            P.emit(dve, REC.memset(prevp[:], 0.0), [], [prevp.b])
            P.emit(dve, REC.memset(xdtp[:], 0.0), [], [xdtp.b])
            P.emit(act, REC.activation(arow[:, 0, :], rows[:, 1028:1032], AF.Exp), [rows.b], [arow.b])
            P.emit(dve, REC.tensor_scalar(arow[:, 0, :], arow[:, 0, :], -1.0, None, ALU.mult), [arow.b], [arow.b])

            for blk in range(NB):
                t0 = blk * 512
                cqn = cqns[blk % 2]
                P.shift = -SHIFT_M
                if l == 0 or not F_ROPECACHE:
                    P.dma(sp, rint[0:32, 0, :], pos_d[:, t0:t0 + 512], [], [rint.b])
                    P.emit(dve, REC.tensor_copy(rtmp[0:32, 0, :], rint[0:32, 0, :]), [rint.b], [rtmp.b])
                    P.emit(dve, REC.tensor_scalar(rtmp[0:32, 0, :], rtmp[0:32, 0, :], ropec[:, 0:1], None, ALU.mult),
                           [rtmp.b, ropec.b], [rtmp.b])
                    for ti, phase in ((0, PI / 2), (1, 0.0)):
                        P.emit(dve, REC.tensor_scalar(rtmp[0:32, 1, :], rtmp[0:32, 0, :], 1.0 / (2 * PI), phase / (2 * PI) + 0.5, ALU.mult, ALU.add),
                               [rtmp.b], [rtmp.b])
                        P.emit(dve, REC.tensor_copy(rint[0:32, 0, :], rtmp[0:32, 1, :]), [rtmp.b], [rint.b])
                        P.emit(dve, REC.tensor_copy(rtmp[0:32, 1, :], rint[0:32, 0, :]), [rint.b], [rtmp.b])
                        P.emit(dve, REC.tensor_scalar(rtmp[0:32, 1, :], rtmp[0:32, 1, :], -2 * PI, None, ALU.mult), [rtmp.b], [rtmp.b])
                        P.emit(dve, REC.scalar_tensor_tensor(tabs[0:32, ti, :], rtmp[0:32, 0, :], phase, rtmp[0:32, 1, :], ALU.add, ALU.add),
                               [rtmp.b], [tabs.b])
                        P.emit(dve, REC.tensor_scalar(rtmp[0:32, 1, :], tabs[0:32, ti, :], -PI, 2 * PI, ALU.is_lt, ALU.mult), [tabs.b], [rtmp.b])
                        P.emit(dve, REC.tensor_tensor(tabs[0:32, ti, :], tabs[0:32, ti, :], rtmp[0:32, 1, :], ALU.add), [tabs.b, rtmp.b], [tabs.b])
                        P.emit(dve, REC.tensor_scalar(rtmp[0:32, 1, :], tabs[0:32, ti, :], PI, -2 * PI, ALU.is_gt, ALU.mult), [tabs.b], [rtmp.b])
                        P.emit(dve, REC.tensor_tensor(tabs[0:32, ti, :], tabs[0:32, ti, :], rtmp[0:32, 1, :], ALU.add), [tabs.b, rtmp.b], [tabs.b])
                        if ti == 0:
                            P.emit(act, REC.activation(tabs[0:32, 0, :], tabs[0:32, 0, :], AF.Sin), [tabs.b], [tabs.b])
                        else:
                            P.emit(act, REC.activation(tabs[0:32, 1, :], tabs[0:32, 1, :], AF.Sin, scale=ropec[:, 1:2]), [tabs.b, ropec.b], [tabs.b])
                    if F_ROPECACHE:
                        P.dma(sp, tab_d[blk], tabs[0:32, :, :].rearrange("p a b -> p (a b)"), [tabs.b], [b_tab[blk]])
                else:
                    P.dma(sp, tabs[0:32, :, :].rearrange("p a b -> p (a b)"), tab_d[blk], [b_tab[blk]], [tabs.b])
                Ctab = tabs[0:32, 0, :]
                Stab = tabs[0:32, 1, :]

                norm_transpose(src_d, src_buf, t0, l, C_GPRE1)
                if l == 0 and blk == NB - 1:
                    dump("hT0", hT[:, 0, :], [hT.b])

                def inproj(c0, M):
                    ps = PS("mm")
                    for kc in range(8):
                        P.emit(pe, REC.matmul(ps[0:M, :], w_in_sb[:, 8 * c0 + kc * M:8 * c0 + (kc + 1) * M], hT[:, kc, :], start=(kc == 0), stop=(kc == 7)),
                               [bWg[c0], hT.b], [ps.b], sig=(kc == 7))
                    return ps
                for g in range(3):
                    ps = inproj(g * 128, 128)
                    P.emit(act, REC.activation(cq_sb[:, g, :], ps[:, :], AF.Copy), [ps.b], [cq_sb.b])
                    P.emit(act, REC.activation(cq_sq[:, g, :], ps[:, :], AF.Square), [ps.b], [cq_sq.b])
                ps_kr = inproj(384, 32)
                P.emit(dve, REC.tensor_tensor(rtmp[0:32, 0, :], ps_kr[0:32, :], Ctab, ALU.mult), [ps_kr.b, tabs.b], [rtmp.b])
                ps_ks = inproj(416, 32)
                P.emit(dve, REC.tensor_tensor(rtmp[0:32, 1, :], ps_ks[0:32, :], Stab, ALU.mult), [ps_ks.b, tabs.b], [rtmp.b])
                P.emit(dve, REC.tensor_tensor(krot[0:32, 0, :], rtmp[0:32, 0, :], rtmp[0:32, 1, :], ALU.add), [rtmp.b], [krot.b])
                P.emit(dve, REC.tensor_copy(Kcur[64:96, :, :], krot[0:32, 0:1, :].to_broadcast([32, 8, 512])), [krot.b], [Kcur.b])
                for g in range(4):
                    ps = inproj(448 + g * 128, 128)
                    P.emit(act, REC.activation(conv_sb[:, g, :], ps[:, :], AF.Copy), [ps.b], [conv_sb.b])
                for c in range(2):
                    ps = inproj(448 + (4 + c) * 128, 128)
                    P.emit(dve, REC.tensor_tensor(vbuf[:, c, 2:514], conv_sb[:, 2 + c, :], ps[:, :], ALU.mult), [conv_sb.b, ps.b], [vbuf.b])
                    P.emit(dve, REC.tensor_scalar(tmpA[:, :], vbuf[:, c, 0:512], col(l, C_SCW + c * 3 + 0), None, ALU.mult), [vbuf.b, cols.b], [tmpA.b])
                    P.emit(dve, REC.scalar_tensor_tensor(tmpA[:, :], vbuf[:, c, 1:513], col(l, C_SCW + c * 3 + 1), tmpA[:, :], ALU.mult, ALU.add), [vbuf.b, cols.b, tmpA.b], [tmpA.b])
                    P.emit(dve, REC.scalar_tensor_tensor(tmpA[:, :], vbuf[:, c, 2:514], col(l, C_SCW + c * 3 + 2), tmpA[:, :], ALU.mult, ALU.add), [vbuf.b, cols.b, tmpA.b], [tmpA.b])
                    P.emit(dve, REC.tensor_tensor(yT[:, 4 + c, :], tmpA[:, :], conv_sb[:, c, :], ALU.mult), [tmpA.b, conv_sb.b], [yT.b])
                    P.emit(dve, REC.tensor_copy(vbuf[:, c, 0:2], vbuf[:, c, 512:514]), [vbuf.b], [vbuf.b])
                for g in range(2):
                    ps = inproj(448 + 768 + g * 128, 128)
                    P.emit(act, REC.activation(szs[:, g, :], ps[:, :], AF.Silu), [ps.b], [szs.b])
                for c in range(6):
                    ps = inproj(448 + 1024 + c * 128, 128)
                    ub = ubs[c % 2]
                    P.emit(act, REC.activation(ub[:, 0, 3:515], ps[:, :], AF.Copy), [ps.b], [ub.b])
                    P.emit(dve, REC.tensor_copy(ub[:, 0, 0:3], uh[:, c, :]), [uh.b], [ub.b])
                    P.emit(dve, REC.tensor_scalar(tmpA[:, :], ub[:, 0, 0:512], col(l, C_SSW + c * 4 + 0), None, ALU.mult), [ub.b, cols.b], [tmpA.b])
                    for k in range(1, 4):
                        P.emit(dve, REC.scalar_tensor_tensor(tmpA[:, :], ub[:, 0, k:k + 512], col(l, C_SSW + c * 4 + k), tmpA[:, :], ALU.mult, ALU.add),
                               [ub.b, cols.b, tmpA.b], [tmpA.b])
                    if c < 2:
                        P.emit(act, REC.activation(xs_f[:, c, :], tmpA[:, :], AF.Silu, bias=col(l, C_SSB + c)), [tmpA.b, cols.b], [xs_f.b])
                        P.emit(dve, REC.tensor_copy(xs_b[:, c, :], xs_f[:, c, :]), [xs_f.b], [xs_b.b])
                    elif c < 4:
                        P.emit(act, REC.activation(Bt[:, c - 2, :], tmpA[:, :], AF.Silu, bias=col(l, C_SSB + c)), [tmpA.b, cols.b], [Bt.b])
                    else:
                        P.emit(act, REC.activation(Ct[:, c - 4, :], tmpA[:, :], AF.Silu, bias=col(l, C_SSB + c)), [tmpA.b, cols.b], [Ct.b])
                    P.emit(dve, REC.tensor_copy(uh[:, c, :], ub[:, 0, 512:515]), [ub.b], [uh.b])
                ps = PS("aux")
                for tt in range(4):
                    for kc in range(8):
                        P.emit(pe, REC.matmul(ps[:, tt * 4:tt * 4 + 4], hT[:, kc, tt * 128:(tt + 1) * 128], w_dt_sb[:, kc, :],
                                                                          start=(kc == 0), stop=(kc == 7)),
                               [bW, hT.b], [ps.b], sig=(kc == 7))
                dt4 = dtT[:, :, :]
                dtb = rows[:, 1024:1028].unsqueeze(1).to_broadcast([128, 4, 4])
                P.emit(dve, REC.tensor_tensor(dt4, ps[:, 0:16].rearrange("p (a b) -> p a b", a=4), dtb, ALU.add), [ps.b, rows.b], [dtT.b])
                P.emit(act, REC.activation(adt[:, :, :], dt4, AF.Abs), [dtT.b], [adt.b])
                P.emit(act, REC.activation(adt[:, :, :], adt[:, :, :], AF.Exp, scale=-1.0), [adt.b], [adt.b])
                P.emit(act, REC.activation(adt[:, :, :], adt[:, :, :], AF.Ln, bias=1.0), [adt.b], [adt.b])
                P.emit(dve, REC.scalar_tensor_tensor(dt4, dt4, 0.0, adt[:, :, :], ALU.max, ALU.add), [dtT.b, adt.b], [dtT.b])
                P.emit(dve, REC.tensor_tensor(adt[:, :, :], dt4, arow[:, 0:1, :].to_broadcast([128, 4, 4]), ALU.mult), [dtT.b, arow.b], [adt.b])

                for (chs, n, qc) in (((0, 1), 256, C_QN), ((2,), 128, C_KVN)):
                    ps = PS("aux")
                    for i, c in enumerate(chs):
                        P.emit(pe, REC.matmul(ps[:, :], ones_b, cq_sq[:, c, :], start=(i == 0), stop=(i == len(chs) - 1)),
                               [cstb.b, cq_sq.b], [ps.b], sig=(i == len(chs) - 1))
                    P.emit(act, REC.activation(tmpA[:, :], ps[:, :], AF.Ln, scale=1.0 / n, bias=epsc[:, 0:1]), [ps.b, epsc.b], [tmpA.b])
                    P.emit(act, REC.activation(tmpA[:, :], tmpA[:, :], AF.Exp, scale=-0.5), [tmpA.b], [tmpA.b])
                    for i, c in enumerate(chs):
                        P.emit(dve, REC.scalar_tensor_tensor(cqn[:, c, :], cq_sb[:, c, :], col(l, qc + i), tmpA[:, :], ALU.mult, ALU.mult),
                               [cq_sb.b, cols.b, tmpA.b], [cqn.b])
                ckvn = cqn[:, 2, :]
                if l == 0 and blk == NB - 1:
                    for c in range(3):
                        dump("cqn", cqn[:, c, :], [cqn.b], idx=c)
                    dump("dtT", dtT[:, :, :].rearrange("p a b -> p (a b)"), [dtT.b]) if False else None

                for hp in range(4):
                    ps = PS("mm")
                    P.emit(pe, REC.matmul(ps[:, :], w_kn_sb[:, hp * 128:(hp + 1) * 128], ckvn, start=True, stop=True), [bW, cqn.b], [ps.b])
                    P.emit(act, REC.activation(Kcur[0:64, 2 * hp, :], ps[0:64, :], AF.Copy), [ps.b], [Kcur.b])
                    P.emit(dve, REC.tensor_copy(Kcur[0:64, 2 * hp + 1, :], ps[64:128, :]), [ps.b], [Kcur.b])
                for tt in range(4):
                    ps = PS("mm")
                    P.emit(pe, REC.matmul(ps[:, :], cqn[:, 2, tt * 128:(tt + 1) * 128], w_v_sb, start=True, stop=True), [bW, cqn.b], [ps.b])
                    P.emit(act, REC.activation(Vcur[:, :, tt, 0:64], ps[:, :].rearrange("p (h d) -> p h d", h=8), AF.Copy), [ps.b], [Vcur.b])
                if blk < NB - 1:
                    P.dma(sp, kd_d.rearrange("h d s -> d h s")[:, :, t0:t0 + 512], Kcur[0:96, :, :], [Kcur.b], [b_kd])
                    P.dma(sp, vd_d.rearrange("h p j d -> p h (j d)")[:, :, blk * 260:(blk + 1) * 260],
                          Vcur[:, :, :, :].rearrange("p h j d -> p h (j d)"), [Vcur.b], [b_vd])

                P.shift = 0
                for h in range(NH):
                    q = Qh[h % 2]
                    psQ = PS("mm")
                    for c in range(2):
                        P.emit(pe, REC.matmul(psQ[0:96, :], w_q_sb[:, c, h * 96:(h + 1) * 96], cqn[:, c, :], start=(c == 0), stop=(c == 1)),
                               [bW, cqn.b], [psQ.b], sig=(c == 1))
                    psS = PS("mm")
                    for c in range(2):
                        P.emit(pe, REC.matmul(psS[0:32, :], w_qsw_sb[:, c, h * 32:(h + 1) * 32], cqn[:, c, :], start=(c == 0), stop=(c == 1)),
                               [bW, cqn.b], [psS.b], sig=(c == 1))
                    P.emit(act, REC.activation(q[0:64, 0, :], psQ[0:64, :], AF.Copy), [psQ.b], [q.b])
                    P.emit(dve, REC.tensor_tensor(rtmp[0:32, 0, :], psS[0:32, :], Stab, ALU.mult), [psS.b, tabs.b], [rtmp.b])
                    P.emit(dve, REC.tensor_tensor(rtmp[0:32, 1, :], psQ[64:96, :], Ctab, ALU.mult), [psQ.b, tabs.b], [rtmp.b])
                    P.emit(dve, REC.tensor_tensor(q[64:96, 0, :], rtmp[0:32, 0, :], rtmp[0:32, 1, :], ALU.add), [rtmp.b], [q.b])
                    kb = kbuf[h % 2]
                    kbq_h = kbqs[h % 2]
                    vb = vbufs[h % 2]
                    if l == 0 and blk == NB - 1 and h == 0:
                        dump("Q0", q[0:96, 0, :], [q.b], npart=96)
                        dump("K0", Kcur[0:96, 0, :], [Kcur.b], npart=96)
                    if blk > 0:
                        for qi in range(4):
                            lo_, hi_ = qi * 1024, min(t0, (qi + 1) * 1024)
                            if hi_ > lo_:
                                P.dma(sp, kb[0:96, 0, lo_:hi_], kd_d[h, :, lo_:hi_], [b_kd], [kbq_h[qi]])
                        P.dma(sp, vb[:, 0:4 * blk, :], vd_d[h, :, 0:4 * blk, :], [b_vd], [vb.b])
                    psO = PS("acc")
                    nfull = 4 * blk
                    for j in range(nfull + 4):
                        pss = PS("mm")
                        pt = Pt[j % 3]
                        if j < nfull:
                            c0 = 0
                            P.emit(pe, REC.matmul(pss[:, :], kb[0:96, 0, j * 128:(j + 1) * 128], q[0:96, 0, :], start=True, stop=True),
                                   [kbq_h[(j * 128) // 1024], q.b], [pss.b])
                        else:
                            jj = j - nfull
                            c0 = jj * 128
                            P.emit(pe, REC.matmul(pss[:, c0:512], Kcur[0:96, h, c0:c0 + 128], q[0:96, 0, c0:512], start=True, stop=False),
                                   [Kcur.b, q.b], [pss.b], sig=False)
                            P.emit(pe, REC.matmul(pss[:, c0:c0 + 128], ident_b, mneg_b, start=False, stop=True),
                                   [cstb.b], [pss.b])
                        P.emit(act, REC.activation(pt[:, 0, c0:512], pss[:, c0:512], AF.Exp, scale=SCALE), [pss.b], [pt.b])
                        if j < nfull:
                            vl, vlb = vb[:, j, :], vb.b
                        else:
                            vl, vlb = Vcur[:, h, j - nfull, :], Vcur.b
                        P.emit(pe, REC.matmul(psO[0:65, c0:512], vl, pt[:, 0, c0:512], start=(j == 0), stop=(j == nfull + 3)),
                               [vlb, pt.b], [psO.b])
                    P.emit(act, REC.activation(rd[64:65, 0, :], psO[64:65, :], AF.Ln), [psO.b], [rd.b])
                    P.emit(act, REC.activation(rd[64:65, 0, :], rd[64:65, 0, :], AF.Exp, scale=-1.0), [rd.b], [rd.b])
                    psB = PS("mm")
                    P.emit(pe, REC.matmul(psB[0:64, :], ones_f[64:65, 0:64], rd[64:65, 0, :], start=True, stop=True), [cst32.b, rd.b], [psB.b])
                    P.emit(act, REC.activation(osb[0:64, 0, :], psO[0:64, :], AF.Copy), [psO.b], [osb.b])
                    P.emit(dve, REC.tensor_tensor(yT[(h % 2) * 64:(h % 2) * 64 + 64, h // 2, :], osb[0:64, 0, :], psB[0:64, :], ALU.mult),
                           [osb.b, psB.b], [yT.b])

                P.dma(pool, w_out_sb, w_out_d[l].rearrange("(k p) n -> p k n", p=128), [], [Kcur.b, bWo] + kbq)
                psY = [PS("acc"), PS("acc")]
                for tt in range(4):
                    ts = slice(tt * 128, (tt + 1) * 128)
                    P.emit(dve, REC.tensor_copy(prevp[:, 0:4:2, 0:64], state[:, 0:4:2, :]), [state.b], [prevp.b])
                    P.emit(dve, REC.tensor_copy(prevp[:, 1:4:2, 64:128], state[:, 1:4:2, :]), [state.b], [prevp.b])
                    for c in range(2):
                        P.emit(pe, REC.transpose(pst[:, c * 128:(c + 1) * 128], xs_b[:, c, ts], ident_b), [xs_b.b, cstb.b], [pst.b], sig=False)
                    for g in range(2):
                        P.emit(pe, REC.transpose(pst[:, 256 + g * 128:256 + (g + 1) * 128], Bt[:, g, ts], ident_b), [Bt.b, cstb.b], [pst.b], sig=(g == 1))
                    xtok = pst[:, 0:256].rearrange("p (h d) -> p h d", h=4)
                    P.emit(dve, REC.tensor_tensor(xdtp[:, 0:4:2, 0:64], xtok[:, 0:4:2, :], dtT[:, tt, 0:4:2].unsqueeze(2).to_broadcast([128, 2, 64]), ALU.mult),
                           [pst.b, dtT.b], [xdtp.b])
                    P.emit(dve, REC.tensor_tensor(xdtp[:, 1:4:2, 64:128], xtok[:, 1:4:2, :], dtT[:, tt, 1:4:2].unsqueeze(2).to_broadcast([128, 2, 64]), ALU.mult),
                           [pst.b, dtT.b], [xdtp.b])
                    P.emit(act, REC.activation(Btok[:, :, :], pst[:, 256:512].rearrange("p (g n) -> p g n", g=2), AF.Copy), [pst.b], [Btok.b])
                    P.emit(dve, REC.tensor_tensor(adtri[:, :, :], tri_f.unsqueeze(1).to_broadcast([128, 4, 128]),
                                                                 adt[:, tt, :].unsqueeze(2).to_broadcast([128, 4, 128]), ALU.mult), [cst32.b, adt.b], [adtri.b])
                    psR = PS("aux")
                    P.emit(pe, REC.matmul(psR[:, :], ones_f, adtri[:, :, :].rearrange("p h l -> p (h l)"), start=True, stop=True), [cst32.b, adtri.b], [psR.b])
                    psA = PS("aux")
                    P.emit(pe, REC.matmul(psA[:, 0:4], tri_f, adt[:, tt, :], start=True, stop=True), [cst32.b, adt.b], [psA.b])
                    acol = s4[:, 0, :]
                    P.emit(dve, REC.tensor_copy(acol, psA[:, 0:4]), [psA.b], [s4.b])
                    psR3 = psR[:, :].rearrange("p (h l) -> p h l", h=4)
                    P.emit(dve, REC.tensor_tensor(dec[:, :, :], psR3, acol.unsqueeze(2).to_broadcast([128, 4, 128]), ALU.subtract), [psR.b, s4.b], [dec.b])
                    P.emit(dve, REC.tensor_scalar(dec[:, :, :], dec[:, :, :], 0.0, None, ALU.min), [dec.b], [dec.b])
                    P.emit(act, REC.activation(dec[:, :, :], dec[:, :, :], AF.Exp), [dec.b], [dec.b])
                    P.emit(dve, REC.tensor_tensor(dec[:, :, :], dec[:, :, :], tri_f.unsqueeze(1).to_broadcast([128, 4, 128]), ALU.mult), [dec.b, cst32.b], [dec.b])
                    P.emit(act, REC.activation(drow[:, :, :], psR3, AF.Exp), [psR.b], [drow.b])
                    P.emit(dve, REC.tensor_tensor(s4[:, 1, :], psR3[:, :, 127], acol, ALU.subtract), [psR.b, s4.b], [s4.b])
                    P.emit(act, REC.activation(s4[:, 2, :], s4[:, 1, :], AF.Exp), [s4.b], [s4.b])
                    P.emit(act, REC.activation(s4[:, 3, :], psR3[:, :, 127], AF.Exp), [psR.b], [s4.b])
                    P.emit(dve, REC.tensor_tensor(s4[:, 4, :], s4[:, 2, :], dtT[:, tt, :], ALU.mult), [s4.b, dtT.b], [s4.b])
                    P.emit(dve, REC.tensor_tensor(xdtd[:, :, :], xtok, s4[:, 4, :].unsqueeze(2).to_broadcast([128, 4, 64]), ALU.mult), [pst.b, s4.b], [xdtd.b])
                    for g in range(2):
                        P.emit(dve, REC.tensor_tensor(Cs[:, 2 * g:2 * g + 2, :], drow[:, 2 * g:2 * g + 2, :],
                                                                          Ct[:, g:g + 1, ts].to_broadcast([128, 2, 128]), ALU.mult), [drow.b, Ct.b], [Cs.b])
                    psG = PS("mm")
                    for g in range(2):
                        P.emit(pe, REC.matmul(psG[:, g * 128:(g + 1) * 128], Bt[:, g, ts], Ct[:, g, ts], start=True, stop=True), [Bt.b, Ct.b], [psG.b], sig=(g == 1))
                    for g in range(2):
                        P.emit(dve, REC.tensor_tensor(scT[:, 2 * g:2 * g + 2, :], dec[:, 2 * g:2 * g + 2, :],
                                                                             psG[:, g * 128:(g + 1) * 128].unsqueeze(1).to_broadcast([128, 2, 128]), ALU.mult), [dec.b, psG.b], [scT.b])
                    for k in range(2):
                        ops = [(xdtp[:, 2 * k, :], scT[:, 2 * k, :], [xdtp.b, scT.b]), (xdtp[:, 2 * k + 1, :], scT[:, 2 * k + 1, :], [xdtp.b, scT.b]),
                               (prevp[:, 2 * k, :], Cs[:, 2 * k, :], [prevp.b, Cs.b]), (prevp[:, 2 * k + 1, :], Cs[:, 2 * k + 1, :], [prevp.b, Cs.b])]
                        for i, (lh, rh, rb) in enumerate(ops):
                            P.emit(pe, REC.matmul(psY[k][:, ts], lh, rh, start=(i == 0), stop=(i == 3)), rb, [psY[k].b], sig=(i == 3))
                    psSt = PS("aux")
                    for g in range(2):
                        P.emit(pe, REC.matmul(psSt[:, g * 128:(g + 1) * 128], Btok[:, g, :], xdtd[:, 2 * g:2 * g + 2, :].rearrange("p h d -> p (h d)"), start=True, stop=True),
                               [Btok.b, xdtd.b], [psSt.b], sig=(g == 1))
                    P.emit(dve, REC.tensor_tensor(state[:, :, :], state[:, :, :], s4[:, 3, :].unsqueeze(2).to_broadcast([128, 4, 64]), ALU.mult), [state.b, s4.b], [state.b])
                    P.emit(dve, REC.tensor_tensor(state[:, :, :], state[:, :, :], psSt[:, 0:256].rearrange("p (h d) -> p h d", h=4), ALU.add), [state.b, psSt.b], [state.b])
                for k in range(2):
                    P.emit(dve, REC.scalar_tensor_tensor(gat[:, k, :], xs_f[:, k, :], col(l, C_DSK + k), psY[k][:, :], ALU.mult, ALU.add), [xs_f.b, cols.b, psY[k].b], [gat.b])
                    P.emit(dve, REC.tensor_tensor(gat[:, k, :], gat[:, k, :], szs[:, k, :], ALU.mult), [gat.b, szs.b], [gat.b])
                    P.emit(act, REC.activation(gsq[:, k, :], gat[:, k, :], AF.Square), [gat.b], [gsq.b])
                ps = PS("aux")
                for k in range(2):
                    P.emit(pe, REC.matmul(ps[:, :], ones_b, gsq[:, k, :], start=(k == 0), stop=(k == 1)), [cstb.b, gsq.b], [ps.b], sig=(k == 1))
                P.emit(act, REC.activation(tmpA[:, :], ps[:, :], AF.Ln, scale=1.0 / 256, bias=epsc[:, 0:1]), [ps.b, epsc.b], [tmpA.b])
                P.emit(act, REC.activation(tmpA[:, :], tmpA[:, :], AF.Exp, scale=-0.5), [tmpA.b], [tmpA.b])
                for k in range(2):
                    P.emit(dve, REC.scalar_tensor_tensor(yT[:, 6 + k, :], gat[:, k, :], col(l, C_SSN + k), tmpA[:, :], ALU.mult, ALU.mult), [gat.b, cols.b, tmpA.b], [yT.b])

                if l == 0 and blk == NB - 1:
                    for kc in range(8):
                        dump("yT", yT[:, kc, :], [yT.b], idx=kc)
                for tt in range(4):
                    pl = []
                    for half in range(2):
                        ps = PS("mm")
                        for kc in range(8):
                            P.emit(pe, REC.matmul(ps[:, :], yT[:, kc, tt * 128:(tt + 1) * 128], w_out_sb[:, kc, half * 512:(half + 1) * 512],
                                                                                        start=(kc == 0), stop=(kc == 7)), [yT.b, bWo, Kcur.b] + kbq, [ps.b], sig=(kc == 7))
                        pl.append(ps)
                    post_norm_residual(pl, src_d, src_buf, xa_d, b_xa, t0 + tt * 128, tt, nocopy=True)

            bWF = Buf("wF%d" % l)
            P.barrier()
            set_pools({"mm": [0, 1, 2, 3, 4, 5, 6]}, [pst])
            w_up_sb = wreg.t[:, 0:8 * 2 * FF].rearrange("p (j k n) -> p j k n", j=44, k=8)
            w_down_sb = wreg.t[:, 8 * 2 * FF:8 * 2 * FF + NFC * D].rearrange("p (k n) -> p k n", k=NFC)
            assert (8 * 2 * FF + NFC * D) * 2 <= WREG
            bWU = [Buf("wu%d_%d" % (l, j)) for j in range(44)]
            for j in range(44):
                reg_buf("wreg", 2048 * j, 2048 * (j + 1), bWU[j])
            reg_buf("wreg", 2 * 8 * 2 * FF, 2 * (8 * 2 * FF + NFC * D), bWF)
            for i in range(NFC):
                for j in (i, NFC + i):
                    P.dma(pool, w_up_sb[:, j, :, :], w_up_d[l, j].rearrange("p (k n) -> p k n", k=8), [], [bWU[j]])
            wdv = w_down_d[l].rearrange("(k p) n -> p k n", p=128)
            P.dma(pool, w_down_sb[:, 0:11, :], wdv[:, 0:11, :], [], [bWF])
            P.dma(pool, w_down_sb[:, 11:22, :], wdv[:, 11:22, :], [], [bWF])
            P.dma(sp, rows[:, 0:1024], rows_d[l, :, 1024:2048], [], [rows.b])
            gT = Tile(reg2.t[:, 0:NFC * 512].rearrange("p (k n) -> p k n", k=NFC), Buf("gT%d" % l))
            reg_buf("reg2", 0, 2 * NFC * 512, gT.b)
            o2 = NFC * 512
            def carve2(shape):
                nonlocal o2
                n = int(np.prod(shape))
                a = reg2.t[:, o2:o2 + 2 * n].bitcast(F32)
                o2 += 2 * n
                t_ = Tile(a.rearrange("p (a b) -> p a b", a=shape[0]), Buf("r2_%d" % o2))
                reg_buf("reg2", 2 * (o2 - 2 * n), 2 * o2, t_.b)
                return t_
            ug = [carve2([1, 514]) for _ in range(2)]
            uu = [carve2([1, 514]) for _ in range(2)]
            fh = carve2([44, 2])
            tB = carve2([1, 512])
            tC = carve2([1, 512])
            tmpB = Tile(tB.t[:, 0, :], tB.b)
            tmpC = Tile(tC.t[:, 0, :], tC.b)
            assert o2 * 2 <= REG2, o2 * 2
            P.emit(dve, REC.memset(fh[:, :, :], 0.0), [], [fh.b])

            for blk in range(NB):
                t0 = blk * 512
                norm_transpose(xa_d, b_xa, t0, l, C_GPRE2)
                for i in range(NFC):
                    br = []
                    for bi, (ub, cbase) in enumerate(((ug[i % 2], i * 128), (uu[i % 2], FF + i * 128))):
                        ps = PS("mm")
                        for kc in range(8):
                            P.emit(pe, REC.matmul(ps[:, :], w_up_sb[:, i + bi * NFC, kc, :], hT[:, kc, :], start=(kc == 0), stop=(kc == 7)),
                                   [bWU[i + bi * NFC], hT.b], [ps.b], sig=(kc == 7))
                        j = i + bi * NFC
                        veng = dve
                        acc = tmpA if bi == 0 else tmpB
                        P.emit(act, REC.activation(ub[:, 0, 2:514], ps[:, :], AF.Copy), [ps.b], [ub.b])
                        P.emit(act, REC.activation(acc[:, :], ps[:, :], AF.Identity, scale=col(l, C_FW + j * 3 + 2), bias=col(l, C_FB + j)), [ps.b, cols.b], [acc.b])
                        P.emit(dve, REC.tensor_copy(ub[:, 0, 0:2], fh[:, j, :]), [fh.b], [ub.b])
                        for k in (0, 1):
                            P.emit(veng, REC.scalar_tensor_tensor(acc[:, :], ub[:, 0, k:k + 512], col(l, C_FW + j * 3 + k), acc[:, :], ALU.mult, ALU.add),
                                   [ub.b, cols.b, acc.b], [acc.b])
                        P.emit(dve, REC.tensor_copy(fh[:, j, :], ub[:, 0, 512:514]), [ub.b], [fh.b])
                    P.emit(act, REC.activation(tmpC[:, :], tmpA[:, :], AF.Silu), [tmpA.b], [tmpC.b])
                    P.emit(dve, REC.tensor_tensor(gT[:, i, :], tmpC[:, :], tmpB[:, :], ALU.mult), [tmpC.b, tmpB.b], [gT.b])
                for tt in range(4):
                    pl = []
                    for half in range(2):
                        ps = PS("mm")
                        for i in range(NFC):
                            P.emit(pe, REC.matmul(ps[:, :], gT[:, i, tt * 128:(tt + 1) * 128], w_down_sb[:, i, half * 512:(half + 1) * 512],
                                                                                       start=(i == 0), stop=(i == NFC - 1)), [gT.b, bWF], [ps.b], sig=(i == NFC - 1))
                        pl.append(ps)
                    post_norm_residual(pl, xa_d, b_xa, dst_d, dst_buf, t0 + tt * 128, tt)

        P.finalize()
        block = es.enter_context(nc.Block())

        @block.tensor
        def _(e):
            _replay(pe.q, e)

        @block.scalar
        def _(e):
            _replay(act.q, e)

        @block.vector
        def _(e):
            _replay(dve.q, e)

        @block.gpsimd
        def _(e):
            _replay(pool.q, e)

        @block.sync
        def _(e):
            _replay(sp.q, e)
    return nc


def _cols_layer(p, l):
    c = np.zeros((128, NCOLS), np.float32)
    def chunks(v, n):
        return np.asarray(v, np.float32).reshape(n, 128).T
    c[:, C_GPRE1:C_GPRE1 + 8] = chunks(p["norm_mix_pre"][l], 8)
    c[:, C_GPRE2:C_GPRE2 + 8] = chunks(p["norm_ffn_pre"][l], 8)
    c[:, C_QN:C_QN + 2] = chunks(p["mla_q_norm"][l], 2)
    c[:, C_KVN:C_KVN + 1] = chunks(p["mla_kv_norm"][l], 1)
    scw = np.asarray(p["sc_conv_w"][l], np.float32)
    for ch in range(2):
        for k in range(3):
            c[:, C_SCW + ch * 3 + k] = scw[k, ch * 128:(ch + 1) * 128]
    ssw = np.asarray(p["ssd_conv_w"][l], np.float32)
    ssb = np.asarray(p["ssd_conv_b"][l], np.float32)
    for ch in range(6):
        for k in range(4):
            c[:, C_SSW + ch * 4 + k] = ssw[k, ch * 128:(ch + 1) * 128]
        c[:, C_SSB + ch] = ssb[ch * 128:(ch + 1) * 128]
    dsk = np.repeat(np.asarray(p["ssd_d"][l], np.float32), 64)
    c[:, C_DSK:C_DSK + 2] = chunks(dsk, 2)
    c[:, C_SSN:C_SSN + 2] = chunks(p["ssd_norm"][l], 2)
    fw = np.asarray(p["ffn_conv_w"][l], np.float32)
    fb = np.asarray(p["ffn_conv_b"][l], np.float32)
    for j in range(44):
        for k in range(3):
            c[:, C_FW + j * 3 + k] = fw[k, j * 128:(j + 1) * 128]
        c[:, C_FB + j] = fb[j * 128:(j + 1) * 128]
    return c


def prep_shared(p, NL):
    f = lambda a: np.ascontiguousarray(np.asarray(a, np.float32))
    w_in = f(p["w_in"])[:NL]
    sw = np.concatenate([np.arange(16, 32), np.arange(0, 16)])
    kr = w_in[:, :, 384:416]
    w_in_r = np.concatenate([w_in[:, :, 0:384], kr, kr[:, :, sw], w_in[:, :, 416:2208]], axis=2)
    assert w_in_r.shape[2] == NCOLW
    groups = [(0, 128), (128, 128), (256, 128), (384, 32), (416, 32)] + [(448 + g * 128, 128) for g in range(14)]
    w4 = w_in_r.reshape(NL, 8, 128, NCOLW)
    w_in_g = np.concatenate([np.transpose(w4[:, :, :, c0:c0 + M], (0, 2, 1, 3)).reshape(NL, 128, 8 * M) for (c0, M) in groups], axis=2)
    assert w_in_g.shape[2] == 8 * NCOLW
    w_dt = w_in[:, :, 2208:2212]
    wq = f(p["mla_w_q_up"])[:NL]
    wq4 = wq.reshape(NL, 256, 8, 96)
    w_qsw = wq4[:, :, :, 64:96][:, :, :, sw].reshape(NL, 256, 256)
    wkv = f(p["mla_w_kv_up"])[:NL].reshape(NL, 128, 8, 128)
    w_kn = wkv[:, :, :, 0:64].reshape(NL, 128, 512)
    w_v = wkv[:, :, :, 64:128].reshape(NL, 128, 512)
    wu = f(p["ffn_w_up"])[:NL].reshape(NL, 8, 128, 44, 128)
    w_up_g = np.ascontiguousarray(np.transpose(wu, (0, 3, 2, 1, 4)).reshape(NL, 44, 128, 1024))
    cols = np.stack([_cols_layer(p, l) for l in range(NL)], axis=1)
    rows = np.zeros((NL, 128, 2056), np.float32)
    for l in range(NL):
        rows[l, :, 0:1024] = np.asarray(p["norm_mix_post"][l], np.float32)[None, :]
        rows[l, :, 1024:2048] = np.asarray(p["norm_ffn_post"][l], np.float32)[None, :]
        rows[l, :, 2048:2052] = np.asarray(p["ssd_dt_bias"][l], np.float32)[None, :]
        rows[l, :, 2052:2056] = np.asarray(p["ssd_a_log"][l], np.float32)[None, :]
    inv_freq = (1.0 / (10000.0 ** (np.arange(0, 32, 2, dtype=np.float32) / np.float32(32)))).astype(np.float32)
    ropec = np.zeros((32, 2), np.float32)
    ropec[:, 0] = np.concatenate([inv_freq, inv_freq])
    ropec[:, 1] = np.concatenate([-np.ones(16), np.ones(16)])
    tri_ = np.triu(np.ones((128, 128)))
    consts = np.concatenate([np.eye(128), tri_, np.ones((128, 128)), (1.0 - tri_) * -30000.0], axis=1).astype(np.float32)
    return {
        "ropec": ropec, "consts": consts, "cols": np.ascontiguousarray(cols), "rows": rows,
        "w_in_g": np.ascontiguousarray(w_in_g), "w_dt": np.ascontiguousarray(w_dt),
        "w_q": np.ascontiguousarray(wq), "w_qsw": np.ascontiguousarray(w_qsw),
        "w_kn": np.ascontiguousarray(w_kn), "w_v": np.ascontiguousarray(w_v),
        "w_out": f(p["w_out"])[:NL], "w_up_g": w_up_g, "w_down": f(p["ffn_w_down"])[:NL],
    }


def run(inputs, S, NL, ncores, dbg=None):
    shared = prep_shared(inputs, NL)
    x = np.asarray(inputs["x"], np.float32)
    pos = np.asarray(inputs["positions"], np.int32)
    in_maps = []
    for c in range(ncores):
        m = dict(shared)
        m["x"] = np.ascontiguousarray(x[c, :S])
        m["posrep"] = np.ascontiguousarray(np.broadcast_to(pos[c, :S][None, :], (32, S)))
        in_maps.append(m)
    nc = build(S, NL, dbg)
    res = run_bass_kernel_spmd(nc, in_maps, core_ids=list(range(ncores)))
    return res


def kernel(**inputs):
    res = run(inputs, 4096, 4, 8)
    return np.stack([r["y"] for r in res.results], axis=0).astype(np.float32)
```

```python
import math
from contextlib import ExitStack
import numpy as np
import concourse.bass as bass
import concourse.mybir as mybir
from concourse.bass_utils import run_bass_kernel_spmd

F32 = mybir.dt.float32
BF16 = mybir.dt.bfloat16
I32 = mybir.dt.int32
AF = mybir.ActivationFunctionType
ALU = mybir.AluOpType

D = 1024
NH = 8
FF = 2816
NFC = 22
EPS = 1e-6
SCALE = 96 ** -0.5
NCOLW = 2240
PI = float(np.pi)
F_ROPECACHE = True
import os
SHIFT_M = 0
PRIO_RANK = 0
NOBARRIER = 1
RANK_ENGS = ('pe',)

C_GPRE1 = 0
C_GPRE2 = 8
C_QN = 16
C_KVN = 18
C_SCW = 19
C_SSW = 25
C_SSB = 49
C_DSK = 55
C_SSN = 57
C_FW = 59
C_FB = 191
NCOLS = 235


class _Rec:
    def __getattr__(self, name):
        def f(*args, **kwargs):
            return (name, args, kwargs)
        return f


REC = _Rec()


def _run(fn, e):
    if callable(fn):
        return fn(e)
    name, args, kwargs = fn
    return getattr(e, name)(*args, **kwargs)


class Src:
    def __init__(self, sem, step):
        self.sem, self.step, self.count = sem, step, 0


class Buf:
    def __init__(self, name):
        self.name, self.writer, self.readers = name, None, []


class Eng:
    def __init__(self, name, src, inorder=False):
        self.name, self.src, self.q, self.waited, self.inorder = name, src, [], {}, inorder


class Tile:
    def __init__(self, t, b):
        self.t, self.b = t, b

    def __getitem__(self, k):
        return self.t[k]


class Op:
    __slots__ = ("eng", "insts", "reads", "writes", "idx", "preds", "succs", "cost", "is_dma", "lat",
                 "npred", "ready", "finish", "src", "val", "prev_val", "seg", "key", "rank")

    def __init__(self, eng):
        self.eng, self.insts, self.reads, self.writes = eng, [], [], []
        self.preds, self.succs = set(), []
        self.cost, self.is_dma, self.lat = 0.0, False, 0.0
        self.ready, self.finish = 0.0, 0.0
        self.src = self.val = self.prev_val = None


def _free_size(ap):
    n = 1
    for d in ap.shape[1:]:
        n *= int(d)
    return n


def _est_cost(eng, fn):
    name, args, kwargs = fn
    try:
        if name == "matmul":
            rhs = args[2]
            n = _free_size(rhs)
            return (max(28.0, n / 2.4) + 6.0) * (4.0 if rhs.dtype == F32 else 1.0)
        if name == "transpose":
            return 75.0
        if name == "dma_start":
            return 120.0
        out = args[0] if args else kwargs.get("out")
        n = _free_size(out)
        if eng.name == "act":
            return 150.0 + 0.85 * n
        if name == "reciprocal":
            return 160.0 + 2.6 * n
        if name == "memset":
            return 100.0 + 0.3 * n
        if eng.name == "pool":
            return 250.0 + 2.1 * n
        return 150.0 + 1.04 * n
    except Exception:
        return 300.0


class Prog:
    LOOK_W = 24
    LOOK_IDX = 6000

    def __init__(self, nc, es, n_dma_sems=24):
        self.nc, self.es = nc, es
        def sem(n):
            return es.enter_context(nc.semaphore(n))
        self.pe = Eng("pe", Src(sem("s_pe"), 1), inorder=True)
        self.act = Eng("act", Src(sem("s_act"), 1))
        self.dve = Eng("dve", Src(sem("s_dve"), 1))
        self.pool = Eng("pool", Src(sem("s_pool"), 1))
        self.sp = Eng("sp", Src(sem("s_sp"), 1))
        self.engs = [self.pe, self.act, self.dve, self.pool, self.sp]
        self.dma_srcs = [Src(sem("s_dma%d" % i), 16) for i in range(n_dma_sems)]
        self.dma_rr = {"sp": 0, "pool": 0}
        self.dma_part = {"sp": self.dma_srcs[:n_dma_sems - 8], "pool": self.dma_srcs[n_dma_sems - 8:]}
        self.ops = []
        self.cur = {}
        self.seg_start = [0]
        self.ninst = 0
        self.shift = 0

    def tile(self, name, shape, dt):
        t = self.es.enter_context(self.nc.sbuf_tensor("sb_" + name, shape, dt))
        return Tile(t, Buf(name))

    def emit(self, eng, fn, reads=(), writes=(), sig=True, dma_bytes=None):
        op = self.cur.get(eng.name)
        if op is None:
            op = Op(eng)
            self.cur[eng.name] = op
        op.insts.append(fn)
        op.reads.extend(reads)
        op.writes.extend(writes)
        op.cost += _est_cost(eng, fn)
        self.ninst += 1
        if dma_bytes is not None:
            op.is_dma = True
            op.lat = float(dma_bytes)
        if sig:
            self.cur[eng.name] = None
            self._close(op)

    def _close(self, op):
        seg0 = self.seg_start[-1]
        op.idx = len(self.ops)
        op.key = op.idx + self.shift
        op.seg = len(self.seg_start) - 1
        preds = set()
        for b in op.reads:
            if b.writer is not None:
                preds.add(b.writer)
        for b in op.writes:
            if b.writer is not None:
                preds.add(b.writer)
            preds.update(b.readers)
        preds.discard(op)
        op.preds = {p for p in preds if p.idx >= seg0}
        wset = set(id(b) for b in op.writes)
        for b in op.writes:
            b.writer = op
            b.readers = []
        for b in op.reads:
            if id(b) not in wset:
                b.readers.append(op)
        self.ops.append(op)

    def dma(self, eng, out, in_, reads=(), writes=()):
        nbytes = 1
        for d in out.shape:
            nbytes *= int(d)
        nbytes *= 2 if out.dtype == BF16 else 4
        nb2 = 1
        for d in in_.shape:
            nb2 *= int(d)
        nb2 *= 2 if in_.dtype == BF16 else 4
        self.emit(eng, REC.dma_start(out=out, in_=in_), reads, writes, dma_bytes=max(nbytes, nb2))

    def barrier(self):
        if NOBARRIER:
            return
        assert all(v is None for v in self.cur.values())
        if len(self.ops) > self.seg_start[-1]:
            self.seg_start.append(len(self.ops))

    def _schedule(self, ops):
        import bisect
        for op in ops:
            op.succs = []
            op.ready = 0.0
        for op in ops:
            op.npred = len(op.preds)
            for p in op.preds:
                p.succs.append(op)
        if RANK_ENGS:
            for op in reversed(ops):
                m = 0.0
                for sc in op.succs:
                    if sc.rank > m:
                        m = sc.rank
                op.rank = m + op.cost + (op.lat / 170.0 + 2000.0 if op.is_dma else 0.0)
                if op.eng.name in RANK_ENGS:
                    op.key = -op.rank
        avail = {e.name: [] for e in self.engs}
        t_free = {e.name: 0.0 for e in self.engs}
        order = {e.name: [] for e in self.engs}
        for op in ops:
            if op.npred == 0:
                avail[op.eng.name].append((op.key, op.idx, op))
        for e in self.engs:
            avail[e.name].sort(key=lambda x: (x[0], x[1]))
        scheduled = 0
        dma_free = 0.0
        n = len(ops)
        base = ops[0].idx
        done = [False] * n
        lo = 0
        while scheduled < n:
            while lo < n and done[lo]:
                lo += 1
            lim = base + lo + self.LOOK_IDX
            best = None
            for e in self.engs:
                lst = avail[e.name]
                if not lst:
                    continue
                tf = t_free[e.name]
                for (k_, idx, op) in lst[:self.LOOK_W]:
                    if idx > lim and best is not None:
                        continue
                    st = op.ready if op.ready > tf else tf
                    key = (st, k_, idx)
                    if best is None or key < best[0]:
                        best = (key, e, op)
            (st, k_, idx), e, op = best
            avail[e.name].remove((k_, idx, op))
            if op.is_dma:
                t_free[e.name] = st + op.cost
                xs = max(st + op.cost, dma_free)
                dma_free = xs + op.lat / 170.0
                op.finish = dma_free + 2000.0
            else:
                op.finish = st + op.cost
                t_free[e.name] = op.finish
            order[e.name].append(op)
            done[op.idx - base] = True
            scheduled += 1
            for sc in op.succs:
                if sc.ready < op.finish:
                    sc.ready = op.finish
                sc.npred -= 1
                if sc.npred == 0:
                    bisect.insort(avail[sc.eng.name], (sc.key, sc.idx, sc))
        return order, max(op.finish for op in ops)

    def _wait(self, eng, src, val):
        if val <= 0 or eng.waited.get(src, 0) >= val:
            return
        eng.waited[src] = val
        eng.q.append(REC.wait_ge(src.sem, val))

    def _barrier_waits(self):
        srcs = [e.src for e in self.engs] + self.dma_srcs
        for e in self.engs:
            for sr in srcs:
                if sr is not e.src:
                    self._wait(e, sr, sr.count)

    def finalize(self):
        assert all(v is None for v in self.cur.values())
        bounds = self.seg_start + [len(self.ops)]
        total = 0.0
        for si in range(len(bounds) - 1):
            ops = self.ops[bounds[si]:bounds[si + 1]]
            if not ops:
                continue
            if si > 0:
                self._barrier_waits()
            order, span = self._schedule(ops)
            total += span
            sb = {}
            for op in ops:
                sb[op.eng.name] = sb.get(op.eng.name, 0.0) + op.cost
            print("seg", si, "span us %.1f" % (span / 1e3), {k: round(v / 1e3, 1) for k, v in sb.items()}, flush=True)
            for e in self.engs:
                for op in order[e.name]:
                    if op.is_dma:
                        part = self.dma_part[e.name]
                        src = part[self.dma_rr[e.name] % len(part)]
                        self.dma_rr[e.name] += 1
                        op.prev_val = src.count
                    else:
                        src = e.src
                    src.count += src.step
                    op.src, op.val = src, src.count
            for e in self.engs:
                for op in order[e.name]:
                    need = {}
                    for p in op.preds:
                        if e.inorder and p.src is e.src:
                            continue
                        if need.get(p.src, 0) < p.val:
                            need[p.src] = p.val
                    for sr, v in need.items():
                        self._wait(e, sr, v)
                    if op.is_dma:
                        self._wait(e, op.src, op.prev_val)
                    last = len(op.insts) - 1
                    for i, fn in enumerate(op.insts):
                        if i == last:
                            e.q.append(("__sig__", fn, op.src.sem, op.src.step))
                        else:
                            e.q.append(fn)
        for e in (self.sp, self.pool):
            for s_ in self.dma_srcs:
                self._wait(e, s_, s_.count)
        busy = {}
        for op in self.ops:
            busy[op.eng.name] = busy.get(op.eng.name, 0.0) + op.cost
        print("busy ms:", {k: round(v / 1e6, 3) for k, v in busy.items()}, flush=True)
        print("ops:", len(self.ops), "insts:", self.ninst, "est span ms: %.3f" % (total / 1e6),
              {e.name: len(e.q) for e in self.engs}, flush=True)


def _replay(q, e):
    for item in q:
        if item[0] == "__sig__":
            _, fn, sem, step = item
            _run(fn, e).then_inc(sem, step)
        else:
            _run(item, e)


def build(S, NL, dbg=None):
    NB = S // 512
    NKB = S // 128
    nc = bass.Bass("TRN2", target_bir_lowering=False)

    def din(name, shape, dt=F32):
        return nc.dram_tensor(name, shape, dt, kind="ExternalInput").ap()

    x_d = din("x", [S, D])
    pos_d = din("posrep", [32, S], I32)
    rc_d = din("ropec", [32, 2])
    cst_d = din("consts", [128, 512])
    cols_d = din("cols", [128, NL, NCOLS])
    rows_d = din("rows", [NL, 128, 2056])
    w_in_d = din("w_in_g", [NL, 128, 8 * NCOLW])
    w_dt_d = din("w_dt", [NL, D, 4])
    w_q_d = din("w_q", [NL, 256, 768])
    w_qsw_d = din("w_qsw", [NL, 256, 256])
    w_kn_d = din("w_kn", [NL, 128, 512])
    w_v_d = din("w_v", [NL, 128, 512])
    w_out_d = din("w_out", [NL, D, D])
    w_up_d = din("w_up_g", [NL, 44, 128, 1024])
    w_down_d = din("w_down", [NL, FF, D])
    y_d = nc.dram_tensor("y", [S, D], F32, kind="ExternalOutput").ap()
    xa_d = nc.dram_tensor("xa", [S, D], F32, kind="Internal").ap()
    xb_d = [nc.dram_tensor("xb%d" % i, [S, D], F32, kind="Internal").ap() for i in range(2)]
    kd_d = nc.dram_tensor("kd", [NH, 96, S], BF16, kind="Internal").ap()
    vd_d = nc.dram_tensor("vd", [NH, 128, NKB, 65], BF16, kind="Internal").ap()
    tab_d = nc.dram_tensor("tabd", [NB, 32, 1024], F32, kind="Internal").ap() if F_ROPECACHE else None
    dbg_d = {}
    if dbg:
        for name, shape in dbg.items():
            dbg_d[name] = nc.dram_tensor("dbg_" + name, shape, F32, kind="ExternalOutput").ap()

    es = ExitStack()
    with es:
        P = Prog(nc, es)
        pe, act, dve, pool, sp = P.pe, P.act, P.dve, P.pool, P.sp
        T = P.tile
        b_xa, b_kd = Buf("xa"), Buf("kd")
        b_vd = Buf("vd")
        b_tab = [Buf("tab%d" % i) for i in range(NB)]
        b_xb = [Buf("xb0"), Buf("xb1")]
        b_y = Buf("y")
        psb = []
        for i in range(7):
            t = es.enter_context(nc.psum_tensor("ps%d" % i, [128, 512], F32))
            psb.append(Tile(t, Buf("ps%d" % i)))
        pst = Tile(es.enter_context(nc.psum_tensor("pst", [128, 1024], BF16)), Buf("pst"))
        pools = {"mm": [0, 1, 2], "acc": [3, 4], "aux": [5, 6]}
        prr = {"mm": 0, "acc": 0, "aux": 0}

        pst_list = [pst]

        def set_pools(cfg, psts):
            pools.clear()
            pools.update(cfg)
            pst_list[:] = psts

        def PS(pool_name):
            lst = pools[pool_name]
            i = lst[prr[pool_name] % len(lst)]
            prr[pool_name] += 1
            return psb[i]

        cst32 = T("cst32", [128, 256], F32)
        cstb = T("cstb", [128, 512], BF16)
        cols = T("cols", [128, NCOLS], F32)
        ropec = T("ropec", [32, 2], F32)
        epsc = T("epsc", [128, 1], F32)
        P.dma(sp, cst32[:], cst_d[:, 128:384], writes=[cst32.b])
        P.dma(pool, cstb[:], cst_d, writes=[cstb.b])
        P.dma(sp, ropec[:], rc_d, writes=[ropec.b])
        P.emit(dve, REC.memset(epsc[:], EPS), [], [epsc.b])
        ident_b = cstb[:, 0:128]
        tri_b = cstb[:, 128:256]
        ones_b = cstb[:, 256:384]
        mneg_b = cstb[:, 384:512]
        tri_f = cst32[:, 0:128]
        ones_f = cst32[:, 128:256]

        xin = [T("xin%d" % i, [128, D], F32) for i in range(2)]
        mixs = [T("mix%d" % i, [128, D], F32) for i in range(2)]
        xr = T("xr", [128, D], F32)
        junkP = T("junkP", [128, 512], BF16)
        smP = T("smP", [128, 8], F32)
        hbs = [T("hb0", [128, D], BF16)]
        hT = T("hT", [128, 8, 512], BF16)
        junk = T("junk", [128, 512], BF16)
        sms = [T("sm%d" % i, [128, 16], F32) for i in range(2)]
        rows = T("rows", [128, 1032], F32)
        tmpA = T("tmpA", [128, 512], F32)
        WREG = 135168
        wreg = T("wreg", [128, WREG // 2], BF16)
        REG2 = 35200
        reg2 = T("reg2", [128, REG2 // 2], BF16)

        def col(l, c, n=1):
            return cols[:, c:c + n]

        def rstd_from_ss(ss_ap, ss_bufs, n, out_ap, out_buf):
            P.emit(act, REC.activation(out_ap, ss_ap, AF.Ln, scale=1.0 / n, bias=epsc[:, 0:1]),
                   list(ss_bufs) + [epsc.b], [out_buf])
            P.emit(act, REC.activation(out_ap, out_ap, AF.Exp, scale=-0.5), [out_buf], [out_buf])

        def dump(name, ap, bufs, npart=128, idx=None):
            if not dbg or name not in dbg:
                return
            dst = dbg_d[name] if idx is None else dbg_d[name][idx]
            P.emit(dve, REC.tensor_copy(tmpA[0:npart, :], ap), bufs, [tmpA.b])
            P.dma(sp, dst[0:npart, :], tmpA[0:npart, :], [tmpA.b], [])

        def norm_transpose(src_d, src_buf, row0, l, gcol):
            for tt in range(4):
                xt = xin[tt % 2]
                sm = sms[0]
                hb = hbs[0]
                pt_ = pst_list[tt % len(pst_list)]
                r0 = row0 + tt * 128
                P.dma(sp, xt[:], src_d[r0:r0 + 128, :], [src_buf], [xt.b])
                P.emit(act, REC.activation(junk[:, :], xt[:, 0:512], AF.Square, accum_out=sm[:, 0:1]),
                       [xt.b], [junk.b, sm.b])
                P.emit(act, REC.activation(junk[:, :], xt[:, 512:1024], AF.Square, accum_out=sm[:, 1:2]),
                       [xt.b], [junk.b, sm.b])
                P.emit(dve, REC.tensor_tensor(sm[:, 2:3], sm[:, 0:1], sm[:, 1:2], ALU.add), [sm.b], [sm.b])
                rstd_from_ss(sm[:, 2:3], [sm.b], D, sm[:, 3:4], sm.b)
                P.emit(dve, REC.tensor_scalar(hb[:], xt[:], sm[:, 3:4], None, ALU.mult),
                       [xt.b, sm.b], [hb.b])
                for kc in range(8):
                    P.emit(pe, REC.transpose(pt_[:, kc * 128:(kc + 1) * 128], hb[:, kc * 128:(kc + 1) * 128], ident_b),
                           [hb.b, cstb.b], [pt_.b], sig=(kc == 7))
                P.emit(dve, REC.tensor_tensor(
                    hT[:, :, tt * 128:(tt + 1) * 128],
                    pt_[:, :].rearrange("p (k t) -> p k t", k=8),
                    col(l, gcol, 8).unsqueeze(2).to_broadcast([128, 8, 128]), ALU.mult),
                    [pt_.b, cols.b], [hT.b])

        def post_norm_residual(ps_list, src_d, src_buf, dst_d, dst_buf, r0, k, nocopy=False):
            xt = xr
            mx = mixs[k % 2]
            P.dma(sp, xt[:], src_d[r0:r0 + 128, :], [src_buf], [xt.b])
            for half in range(2):
                ps = ps_list[half]
                if nocopy:
                    P.emit(act, REC.activation(junkP[:, :], ps[:, :], AF.Square, accum_out=smP[:, 4 + half:5 + half]),
                           [ps.b], [junkP.b, smP.b])
                else:
                    P.emit(act, REC.activation(mx[:, half * 512:(half + 1) * 512], ps[:, :], AF.Copy), [ps.b], [mx.b])
                    P.emit(act, REC.activation(junkP[:, :], mx[:, half * 512:(half + 1) * 512], AF.Square,
                                               accum_out=smP[:, 4 + half:5 + half]), [mx.b], [junkP.b, smP.b])
            P.emit(dve, REC.tensor_tensor(smP[:, 6:7], smP[:, 4:5], smP[:, 5:6], ALU.add), [smP.b], [smP.b])
            rstd_from_ss(smP[:, 6:7], [smP.b], D, smP[:, 7:8], smP.b)
            if nocopy:
                for half in range(2):
                    ps = ps_list[half]
                    P.emit(dve, REC.scalar_tensor_tensor(mx[:, half * 512:(half + 1) * 512], ps[:, :], smP[:, 7:8], rows[:, half * 512:(half + 1) * 512], ALU.mult, ALU.mult),
                           [ps.b, smP.b, rows.b], [mx.b])
            else:
                P.emit(dve, REC.scalar_tensor_tensor(mx[:], mx[:], smP[:, 7:8], rows[:, 0:1024], ALU.mult, ALU.mult),
                       [mx.b, smP.b, rows.b], [mx.b])
            P.emit(dve, REC.tensor_tensor(mx[:], mx[:], xt[:], ALU.add), [mx.b, xt.b], [mx.b])
            P.dma(sp, dst_d[r0:r0 + 128, :], mx[:], [mx.b], [dst_buf])

        REG = {"wreg": [], "reg2": []}

        def reg_buf(region, sb_, eb_, buf):
            for (s0, e0, ob) in REG[region]:
                if s0 < eb_ and sb_ < e0 and ob is not buf:
                    if ob.writer is not None and ob.writer not in buf.readers:
                        buf.readers.append(ob.writer)
                    for r_ in ob.readers:
                        if r_ not in buf.readers:
                            buf.readers.append(r_)
            REG[region].append((sb_, eb_, buf))

        for l in range(NL):
            src_d, src_buf = (x_d, Buf("xsrc")) if l == 0 else (xb_d[(l - 1) % 2], b_xb[(l - 1) % 2])
            dst_d, dst_buf = (y_d, b_y) if l == NL - 1 else (xb_d[l % 2], b_xb[l % 2])

            o = 0
            def carve(n_el):
                nonlocal o
                a = wreg.t[:, o:o + n_el]
                o += n_el
                return a
            w_in_sb = carve(8 * NCOLW)
            w_dt_sb = carve(8 * 4).rearrange("p (k n) -> p k n", k=8)
            w_q_sb = carve(2 * 768).rearrange("p (k n) -> p k n", k=2)
            w_qsw_sb = carve(2 * 256).rearrange("p (k n) -> p k n", k=2)
            w_kn_sb = carve(512)
            w_v_sb = carve(512)
            def carve_t(shape, dt):
                nonlocal o
                n = int(np.prod(shape))
                if dt == F32:
                    o += (o % 2)
                    a = wreg.t[:, o:o + 2 * n].bitcast(F32)
                    o += 2 * n
                else:
                    a = wreg.t[:, o:o + n]
                    o += n
                carve_t.last = (2 * (o - (2 * n if dt == F32 else n)), 2 * o)
                if len(shape) == 2:
                    return a.rearrange("p (a b) -> p a b", a=shape[0])
                if len(shape) == 3:
                    return a.rearrange("p (a b c) -> p a b c", a=shape[0], b=shape[1])
                return a
            bW = Buf("wM%d" % l)
            GROUPS = [(0, 128), (128, 128), (256, 128), (384, 32), (416, 32)] + [(448 + g * 128, 128) for g in range(14)]
            bWg = {}
            bWo = Buf("wo%d" % l)
            P.barrier()
            reg_buf("wreg", 2 * 8 * NCOLW, 2 * o, bW)
            set_pools({"mm": [0, 1, 2], "acc": [3, 4], "aux": [5, 6]}, [pst])
            for (dst, srcap) in [
                (w_dt_sb, w_dt_d[l].rearrange("(k p) n -> p k n", p=128)),
                (w_q_sb, w_q_d[l].rearrange("(k p) n -> p k n", p=128)),
                (w_qsw_sb, w_qsw_d[l].rearrange("(k p) n -> p k n", p=128)),
                (w_kn_sb, w_kn_d[l]),
                (w_v_sb, w_v_d[l]),
            ]:
                P.dma(pool, dst, srcap, [], [bW])
            for (c0g, Mg) in GROUPS:
                bWg[c0g] = Buf("wg%d_%d" % (l, c0g))
                reg_buf("wreg", 2 * 8 * c0g, 2 * 8 * (c0g + Mg), bWg[c0g])
                P.dma(pool, w_in_sb[:, 8 * c0g:8 * (c0g + Mg)], w_in_d[l, :, 8 * c0g:8 * (c0g + Mg)], [], [bWg[c0g]])
            P.dma(sp, cols[:, :], cols_d[:, l, :], [], [cols.b])
            P.dma(sp, rows[:, 0:1024], rows_d[l, :, 0:1024], [], [rows.b])
            P.dma(sp, rows[:, 1024:1032], rows_d[l, :, 2048:2056], [], [rows.b])

            def MT(name, shape, dt):
                t_ = Tile(carve_t(shape, dt), Buf(name))
                reg_buf("wreg", carve_t.last[0], carve_t.last[1], t_.b)
                return t_
            cq_sb = MT("cq", [3, 512], BF16)
            cq_sq = MT("cqsq", [3, 512], BF16)
            cqns = [MT("cqn%d" % i, [3, 512], BF16) for i in range(2)]
            conv_sb = MT("conv", [4, 512], F32)
            szs = MT("szs", [2, 512], F32)
            ubs = [MT("ub%d" % i, [1, 515], F32) for i in range(2)]
            uh = MT("uh", [6, 3], F32)
            vbuf = MT("vbuf", [2, 514], F32)
            xs_f = MT("xsf", [2, 512], F32)
            xs_b = MT("xsb", [2, 512], BF16)
            Bt = MT("Bt", [2, 512], BF16)
            Ct = MT("Ct", [2, 512], BF16)
            o_wout = o
            Kcur = MT("Kcur", [8, 512], BF16)
            kbuf = [MT("kbuf0", [1, 4096], BF16)]
            kbq = [Buf("kbq%d" % i) for i in range(4)]
            for i_ in range(4):
                reg_buf("wreg", carve_t.last[0] + 2048 * i_, carve_t.last[0] + 2048 * (i_ + 1), kbq[i_])
            reg_buf("wreg", 2 * o_wout, 2 * (o_wout + 8 * D), bWo)
            w_out_sb = wreg.t[:, o_wout:o_wout + 8 * D].rearrange("p (k n) -> p k n", k=8)
            assert o - o_wout == 8 * D
            Qh = [MT("Qh%d" % i, [1, 512], BF16) for i in range(2)]
            Pt = [MT("Pt%d" % i, [1, 512], BF16) for i in range(3)]
            tabs = MT("tabs", [2, 512], F32)
            rtmp = MT("rtmp", [2, 512], F32)
            rint = Tile(carve_t([1, 512], F32).bitcast(I32), Buf("rint"))
            krot = MT("krot", [1, 512], F32)
            rd = Tile(krot.t, Buf("rd"))
            osb = MT("osb", [1, 512], F32)
            dtT = MT("dtT", [4, 4], F32)
            adt = MT("adt", [4, 4], F32)
            arow = MT("arow", [1, 4], F32)
            state = MT("state", [4, 64], F32)
            prevp = MT("prevp", [4, 128], BF16)
            xdtp = MT("xdtp", [4, 128], BF16)
            xdtd = MT("xdtd", [4, 64], BF16)
            Btok = MT("Btok", [2, 128], BF16)
            adtri = MT("adtri", [4, 128], F32)
            dec = MT("dec", [4, 128], F32)
            drow = MT("drow", [4, 128], F32)
            Cs = MT("Cs", [4, 128], BF16)
            scT = MT("scT", [4, 128], BF16)
            s4 = MT("s4", [8, 4], F32)
            assert o * 2 <= WREG, o * 2
            o3 = 0
            def R2T(name, shape, dt):
                nonlocal o3
                n = int(np.prod(shape))
                if dt == F32:
                    a_ = reg2.t[:, o3:o3 + 2 * n].bitcast(F32)
                    o3 += 2 * n
                else:
                    a_ = reg2.t[:, o3:o3 + n]
                    o3 += n
                if len(shape) == 2:
                    a_ = a_.rearrange("p (a b) -> p a b", a=shape[0])
                elif len(shape) == 3:
                    a_ = a_.rearrange("p (a b c) -> p a b c", a=shape[0], b=shape[1])
                t_ = Tile(a_, Buf(name))
                R2T.last = (2 * (o3 - (2 * n if dt == F32 else n)), 2 * o3)
                reg_buf("reg2", R2T.last[0], R2T.last[1], t_.b)
                return t_
            gat = R2T("gat", [2, 512], F32)
            yT = R2T("yT", [8, 512], BF16)
            gsq = R2T("gsq", [2, 512], BF16)
            Vcur = R2T("Vcur", [8, 4, 65], BF16)
            vbufs = [R2T("vbuf%d" % i, [28, 65], BF16) for i in range(2)]
            kbuf.append(R2T("kbuf1", [1, 4096], BF16))
            kbqs = [kbq, [Buf("kbq1_%d" % i) for i in range(4)]]
            for i_ in range(4):
                reg_buf("reg2", R2T.last[0] + 2048 * i_, R2T.last[0] + 2048 * (i_ + 1), kbqs[1][i_])
            assert o3 * 2 <= REG2, o3 * 2

            P.emit(dve, REC.memset(Vcur[:, :, :, 64:65], 1.0), [], [Vcur.b])
            P.emit(dve, REC.memset(vbuf[:, :, 0:2], 0.0), [], [vbuf.b])
            P.emit(dve, REC.memset(uh[:, :, :], 0.0), [], [uh.b])
            P.emit(dve, REC.memset(state[:], 0.0), [], [state.b])
            P.emit(dve, REC.memset(prevp[:], 0.0), [], [prevp.b])
            P.emit(dve, REC.memset(xdtp[:], 0.0), [], [xdtp.b])
            P.emit(act, REC.activation(arow[:, 0, :], rows[:, 1028:1032], AF.Exp), [rows.b], [arow.b])
            P.emit(dve, REC.tensor_scalar(arow[:, 0, :], arow[:, 0, :], -1.0, None, ALU.mult), [arow.b], [arow.b])

            for blk in range(NB):
                t0 = blk * 512
                cqn = cqns[blk % 2]
                P.shift = -SHIFT_M
                if l == 0 or not F_ROPECACHE:
                    P.dma(sp, rint[0:32, 0, :], pos_d[:, t0:t0 + 512], [], [rint.b])
                    P.emit(dve, REC.tensor_copy(rtmp[0:32, 0, :], rint[0:32, 0, :]), [rint.b], [rtmp.b])
                    P.emit(dve, REC.tensor_scalar(rtmp[0:32, 0, :], rtmp[0:32, 0, :], ropec[:, 0:1], None, ALU.mult),
                           [rtmp.b, ropec.b], [rtmp.b])
                    for ti, phase in ((0, PI / 2), (1, 0.0)):
                        P.emit(dve, REC.tensor_scalar(rtmp[0:32, 1, :], rtmp[0:32, 0, :], 1.0 / (2 * PI), phase / (2 * PI) + 0.5, ALU.mult, ALU.add),
                               [rtmp.b], [rtmp.b])
                        P.emit(dve, REC.tensor_copy(rint[0:32, 0, :], rtmp[0:32, 1, :]), [rtmp.b], [rint.b])
                        P.emit(dve, REC.tensor_copy(rtmp[0:32, 1, :], rint[0:32, 0, :]), [rint.b], [rtmp.b])
                        P.emit(dve, REC.tensor_scalar(rtmp[0:32, 1, :], rtmp[0:32, 1, :], -2 * PI, None, ALU.mult), [rtmp.b], [rtmp.b])
                        P.emit(dve, REC.scalar_tensor_tensor(tabs[0:32, ti, :], rtmp[0:32, 0, :], phase, rtmp[0:32, 1, :], ALU.add, ALU.add),
                               [rtmp.b], [tabs.b])
                        P.emit(dve, REC.tensor_scalar(rtmp[0:32, 1, :], tabs[0:32, ti, :], -PI, 2 * PI, ALU.is_lt, ALU.mult), [tabs.b], [rtmp.b])
                        P.emit(dve, REC.tensor_tensor(tabs[0:32, ti, :], tabs[0:32, ti, :], rtmp[0:32, 1, :], ALU.add), [tabs.b, rtmp.b], [tabs.b])
                        P.emit(dve, REC.tensor_scalar(rtmp[0:32, 1, :], tabs[0:32, ti, :], PI, -2 * PI, ALU.is_gt, ALU.mult), [tabs.b], [rtmp.b])
                        P.emit(dve, REC.tensor_tensor(tabs[0:32, ti, :], tabs[0:32, ti, :], rtmp[0:32, 1, :], ALU.add), [tabs.b, rtmp.b], [tabs.b])
                        if ti == 0:
                            P.emit(act, REC.activation(tabs[0:32, 0, :], tabs[0:32, 0, :], AF.Sin), [tabs.b], [tabs.b])
                        else:
                            P.emit(act, REC.activation(tabs[0:32, 1, :], tabs[0:32, 1, :], AF.Sin, scale=ropec[:, 1:2]), [tabs.b, ropec.b], [tabs.b])
                    if F_ROPECACHE:
                        P.dma(sp, tab_d[blk], tabs[0:32, :, :].rearrange("p a b -> p (a b)"), [tabs.b], [b_tab[blk]])
                else:
                    P.dma(sp, tabs[0:32, :, :].rearrange("p a b -> p (a b)"), tab_d[blk], [b_tab[blk]], [tabs.b])
                Ctab = tabs[0:32, 0, :]
                Stab = tabs[0:32, 1, :]

                norm_transpose(src_d, src_buf, t0, l, C_GPRE1)
                if l == 0 and blk == NB - 1:
                    dump("hT0", hT[:, 0, :], [hT.b])

                def inproj(c0, M):
                    ps = PS("mm")
                    for kc in range(8):
                        P.emit(pe, REC.matmul(ps[0:M, :], w_in_sb[:, 8 * c0 + kc * M:8 * c0 + (kc + 1) * M], hT[:, kc, :], start=(kc == 0), stop=(kc == 7)),
                               [bWg[c0], hT.b], [ps.b], sig=(kc == 7))
                    return ps
                for g in range(3):
                    ps = inproj(g * 128, 128)
                    P.emit(act, REC.activation(cq_sb[:, g, :], ps[:, :], AF.Copy), [ps.b], [cq_sb.b])
                    P.emit(act, REC.activation(cq_sq[:, g, :], ps[:, :], AF.Square), [ps.b], [cq_sq.b])
                ps_kr = inproj(384, 32)
                P.emit(dve, REC.tensor_tensor(rtmp[0:32, 0, :], ps_kr[0:32, :], Ctab, ALU.mult), [ps_kr.b, tabs.b], [rtmp.b])
                ps_ks = inproj(416, 32)
                P.emit(dve, REC.tensor_tensor(rtmp[0:32, 1, :], ps_ks[0:32, :], Stab, ALU.mult), [ps_ks.b, tabs.b], [rtmp.b])
                P.emit(dve, REC.tensor_tensor(krot[0:32, 0, :], rtmp[0:32, 0, :], rtmp[0:32, 1, :], ALU.add), [rtmp.b], [krot.b])
                P.emit(dve, REC.tensor_copy(Kcur[64:96, :, :], krot[0:32, 0:1, :].to_broadcast([32, 8, 512])), [krot.b], [Kcur.b])
                for g in range(4):
                    ps = inproj(448 + g * 128, 128)
                    P.emit(act, REC.activation(conv_sb[:, g, :], ps[:, :], AF.Copy), [ps.b], [conv_sb.b])
                for c in range(2):
                    ps = inproj(448 + (4 + c) * 128, 128)
                    P.emit(dve, REC.tensor_tensor(vbuf[:, c, 2:514], conv_sb[:, 2 + c, :], ps[:, :], ALU.mult), [conv_sb.b, ps.b], [vbuf.b])
                    P.emit(dve, REC.tensor_scalar(tmpA[:, :], vbuf[:, c, 0:512], col(l, C_SCW + c * 3 + 0), None, ALU.mult), [vbuf.b, cols.b], [tmpA.b])
                    P.emit(dve, REC.scalar_tensor_tensor(tmpA[:, :], vbuf[:, c, 1:513], col(l, C_SCW + c * 3 + 1), tmpA[:, :], ALU.mult, ALU.add), [vbuf.b, cols.b, tmpA.b], [tmpA.b])
                    P.emit(dve, REC.scalar_tensor_tensor(tmpA[:, :], vbuf[:, c, 2:514], col(l, C_SCW + c * 3 + 2), tmpA[:, :], ALU.mult, ALU.add), [vbuf.b, cols.b, tmpA.b], [tmpA.b])
                    P.emit(dve, REC.tensor_tensor(yT[:, 4 + c, :], tmpA[:, :], conv_sb[:, c, :], ALU.mult), [tmpA.b, conv_sb.b], [yT.b])
                    P.emit(dve, REC.tensor_copy(vbuf[:, c, 0:2], vbuf[:, c, 512:514]), [vbuf.b], [vbuf.b])
                for g in range(2):
                    ps = inproj(448 + 768 + g * 128, 128)
                    P.emit(act, REC.activation(szs[:, g, :], ps[:, :], AF.Silu), [ps.b], [szs.b])
                for c in range(6):
                    ps = inproj(448 + 1024 + c * 128, 128)
                    ub = ubs[c % 2]
                    P.emit(act, REC.activation(ub[:, 0, 3:515], ps[:, :], AF.Copy), [ps.b], [ub.b])
                    P.emit(dve, REC.tensor_copy(ub[:, 0, 0:3], uh[:, c, :]), [uh.b], [ub.b])
                    P.emit(dve, REC.tensor_scalar(tmpA[:, :], ub[:, 0, 0:512], col(l, C_SSW + c * 4 + 0), None, ALU.mult), [ub.b, cols.b], [tmpA.b])
                    for k in range(1, 4):
                        P.emit(dve, REC.scalar_tensor_tensor(tmpA[:, :], ub[:, 0, k:k + 512], col(l, C_SSW + c * 4 + k), tmpA[:, :], ALU.mult, ALU.add),
                               [ub.b, cols.b, tmpA.b], [tmpA.b])
                    if c < 2:
                        P.emit(act, REC.activation(xs_f[:, c, :], tmpA[:, :], AF.Silu, bias=col(l, C_SSB + c)), [tmpA.b, cols.b], [xs_f.b])
                        P.emit(dve, REC.tensor_copy(xs_b[:, c, :], xs_f[:, c, :]), [xs_f.b], [xs_b.b])
                    elif c < 4:
                        P.emit(act, REC.activation(Bt[:, c - 2, :], tmpA[:, :], AF.Silu, bias=col(l, C_SSB + c)), [tmpA.b, cols.b], [Bt.b])
                    else:
                        P.emit(act, REC.activation(Ct[:, c - 4, :], tmpA[:, :], AF.Silu, bias=col(l, C_SSB + c)), [tmpA.b, cols.b], [Ct.b])
                    P.emit(dve, REC.tensor_copy(uh[:, c, :], ub[:, 0, 512:515]), [ub.b], [uh.b])
                ps = PS("aux")
                for tt in range(4):
                    for kc in range(8):
                        P.emit(pe, REC.matmul(ps[:, tt * 4:tt * 4 + 4], hT[:, kc, tt * 128:(tt + 1) * 128], w_dt_sb[:, kc, :],
                                                                          start=(kc == 0), stop=(kc == 7)),
                               [bW, hT.b], [ps.b], sig=(kc == 7))
                dt4 = dtT[:, :, :]
                dtb = rows[:, 1024:1028].unsqueeze(1).to_broadcast([128, 4, 4])
                P.emit(dve, REC.tensor_tensor(dt4, ps[:, 0:16].rearrange("p (a b) -> p a b", a=4), dtb, ALU.add), [ps.b, rows.b], [dtT.b])
                P.emit(act, REC.activation(adt[:, :, :], dt4, AF.Abs), [dtT.b], [adt.b])
                P.emit(act, REC.activation(adt[:, :, :], adt[:, :, :], AF.Exp, scale=-1.0), [adt.b], [adt.b])
                P.emit(act, REC.activation(adt[:, :, :], adt[:, :, :], AF.Ln, bias=1.0), [adt.b], [adt.b])
                P.emit(dve, REC.scalar_tensor_tensor(dt4, dt4, 0.0, adt[:, :, :], ALU.max, ALU.add), [dtT.b, adt.b], [dtT.b])
                P.emit(dve, REC.tensor_tensor(adt[:, :, :], dt4, arow[:, 0:1, :].to_broadcast([128, 4, 4]), ALU.mult), [dtT.b, arow.b], [adt.b])

                for (chs, n, qc) in (((0, 1), 256, C_QN), ((2,), 128, C_KVN)):
                    ps = PS("aux")
                    for i, c in enumerate(chs):
                        P.emit(pe, REC.matmul(ps[:, :], ones_b, cq_sq[:, c, :], start=(i == 0), stop=(i == len(chs) - 1)),
                               [cstb.b, cq_sq.b], [ps.b], sig=(i == len(chs) - 1))
                    P.emit(act, REC.activation(tmpA[:, :], ps[:, :], AF.Ln, scale=1.0 / n, bias=epsc[:, 0:1]), [ps.b, epsc.b], [tmpA.b])
                    P.emit(act, REC.activation(tmpA[:, :], tmpA[:, :], AF.Exp, scale=-0.5), [tmpA.b], [tmpA.b])
                    for i, c in enumerate(chs):
                        P.emit(dve, REC.scalar_tensor_tensor(cqn[:, c, :], cq_sb[:, c, :], col(l, qc + i), tmpA[:, :], ALU.mult, ALU.mult),
                               [cq_sb.b, cols.b, tmpA.b], [cqn.b])
                ckvn = cqn[:, 2, :]
                if l == 0 and blk == NB - 1:
                    for c in range(3):
                        dump("cqn", cqn[:, c, :], [cqn.b], idx=c)
                    dump("dtT", dtT[:, :, :].rearrange("p a b -> p (a b)"), [dtT.b]) if False else None

                for hp in range(4):
                    ps = PS("mm")
                    P.emit(pe, REC.matmul(ps[:, :], w_kn_sb[:, hp * 128:(hp + 1) * 128], ckvn, start=True, stop=True), [bW, cqn.b], [ps.b])
                    P.emit(act, REC.activation(Kcur[0:64, 2 * hp, :], ps[0:64, :], AF.Copy), [ps.b], [Kcur.b])
                    P.emit(dve, REC.tensor_copy(Kcur[0:64, 2 * hp + 1, :], ps[64:128, :]), [ps.b], [Kcur.b])
                for tt in range(4):
                    ps = PS("mm")
                    P.emit(pe, REC.matmul(ps[:, :], cqn[:, 2, tt * 128:(tt + 1) * 128], w_v_sb, start=True, stop=True), [bW, cqn.b], [ps.b])
                    P.emit(act, REC.activation(Vcur[:, :, tt, 0:64], ps[:, :].rearrange("p (h d) -> p h d", h=8), AF.Copy), [ps.b], [Vcur.b])
                if blk < NB - 1:
                    P.dma(sp, kd_d.rearrange("h d s -> d h s")[:, :, t0:t0 + 512], Kcur[0:96, :, :], [Kcur.b], [b_kd])
                    P.dma(sp, vd_d.rearrange("h p j d -> p h (j d)")[:, :, blk * 260:(blk + 1) * 260],
                          Vcur[:, :, :, :].rearrange("p h j d -> p h (j d)"), [Vcur.b], [b_vd])

                P.shift = 0
                for h in range(NH):
                    q = Qh[h % 2]
                    psQ = PS("mm")
                    for c in range(2):
                        P.emit(pe, REC.matmul(psQ[0:96, :], w_q_sb[:, c, h * 96:(h + 1) * 96], cqn[:, c, :], start=(c == 0), stop=(c == 1)),
                               [bW, cqn.b], [psQ.b], sig=(c == 1))
                    psS = PS("mm")
                    for c in range(2):
                        P.emit(pe, REC.matmul(psS[0:32, :], w_qsw_sb[:, c, h * 32:(h + 1) * 32], cqn[:, c, :], start=(c == 0), stop=(c == 1)),
                               [bW, cqn.b], [psS.b], sig=(c == 1))
                    P.emit(act, REC.activation(q[0:64, 0, :], psQ[0:64, :], AF.Copy), [psQ.b], [q.b])
                    P.emit(dve, REC.tensor_tensor(rtmp[0:32, 0, :], psS[0:32, :], Stab, ALU.mult), [psS.b, tabs.b], [rtmp.b])
                    P.emit(dve, REC.tensor_tensor(rtmp[0:32, 1, :], psQ[64:96, :], Ctab, ALU.mult), [psQ.b, tabs.b], [rtmp.b])
                    P.emit(dve, REC.tensor_tensor(q[64:96, 0, :], rtmp[0:32, 0, :], rtmp[0:32, 1, :], ALU.add), [rtmp.b], [q.b])
                    kb = kbuf[h % 2]
                    kbq_h = kbqs[h % 2]
                    vb = vbufs[h % 2]
                    if l == 0 and blk == NB - 1 and h == 0:
                        dump("Q0", q[0:96, 0, :], [q.b], npart=96)
                        dump("K0", Kcur[0:96, 0, :], [Kcur.b], npart=96)
                    if blk > 0:
                        for qi in range(4):
                            lo_, hi_ = qi * 1024, min(t0, (qi + 1) * 1024)
                            if hi_ > lo_:
                                P.dma(sp, kb[0:96, 0, lo_:hi_], kd_d[h, :, lo_:hi_], [b_kd], [kbq_h[qi]])
                        P.dma(sp, vb[:, 0:4 * blk, :], vd_d[h, :, 0:4 * blk, :], [b_vd], [vb.b])
                    psO = PS("acc")
                    nfull = 4 * blk
                    for j in range(nfull + 4):
                        pss = PS("mm")
                        pt = Pt[j % 3]
                        if j < nfull:
                            c0 = 0
                            P.emit(pe, REC.matmul(pss[:, :], kb[0:96, 0, j * 128:(j + 1) * 128], q[0:96, 0, :], start=True, stop=True),
                                   [kbq_h[(j * 128) // 1024], q.b], [pss.b])
                        else:
                            jj = j - nfull
                            c0 = jj * 128
                            P.emit(pe, REC.matmul(pss[:, c0:512], Kcur[0:96, h, c0:c0 + 128], q[0:96, 0, c0:512], start=True, stop=False),
                                   [Kcur.b, q.b], [pss.b], sig=False)
                            P.emit(pe, REC.matmul(pss[:, c0:c0 + 128], ident_b, mneg_b, start=False, stop=True),
                                   [cstb.b], [pss.b])
                        P.emit(act, REC.activation(pt[:, 0, c0:512], pss[:, c0:512], AF.Exp, scale=SCALE), [pss.b], [pt.b])
                        if j < nfull:
                            vl, vlb = vb[:, j, :], vb.b
                        else:
                            vl, vlb = Vcur[:, h, j - nfull, :], Vcur.b
                        P.emit(pe, REC.matmul(psO[0:65, c0:512], vl, pt[:, 0, c0:512], start=(j == 0), stop=(j == nfull + 3)),
                               [vlb, pt.b], [psO.b])
                    P.emit(act, REC.activation(rd[64:65, 0, :], psO[64:65, :], AF.Ln), [psO.b], [rd.b])
                    P.emit(act, REC.activation(rd[64:65, 0, :], rd[64:65, 0, :], AF.Exp, scale=-1.0), [rd.b], [rd.b])
                    psB = PS("mm")
                    P.emit(pe, REC.matmul(psB[0:64, :], ones_f[64:65, 0:64], rd[64:65, 0, :], start=True, stop=True), [cst32.b, rd.b], [psB.b])
                    P.emit(act, REC.activation(osb[0:64, 0, :], psO[0:64, :], AF.Copy), [psO.b], [osb.b])
                    P.emit(dve, REC.tensor_tensor(yT[(h % 2) * 64:(h % 2) * 64 + 64, h // 2, :], osb[0:64, 0, :], psB[0:64, :], ALU.mult),
                           [osb.b, psB.b], [yT.b])

                P.dma(pool, w_out_sb, w_out_d[l].rearrange("(k p) n -> p k n", p=128), [], [Kcur.b, bWo] + kbq)
                psY = [PS("acc"), PS("acc")]
                for tt in range(4):
                    ts = slice(tt * 128, (tt + 1) * 128)
                    P.emit(dve, REC.tensor_copy(prevp[:, 0:4:2, 0:64], state[:, 0:4:2, :]), [state.b], [prevp.b])
                    P.emit(dve, REC.tensor_copy(prevp[:, 1:4:2, 64:128], state[:, 1:4:2, :]), [state.b], [prevp.b])
                    for c in range(2):
                        P.emit(pe, REC.transpose(pst[:, c * 128:(c + 1) * 128], xs_b[:, c, ts], ident_b), [xs_b.b, cstb.b], [pst.b], sig=False)
                    for g in range(2):
                        P.emit(pe, REC.transpose(pst[:, 256 + g * 128:256 + (g + 1) * 128], Bt[:, g, ts], ident_b), [Bt.b, cstb.b], [pst.b], sig=(g == 1))
                    xtok = pst[:, 0:256].rearrange("p (h d) -> p h d", h=4)
                    P.emit(dve, REC.tensor_tensor(xdtp[:, 0:4:2, 0:64], xtok[:, 0:4:2, :], dtT[:, tt, 0:4:2].unsqueeze(2).to_broadcast([128, 2, 64]), ALU.mult),
                           [pst.b, dtT.b], [xdtp.b])
                    P.emit(dve, REC.tensor_tensor(xdtp[:, 1:4:2, 64:128], xtok[:, 1:4:2, :], dtT[:, tt, 1:4:2].unsqueeze(2).to_broadcast([128, 2, 64]), ALU.mult),
                           [pst.b, dtT.b], [xdtp.b])
                    P.emit(act, REC.activation(Btok[:, :, :], pst[:, 256:512].rearrange("p (g n) -> p g n", g=2), AF.Copy), [pst.b], [Btok.b])
                    P.emit(dve, REC.tensor_tensor(adtri[:, :, :], tri_f.unsqueeze(1).to_broadcast([128, 4, 128]),
                                                                 adt[:, tt, :].unsqueeze(2).to_broadcast([128, 4, 128]), ALU.mult), [cst32.b, adt.b], [adtri.b])
                    psR = PS("aux")
                    P.emit(pe, REC.matmul(psR[:, :], ones_f, adtri[:, :, :].rearrange("p h l -> p (h l)"), start=True, stop=True), [cst32.b, adtri.b], [psR.b])
                    psA = PS("aux")
                    P.emit(pe, REC.matmul(psA[:, 0:4], tri_f, adt[:, tt, :], start=True, stop=True), [cst32.b, adt.b], [psA.b])
                    acol = s4[:, 0, :]
                    P.emit(dve, REC.tensor_copy(acol, psA[:, 0:4]), [psA.b], [s4.b])
                    psR3 = psR[:, :].rearrange("p (h l) -> p h l", h=4)
                    P.emit(dve, REC.tensor_tensor(dec[:, :, :], psR3, acol.unsqueeze(2).to_broadcast([128, 4, 128]), ALU.subtract), [psR.b, s4.b], [dec.b])
                    P.emit(dve, REC.tensor_scalar(dec[:, :, :], dec[:, :, :], 0.0, None, ALU.min), [dec.b], [dec.b])
                    P.emit(act, REC.activation(dec[:, :, :], dec[:, :, :], AF.Exp), [dec.b], [dec.b])
                    P.emit(dve, REC.tensor_tensor(dec[:, :, :], dec[:, :, :], tri_f.unsqueeze(1).to_broadcast([128, 4, 128]), ALU.mult), [dec.b, cst32.b], [dec.b])
                    P.emit(act, REC.activation(drow[:, :, :], psR3, AF.Exp), [psR.b], [drow.b])
                    P.emit(dve, REC.tensor_tensor(s4[:, 1, :], psR3[:, :, 127], acol, ALU.subtract), [psR.b, s4.b], [s4.b])
                    P.emit(act, REC.activation(s4[:, 2, :], s4[:, 1, :], AF.Exp), [s4.b], [s4.b])
                    P.emit(act, REC.activation(s4[:, 3, :], psR3[:, :, 127], AF.Exp), [psR.b], [s4.b])
                    P.emit(dve, REC.tensor_tensor(s4[:, 4, :], s4[:, 2, :], dtT[:, tt, :], ALU.mult), [s4.b, dtT.b], [s4.b])
                    P.emit(dve, REC.tensor_tensor(xdtd[:, :, :], xtok, s4[:, 4, :].unsqueeze(2).to_broadcast([128, 4, 64]), ALU.mult), [pst.b, s4.b], [xdtd.b])
                    for g in range(2):
                        P.emit(dve, REC.tensor_tensor(Cs[:, 2 * g:2 * g + 2, :], drow[:, 2 * g:2 * g + 2, :],
                                                                          Ct[:, g:g + 1, ts].to_broadcast([128, 2, 128]), ALU.mult), [drow.b, Ct.b], [Cs.b])
                    psG = PS("mm")
                    for g in range(2):
                        P.emit(pe, REC.matmul(psG[:, g * 128:(g + 1) * 128], Bt[:, g, ts], Ct[:, g, ts], start=True, stop=True), [Bt.b, Ct.b], [psG.b], sig=(g == 1))
                    for g in range(2):
                        P.emit(dve, REC.tensor_tensor(scT[:, 2 * g:2 * g + 2, :], dec[:, 2 * g:2 * g + 2, :],
                                                                             psG[:, g * 128:(g + 1) * 128].unsqueeze(1).to_broadcast([128, 2, 128]), ALU.mult), [dec.b, psG.b], [scT.b])
                    for k in range(2):
                        ops = [(xdtp[:, 2 * k, :], scT[:, 2 * k, :], [xdtp.b, scT.b]), (xdtp[:, 2 * k + 1, :], scT[:, 2 * k + 1, :], [xdtp.b, scT.b]),
                               (prevp[:, 2 * k, :], Cs[:, 2 * k, :], [prevp.b, Cs.b]), (prevp[:, 2 * k + 1, :], Cs[:, 2 * k + 1, :], [prevp.b, Cs.b])]
                        for i, (lh, rh, rb) in enumerate(ops):
                            P.emit(pe, REC.matmul(psY[k][:, ts], lh, rh, start=(i == 0), stop=(i == 3)), rb, [psY[k].b], sig=(i == 3))
                    psSt = PS("aux")
                    for g in range(2):
                        P.emit(pe, REC.matmul(psSt[:, g * 128:(g + 1) * 128], Btok[:, g, :], xdtd[:, 2 * g:2 * g + 2, :].rearrange("p h d -> p (h d)"), start=True, stop=True),
                               [Btok.b, xdtd.b], [psSt.b], sig=(g == 1))
                    P.emit(dve, REC.tensor_tensor(state[:, :, :], state[:, :, :], s4[:, 3, :].unsqueeze(2).to_broadcast([128, 4, 64]), ALU.mult), [state.b, s4.b], [state.b])
                    P.emit(dve, REC.tensor_tensor(state[:, :, :], state[:, :, :], psSt[:, 0:256].rearrange("p (h d) -> p h d", h=4), ALU.add), [state.b, psSt.b], [state.b])
                for k in range(2):
                    P.emit(dve, REC.scalar_tensor_tensor(gat[:, k, :], xs_f[:, k, :], col(l, C_DSK + k), psY[k][:, :], ALU.mult, ALU.add), [xs_f.b, cols.b, psY[k].b], [gat.b])
                    P.emit(dve, REC.tensor_tensor(gat[:, k, :], gat[:, k, :], szs[:, k, :], ALU.mult), [gat.b, szs.b], [gat.b])
                    P.emit(act, REC.activation(gsq[:, k, :], gat[:, k, :], AF.Square), [gat.b], [gsq.b])
                ps = PS("aux")
                for k in range(2):
                    P.emit(pe, REC.matmul(ps[:, :], ones_b, gsq[:, k, :], start=(k == 0), stop=(k == 1)), [cstb.b, gsq.b], [ps.b], sig=(k == 1))
                P.emit(act, REC.activation(tmpA[:, :], ps[:, :], AF.Ln, scale=1.0 / 256, bias=epsc[:, 0:1]), [ps.b, epsc.b], [tmpA.b])
                P.emit(act, REC.activation(tmpA[:, :], tmpA[:, :], AF.Exp, scale=-0.5), [tmpA.b], [tmpA.b])
                for k in range(2):
                    P.emit(dve, REC.scalar_tensor_tensor(yT[:, 6 + k, :], gat[:, k, :], col(l, C_SSN + k), tmpA[:, :], ALU.mult, ALU.mult), [gat.b, cols.b, tmpA.b], [yT.b])

                if l == 0 and blk == NB - 1:
                    for kc in range(8):
                        dump("yT", yT[:, kc, :], [yT.b], idx=kc)
                for tt in range(4):
                    pl = []
                    for half in range(2):
                        ps = PS("mm")
                        for kc in range(8):
                            P.emit(pe, REC.matmul(ps[:, :], yT[:, kc, tt * 128:(tt + 1) * 128], w_out_sb[:, kc, half * 512:(half + 1) * 512],
                                                                                        start=(kc == 0), stop=(kc == 7)), [yT.b, bWo, Kcur.b] + kbq, [ps.b], sig=(kc == 7))
                        pl.append(ps)
                    post_norm_residual(pl, src_d, src_buf, xa_d, b_xa, t0 + tt * 128, tt, nocopy=True)

            bWF = Buf("wF%d" % l)
            P.barrier()
            set_pools({"mm": [0, 1, 2, 3, 4, 5, 6]}, [pst])
            w_up_sb = wreg.t[:, 0:8 * 2 * FF].rearrange("p (j k n) -> p j k n", j=44, k=8)
            w_down_sb = wreg.t[:, 8 * 2 * FF:8 * 2 * FF + NFC * D].rearrange("p (k n) -> p k n", k=NFC)
            assert (8 * 2 * FF + NFC * D) * 2 <= WREG
            bWU = [Buf("wu%d_%d" % (l, j)) for j in range(44)]
            def slot(j_):
                return 2 * (j_ % NFC) + (j_ // NFC)
            for j in range(44):
                reg_buf("wreg", 2048 * slot(j), 2048 * (slot(j) + 1), bWU[j])
            reg_buf("wreg", 2 * 8 * 2 * FF, 2 * (8 * 2 * FF + NFC * D), bWF)
            for i in range(NFC):
                for j in (i, NFC + i):
                    P.dma(pool, w_up_sb[:, slot(j), :, :], w_up_d[l, j].rearrange("p (k n) -> p k n", k=8), [], [bWU[j]])
            wdv = w_down_d[l].rearrange("(k p) n -> p k n", p=128)
            P.dma(pool, w_down_sb[:, 0:11, :], wdv[:, 0:11, :], [], [bWF])
            P.dma(pool, w_down_sb[:, 11:22, :], wdv[:, 11:22, :], [], [bWF])
            P.dma(sp, rows[:, 0:1024], rows_d[l, :, 1024:2048], [], [rows.b])
            gT = Tile(reg2.t[:, 0:NFC * 512].rearrange("p (k n) -> p k n", k=NFC), Buf("gT%d" % l))
            reg_buf("reg2", 0, 2 * NFC * 512, gT.b)
            o2 = NFC * 512
            def carve2(shape):
                nonlocal o2
                n = int(np.prod(shape))
                a = reg2.t[:, o2:o2 + 2 * n].bitcast(F32)
                o2 += 2 * n
                t_ = Tile(a.rearrange("p (a b) -> p a b", a=shape[0]), Buf("r2_%d" % o2))
                reg_buf("reg2", 2 * (o2 - 2 * n), 2 * o2, t_.b)
                return t_
            ug = [carve2([1, 514]) for _ in range(2)]
            uu = [carve2([1, 514]) for _ in range(2)]
            fh = carve2([44, 2])
            tB = carve2([1, 512])
            tC = carve2([1, 512])
            tCs = [Tile(tB.t[:, 0, :], tB.b), Tile(tC.t[:, 0, :], tC.b)]
            assert o2 * 2 <= REG2, o2 * 2
            P.emit(dve, REC.memset(fh[:, :, :], 0.0), [], [fh.b])

            for blk in range(NB):
                t0 = blk * 512
                norm_transpose(xa_d, b_xa, t0, l, C_GPRE2)
                for i in range(NFC):
                    br = []
                    for bi, (ub, cbase) in enumerate(((ug[i % 2], i * 128), (uu[i % 2], FF + i * 128))):
                        ps = PS("mm")
                        for kc in range(8):
                            P.emit(pe, REC.matmul(ps[:, :], w_up_sb[:, slot(i + bi * NFC), kc, :], hT[:, kc, :], start=(kc == 0), stop=(kc == 7)),
                                   [bWU[i + bi * NFC], hT.b], [ps.b], sig=(kc == 7))
                        j = i + bi * NFC
                        br.append(ps)
                        P.emit(act, REC.activation(ub[:, 0, 2:514], ps[:, :], AF.Copy), [ps.b], [ub.b])
                        P.emit(act, REC.activation(ps[:, :], ps[:, :], AF.Identity, scale=col(l, C_FW + j * 3 + 2), bias=col(l, C_FB + j)), [ps.b, cols.b], [ps.b])
                        P.emit(act, REC.activation(ub[:, 0, 0:2], fh[:, j, :], AF.Copy), [fh.b], [ub.b])
                        for k in (0, 1):
                            P.emit(dve, REC.scalar_tensor_tensor(ps[:, :], ub[:, 0, k:k + 512], col(l, C_FW + j * 3 + k), ps[:, :], ALU.mult, ALU.add),
                                   [ub.b, cols.b, ps.b], [ps.b])
                        P.emit(act, REC.activation(fh[:, j, :], ub[:, 0, 512:514], AF.Copy), [ub.b], [fh.b])
                    tc_ = tCs[i % 2]
                    P.emit(act, REC.activation(tc_[:, :], br[0][:, :], AF.Silu), [br[0].b], [tc_.b])
                    P.emit(dve, REC.tensor_tensor(gT[:, i, :], tc_[:, :], br[1][:, :], ALU.mult), [tc_.b, br[1].b], [gT.b])
                for tt in range(4):
                    pl = []
                    for half in range(2):
                        ps = PS("mm")
                        for i in range(NFC):
                            P.emit(pe, REC.matmul(ps[:, :], gT[:, i, tt * 128:(tt + 1) * 128], w_down_sb[:, i, half * 512:(half + 1) * 512],
                                                                                       start=(i == 0), stop=(i == NFC - 1)), [gT.b, bWF], [ps.b], sig=(i == NFC - 1))
                        pl.append(ps)
                    post_norm_residual(pl, xa_d, b_xa, dst_d, dst_buf, t0 + tt * 128, tt)

        P.finalize()
        block = es.enter_context(nc.Block())

        @block.tensor
        def _(e):
            _replay(pe.q, e)

        @block.scalar
        def _(e):
            _replay(act.q, e)

        @block.vector
        def _(e):
            _replay(dve.q, e)

        @block.gpsimd
        def _(e):
            _replay(pool.q, e)

        @block.sync
        def _(e):
            _replay(sp.q, e)
    return nc


def _cols_layer(p, l):
    c = np.zeros((128, NCOLS), np.float32)
    def chunks(v, n):
        return np.asarray(v, np.float32).reshape(n, 128).T
    c[:, C_GPRE1:C_GPRE1 + 8] = chunks(p["norm_mix_pre"][l], 8)
    c[:, C_GPRE2:C_GPRE2 + 8] = chunks(p["norm_ffn_pre"][l], 8)
    c[:, C_QN:C_QN + 2] = chunks(p["mla_q_norm"][l], 2)
    c[:, C_KVN:C_KVN + 1] = chunks(p["mla_kv_norm"][l], 1)
    scw = np.asarray(p["sc_conv_w"][l], np.float32)
    for ch in range(2):
        for k in range(3):
            c[:, C_SCW + ch * 3 + k] = scw[k, ch * 128:(ch + 1) * 128]
    ssw = np.asarray(p["ssd_conv_w"][l], np.float32)
    ssb = np.asarray(p["ssd_conv_b"][l], np.float32)
    for ch in range(6):
        for k in range(4):
            c[:, C_SSW + ch * 4 + k] = ssw[k, ch * 128:(ch + 1) * 128]
        c[:, C_SSB + ch] = ssb[ch * 128:(ch + 1) * 128]
    dsk = np.repeat(np.asarray(p["ssd_d"][l], np.float32), 64)
    c[:, C_DSK:C_DSK + 2] = chunks(dsk, 2)
    c[:, C_SSN:C_SSN + 2] = chunks(p["ssd_norm"][l], 2)
    fw = np.asarray(p["ffn_conv_w"][l], np.float32)
    fb = np.asarray(p["ffn_conv_b"][l], np.float32)
    for j in range(44):
        for k in range(3):
            c[:, C_FW + j * 3 + k] = fw[k, j * 128:(j + 1) * 128]
        c[:, C_FB + j] = fb[j * 128:(j + 1) * 128]
    return c


def prep_shared(p, NL):
    f = lambda a: np.ascontiguousarray(np.asarray(a, np.float32))
    w_in = f(p["w_in"])[:NL]
    sw = np.concatenate([np.arange(16, 32), np.arange(0, 16)])
    kr = w_in[:, :, 384:416]
    w_in_r = np.concatenate([w_in[:, :, 0:384], kr, kr[:, :, sw], w_in[:, :, 416:2208]], axis=2)
    assert w_in_r.shape[2] == NCOLW
    groups = [(0, 128), (128, 128), (256, 128), (384, 32), (416, 32)] + [(448 + g * 128, 128) for g in range(14)]
    w4 = w_in_r.reshape(NL, 8, 128, NCOLW)
    w_in_g = np.concatenate([np.transpose(w4[:, :, :, c0:c0 + M], (0, 2, 1, 3)).reshape(NL, 128, 8 * M) for (c0, M) in groups], axis=2)
    assert w_in_g.shape[2] == 8 * NCOLW
    w_dt = w_in[:, :, 2208:2212]
    wq = f(p["mla_w_q_up"])[:NL]
    wq4 = wq.reshape(NL, 256, 8, 96)
    w_qsw = wq4[:, :, :, 64:96][:, :, :, sw].reshape(NL, 256, 256)
    wkv = f(p["mla_w_kv_up"])[:NL].reshape(NL, 128, 8, 128)
    w_kn = wkv[:, :, :, 0:64].reshape(NL, 128, 512)
    w_v = wkv[:, :, :, 64:128].reshape(NL, 128, 512)
    wu = f(p["ffn_w_up"])[:NL].reshape(NL, 8, 128, 44, 128)
    w_up_g = np.ascontiguousarray(np.transpose(wu, (0, 3, 2, 1, 4)).reshape(NL, 44, 128, 1024))
    cols = np.stack([_cols_layer(p, l) for l in range(NL)], axis=1)
    rows = np.zeros((NL, 128, 2056), np.float32)
    for l in range(NL):
        rows[l, :, 0:1024] = np.asarray(p["norm_mix_post"][l], np.float32)[None, :]
        rows[l, :, 1024:2048] = np.asarray(p["norm_ffn_post"][l], np.float32)[None, :]
        rows[l, :, 2048:2052] = np.asarray(p["ssd_dt_bias"][l], np.float32)[None, :]
        rows[l, :, 2052:2056] = np.asarray(p["ssd_a_log"][l], np.float32)[None, :]
    inv_freq = (1.0 / (10000.0 ** (np.arange(0, 32, 2, dtype=np.float32) / np.float32(32)))).astype(np.float32)
    ropec = np.zeros((32, 2), np.float32)
    ropec[:, 0] = np.concatenate([inv_freq, inv_freq])
    ropec[:, 1] = np.concatenate([-np.ones(16), np.ones(16)])
    tri_ = np.triu(np.ones((128, 128)))
    consts = np.concatenate([np.eye(128), tri_, np.ones((128, 128)), (1.0 - tri_) * -30000.0], axis=1).astype(np.float32)
    return {
        "ropec": ropec, "consts": consts, "cols": np.ascontiguousarray(cols), "rows": rows,
        "w_in_g": np.ascontiguousarray(w_in_g), "w_dt": np.ascontiguousarray(w_dt),
        "w_q": np.ascontiguousarray(wq), "w_qsw": np.ascontiguousarray(w_qsw),
        "w_kn": np.ascontiguousarray(w_kn), "w_v": np.ascontiguousarray(w_v),
        "w_out": f(p["w_out"])[:NL], "w_up_g": w_up_g, "w_down": f(p["ffn_w_down"])[:NL],
    }


def run(inputs, S, NL, ncores, dbg=None):
    shared = prep_shared(inputs, NL)
    x = np.asarray(inputs["x"], np.float32)
    pos = np.asarray(inputs["positions"], np.int32)
    in_maps = []
    for c in range(ncores):
        m = dict(shared)
        m["x"] = np.ascontiguousarray(x[c, :S])
        m["posrep"] = np.ascontiguousarray(np.broadcast_to(pos[c, :S][None, :], (32, S)))
        in_maps.append(m)
    nc = build(S, NL, dbg)
    res = run_bass_kernel_spmd(nc, in_maps, core_ids=list(range(ncores)))
    return res


def kernel(**inputs):
    res = run(inputs, 4096, 4, 8)
    return np.stack([r["y"] for r in res.results], axis=0).astype(np.float32)
```

```python
import math
from contextlib import ExitStack
import numpy as np
import concourse.bass as bass
import concourse.mybir as mybir
from concourse.bass_utils import run_bass_kernel_spmd

F32 = mybir.dt.float32
BF16 = mybir.dt.bfloat16
I32 = mybir.dt.int32
AF = mybir.ActivationFunctionType
ALU = mybir.AluOpType

D = 1024
NH = 8
FF = 2816
NFC = 22
EPS = 1e-6
SCALE = 96 ** -0.5
NCOLW = 2240
PI = float(np.pi)
F_ROPECACHE = True
import os
SHIFT_M = 0
PRIO_RANK = 0
NOBARRIER = 1
RANK_ENGS = ('pe',)

C_GPRE1 = 0
C_GPRE2 = 8
C_QN = 16
C_KVN = 18
C_SCW = 19
C_SSW = 25
C_SSB = 49
C_DSK = 55
C_SSN = 57
C_FW = 59
C_FB = 191
NCOLS = 235


class _Rec:
    def __getattr__(self, name):
        def f(*args, **kwargs):
            return (name, args, kwargs)
        return f


REC = _Rec()


def _run(fn, e):
    if callable(fn):
        return fn(e)
    name, args, kwargs = fn
    return getattr(e, name)(*args, **kwargs)


class Src:
    def __init__(self, sem, step):
        self.sem, self.step, self.count = sem, step, 0


class Buf:
    def __init__(self, name):
        self.name, self.writer, self.readers = name, None, []


class Eng:
    def __init__(self, name, src, inorder=False):
        self.name, self.src, self.q, self.waited, self.inorder = name, src, [], {}, inorder


class Tile:
    def __init__(self, t, b):
        self.t, self.b = t, b

    def __getitem__(self, k):
        return self.t[k]


class Op:
    __slots__ = ("eng", "insts", "reads", "writes", "idx", "preds", "succs", "cost", "is_dma", "lat",
                 "npred", "ready", "finish", "src", "val", "prev_val", "seg", "key", "rank")

    def __init__(self, eng):
        self.eng, self.insts, self.reads, self.writes = eng, [], [], []
        self.preds, self.succs = set(), []
        self.cost, self.is_dma, self.lat = 0.0, False, 0.0
        self.ready, self.finish = 0.0, 0.0
        self.src = self.val = self.prev_val = None


def _free_size(ap):
    n = 1
    for d in ap.shape[1:]:
        n *= int(d)
    return n


def _est_cost(eng, fn):
    name, args, kwargs = fn
    try:
        if name == "matmul":
            rhs = args[2]
            n = _free_size(rhs)
            return (max(28.0, n / 2.4) + 6.0) * (4.0 if rhs.dtype == F32 else 1.0)
        if name == "transpose":
            return 75.0
        if name == "dma_start":
            return 120.0
        out = args[0] if args else kwargs.get("out")
        n = _free_size(out)
        if eng.name == "act":
            return 150.0 + 0.85 * n
        if name == "reciprocal":
            return 160.0 + 2.6 * n
        if name == "memset":
            return 100.0 + 0.3 * n
        if eng.name == "pool":
            return 250.0 + 2.1 * n
        return 150.0 + 1.04 * n
    except Exception:
        return 300.0


class Prog:
    LOOK_W = 24
    LOOK_IDX = 6000

    def __init__(self, nc, es, n_dma_sems=24):
        self.nc, self.es = nc, es
        def sem(n):
            return es.enter_context(nc.semaphore(n))
        self.pe = Eng("pe", Src(sem("s_pe"), 1), inorder=True)
        self.act = Eng("act", Src(sem("s_act"), 1))
        self.dve = Eng("dve", Src(sem("s_dve"), 1))
        self.pool = Eng("pool", Src(sem("s_pool"), 1))
        self.sp = Eng("sp", Src(sem("s_sp"), 1))
        self.engs = [self.pe, self.act, self.dve, self.pool, self.sp]
        self.dma_srcs = [Src(sem("s_dma%d" % i), 16) for i in range(n_dma_sems)]
        self.dma_rr = {"sp": 0, "pool": 0}
        self.dma_part = {"sp": self.dma_srcs[:n_dma_sems - 8], "pool": self.dma_srcs[n_dma_sems - 8:]}
        self.ops = []
        self.cur = {}
        self.seg_start = [0]
        self.ninst = 0
        self.shift = 0

    def tile(self, name, shape, dt):
        t = self.es.enter_context(self.nc.sbuf_tensor("sb_" + name, shape, dt))
        return Tile(t, Buf(name))

    def emit(self, eng, fn, reads=(), writes=(), sig=True, dma_bytes=None):
        op = self.cur.get(eng.name)
        if op is None:
            op = Op(eng)
            self.cur[eng.name] = op
        op.insts.append(fn)
        op.reads.extend(reads)
        op.writes.extend(writes)
        op.cost += _est_cost(eng, fn)
        self.ninst += 1
        if dma_bytes is not None:
            op.is_dma = True
            op.lat = float(dma_bytes)
        if sig:
            self.cur[eng.name] = None
            self._close(op)

    def _close(self, op):
        seg0 = self.seg_start[-1]
        op.idx = len(self.ops)
        op.key = op.idx + self.shift
        op.seg = len(self.seg_start) - 1
        preds = set()
        for b in op.reads:
            if b.writer is not None:
                preds.add(b.writer)
        for b in op.writes:
            if b.writer is not None:
                preds.add(b.writer)
            preds.update(b.readers)
        preds.discard(op)
        op.preds = {p for p in preds if p.idx >= seg0}
        wset = set(id(b) for b in op.writes)
        for b in op.writes:
            b.writer = op
            b.readers = []
        for b in op.reads:
            if id(b) not in wset:
                b.readers.append(op)
        self.ops.append(op)

    def dma(self, eng, out, in_, reads=(), writes=()):
        nbytes = 1
        for d in out.shape:
            nbytes *= int(d)
        nbytes *= 2 if out.dtype == BF16 else 4
        nb2 = 1
        for d in in_.shape:
            nb2 *= int(d)
        nb2 *= 2 if in_.dtype == BF16 else 4
        self.emit(eng, REC.dma_start(out=out, in_=in_), reads, writes, dma_bytes=max(nbytes, nb2))

    def barrier(self):
        if NOBARRIER:
            return
        assert all(v is None for v in self.cur.values())
        if len(self.ops) > self.seg_start[-1]:
            self.seg_start.append(len(self.ops))

    def _schedule(self, ops):
        import bisect
        for op in ops:
            op.succs = []
            op.ready = 0.0
        for op in ops:
            op.npred = len(op.preds)
            for p in op.preds:
                p.succs.append(op)
        if RANK_ENGS:
            for op in reversed(ops):
                m = 0.0
                for sc in op.succs:
                    if sc.rank > m:
                        m = sc.rank
                op.rank = m + op.cost + (op.lat / 170.0 + 2000.0 if op.is_dma else 0.0)
                if op.eng.name in RANK_ENGS:
                    op.key = -op.rank
        avail = {e.name: [] for e in self.engs}
        t_free = {e.name: 0.0 for e in self.engs}
        order = {e.name: [] for e in self.engs}
        for op in ops:
            if op.npred == 0:
                avail[op.eng.name].append((op.key, op.idx, op))
        for e in self.engs:
            avail[e.name].sort(key=lambda x: (x[0], x[1]))
        scheduled = 0
        dma_free = 0.0
        n = len(ops)
        base = ops[0].idx
        done = [False] * n
        lo = 0
        while scheduled < n:
            while lo < n and done[lo]:
                lo += 1
            lim = base + lo + self.LOOK_IDX
            best = None
            for e in self.engs:
                lst = avail[e.name]
                if not lst:
                    continue
                tf = t_free[e.name]
                for (k_, idx, op) in lst[:self.LOOK_W]:
                    if idx > lim and best is not None:
                        continue
                    st = op.ready if op.ready > tf else tf
                    key = (st, k_, idx)
                    if best is None or key < best[0]:
                        best = (key, e, op)
            (st, k_, idx), e, op = best
            avail[e.name].remove((k_, idx, op))
            if op.is_dma:
                t_free[e.name] = st + op.cost
                xs = max(st + op.cost, dma_free)
                dma_free = xs + op.lat / 170.0
                op.finish = dma_free + 2000.0
            else:
                op.finish = st + op.cost
                t_free[e.name] = op.finish
            order[e.name].append(op)
            done[op.idx - base] = True
            scheduled += 1
            for sc in op.succs:
                if sc.ready < op.finish:
                    sc.ready = op.finish
                sc.npred -= 1
                if sc.npred == 0:
                    bisect.insort(avail[sc.eng.name], (sc.key, sc.idx, sc))
        return order, max(op.finish for op in ops)

    def _wait(self, eng, src, val):
        if val <= 0 or eng.waited.get(src, 0) >= val:
            return
        eng.waited[src] = val
        eng.q.append(REC.wait_ge(src.sem, val))

    def _barrier_waits(self):
        srcs = [e.src for e in self.engs] + self.dma_srcs
        for e in self.engs:
            for sr in srcs:
                if sr is not e.src:
                    self._wait(e, sr, sr.count)

    def finalize(self):
        assert all(v is None for v in self.cur.values())
        bounds = self.seg_start + [len(self.ops)]
        total = 0.0
        for si in range(len(bounds) - 1):
            ops = self.ops[bounds[si]:bounds[si + 1]]
            if not ops:
                continue
            if si > 0:
                self._barrier_waits()
            order, span = self._schedule(ops)
            total += span
            sb = {}
            for op in ops:
                sb[op.eng.name] = sb.get(op.eng.name, 0.0) + op.cost
            print("seg", si, "span us %.1f" % (span / 1e3), {k: round(v / 1e3, 1) for k, v in sb.items()}, flush=True)
            for e in self.engs:
                for op in order[e.name]:
                    if op.is_dma:
                        part = self.dma_part[e.name]
                        src = part[self.dma_rr[e.name] % len(part)]
                        self.dma_rr[e.name] += 1
                        op.prev_val = src.count
                    else:
                        src = e.src
                    src.count += src.step
                    op.src, op.val = src, src.count
            for e in self.engs:
                for op in order[e.name]:
                    need = {}
                    for p in op.preds:
                        if e.inorder and p.src is e.src:
                            continue
                        if need.get(p.src, 0) < p.val:
                            need[p.src] = p.val
                    for sr, v in need.items():
                        self._wait(e, sr, v)
                    if op.is_dma:
                        self._wait(e, op.src, op.prev_val)
                    last = len(op.insts) - 1
                    for i, fn in enumerate(op.insts):
                        if i == last:
                            e.q.append(("__sig__", fn, op.src.sem, op.src.step))
                        else:
                            e.q.append(fn)
        for e in (self.sp, self.pool):
            for s_ in self.dma_srcs:
                self._wait(e, s_, s_.count)
        busy = {}
        for op in self.ops:
            busy[op.eng.name] = busy.get(op.eng.name, 0.0) + op.cost
        print("busy ms:", {k: round(v / 1e6, 3) for k, v in busy.items()}, flush=True)
        print("ops:", len(self.ops), "insts:", self.ninst, "est span ms: %.3f" % (total / 1e6),
              {e.name: len(e.q) for e in self.engs}, flush=True)


def _replay(q, e):
    for item in q:
        if item[0] == "__sig__":
            _, fn, sem, step = item
            _run(fn, e).then_inc(sem, step)
        else:
            _run(item, e)


def build(S, NL, dbg=None):
    NB = S // 512
    NKB = S // 128
    nc = bass.Bass("TRN2", target_bir_lowering=False)

    def din(name, shape, dt=F32):
        return nc.dram_tensor(name, shape, dt, kind="ExternalInput").ap()

    x_d = din("x", [S, D])
    pos_d = din("posrep", [32, S], I32)
    rc_d = din("ropec", [32, 2])
    cst_d = din("consts", [128, 512])
    cols_d = din("cols", [128, NL, NCOLS])
    rows_d = din("rows", [NL, 128, 2056])
    w_in_d = din("w_in_g", [NL, 128, 8 * NCOLW])
    w_dt_d = din("w_dt", [NL, D, 4])
    w_q_d = din("w_q", [NL, 256, 768])
    w_qsw_d = din("w_qsw", [NL, 256, 256])
    w_kn_d = din("w_kn", [NL, 128, 512])
    w_v_d = din("w_v", [NL, 128, 512])
    w_out_d = din("w_out", [NL, D, D])
    w_up_d = din("w_up_g", [NL, 44, 128, 1024])
    w_down_d = din("w_down", [NL, FF, D])
    y_d = nc.dram_tensor("y", [S, D], F32, kind="ExternalOutput").ap()
    xa_d = nc.dram_tensor("xa", [S, D], F32, kind="Internal").ap()
    xb_d = [nc.dram_tensor("xb%d" % i, [S, D], F32, kind="Internal").ap() for i in range(2)]
    kd_d = nc.dram_tensor("kd", [NH, 96, S], BF16, kind="Internal").ap()
    vd_d = nc.dram_tensor("vd", [NH, 128, NKB, 65], BF16, kind="Internal").ap()
    tab_d = nc.dram_tensor("tabd", [NB, 32, 1024], F32, kind="Internal").ap() if F_ROPECACHE else None
    dbg_d = {}
    if dbg:
        for name, shape in dbg.items():
            dbg_d[name] = nc.dram_tensor("dbg_" + name, shape, F32, kind="ExternalOutput").ap()

    es = ExitStack()
    with es:
        P = Prog(nc, es)
        pe, act, dve, pool, sp = P.pe, P.act, P.dve, P.pool, P.sp
        T = P.tile
        b_xa, b_kd = Buf("xa"), Buf("kd")
        b_vd = Buf("vd")
        b_tab = [Buf("tab%d" % i) for i in range(NB)]
        b_xb = [Buf("xb0"), Buf("xb1")]
        b_y = Buf("y")
        psb = []
        for i in range(7):
            t = es.enter_context(nc.psum_tensor("ps%d" % i, [128, 512], F32))
            psb.append(Tile(t, Buf("ps%d" % i)))
        pst = Tile(es.enter_context(nc.psum_tensor("pst", [128, 1024], BF16)), Buf("pst"))
        pools = {"mm": [0, 1, 2], "acc": [3, 4], "aux": [5, 6]}
        prr = {"mm": 0, "acc": 0, "aux": 0}

        pst_list = [pst]

        def set_pools(cfg, psts):
            pools.clear()
            pools.update(cfg)
            pst_list[:] = psts

        def PS(pool_name):
            lst = pools[pool_name]
            i = lst[prr[pool_name] % len(lst)]
            prr[pool_name] += 1
            return psb[i]

        cst32 = T("cst32", [128, 256], F32)
        cstb = T("cstb", [128, 512], BF16)
        cols = T("cols", [128, NCOLS], F32)
        ropec = T("ropec", [32, 2], F32)
        epsc = T("epsc", [128, 1], F32)
        P.dma(sp, cst32[:], cst_d[:, 128:384], writes=[cst32.b])
        P.dma(pool, cstb[:], cst_d, writes=[cstb.b])
        P.dma(sp, ropec[:], rc_d, writes=[ropec.b])
        P.emit(dve, REC.memset(epsc[:], EPS), [], [epsc.b])
        ident_b = cstb[:, 0:128]
        tri_b = cstb[:, 128:256]
        ones_b = cstb[:, 256:384]
        mneg_b = cstb[:, 384:512]
        tri_f = cst32[:, 0:128]
        ones_f = cst32[:, 128:256]

        xin = [T("xin%d" % i, [128, D], F32) for i in range(2)]
        mixs = [T("mix%d" % i, [128, D], F32) for i in range(2)]
        xr = T("xr", [128, D], F32)
        junkP = T("junkP", [128, 512], BF16)
        smP = T("smP", [128, 8], F32)
        hbs = [T("hb0", [128, D], BF16)]
        hT = T("hT", [128, 8, 512], BF16)
        junk = T("junk", [128, 512], BF16)
        sms = [T("sm%d" % i, [128, 16], F32) for i in range(2)]
        rows = T("rows", [128, 1032], F32)
        tmpA = T("tmpA", [128, 512], F32)
        WREG = 135168
        wreg = T("wreg", [128, WREG // 2], BF16)
        REG2 = 35200
        reg2 = T("reg2", [128, REG2 // 2], BF16)

        def col(l, c, n=1):
            return cols[:, c:c + n]

        def rstd_from_ss(ss_ap, ss_bufs, n, out_ap, out_buf):
            P.emit(act, REC.activation(out_ap, ss_ap, AF.Ln, scale=1.0 / n, bias=epsc[:, 0:1]),
                   list(ss_bufs) + [epsc.b], [out_buf])
            P.emit(act, REC.activation(out_ap, out_ap, AF.Exp, scale=-0.5), [out_buf], [out_buf])

        def dump(name, ap, bufs, npart=128, idx=None):
            if not dbg or name not in dbg:
                return
            dst = dbg_d[name] if idx is None else dbg_d[name][idx]
            P.emit(dve, REC.tensor_copy(tmpA[0:npart, :], ap), bufs, [tmpA.b])
            P.dma(sp, dst[0:npart, :], tmpA[0:npart, :], [tmpA.b], [])

        def norm_transpose(src_d, src_buf, row0, l, gcol):
            for tt in range(4):
                xt = xin[tt % 2]
                sm = sms[0]
                hb = hbs[0]
                pt_ = pst_list[tt % len(pst_list)]
                r0 = row0 + tt * 128
                P.dma(sp, xt[:], src_d[r0:r0 + 128, :], [src_buf], [xt.b])
                P.emit(act, REC.activation(junk[:, :], xt[:, 0:512], AF.Square, accum_out=sm[:, 0:1]),
                       [xt.b], [junk.b, sm.b])
                P.emit(act, REC.activation(junk[:, :], xt[:, 512:1024], AF.Square, accum_out=sm[:, 1:2]),
                       [xt.b], [junk.b, sm.b])
                P.emit(dve, REC.tensor_tensor(sm[:, 2:3], sm[:, 0:1], sm[:, 1:2], ALU.add), [sm.b], [sm.b])
                rstd_from_ss(sm[:, 2:3], [sm.b], D, sm[:, 3:4], sm.b)
                P.emit(dve, REC.tensor_scalar(hb[:], xt[:], sm[:, 3:4], None, ALU.mult),
                       [xt.b, sm.b], [hb.b])
                for kc in range(8):
                    P.emit(pe, REC.transpose(pt_[:, kc * 128:(kc + 1) * 128], hb[:, kc * 128:(kc + 1) * 128], ident_b),
                           [hb.b, cstb.b], [pt_.b], sig=(kc == 7))
                P.emit(dve, REC.tensor_tensor(
                    hT[:, :, tt * 128:(tt + 1) * 128],
                    pt_[:, :].rearrange("p (k t) -> p k t", k=8),
                    col(l, gcol, 8).unsqueeze(2).to_broadcast([128, 8, 128]), ALU.mult),
                    [pt_.b, cols.b], [hT.b])

        def post_norm_residual(ps_list, src_d, src_buf, dst_d, dst_buf, r0, k, nocopy=False):
            xt = xr
            mx = mixs[k % 2]
            P.dma(sp, xt[:], src_d[r0:r0 + 128, :], [src_buf], [xt.b])
            for half in range(2):
                ps = ps_list[half]
                if nocopy:
                    P.emit(act, REC.activation(junkP[:, :], ps[:, :], AF.Square, accum_out=smP[:, 4 + half:5 + half]),
                           [ps.b], [junkP.b, smP.b])
                else:
                    P.emit(act, REC.activation(mx[:, half * 512:(half + 1) * 512], ps[:, :], AF.Copy), [ps.b], [mx.b])
                    P.emit(act, REC.activation(junkP[:, :], mx[:, half * 512:(half + 1) * 512], AF.Square,
                                               accum_out=smP[:, 4 + half:5 + half]), [mx.b], [junkP.b, smP.b])
            P.emit(dve, REC.tensor_tensor(smP[:, 6:7], smP[:, 4:5], smP[:, 5:6], ALU.add), [smP.b], [smP.b])
            rstd_from_ss(smP[:, 6:7], [smP.b], D, smP[:, 7:8], smP.b)
            if nocopy:
                for half in range(2):
                    ps = ps_list[half]
                    P.emit(dve, REC.scalar_tensor_tensor(mx[:, half * 512:(half + 1) * 512], ps[:, :], smP[:, 7:8], rows[:, half * 512:(half + 1) * 512], ALU.mult, ALU.mult),
                           [ps.b, smP.b, rows.b], [mx.b])
            else:
                P.emit(dve, REC.scalar_tensor_tensor(mx[:], mx[:], smP[:, 7:8], rows[:, 0:1024], ALU.mult, ALU.mult),
                       [mx.b, smP.b, rows.b], [mx.b])
            P.emit(dve, REC.tensor_tensor(mx[:], mx[:], xt[:], ALU.add), [mx.b, xt.b], [mx.b])
            P.dma(sp, dst_d[r0:r0 + 128, :], mx[:], [mx.b], [dst_buf])

        REG = {"wreg": [], "reg2": []}

        def reg_buf(region, sb_, eb_, buf):
            for (s0, e0, ob) in REG[region]:
                if s0 < eb_ and sb_ < e0 and ob is not buf:
                    if ob.writer is not None and ob.writer not in buf.readers:
                        buf.readers.append(ob.writer)
                    for r_ in ob.readers:
                        if r_ not in buf.readers:
                            buf.readers.append(r_)
            REG[region].append((sb_, eb_, buf))

        for l in range(NL):
            src_d, src_buf = (x_d, Buf("xsrc")) if l == 0 else (xb_d[(l - 1) % 2], b_xb[(l - 1) % 2])
            dst_d, dst_buf = (y_d, b_y) if l == NL - 1 else (xb_d[l % 2], b_xb[l % 2])

            o = 0
            def carve(n_el):
                nonlocal o
                a = wreg.t[:, o:o + n_el]
                o += n_el
                return a
            w_in_sb = carve(8 * NCOLW)
            w_dt_sb = carve(8 * 4).rearrange("p (k n) -> p k n", k=8)
            w_q_sb = carve(2 * 768).rearrange("p (k n) -> p k n", k=2)
            w_qsw_sb = carve(2 * 256).rearrange("p (k n) -> p k n", k=2)
            w_kn_sb = carve(512)
            w_v_sb = carve(512)
            def carve_t(shape, dt):
                nonlocal o
                n = int(np.prod(shape))
                if dt == F32:
                    o += (o % 2)
                    a = wreg.t[:, o:o + 2 * n].bitcast(F32)
                    o += 2 * n
                else:
                    a = wreg.t[:, o:o + n]
                    o += n
                carve_t.last = (2 * (o - (2 * n if dt == F32 else n)), 2 * o)
                if len(shape) == 2:
                    return a.rearrange("p (a b) -> p a b", a=shape[0])
                if len(shape) == 3:
                    return a.rearrange("p (a b c) -> p a b c", a=shape[0], b=shape[1])
                return a
            bW = Buf("wM%d" % l)
            GROUPS = [(0, 128), (128, 128), (256, 128), (384, 32), (416, 32)] + [(448 + g * 128, 128) for g in range(14)]
            bWg = {}
            bWo = Buf("wo%d" % l)
            P.barrier()
            reg_buf("wreg", 2 * 8 * NCOLW, 2 * o, bW)
            set_pools({"mm": [0, 1, 2], "acc": [3, 4], "aux": [5, 6]}, [pst])
            for (dst, srcap) in [
                (w_dt_sb, w_dt_d[l].rearrange("(k p) n -> p k n", p=128)),
                (w_q_sb, w_q_d[l].rearrange("(k p) n -> p k n", p=128)),
                (w_qsw_sb, w_qsw_d[l].rearrange("(k p) n -> p k n", p=128)),
                (w_kn_sb, w_kn_d[l]),
                (w_v_sb, w_v_d[l]),
            ]:
                P.dma(pool, dst, srcap, [], [bW])
            for (c0g, Mg) in GROUPS:
                bWg[c0g] = Buf("wg%d_%d" % (l, c0g))
                reg_buf("wreg", 2 * 8 * c0g, 2 * 8 * (c0g + Mg), bWg[c0g])
                P.dma(pool, w_in_sb[:, 8 * c0g:8 * (c0g + Mg)], w_in_d[l, :, 8 * c0g:8 * (c0g + Mg)], [], [bWg[c0g]])
            P.dma(sp, cols[:, :], cols_d[:, l, :], [], [cols.b])
            P.dma(sp, rows[:, 0:1024], rows_d[l, :, 0:1024], [], [rows.b])
            P.dma(sp, rows[:, 1024:1032], rows_d[l, :, 2048:2056], [], [rows.b])

            def MT(name, shape, dt):
                t_ = Tile(carve_t(shape, dt), Buf(name))
                reg_buf("wreg", carve_t.last[0], carve_t.last[1], t_.b)
                return t_
            cq_sb = MT("cq", [3, 512], BF16)
            cq_sq = MT("cqsq", [3, 512], BF16)
            cqns = [MT("cqn%d" % i, [3, 512], BF16) for i in range(2)]
            conv_sb = MT("conv", [4, 512], F32)
            szs = MT("szs", [2, 512], F32)
            ubs = [MT("ub%d" % i, [1, 515], F32) for i in range(2)]
            uh = MT("uh", [6, 3], F32)
            vbuf = MT("vbuf", [2, 514], F32)
            xs_f = MT("xsf", [2, 512], F32)
            xs_b = MT("xsb", [2, 512], BF16)
            Bt = MT("Bt", [2, 512], BF16)
            Ct = MT("Ct", [2, 512], BF16)
            o_wout = o
            Kcur = MT("Kcur", [8, 512], BF16)
            kbuf = [MT("kbuf0", [1, 4096], BF16)]
            kbq = [Buf("kbq%d" % i) for i in range(4)]
            for i_ in range(4):
                reg_buf("wreg", carve_t.last[0] + 2048 * i_, carve_t.last[0] + 2048 * (i_ + 1), kbq[i_])
            reg_buf("wreg", 2 * o_wout, 2 * (o_wout + 8 * D), bWo)
            w_out_sb = wreg.t[:, o_wout:o_wout + 8 * D].rearrange("p (k n) -> p k n", k=8)
            assert o - o_wout == 8 * D
            Qh = [MT("Qh%d" % i, [1, 512], BF16) for i in range(2)]
            Pt = [MT("Pt%d" % i, [1, 512], BF16) for i in range(3)]
            tabs = MT("tabs", [2, 512], F32)
            rtmp = MT("rtmp", [2, 512], F32)
            rint = Tile(carve_t([1, 512], F32).bitcast(I32), Buf("rint"))
            krot = MT("krot", [1, 512], F32)
            rd = Tile(krot.t, Buf("rd"))
            osb = MT("osb", [1, 512], F32)
            dtT = MT("dtT", [4, 4], F32)
            adt = MT("adt", [4, 4], F32)
            arow = MT("arow", [1, 4], F32)
            state = MT("state", [4, 64], F32)
            prevp = MT("prevp", [4, 128], BF16)
            xdtp = MT("xdtp", [4, 128], BF16)
            xdtd = MT("xdtd", [4, 64], BF16)
            Btok = MT("Btok", [2, 128], BF16)
            adtri = MT("adtri", [4, 128], F32)
            dec = MT("dec", [4, 128], F32)
            drow = MT("drow", [4, 128], F32)
            Cs = MT("Cs", [4, 128], BF16)
            scT = MT("scT", [4, 128], BF16)
            s4 = MT("s4", [8, 4], F32)
            assert o * 2 <= WREG, o * 2
            o3 = 0
            def R2T(name, shape, dt):
                nonlocal o3
                n = int(np.prod(shape))
                if dt == F32:
                    a_ = reg2.t[:, o3:o3 + 2 * n].bitcast(F32)
                    o3 += 2 * n
                else:
                    a_ = reg2.t[:, o3:o3 + n]
                    o3 += n
                if len(shape) == 2:
                    a_ = a_.rearrange("p (a b) -> p a b", a=shape[0])
                elif len(shape) == 3:
                    a_ = a_.rearrange("p (a b c) -> p a b c", a=shape[0], b=shape[1])
                t_ = Tile(a_, Buf(name))
                R2T.last = (2 * (o3 - (2 * n if dt == F32 else n)), 2 * o3)
                reg_buf("reg2", R2T.last[0], R2T.last[1], t_.b)
                return t_
            gat = R2T("gat", [2, 512], F32)
            yT = R2T("yT", [8, 512], BF16)
            gsq = R2T("gsq", [2, 512], BF16)
            Vcur = R2T("Vcur", [8, 4, 65], BF16)
            vbufs = [R2T("vbuf%d" % i, [28, 65], BF16) for i in range(2)]
            kbuf.append(R2T("kbuf1", [1, 4096], BF16))
            kbqs = [kbq, [Buf("kbq1_%d" % i) for i in range(4)]]
            for i_ in range(4):
                reg_buf("reg2", R2T.last[0] + 2048 * i_, R2T.last[0] + 2048 * (i_ + 1), kbqs[1][i_])
            assert o3 * 2 <= REG2, o3 * 2

            P.emit(dve, REC.memset(Vcur[:, :, :, 64:65], 1.0), [], [Vcur.b])
            P.emit(dve, REC.memset(vbuf[:, :, 0:2], 0.0), [], [vbuf.b])
            P.emit(dve, REC.memset(uh[:, :, :], 0.0), [], [uh.b])
            P.emit(dve, REC.memset(state[:], 0.0), [], [state.b])
            P.emit(dve, REC.memset(prevp[:], 0.0), [], [prevp.b])
            P.emit(dve, REC.memset(xdtp[:], 0.0), [], [xdtp.b])
            P.emit(act, REC.activation(arow[:, 0, :], rows[:, 1028:1032], AF.Exp), [rows.b], [arow.b])
            P.emit(dve, REC.tensor_scalar(arow[:, 0, :], arow[:, 0, :], -1.0, None, ALU.mult), [arow.b], [arow.b])

            for blk in range(NB):
                t0 = blk * 512
                cqn = cqns[blk % 2]
                P.shift = -SHIFT_M
                if l == 0 or not F_ROPECACHE:
                    P.dma(sp, rint[0:32, 0, :], pos_d[:, t0:t0 + 512], [], [rint.b])
                    P.emit(dve, REC.tensor_copy(rtmp[0:32, 0, :], rint[0:32, 0, :]), [rint.b], [rtmp.b])
                    P.emit(dve, REC.tensor_scalar(rtmp[0:32, 0, :], rtmp[0:32, 0, :], ropec[:, 0:1], None, ALU.mult),
                           [rtmp.b, ropec.b], [rtmp.b])
                    for ti, phase in ((0, PI / 2), (1, 0.0)):
                        P.emit(dve, REC.tensor_scalar(rtmp[0:32, 1, :], rtmp[0:32, 0, :], 1.0 / (2 * PI), phase / (2 * PI) + 0.5, ALU.mult, ALU.add),
                               [rtmp.b], [rtmp.b])
                        P.emit(dve, REC.tensor_copy(rint[0:32, 0, :], rtmp[0:32, 1, :]), [rtmp.b], [rint.b])
                        P.emit(dve, REC.tensor_copy(rtmp[0:32, 1, :], rint[0:32, 0, :]), [rint.b], [rtmp.b])
                        P.emit(dve, REC.tensor_scalar(rtmp[0:32, 1, :], rtmp[0:32, 1, :], -2 * PI, None, ALU.mult), [rtmp.b], [rtmp.b])
                        P.emit(dve, REC.scalar_tensor_tensor(tabs[0:32, ti, :], rtmp[0:32, 0, :], phase, rtmp[0:32, 1, :], ALU.add, ALU.add),
                               [rtmp.b], [tabs.b])
                        P.emit(dve, REC.tensor_scalar(rtmp[0:32, 1, :], tabs[0:32, ti, :], -PI, 2 * PI, ALU.is_lt, ALU.mult), [tabs.b], [rtmp.b])
                        P.emit(dve, REC.tensor_tensor(tabs[0:32, ti, :], tabs[0:32, ti, :], rtmp[0:32, 1, :], ALU.add), [tabs.b, rtmp.b], [tabs.b])
                        P.emit(dve, REC.tensor_scalar(rtmp[0:32, 1, :], tabs[0:32, ti, :], PI, -2 * PI, ALU.is_gt, ALU.mult), [tabs.b], [rtmp.b])
                        P.emit(dve, REC.tensor_tensor(tabs[0:32, ti, :], tabs[0:32, ti, :], rtmp[0:32, 1, :], ALU.add), [tabs.b, rtmp.b], [tabs.b])
                        if ti == 0:
                            P.emit(act, REC.activation(tabs[0:32, 0, :], tabs[0:32, 0, :], AF.Sin), [tabs.b], [tabs.b])
                        else:
                            P.emit(act, REC.activation(tabs[0:32, 1, :], tabs[0:32, 1, :], AF.Sin, scale=ropec[:, 1:2]), [tabs.b, ropec.b], [tabs.b])
                    if F_ROPECACHE:
                        P.dma(sp, tab_d[blk], tabs[0:32, :, :].rearrange("p a b -> p (a b)"), [tabs.b], [b_tab[blk]])
                else:
                    P.dma(sp, tabs[0:32, :, :].rearrange("p a b -> p (a b)"), tab_d[blk], [b_tab[blk]], [tabs.b])
                Ctab = tabs[0:32, 0, :]
                Stab = tabs[0:32, 1, :]

                norm_transpose(src_d, src_buf, t0, l, C_GPRE1)
                if l == 0 and blk == NB - 1:
                    dump("hT0", hT[:, 0, :], [hT.b])

                def inproj(c0, M):
                    ps = PS("mm")
                    for kc in range(8):
                        P.emit(pe, REC.matmul(ps[0:M, :], w_in_sb[:, 8 * c0 + kc * M:8 * c0 + (kc + 1) * M], hT[:, kc, :], start=(kc == 0), stop=(kc == 7)),
                               [bWg[c0], hT.b], [ps.b], sig=(kc == 7))
                    return ps
                for g in range(3):
                    ps = inproj(g * 128, 128)
                    P.emit(act, REC.activation(cq_sb[:, g, :], ps[:, :], AF.Copy), [ps.b], [cq_sb.b])
                    P.emit(act, REC.activation(cq_sq[:, g, :], ps[:, :], AF.Square), [ps.b], [cq_sq.b])
                ps_kr = inproj(384, 32)
                P.emit(dve, REC.tensor_tensor(rtmp[0:32, 0, :], ps_kr[0:32, :], Ctab, ALU.mult), [ps_kr.b, tabs.b], [rtmp.b])
                ps_ks = inproj(416, 32)
                P.emit(dve, REC.tensor_tensor(rtmp[0:32, 1, :], ps_ks[0:32, :], Stab, ALU.mult), [ps_ks.b, tabs.b], [rtmp.b])
                P.emit(dve, REC.tensor_tensor(krot[0:32, 0, :], rtmp[0:32, 0, :], rtmp[0:32, 1, :], ALU.add), [rtmp.b], [krot.b])
                P.emit(dve, REC.tensor_copy(Kcur[64:96, :, :], krot[0:32, 0:1, :].to_broadcast([32, 8, 512])), [krot.b], [Kcur.b])
                for g in range(4):
                    ps = inproj(448 + g * 128, 128)
                    P.emit(act, REC.activation(conv_sb[:, g, :], ps[:, :], AF.Copy), [ps.b], [conv_sb.b])
                for c in range(2):
                    ps = inproj(448 + (4 + c) * 128, 128)
                    P.emit(dve, REC.tensor_tensor(vbuf[:, c, 2:514], conv_sb[:, 2 + c, :], ps[:, :], ALU.mult), [conv_sb.b, ps.b], [vbuf.b])
                    P.emit(dve, REC.tensor_scalar(tmpA[:, :], vbuf[:, c, 0:512], col(l, C_SCW + c * 3 + 0), None, ALU.mult), [vbuf.b, cols.b], [tmpA.b])
                    P.emit(dve, REC.scalar_tensor_tensor(tmpA[:, :], vbuf[:, c, 1:513], col(l, C_SCW + c * 3 + 1), tmpA[:, :], ALU.mult, ALU.add), [vbuf.b, cols.b, tmpA.b], [tmpA.b])
                    P.emit(dve, REC.scalar_tensor_tensor(tmpA[:, :], vbuf[:, c, 2:514], col(l, C_SCW + c * 3 + 2), tmpA[:, :], ALU.mult, ALU.add), [vbuf.b, cols.b, tmpA.b], [tmpA.b])
                    P.emit(dve, REC.tensor_tensor(yT[:, 4 + c, :], tmpA[:, :], conv_sb[:, c, :], ALU.mult), [tmpA.b, conv_sb.b], [yT.b])
                    P.emit(dve, REC.tensor_copy(vbuf[:, c, 0:2], vbuf[:, c, 512:514]), [vbuf.b], [vbuf.b])
                for g in range(2):
                    ps = inproj(448 + 768 + g * 128, 128)
                    P.emit(act, REC.activation(szs[:, g, :], ps[:, :], AF.Silu), [ps.b], [szs.b])
                for c in range(6):
                    ps = inproj(448 + 1024 + c * 128, 128)
                    ub = ubs[c % 2]
                    P.emit(act, REC.activation(ub[:, 0, 3:515], ps[:, :], AF.Copy), [ps.b], [ub.b])
                    P.emit(dve, REC.tensor_copy(ub[:, 0, 0:3], uh[:, c, :]), [uh.b], [ub.b])
                    P.emit(dve, REC.tensor_scalar(tmpA[:, :], ub[:, 0, 0:512], col(l, C_SSW + c * 4 + 0), None, ALU.mult), [ub.b, cols.b], [tmpA.b])
                    for k in range(1, 4):
                        P.emit(dve, REC.scalar_tensor_tensor(tmpA[:, :], ub[:, 0, k:k + 512], col(l, C_SSW + c * 4 + k), tmpA[:, :], ALU.mult, ALU.add),
                               [ub.b, cols.b, tmpA.b], [tmpA.b])
                    if c < 2:
                        P.emit(act, REC.activation(xs_f[:, c, :], tmpA[:, :], AF.Silu, bias=col(l, C_SSB + c)), [tmpA.b, cols.b], [xs_f.b])
                        P.emit(dve, REC.tensor_copy(xs_b[:, c, :], xs_f[:, c, :]), [xs_f.b], [xs_b.b])
                    elif c < 4:
                        P.emit(act, REC.activation(Bt[:, c - 2, :], tmpA[:, :], AF.Silu, bias=col(l, C_SSB + c)), [tmpA.b, cols.b], [Bt.b])
                    else:
                        P.emit(act, REC.activation(Ct[:, c - 4, :], tmpA[:, :], AF.Silu, bias=col(l, C_SSB + c)), [tmpA.b, cols.b], [Ct.b])
                    P.emit(dve, REC.tensor_copy(uh[:, c, :], ub[:, 0, 512:515]), [ub.b], [uh.b])
                ps = PS("aux")
                for tt in range(4):
                    for kc in range(8):
                        P.emit(pe, REC.matmul(ps[:, tt * 4:tt * 4 + 4], hT[:, kc, tt * 128:(tt + 1) * 128], w_dt_sb[:, kc, :],
                                                                          start=(kc == 0), stop=(kc == 7)),
                               [bW, hT.b], [ps.b], sig=(kc == 7))
                dt4 = dtT[:, :, :]
                dtb = rows[:, 1024:1028].unsqueeze(1).to_broadcast([128, 4, 4])
                P.emit(dve, REC.tensor_tensor(dt4, ps[:, 0:16].rearrange("p (a b) -> p a b", a=4), dtb, ALU.add), [ps.b, rows.b], [dtT.b])
                P.emit(act, REC.activation(adt[:, :, :], dt4, AF.Abs), [dtT.b], [adt.b])
                P.emit(act, REC.activation(adt[:, :, :], adt[:, :, :], AF.Exp, scale=-1.0), [adt.b], [adt.b])
                P.emit(act, REC.activation(adt[:, :, :], adt[:, :, :], AF.Ln, bias=1.0), [adt.b], [adt.b])
                P.emit(dve, REC.scalar_tensor_tensor(dt4, dt4, 0.0, adt[:, :, :], ALU.max, ALU.add), [dtT.b, adt.b], [dtT.b])
                P.emit(dve, REC.tensor_tensor(adt[:, :, :], dt4, arow[:, 0:1, :].to_broadcast([128, 4, 4]), ALU.mult), [dtT.b, arow.b], [adt.b])

                for (chs, n, qc) in (((0, 1), 256, C_QN), ((2,), 128, C_KVN)):
                    ps = PS("aux")
                    for i, c in enumerate(chs):
                        P.emit(pe, REC.matmul(ps[:, :], ones_b, cq_sq[:, c, :], start=(i == 0), stop=(i == len(chs) - 1)),
                               [cstb.b, cq_sq.b], [ps.b], sig=(i == len(chs) - 1))
                    P.emit(act, REC.activation(tmpA[:, :], ps[:, :], AF.Ln, scale=1.0 / n, bias=epsc[:, 0:1]), [ps.b, epsc.b], [tmpA.b])
                    P.emit(act, REC.activation(tmpA[:, :], tmpA[:, :], AF.Exp, scale=-0.5), [tmpA.b], [tmpA.b])
                    for i, c in enumerate(chs):
                        P.emit(dve, REC.scalar_tensor_tensor(cqn[:, c, :], cq_sb[:, c, :], col(l, qc + i), tmpA[:, :], ALU.mult, ALU.mult),
                               [cq_sb.b, cols.b, tmpA.b], [cqn.b])
                ckvn = cqn[:, 2, :]
                if l == 0 and blk == NB - 1:
                    for c in range(3):
                        dump("cqn", cqn[:, c, :], [cqn.b], idx=c)
                    dump("dtT", dtT[:, :, :].rearrange("p a b -> p (a b)"), [dtT.b]) if False else None

                for hp in range(4):
                    ps = PS("mm")
                    P.emit(pe, REC.matmul(ps[:, :], w_kn_sb[:, hp * 128:(hp + 1) * 128], ckvn, start=True, stop=True), [bW, cqn.b], [ps.b])
                    P.emit(act, REC.activation(Kcur[0:64, 2 * hp, :], ps[0:64, :], AF.Copy), [ps.b], [Kcur.b])
                    P.emit(dve, REC.tensor_copy(Kcur[0:64, 2 * hp + 1, :], ps[64:128, :]), [ps.b], [Kcur.b])
                for tt in range(4):
                    ps = PS("mm")
                    P.emit(pe, REC.matmul(ps[:, :], cqn[:, 2, tt * 128:(tt + 1) * 128], w_v_sb, start=True, stop=True), [bW, cqn.b], [ps.b])
                    P.emit(act, REC.activation(Vcur[:, :, tt, 0:64], ps[:, :].rearrange("p (h d) -> p h d", h=8), AF.Copy), [ps.b], [Vcur.b])
                if blk < NB - 1:
                    P.dma(sp, kd_d.rearrange("h d s -> d h s")[:, :, t0:t0 + 512], Kcur[0:96, :, :], [Kcur.b], [b_kd])
                    P.dma(sp, vd_d.rearrange("h p j d -> p h (j d)")[:, :, blk * 260:(blk + 1) * 260],
                          Vcur[:, :, :, :].rearrange("p h j d -> p h (j d)"), [Vcur.b], [b_vd])

                P.shift = 0
                for h in range(NH):
                    q = Qh[h % 2]
                    psQ = PS("mm")
                    for c in range(2):
                        P.emit(pe, REC.matmul(psQ[0:96, :], w_q_sb[:, c, h * 96:(h + 1) * 96], cqn[:, c, :], start=(c == 0), stop=(c == 1)),
                               [bW, cqn.b], [psQ.b], sig=(c == 1))
                    psS = PS("mm")
                    for c in range(2):
                        P.emit(pe, REC.matmul(psS[0:32, :], w_qsw_sb[:, c, h * 32:(h + 1) * 32], cqn[:, c, :], start=(c == 0), stop=(c == 1)),
                               [bW, cqn.b], [psS.b], sig=(c == 1))
                    P.emit(act, REC.activation(q[0:64, 0, :], psQ[0:64, :], AF.Copy), [psQ.b], [q.b])
                    P.emit(dve, REC.tensor_tensor(rtmp[0:32, 0, :], psS[0:32, :], Stab, ALU.mult), [psS.b, tabs.b], [rtmp.b])
                    P.emit(dve, REC.tensor_tensor(rtmp[0:32, 1, :], psQ[64:96, :], Ctab, ALU.mult), [psQ.b, tabs.b], [rtmp.b])
                    P.emit(dve, REC.tensor_tensor(q[64:96, 0, :], rtmp[0:32, 0, :], rtmp[0:32, 1, :], ALU.add), [rtmp.b], [q.b])
                    kb = kbuf[h % 2]
                    kbq_h = kbqs[h % 2]
                    vb = vbufs[h % 2]
                    if l == 0 and blk == NB - 1 and h == 0:
                        dump("Q0", q[0:96, 0, :], [q.b], npart=96)
                        dump("K0", Kcur[0:96, 0, :], [Kcur.b], npart=96)
                    if blk > 0:
                        for qi in range(4):
                            lo_, hi_ = qi * 1024, min(t0, (qi + 1) * 1024)
                            if hi_ > lo_:
                                P.dma(sp, kb[0:96, 0, lo_:hi_], kd_d[h, :, lo_:hi_], [b_kd], [kbq_h[qi]])
                        P.dma(sp, vb[:, 0:4 * blk, :], vd_d[h, :, 0:4 * blk, :], [b_vd], [vb.b])
                    psO = PS("acc")
                    nfull = 4 * blk
                    for j in range(nfull + 4):
                        pss = PS("mm")
                        pt = Pt[j % 3]
                        if j < nfull:
                            c0 = 0
                            P.emit(pe, REC.matmul(pss[:, :], kb[0:96, 0, j * 128:(j + 1) * 128], q[0:96, 0, :], start=True, stop=True),
                                   [kbq_h[(j * 128) // 1024], q.b], [pss.b])
                        else:
                            jj = j - nfull
                            c0 = jj * 128
                            P.emit(pe, REC.matmul(pss[:, c0:512], Kcur[0:96, h, c0:c0 + 128], q[0:96, 0, c0:512], start=True, stop=False),
                                   [Kcur.b, q.b], [pss.b], sig=False)
                            P.emit(pe, REC.matmul(pss[:, c0:c0 + 128], ident_b, mneg_b, start=False, stop=True),
                                   [cstb.b], [pss.b])
                        P.emit(act, REC.activation(pt[:, 0, c0:512], pss[:, c0:512], AF.Exp, scale=SCALE), [pss.b], [pt.b])
                        if j < nfull:
                            vl, vlb = vb[:, j, :], vb.b
                        else:
                            vl, vlb = Vcur[:, h, j - nfull, :], Vcur.b
                        P.emit(pe, REC.matmul(psO[0:65, c0:512], vl, pt[:, 0, c0:512], start=(j == 0), stop=(j == nfull + 3)),
                               [vlb, pt.b], [psO.b])
                    P.emit(act, REC.activation(rd[64:65, 0, :], psO[64:65, :], AF.Ln), [psO.b], [rd.b])
                    P.emit(act, REC.activation(rd[64:65, 0, :], rd[64:65, 0, :], AF.Exp, scale=-1.0), [rd.b], [rd.b])
                    psB = PS("mm")
                    P.emit(pe, REC.matmul(psB[0:64, :], ones_f[64:65, 0:64], rd[64:65, 0, :], start=True, stop=True), [cst32.b, rd.b], [psB.b])
                    P.emit(act, REC.activation(osb[0:64, 0, :], psO[0:64, :], AF.Copy), [psO.b], [osb.b])
                    P.emit(dve, REC.tensor_tensor(yT[(h % 2) * 64:(h % 2) * 64 + 64, h // 2, :], osb[0:64, 0, :], psB[0:64, :], ALU.mult),
                           [osb.b, psB.b], [yT.b])

                P.dma(pool, w_out_sb, w_out_d[l].rearrange("(k p) n -> p k n", p=128), [], [Kcur.b, bWo] + kbq)
                psY = [PS("acc"), PS("acc")]
                for tt in range(4):
                    ts = slice(tt * 128, (tt + 1) * 128)
                    P.emit(dve, REC.tensor_copy(prevp[:, 0:4:2, 0:64], state[:, 0:4:2, :]), [state.b], [prevp.b])
                    P.emit(dve, REC.tensor_copy(prevp[:, 1:4:2, 64:128], state[:, 1:4:2, :]), [state.b], [prevp.b])
                    for c in range(2):
                        P.emit(pe, REC.transpose(pst[:, c * 128:(c + 1) * 128], xs_b[:, c, ts], ident_b), [xs_b.b, cstb.b], [pst.b], sig=False)
                    for g in range(2):
                        P.emit(pe, REC.transpose(pst[:, 256 + g * 128:256 + (g + 1) * 128], Bt[:, g, ts], ident_b), [Bt.b, cstb.b], [pst.b], sig=(g == 1))
                    xtok = pst[:, 0:256].rearrange("p (h d) -> p h d", h=4)
                    P.emit(dve, REC.tensor_tensor(xdtp[:, 0:4:2, 0:64], xtok[:, 0:4:2, :], dtT[:, tt, 0:4:2].unsqueeze(2).to_broadcast([128, 2, 64]), ALU.mult),
                           [pst.b, dtT.b], [xdtp.b])
                    P.emit(dve, REC.tensor_tensor(xdtp[:, 1:4:2, 64:128], xtok[:, 1:4:2, :], dtT[:, tt, 1:4:2].unsqueeze(2).to_broadcast([128, 2, 64]), ALU.mult),
                           [pst.b, dtT.b], [xdtp.b])
                    P.emit(act, REC.activation(Btok[:, :, :], pst[:, 256:512].rearrange("p (g n) -> p g n", g=2), AF.Copy), [pst.b], [Btok.b])
                    P.emit(dve, REC.tensor_tensor(adtri[:, :, :], tri_f.unsqueeze(1).to_broadcast([128, 4, 128]),
                                                                 adt[:, tt, :].unsqueeze(2).to_broadcast([128, 4, 128]), ALU.mult), [cst32.b, adt.b], [adtri.b])
                    psR = PS("aux")
                    P.emit(pe, REC.matmul(psR[:, :], ones_f, adtri[:, :, :].rearrange("p h l -> p (h l)"), start=True, stop=True), [cst32.b, adtri.b], [psR.b])
                    psA = PS("aux")
                    P.emit(pe, REC.matmul(psA[:, 0:4], tri_f, adt[:, tt, :], start=True, stop=True), [cst32.b, adt.b], [psA.b])
                    acol = s4[:, 0, :]
                    P.emit(dve, REC.tensor_copy(acol, psA[:, 0:4]), [psA.b], [s4.b])
                    psR3 = psR[:, :].rearrange("p (h l) -> p h l", h=4)
                    P.emit(dve, REC.tensor_tensor(dec[:, :, :], psR3, acol.unsqueeze(2).to_broadcast([128, 4, 128]), ALU.subtract), [psR.b, s4.b], [dec.b])
                    P.emit(dve, REC.tensor_scalar(dec[:, :, :], dec[:, :, :], 0.0, None, ALU.min), [dec.b], [dec.b])
                    P.emit(act, REC.activation(dec[:, :, :], dec[:, :, :], AF.Exp), [dec.b], [dec.b])
                    P.emit(dve, REC.tensor_tensor(dec[:, :, :], dec[:, :, :], tri_f.unsqueeze(1).to_broadcast([128, 4, 128]), ALU.mult), [dec.b, cst32.b], [dec.b])
                    P.emit(act, REC.activation(drow[:, :, :], psR3, AF.Exp), [psR.b], [drow.b])
                    P.emit(dve, REC.tensor_tensor(s4[:, 1, :], psR3[:, :, 127], acol, ALU.subtract), [psR.b, s4.b], [s4.b])
                    P.emit(act, REC.activation(s4[:, 2, :], s4[:, 1, :], AF.Exp), [s4.b], [s4.b])
                    P.emit(act, REC.activation(s4[:, 3, :], psR3[:, :, 127], AF.Exp), [psR.b], [s4.b])
                    P.emit(dve, REC.tensor_tensor(s4[:, 4, :], s4[:, 2, :], dtT[:, tt, :], ALU.mult), [s4.b, dtT.b], [s4.b])
                    P.emit(dve, REC.tensor_tensor(xdtd[:, :, :], xtok, s4[:, 4, :].unsqueeze(2).to_broadcast([128, 4, 64]), ALU.mult), [pst.b, s4.b], [xdtd.b])
                    for g in range(2):
                        P.emit(dve, REC.tensor_tensor(Cs[:, 2 * g:2 * g + 2, :], drow[:, 2 * g:2 * g + 2, :],
                                                                          Ct[:, g:g + 1, ts].to_broadcast([128, 2, 128]), ALU.mult), [drow.b, Ct.b], [Cs.b])
                    psG = PS("mm")
                    for g in range(2):
                        P.emit(pe, REC.matmul(psG[:, g * 128:(g + 1) * 128], Bt[:, g, ts], Ct[:, g, ts], start=True, stop=True), [Bt.b, Ct.b], [psG.b], sig=(g == 1))
                    for g in range(2):
                        P.emit(dve, REC.tensor_tensor(scT[:, 2 * g:2 * g + 2, :], dec[:, 2 * g:2 * g + 2, :],
                                                                             psG[:, g * 128:(g + 1) * 128].unsqueeze(1).to_broadcast([128, 2, 128]), ALU.mult), [dec.b, psG.b], [scT.b])
                    for k in range(2):
                        ops = [(xdtp[:, 2 * k, :], scT[:, 2 * k, :], [xdtp.b, scT.b]), (xdtp[:, 2 * k + 1, :], scT[:, 2 * k + 1, :], [xdtp.b, scT.b]),
                               (prevp[:, 2 * k, :], Cs[:, 2 * k, :], [prevp.b, Cs.b]), (prevp[:, 2 * k + 1, :], Cs[:, 2 * k + 1, :], [prevp.b, Cs.b])]
                        for i, (lh, rh, rb) in enumerate(ops):
                            P.emit(pe, REC.matmul(psY[k][:, ts], lh, rh, start=(i == 0), stop=(i == 3)), rb, [psY[k].b], sig=(i == 3))
                    psSt = PS("aux")
                    for g in range(2):
                        P.emit(pe, REC.matmul(psSt[:, g * 128:(g + 1) * 128], Btok[:, g, :], xdtd[:, 2 * g:2 * g + 2, :].rearrange("p h d -> p (h d)"), start=True, stop=True),
                               [Btok.b, xdtd.b], [psSt.b], sig=(g == 1))
                    P.emit(dve, REC.tensor_tensor(state[:, :, :], state[:, :, :], s4[:, 3, :].unsqueeze(2).to_broadcast([128, 4, 64]), ALU.mult), [state.b, s4.b], [state.b])
                    P.emit(dve, REC.tensor_tensor(state[:, :, :], state[:, :, :], psSt[:, 0:256].rearrange("p (h d) -> p h d", h=4), ALU.add), [state.b, psSt.b], [state.b])
                for k in range(2):
                    P.emit(dve, REC.scalar_tensor_tensor(gat[:, k, :], xs_f[:, k, :], col(l, C_DSK + k), psY[k][:, :], ALU.mult, ALU.add), [xs_f.b, cols.b, psY[k].b], [gat.b])
                    P.emit(dve, REC.tensor_tensor(gat[:, k, :], gat[:, k, :], szs[:, k, :], ALU.mult), [gat.b, szs.b], [gat.b])
                    P.emit(act, REC.activation(gsq[:, k, :], gat[:, k, :], AF.Square), [gat.b], [gsq.b])
                ps = PS("aux")
                for k in range(2):
                    P.emit(pe, REC.matmul(ps[:, :], ones_b, gsq[:, k, :], start=(k == 0), stop=(k == 1)), [cstb.b, gsq.b], [ps.b], sig=(k == 1))
                P.emit(act, REC.activation(tmpA[:, :], ps[:, :], AF.Ln, scale=1.0 / 256, bias=epsc[:, 0:1]), [ps.b, epsc.b], [tmpA.b])
                P.emit(act, REC.activation(tmpA[:, :], tmpA[:, :], AF.Exp, scale=-0.5), [tmpA.b], [tmpA.b])
                for k in range(2):
                    P.emit(dve, REC.scalar_tensor_tensor(yT[:, 6 + k, :], gat[:, k, :], col(l, C_SSN + k), tmpA[:, :], ALU.mult, ALU.mult), [gat.b, cols.b, tmpA.b], [yT.b])

                if l == 0 and blk == NB - 1:
                    for kc in range(8):
                        dump("yT", yT[:, kc, :], [yT.b], idx=kc)
                for tt in range(4):
                    pl = []
                    for half in range(2):
                        ps = PS("mm")
                        for kc in range(8):
                            P.emit(pe, REC.matmul(ps[:, :], yT[:, kc, tt * 128:(tt + 1) * 128], w_out_sb[:, kc, half * 512:(half + 1) * 512],
                                                                                        start=(kc == 0), stop=(kc == 7)), [yT.b, bWo, Kcur.b] + kbq, [ps.b], sig=(kc == 7))
                        pl.append(ps)
                    post_norm_residual(pl, src_d, src_buf, xa_d, b_xa, t0 + tt * 128, tt, nocopy=True)

            bWF = Buf("wF%d" % l)
            P.barrier()
            set_pools({"mm": [0, 1, 2, 3, 4, 5, 6]}, [pst])
            w_up_sb = wreg.t[:, 0:8 * 2 * FF].rearrange("p (j k n) -> p j k n", j=44, k=8)
            w_down_sb = wreg.t[:, 8 * 2 * FF:8 * 2 * FF + NFC * D].rearrange("p (k n) -> p k n", k=NFC)
            assert (8 * 2 * FF + NFC * D) * 2 <= WREG
            bWU = [Buf("wu%d_%d" % (l, j)) for j in range(44)]
            def slot(j_):
                return 2 * (j_ % NFC) + (j_ // NFC)
            for j in range(44):
                reg_buf("wreg", 2048 * slot(j), 2048 * (slot(j) + 1), bWU[j])
            reg_buf("wreg", 2 * 8 * 2 * FF, 2 * (8 * 2 * FF + NFC * D), bWF)
            for i in range(NFC):
                for j in (i, NFC + i):
                    P.dma(pool, w_up_sb[:, slot(j), :, :], w_up_d[l, j].rearrange("p (k n) -> p k n", k=8), [], [bWU[j]])
            wdv = w_down_d[l].rearrange("(k p) n -> p k n", p=128)
            P.dma(pool, w_down_sb[:, 0:11, :], wdv[:, 0:11, :], [], [bWF])
            P.dma(pool, w_down_sb[:, 11:22, :], wdv[:, 11:22, :], [], [bWF])
            P.dma(sp, rows[:, 0:1024], rows_d[l, :, 1024:2048], [], [rows.b])
            gT = Tile(reg2.t[:, 0:NFC * 512].rearrange("p (k n) -> p k n", k=NFC), Buf("gT%d" % l))
            reg_buf("reg2", 0, 2 * NFC * 512, gT.b)
            o2 = NFC * 512
            def carve2(shape):
                nonlocal o2
                n = int(np.prod(shape))
                a = reg2.t[:, o2:o2 + 2 * n].bitcast(F32)
                o2 += 2 * n
                t_ = Tile(a.rearrange("p (a b) -> p a b", a=shape[0]), Buf("r2_%d" % o2))
                reg_buf("reg2", 2 * (o2 - 2 * n), 2 * o2, t_.b)
                return t_
            ug = [carve2([1, 514]) for _ in range(2)]
            uu = [carve2([1, 514]) for _ in range(2)]
            fh = carve2([44, 2])
            tB = carve2([1, 512])
            tC = carve2([1, 512])
            tCs = [Tile(tB.t[:, 0, :], tB.b), Tile(tC.t[:, 0, :], tC.b)]
            assert o2 * 2 <= REG2, o2 * 2
            P.emit(dve, REC.memset(fh[:, :, :], 0.0), [], [fh.b])

            for blk in range(NB):
                t0 = blk * 512
                norm_transpose(xa_d, b_xa, t0, l, C_GPRE2)
                for i in range(NFC):
                    br = []
                    for bi, (ub, cbase) in enumerate(((ug[i % 2], i * 128), (uu[i % 2], FF + i * 128))):
                        ps = PS("mm")
                        for kc in range(8):
                            P.emit(pe, REC.matmul(ps[:, :], w_up_sb[:, slot(i + bi * NFC), kc, :], hT[:, kc, :], start=(kc == 0), stop=(kc == 7)),
                                   [bWU[i + bi * NFC], hT.b], [ps.b], sig=(kc == 7))
                        j = i + bi * NFC
                        br.append(ps)
                        tc_ = tCs[i % 2]
                        if bi == 0:
                            acc_ap, acc_b = tc_[:, :], tc_.b
                        else:
                            acc_ap, acc_b = ps[:, :], ps.b
                        P.emit(act, REC.activation(ub[:, 0, 2:514], ps[:, :], AF.Copy), [ps.b], [ub.b])
                        P.emit(act, REC.activation(acc_ap, ps[:, :], AF.Identity, scale=col(l, C_FW + j * 3 + 2), bias=col(l, C_FB + j)), [ps.b, cols.b], [acc_b])
                        P.emit(act, REC.activation(ub[:, 0, 0:2], fh[:, j, :], AF.Copy), [fh.b], [ub.b])
                        for k in (0, 1):
                            P.emit(dve, REC.scalar_tensor_tensor(acc_ap, ub[:, 0, k:k + 512], col(l, C_FW + j * 3 + k), acc_ap, ALU.mult, ALU.add),
                                   [ub.b, cols.b, acc_b], [acc_b])
                        P.emit(act, REC.activation(fh[:, j, :], ub[:, 0, 512:514], AF.Copy), [ub.b], [fh.b])
                    tc_ = tCs[i % 2]
                    P.emit(act, REC.activation(tc_[:, :], tc_[:, :], AF.Silu), [tc_.b], [tc_.b])
                    P.emit(dve, REC.tensor_tensor(gT[:, i, :], tc_[:, :], br[1][:, :], ALU.mult), [tc_.b, br[1].b], [gT.b])
                for tt in range(4):
                    pl = []
                    for half in range(2):
                        ps = PS("mm")
                        for i in range(NFC):
                            P.emit(pe, REC.matmul(ps[:, :], gT[:, i, tt * 128:(tt + 1) * 128], w_down_sb[:, i, half * 512:(half + 1) * 512],
                                                                                       start=(i == 0), stop=(i == NFC - 1)), [gT.b, bWF], [ps.b], sig=(i == NFC - 1))
                        pl.append(ps)
                    post_norm_residual(pl, xa_d, b_xa, dst_d, dst_buf, t0 + tt * 128, tt)

        P.finalize()
        block = es.enter_context(nc.Block())

        @block.tensor
        def _(e):
            _replay(pe.q, e)

        @block.scalar
        def _(e):
            _replay(act.q, e)

        @block.vector
        def _(e):
            _replay(dve.q, e)

        @block.gpsimd
        def _(e):
            _replay(pool.q, e)

        @block.sync
        def _(e):
            _replay(sp.q, e)
    return nc


def _cols_layer(p, l):
    c = np.zeros((128, NCOLS), np.float32)
    def chunks(v, n):
        return np.asarray(v, np.float32).reshape(n, 128).T
    c[:, C_GPRE1:C_GPRE1 + 8] = chunks(p["norm_mix_pre"][l], 8)
    c[:, C_GPRE2:C_GPRE2 + 8] = chunks(p["norm_ffn_pre"][l], 8)
    c[:, C_QN:C_QN + 2] = chunks(p["mla_q_norm"][l], 2)
    c[:, C_KVN:C_KVN + 1] = chunks(p["mla_kv_norm"][l], 1)
    scw = np.asarray(p["sc_conv_w"][l], np.float32)
    for ch in range(2):
        for k in range(3):
            c[:, C_SCW + ch * 3 + k] = scw[k, ch * 128:(ch + 1) * 128]
    ssw = np.asarray(p["ssd_conv_w"][l], np.float32)
    ssb = np.asarray(p["ssd_conv_b"][l], np.float32)
    for ch in range(6):
        for k in range(4):
            c[:, C_SSW + ch * 4 + k] = ssw[k, ch * 128:(ch + 1) * 128]
        c[:, C_SSB + ch] = ssb[ch * 128:(ch + 1) * 128]
    dsk = np.repeat(np.asarray(p["ssd_d"][l], np.float32), 64)
    c[:, C_DSK:C_DSK + 2] = chunks(dsk, 2)
    c[:, C_SSN:C_SSN + 2] = chunks(p["ssd_norm"][l], 2)
    fw = np.asarray(p["ffn_conv_w"][l], np.float32)
    fb = np.asarray(p["ffn_conv_b"][l], np.float32)
    for j in range(44):
        for k in range(3):
            c[:, C_FW + j * 3 + k] = fw[k, j * 128:(j + 1) * 128]
        c[:, C_FB + j] = fb[j * 128:(j + 1) * 128]
    return c


def prep_shared(p, NL):
    f = lambda a: np.ascontiguousarray(np.asarray(a, np.float32))
    w_in = f(p["w_in"])[:NL]
    sw = np.concatenate([np.arange(16, 32), np.arange(0, 16)])
    kr = w_in[:, :, 384:416]
    w_in_r = np.concatenate([w_in[:, :, 0:384], kr, kr[:, :, sw], w_in[:, :, 416:2208]], axis=2)
    assert w_in_r.shape[2] == NCOLW
    groups = [(0, 128), (128, 128), (256, 128), (384, 32), (416, 32)] + [(448 + g * 128, 128) for g in range(14)]
    w4 = w_in_r.reshape(NL, 8, 128, NCOLW)
    w_in_g = np.concatenate([np.transpose(w4[:, :, :, c0:c0 + M], (0, 2, 1, 3)).reshape(NL, 128, 8 * M) for (c0, M) in groups], axis=2)
    assert w_in_g.shape[2] == 8 * NCOLW
    w_dt = w_in[:, :, 2208:2212]
    wq = f(p["mla_w_q_up"])[:NL]
    wq4 = wq.reshape(NL, 256, 8, 96)
    w_qsw = wq4[:, :, :, 64:96][:, :, :, sw].reshape(NL, 256, 256)
    wkv = f(p["mla_w_kv_up"])[:NL].reshape(NL, 128, 8, 128)
    w_kn = wkv[:, :, :, 0:64].reshape(NL, 128, 512)
    w_v = wkv[:, :, :, 64:128].reshape(NL, 128, 512)
    wu = f(p["ffn_w_up"])[:NL].reshape(NL, 8, 128, 44, 128)
    w_up_g = np.ascontiguousarray(np.transpose(wu, (0, 3, 2, 1, 4)).reshape(NL, 44, 128, 1024))
    cols = np.stack([_cols_layer(p, l) for l in range(NL)], axis=1)
    rows = np.zeros((NL, 128, 2056), np.float32)
    for l in range(NL):
        rows[l, :, 0:1024] = np.asarray(p["norm_mix_post"][l], np.float32)[None, :]
        rows[l, :, 1024:2048] = np.asarray(p["norm_ffn_post"][l], np.float32)[None, :]
        rows[l, :, 2048:2052] = np.asarray(p["ssd_dt_bias"][l], np.float32)[None, :]
        rows[l, :, 2052:2056] = np.asarray(p["ssd_a_log"][l], np.float32)[None, :]
    inv_freq = (1.0 / (10000.0 ** (np.arange(0, 32, 2, dtype=np.float32) / np.float32(32)))).astype(np.float32)
    ropec = np.zeros((32, 2), np.float32)
    ropec[:, 0] = np.concatenate([inv_freq, inv_freq])
    ropec[:, 1] = np.concatenate([-np.ones(16), np.ones(16)])
    tri_ = np.triu(np.ones((128, 128)))
    consts = np.concatenate([np.eye(128), tri_, np.ones((128, 128)), (1.0 - tri_) * -30000.0], axis=1).astype(np.float32)
    return {
        "ropec": ropec, "consts": consts, "cols": np.ascontiguousarray(cols), "rows": rows,
        "w_in_g": np.ascontiguousarray(w_in_g), "w_dt": np.ascontiguousarray(w_dt),
        "w_q": np.ascontiguousarray(wq), "w_qsw": np.ascontiguousarray(w_qsw),
        "w_kn": np.ascontiguousarray(w_kn), "w_v": np.ascontiguousarray(w_v),
        "w_out": f(p["w_out"])[:NL], "w_up_g": w_up_g, "w_down": f(p["ffn_w_down"])[:NL],
    }


def run(inputs, S, NL, ncores, dbg=None):
    shared = prep_shared(inputs, NL)
    x = np.asarray(inputs["x"], np.float32)
    pos = np.asarray(inputs["positions"], np.int32)
    in_maps = []
    for c in range(ncores):
        m = dict(shared)
        m["x"] = np.ascontiguousarray(x[c, :S])
        m["posrep"] = np.ascontiguousarray(np.broadcast_to(pos[c, :S][None, :], (32, S)))
        in_maps.append(m)
    nc = build(S, NL, dbg)
    res = run_bass_kernel_spmd(nc, in_maps, core_ids=list(range(ncores)))
    return res


def kernel(**inputs):
    res = run(inputs, 4096, 4, 8)
    return np.stack([r["y"] for r in res.results], axis=0).astype(np.float32)
```

```python
import math
from contextlib import ExitStack
import numpy as np
import concourse.bass as bass
import concourse.mybir as mybir
from concourse.bass_utils import run_bass_kernel_spmd

F32 = mybir.dt.float32
BF16 = mybir.dt.bfloat16
I32 = mybir.dt.int32
AF = mybir.ActivationFunctionType
ALU = mybir.AluOpType

D = 1024
NH = 8
FF = 2816
NFC = 22
EPS = 1e-6
SCALE = 96 ** -0.5
NCOLW = 2240
PI = float(np.pi)
F_ROPECACHE = True
import os
SHIFT_M = 0
PRIO_RANK = 0
NOBARRIER = 1
RANK_ENGS = ('pe',)

C_GPRE1 = 0
C_GPRE2 = 8
C_QN = 16
C_KVN = 18
C_SCW = 19
C_SSW = 25
C_SSB = 49
C_DSK = 55
C_SSN = 57
C_FW = 59
C_FB = 191
NCOLS = 235


class _Rec:
    def __getattr__(self, name):
        def f(*args, **kwargs):
            return (name, args, kwargs)
        return f


REC = _Rec()


def _run(fn, e):
    if callable(fn):
        return fn(e)
    name, args, kwargs = fn
    return getattr(e, name)(*args, **kwargs)


class Src:
    def __init__(self, sem, step):
        self.sem, self.step, self.count = sem, step, 0


class Buf:
    def __init__(self, name):
        self.name, self.writer, self.readers = name, None, []


class Eng:
    def __init__(self, name, src, inorder=False):
        self.name, self.src, self.q, self.waited, self.inorder = name, src, [], {}, inorder


class Tile:
    def __init__(self, t, b):
        self.t, self.b = t, b

    def __getitem__(self, k):
        return self.t[k]


class Op:
    __slots__ = ("eng", "insts", "reads", "writes", "idx", "preds", "succs", "cost", "is_dma", "lat",
                 "npred", "ready", "finish", "src", "val", "prev_val", "seg", "key", "rank")

    def __init__(self, eng):
        self.eng, self.insts, self.reads, self.writes = eng, [], [], []
        self.preds, self.succs = set(), []
        self.cost, self.is_dma, self.lat = 0.0, False, 0.0
        self.ready, self.finish = 0.0, 0.0
        self.src = self.val = self.prev_val = None


def _free_size(ap):
    n = 1
    for d in ap.shape[1:]:
        n *= int(d)
    return n


def _est_cost(eng, fn):
    name, args, kwargs = fn
    try:
        if name == "matmul":
            rhs = args[2]
            n = _free_size(rhs)
            return (max(28.0, n / 2.4) + 6.0) * (4.0 if rhs.dtype == F32 else 1.0)
        if name == "transpose":
            return 75.0
        if name == "dma_start":
            return 120.0
        out = args[0] if args else kwargs.get("out")
        n = _free_size(out)
        if eng.name == "act":
            return 150.0 + 0.85 * n
        if name == "reciprocal":
            return 160.0 + 2.6 * n
        if name == "memset":
            return 100.0 + 0.3 * n
        if eng.name == "pool":
            return 250.0 + 2.1 * n
        return 150.0 + 1.04 * n
    except Exception:
        return 300.0


class Prog:
    LOOK_W = 24
    LOOK_IDX = 6000

    def __init__(self, nc, es, n_dma_sems=24):
        self.nc, self.es = nc, es
        def sem(n):
            return es.enter_context(nc.semaphore(n))
        self.pe = Eng("pe", Src(sem("s_pe"), 1), inorder=True)
        self.act = Eng("act", Src(sem("s_act"), 1))
        self.dve = Eng("dve", Src(sem("s_dve"), 1))
        self.pool = Eng("pool", Src(sem("s_pool"), 1))
        self.sp = Eng("sp", Src(sem("s_sp"), 1))
        self.engs = [self.pe, self.act, self.dve, self.pool, self.sp]
        self.dma_srcs = [Src(sem("s_dma%d" % i), 16) for i in range(n_dma_sems)]
        self.dma_rr = {"sp": 0, "pool": 0}
        self.dma_part = {"sp": self.dma_srcs[:n_dma_sems - 8], "pool": self.dma_srcs[n_dma_sems - 8:]}
        self.ops = []
        self.cur = {}
        self.seg_start = [0]
        self.ninst = 0
        self.shift = 0

    def tile(self, name, shape, dt):
        t = self.es.enter_context(self.nc.sbuf_tensor("sb_" + name, shape, dt))
        return Tile(t, Buf(name))

    def emit(self, eng, fn, reads=(), writes=(), sig=True, dma_bytes=None):
        op = self.cur.get(eng.name)
        if op is None:
            op = Op(eng)
            self.cur[eng.name] = op
        op.insts.append(fn)
        op.reads.extend(reads)
        op.writes.extend(writes)
        op.cost += _est_cost(eng, fn)
        self.ninst += 1
        if dma_bytes is not None:
            op.is_dma = True
            op.lat = float(dma_bytes)
        if sig:
            self.cur[eng.name] = None
            self._close(op)

    def _close(self, op):
        seg0 = self.seg_start[-1]
        op.idx = len(self.ops)
        op.key = op.idx + self.shift
        op.seg = len(self.seg_start) - 1
        preds = set()
        for b in op.reads:
            if b.writer is not None:
                preds.add(b.writer)
        for b in op.writes:
            if b.writer is not None:
                preds.add(b.writer)
            preds.update(b.readers)
        preds.discard(op)
        op.preds = {p for p in preds if p.idx >= seg0}
        wset = set(id(b) for b in op.writes)
        for b in op.writes:
            b.writer = op
            b.readers = []
        for b in op.reads:
            if id(b) not in wset:
                b.readers.append(op)
        self.ops.append(op)

    def dma(self, eng, out, in_, reads=(), writes=()):
        nbytes = 1
        for d in out.shape:
            nbytes *= int(d)
        nbytes *= 2 if out.dtype == BF16 else 4
        nb2 = 1
        for d in in_.shape:
            nb2 *= int(d)
        nb2 *= 2 if in_.dtype == BF16 else 4
        self.emit(eng, REC.dma_start(out=out, in_=in_), reads, writes, dma_bytes=max(nbytes, nb2))

    def barrier(self):
        if NOBARRIER:
            return
        assert all(v is None for v in self.cur.values())
        if len(self.ops) > self.seg_start[-1]:
            self.seg_start.append(len(self.ops))

    def _schedule(self, ops):
        import bisect
        for op in ops:
            op.succs = []
            op.ready = 0.0
        for op in ops:
            op.npred = len(op.preds)
            for p in op.preds:
                p.succs.append(op)
        if RANK_ENGS:
            for op in reversed(ops):
                m = 0.0
                for sc in op.succs:
                    if sc.rank > m:
                        m = sc.rank
                op.rank = m + op.cost + (op.lat / 170.0 + 2000.0 if op.is_dma else 0.0)
                if op.eng.name in RANK_ENGS:
                    op.key = -op.rank
        avail = {e.name: [] for e in self.engs}
        t_free = {e.name: 0.0 for e in self.engs}
        order = {e.name: [] for e in self.engs}
        for op in ops:
            if op.npred == 0:
                avail[op.eng.name].append((op.key, op.idx, op))
        for e in self.engs:
            avail[e.name].sort(key=lambda x: (x[0], x[1]))
        scheduled = 0
        dma_free = 0.0
        n = len(ops)
        base = ops[0].idx
        done = [False] * n
        lo = 0
        while scheduled < n:
            while lo < n and done[lo]:
                lo += 1
            lim = base + lo + self.LOOK_IDX
            best = None
            for e in self.engs:
                lst = avail[e.name]
                if not lst:
                    continue
                tf = t_free[e.name]
                for (k_, idx, op) in lst[:self.LOOK_W]:
                    if idx > lim and best is not None:
                        continue
                    st = op.ready if op.ready > tf else tf
                    key = (st, k_, idx)
                    if best is None or key < best[0]:
                        best = (key, e, op)
            (st, k_, idx), e, op = best
            avail[e.name].remove((k_, idx, op))
            if op.is_dma:
                t_free[e.name] = st + op.cost
                xs = max(st + op.cost, dma_free)
                dma_free = xs + op.lat / 170.0
                op.finish = dma_free + 2000.0
            else:
                op.finish = st + op.cost
                t_free[e.name] = op.finish
            order[e.name].append(op)
            done[op.idx - base] = True
            scheduled += 1
            for sc in op.succs:
                if sc.ready < op.finish:
                    sc.ready = op.finish
                sc.npred -= 1
                if sc.npred == 0:
                    bisect.insort(avail[sc.eng.name], (sc.key, sc.idx, sc))
        return order, max(op.finish for op in ops)

    def _wait(self, eng, src, val):
        if val <= 0 or eng.waited.get(src, 0) >= val:
            return
        eng.waited[src] = val
        eng.q.append(REC.wait_ge(src.sem, val))

    def _barrier_waits(self):
        srcs = [e.src for e in self.engs] + self.dma_srcs
        for e in self.engs:
            for sr in srcs:
                if sr is not e.src:
                    self._wait(e, sr, sr.count)

    def finalize(self):
        assert all(v is None for v in self.cur.values())
        bounds = self.seg_start + [len(self.ops)]
        total = 0.0
        for si in range(len(bounds) - 1):
            ops = self.ops[bounds[si]:bounds[si + 1]]
            if not ops:
                continue
            if si > 0:
                self._barrier_waits()
            order, span = self._schedule(ops)
            total += span
            sb = {}
            for op in ops:
                sb[op.eng.name] = sb.get(op.eng.name, 0.0) + op.cost
            print("seg", si, "span us %.1f" % (span / 1e3), {k: round(v / 1e3, 1) for k, v in sb.items()}, flush=True)
            for e in self.engs:
                for op in order[e.name]:
                    if op.is_dma:
                        part = self.dma_part[e.name]
                        src = part[self.dma_rr[e.name] % len(part)]
                        self.dma_rr[e.name] += 1
                        op.prev_val = src.count
                    else:
                        src = e.src
                    src.count += src.step
                    op.src, op.val = src, src.count
            for e in self.engs:
                for op in order[e.name]:
                    need = {}
                    for p in op.preds:
                        if e.inorder and p.src is e.src:
                            continue
                        if need.get(p.src, 0) < p.val:
                            need[p.src] = p.val
                    for sr, v in need.items():
                        self._wait(e, sr, v)
                    if op.is_dma:
                        self._wait(e, op.src, op.prev_val)
                    last = len(op.insts) - 1
                    for i, fn in enumerate(op.insts):
                        if i == last:
                            e.q.append(("__sig__", fn, op.src.sem, op.src.step))
                        else:
                            e.q.append(fn)
        for e in (self.sp, self.pool):
            for s_ in self.dma_srcs:
                self._wait(e, s_, s_.count)
        busy = {}
        for op in self.ops:
            busy[op.eng.name] = busy.get(op.eng.name, 0.0) + op.cost
        print("busy ms:", {k: round(v / 1e6, 3) for k, v in busy.items()}, flush=True)
        print("ops:", len(self.ops), "insts:", self.ninst, "est span ms: %.3f" % (total / 1e6),
              {e.name: len(e.q) for e in self.engs}, flush=True)


def _replay(q, e):
    for item in q:
        if item[0] == "__sig__":
            _, fn, sem, step = item
            _run(fn, e).then_inc(sem, step)
        else:
            _run(item, e)


def build(S, NL, dbg=None):
    NB = S // 512
    NKB = S // 128
    nc = bass.Bass("TRN2", target_bir_lowering=False)

    def din(name, shape, dt=F32):
        return nc.dram_tensor(name, shape, dt, kind="ExternalInput").ap()

    x_d = din("x", [S, D])
    pos_d = din("posrep", [32, S], I32)
    rc_d = din("ropec", [32, 2])
    cst_d = din("consts", [128, 512])
    cols_d = din("cols", [128, NL, NCOLS])
    rows_d = din("rows", [NL, 128, 2056])
    w_in_d = din("w_in_g", [NL, 128, 8 * NCOLW])
    w_dt_d = din("w_dt", [NL, D, 4])
    w_q_d = din("w_q", [NL, 256, 768])
    w_qsw_d = din("w_qsw", [NL, 256, 256])
    w_kn_d = din("w_kn", [NL, 128, 512])
    w_v_d = din("w_v", [NL, 128, 512])
    w_out_d = din("w_out", [NL, D, D])
    w_up_d = din("w_up_g", [NL, 44, 128, 1024])
    w_down_d = din("w_down", [NL, FF, D])
    y_d = nc.dram_tensor("y", [S, D], F32, kind="ExternalOutput").ap()
    xa_d = nc.dram_tensor("xa", [S, D], F32, kind="Internal").ap()
    xb_d = [nc.dram_tensor("xb%d" % i, [S, D], F32, kind="Internal").ap() for i in range(2)]
    kd_d = nc.dram_tensor("kd", [NH, 96, S], BF16, kind="Internal").ap()
    vd_d = nc.dram_tensor("vd", [NH, 128, NKB, 65], BF16, kind="Internal").ap()
    tab_d = nc.dram_tensor("tabd", [NB, 32, 1024], F32, kind="Internal").ap() if F_ROPECACHE else None
    dbg_d = {}
    if dbg:
        for name, shape in dbg.items():
            dbg_d[name] = nc.dram_tensor("dbg_" + name, shape, F32, kind="ExternalOutput").ap()

    es = ExitStack()
    with es:
        P = Prog(nc, es)
        pe, act, dve, pool, sp = P.pe, P.act, P.dve, P.pool, P.sp
        T = P.tile
        b_xa, b_kd = Buf("xa"), Buf("kd")
        b_vd = Buf("vd")
        b_tab = [Buf("tab%d" % i) for i in range(NB)]
        b_xb = [Buf("xb0"), Buf("xb1")]
        b_y = Buf("y")
        psb = []
        for i in range(7):
            t = es.enter_context(nc.psum_tensor("ps%d" % i, [128, 512], F32))
            psb.append(Tile(t, Buf("ps%d" % i)))
        pst = Tile(es.enter_context(nc.psum_tensor("pst", [128, 1024], BF16)), Buf("pst"))
        pools = {"mm": [0, 1, 2], "acc": [3, 4], "aux": [5, 6]}
        prr = {"mm": 0, "acc": 0, "aux": 0}

        pst_list = [pst]

        def set_pools(cfg, psts):
            pools.clear()
            pools.update(cfg)
            pst_list[:] = psts

        def PS(pool_name):
            lst = pools[pool_name]
            i = lst[prr[pool_name] % len(lst)]
            prr[pool_name] += 1
            return psb[i]

        cst32 = T("cst32", [128, 256], F32)
        cstb = T("cstb", [128, 512], BF16)
        cols = T("cols", [128, NCOLS], F32)
        ropec = T("ropec", [32, 2], F32)
        epsc = T("epsc", [128, 1], F32)
        P.dma(sp, cst32[:], cst_d[:, 128:384], writes=[cst32.b])
        P.dma(pool, cstb[:], cst_d, writes=[cstb.b])
        P.dma(sp, ropec[:], rc_d, writes=[ropec.b])
        P.emit(dve, REC.memset(epsc[:], EPS), [], [epsc.b])
        ident_b = cstb[:, 0:128]
        tri_b = cstb[:, 128:256]
        ones_b = cstb[:, 256:384]
        mneg_b = cstb[:, 384:512]
        tri_f = cst32[:, 0:128]
        ones_f = cst32[:, 128:256]

        xin = [T("xin%d" % i, [128, D], F32) for i in range(2)]
        mixs = [T("mix%d" % i, [128, D], F32) for i in range(2)]
        xr = T("xr", [128, D], F32)
        junkP = T("junkP", [128, 512], BF16)
        smP = T("smP", [128, 8], F32)
        hbs = [T("hb0", [128, D], BF16)]
        hT = T("hT", [128, 8, 512], BF16)
        junk = T("junk", [128, 512], BF16)
        sms = [T("sm%d" % i, [128, 16], F32) for i in range(2)]
        rows = T("rows", [128, 1032], F32)
        tmpA = T("tmpA", [128, 512], F32)
        WREG = 135168
        wreg = T("wreg", [128, WREG // 2], BF16)
        REG2 = 35200
        reg2 = T("reg2", [128, REG2 // 2], BF16)

        def col(l, c, n=1):
            return cols[:, c:c + n]

        def rstd_from_ss(ss_ap, ss_bufs, n, out_ap, out_buf):
            P.emit(act, REC.activation(out_ap, ss_ap, AF.Ln, scale=1.0 / n, bias=epsc[:, 0:1]),
                   list(ss_bufs) + [epsc.b], [out_buf])
            P.emit(act, REC.activation(out_ap, out_ap, AF.Exp, scale=-0.5), [out_buf], [out_buf])

        def dump(name, ap, bufs, npart=128, idx=None):
            if not dbg or name not in dbg:
                return
            dst = dbg_d[name] if idx is None else dbg_d[name][idx]
            P.emit(dve, REC.tensor_copy(tmpA[0:npart, :], ap), bufs, [tmpA.b])
            P.dma(sp, dst[0:npart, :], tmpA[0:npart, :], [tmpA.b], [])

        def norm_transpose(src_d, src_buf, row0, l, gcol):
            for tt in range(4):
                xt = xin[tt % 2]
                sm = sms[0]
                hb = hbs[0]
                pt_ = pst_list[tt % len(pst_list)]
                r0 = row0 + tt * 128
                P.dma(sp, xt[:], src_d[r0:r0 + 128, :], [src_buf], [xt.b])
                P.emit(act, REC.activation(junk[:, :], xt[:, 0:512], AF.Square, accum_out=sm[:, 0:1]),
                       [xt.b], [junk.b, sm.b])
                P.emit(act, REC.activation(junk[:, :], xt[:, 512:1024], AF.Square, accum_out=sm[:, 1:2]),
                       [xt.b], [junk.b, sm.b])
                P.emit(dve, REC.tensor_tensor(sm[:, 2:3], sm[:, 0:1], sm[:, 1:2], ALU.add), [sm.b], [sm.b])
                rstd_from_ss(sm[:, 2:3], [sm.b], D, sm[:, 3:4], sm.b)
                P.emit(dve, REC.tensor_scalar(hb[:], xt[:], sm[:, 3:4], None, ALU.mult),
                       [xt.b, sm.b], [hb.b])
                for kc in range(8):
                    P.emit(pe, REC.transpose(pt_[:, kc * 128:(kc + 1) * 128], hb[:, kc * 128:(kc + 1) * 128], ident_b),
                           [hb.b, cstb.b], [pt_.b], sig=(kc == 7))
                P.emit(dve, REC.tensor_tensor(
                    hT[:, :, tt * 128:(tt + 1) * 128],
                    pt_[:, :].rearrange("p (k t) -> p k t", k=8),
                    col(l, gcol, 8).unsqueeze(2).to_broadcast([128, 8, 128]), ALU.mult),
                    [pt_.b, cols.b], [hT.b])

        def post_norm_residual(ps_list, src_d, src_buf, dst_d, dst_buf, r0, k, nocopy=False):
            xt = xr
            mx = mixs[k % 2]
            P.dma(sp, xt[:], src_d[r0:r0 + 128, :], [src_buf], [xt.b])
            for half in range(2):
                ps = ps_list[half]
                if nocopy:
                    P.emit(act, REC.activation(junkP[:, :], ps[:, :], AF.Square, accum_out=smP[:, 4 + half:5 + half]),
                           [ps.b], [junkP.b, smP.b])
                else:
                    P.emit(act, REC.activation(mx[:, half * 512:(half + 1) * 512], ps[:, :], AF.Copy), [ps.b], [mx.b])
                    P.emit(act, REC.activation(junkP[:, :], mx[:, half * 512:(half + 1) * 512], AF.Square,
                                               accum_out=smP[:, 4 + half:5 + half]), [mx.b], [junkP.b, smP.b])
            P.emit(dve, REC.tensor_tensor(smP[:, 6:7], smP[:, 4:5], smP[:, 5:6], ALU.add), [smP.b], [smP.b])
            rstd_from_ss(smP[:, 6:7], [smP.b], D, smP[:, 7:8], smP.b)
            if nocopy:
                for half in range(2):
                    ps = ps_list[half]
                    P.emit(dve, REC.scalar_tensor_tensor(mx[:, half * 512:(half + 1) * 512], ps[:, :], smP[:, 7:8], rows[:, half * 512:(half + 1) * 512], ALU.mult, ALU.mult),
                           [ps.b, smP.b, rows.b], [mx.b])
            else:
                P.emit(dve, REC.scalar_tensor_tensor(mx[:], mx[:], smP[:, 7:8], rows[:, 0:1024], ALU.mult, ALU.mult),
                       [mx.b, smP.b, rows.b], [mx.b])
            P.emit(dve, REC.tensor_tensor(mx[:], mx[:], xt[:], ALU.add), [mx.b, xt.b], [mx.b])
            P.dma(sp, dst_d[r0:r0 + 128, :], mx[:], [mx.b], [dst_buf])

        REG = {"wreg": [], "reg2": []}

        def reg_buf(region, sb_, eb_, buf):
            for (s0, e0, ob) in REG[region]:
                if s0 < eb_ and sb_ < e0 and ob is not buf:
                    if ob.writer is not None and ob.writer not in buf.readers:
                        buf.readers.append(ob.writer)
                    for r_ in ob.readers:
                        if r_ not in buf.readers:
                            buf.readers.append(r_)
            REG[region].append((sb_, eb_, buf))

        for l in range(NL):
            src_d, src_buf = (x_d, Buf("xsrc")) if l == 0 else (xb_d[(l - 1) % 2], b_xb[(l - 1) % 2])
            dst_d, dst_buf = (y_d, b_y) if l == NL - 1 else (xb_d[l % 2], b_xb[l % 2])

            o = 0
            def carve(n_el):
                nonlocal o
                a = wreg.t[:, o:o + n_el]
                o += n_el
                return a
            w_in_sb = carve(8 * NCOLW)
            w_dt_sb = carve(8 * 4).rearrange("p (k n) -> p k n", k=8)
            w_q_sb = carve(2 * 768).rearrange("p (k n) -> p k n", k=2)
            w_qsw_sb = carve(2 * 256).rearrange("p (k n) -> p k n", k=2)
            w_kn_sb = carve(512)
            w_v_sb = carve(512)
            def carve_t(shape, dt):
                nonlocal o
                n = int(np.prod(shape))
                if dt == F32:
                    o += (o % 2)
                    a = wreg.t[:, o:o + 2 * n].bitcast(F32)
                    o += 2 * n
                else:
                    a = wreg.t[:, o:o + n]
                    o += n
                carve_t.last = (2 * (o - (2 * n if dt == F32 else n)), 2 * o)
                if len(shape) == 2:
                    return a.rearrange("p (a b) -> p a b", a=shape[0])
                if len(shape) == 3:
                    return a.rearrange("p (a b c) -> p a b c", a=shape[0], b=shape[1])
                return a
            bW = Buf("wM%d" % l)
            GROUPS = [(0, 128), (128, 128), (256, 128), (384, 32), (416, 32)] + [(448 + g * 128, 128) for g in range(14)]
            bWg = {}
            bWo = Buf("wo%d" % l)
            P.barrier()
            reg_buf("wreg", 2 * 8 * NCOLW, 2 * o, bW)
            set_pools({"mm": [0, 1, 2], "acc": [3, 4], "aux": [5, 6]}, [pst])
            for (dst, srcap) in [
                (w_dt_sb, w_dt_d[l].rearrange("(k p) n -> p k n", p=128)),
                (w_q_sb, w_q_d[l].rearrange("(k p) n -> p k n", p=128)),
                (w_qsw_sb, w_qsw_d[l].rearrange("(k p) n -> p k n", p=128)),
                (w_kn_sb, w_kn_d[l]),
                (w_v_sb, w_v_d[l]),
            ]:
                P.dma(pool, dst, srcap, [], [bW])
            for (c0g, Mg) in GROUPS:
                bWg[c0g] = Buf("wg%d_%d" % (l, c0g))
                reg_buf("wreg", 2 * 8 * c0g, 2 * 8 * (c0g + Mg), bWg[c0g])
                P.dma(pool, w_in_sb[:, 8 * c0g:8 * (c0g + Mg)], w_in_d[l, :, 8 * c0g:8 * (c0g + Mg)], [], [bWg[c0g]])
            P.dma(sp, cols[:, :], cols_d[:, l, :], [], [cols.b])
            P.dma(sp, rows[:, 0:1024], rows_d[l, :, 0:1024], [], [rows.b])
            P.dma(sp, rows[:, 1024:1032], rows_d[l, :, 2048:2056], [], [rows.b])

            def MT(name, shape, dt):
                t_ = Tile(carve_t(shape, dt), Buf(name))
                reg_buf("wreg", carve_t.last[0], carve_t.last[1], t_.b)
                return t_
            cq_sb = MT("cq", [3, 512], BF16)
            cq_sq = MT("cqsq", [3, 512], BF16)
            cqns = [MT("cqn%d" % i, [3, 512], BF16) for i in range(2)]
            conv_sb = MT("conv", [4, 512], F32)
            szs = MT("szs", [2, 512], F32)
            ubs = [MT("ub%d" % i, [1, 515], F32) for i in range(2)]
            uh = MT("uh", [6, 3], F32)
            vbuf = MT("vbuf", [2, 514], F32)
            xs_f = MT("xsf", [2, 512], F32)
            xs_b = MT("xsb", [2, 512], BF16)
            Bt = MT("Bt", [2, 512], BF16)
            Ct = MT("Ct", [2, 512], BF16)
            o_wout = o
            Kcur = MT("Kcur", [8, 512], BF16)
            kbuf = [MT("kbuf0", [1, 4096], BF16)]
            kbq = [Buf("kbq%d" % i) for i in range(4)]
            for i_ in range(4):
                reg_buf("wreg", carve_t.last[0] + 2048 * i_, carve_t.last[0] + 2048 * (i_ + 1), kbq[i_])
            reg_buf("wreg", 2 * o_wout, 2 * (o_wout + 8 * D), bWo)
            w_out_sb = wreg.t[:, o_wout:o_wout + 8 * D].rearrange("p (k n) -> p k n", k=8)
            assert o - o_wout == 8 * D
            Qh = [MT("Qh%d" % i, [1, 512], BF16) for i in range(2)]
            Pt = [MT("Pt%d" % i, [1, 512], BF16) for i in range(3)]
            tabs = MT("tabs", [2, 512], F32)
            rtmp = MT("rtmp", [2, 512], F32)
            rint = Tile(carve_t([1, 512], F32).bitcast(I32), Buf("rint"))
            krot = MT("krot", [1, 512], F32)
            rd = Tile(krot.t, Buf("rd"))
            osb = MT("osb", [1, 512], F32)
            dtT = MT("dtT", [4, 4], F32)
            adt = MT("adt", [4, 4], F32)
            arow = MT("arow", [1, 4], F32)
            state = MT("state", [4, 64], F32)
            prevp = MT("prevp", [4, 128], BF16)
            xdtp = MT("xdtp", [4, 128], BF16)
            xdtd = MT("xdtd", [4, 64], BF16)
            Btok = MT("Btok", [2, 128], BF16)
            adtri = MT("adtri", [4, 128], F32)
            dec = MT("dec", [4, 128], F32)
            drow = MT("drow", [4, 128], F32)
            Cs = MT("Cs", [4, 128], BF16)
            scT = MT("scT", [4, 128], BF16)
            s4 = MT("s4", [8, 4], F32)
            assert o * 2 <= WREG, o * 2
            o3 = 0
            def R2T(name, shape, dt):
                nonlocal o3
                n = int(np.prod(shape))
                if dt == F32:
                    a_ = reg2.t[:, o3:o3 + 2 * n].bitcast(F32)
                    o3 += 2 * n
                else:
                    a_ = reg2.t[:, o3:o3 + n]
                    o3 += n
                if len(shape) == 2:
                    a_ = a_.rearrange("p (a b) -> p a b", a=shape[0])
                elif len(shape) == 3:
                    a_ = a_.rearrange("p (a b c) -> p a b c", a=shape[0], b=shape[1])
                t_ = Tile(a_, Buf(name))
                R2T.last = (2 * (o3 - (2 * n if dt == F32 else n)), 2 * o3)
                reg_buf("reg2", R2T.last[0], R2T.last[1], t_.b)
                return t_
            gat = R2T("gat", [2, 512], F32)
            yT = R2T("yT", [8, 512], BF16)
            gsq = R2T("gsq", [2, 512], BF16)
            Vcur = R2T("Vcur", [8, 4, 65], BF16)
            vbufs = [R2T("vbuf%d" % i, [28, 65], BF16) for i in range(2)]
            kbuf.append(R2T("kbuf1", [1, 4096], BF16))
            kbqs = [kbq, [Buf("kbq1_%d" % i) for i in range(4)]]
            for i_ in range(4):
                reg_buf("reg2", R2T.last[0] + 2048 * i_, R2T.last[0] + 2048 * (i_ + 1), kbqs[1][i_])
            assert o3 * 2 <= REG2, o3 * 2

            P.emit(dve, REC.memset(Vcur[:, :, :, 64:65], 1.0), [], [Vcur.b])
            P.emit(dve, REC.memset(vbuf[:, :, 0:2], 0.0), [], [vbuf.b])
            P.emit(dve, REC.memset(uh[:, :, :], 0.0), [], [uh.b])
            P.emit(dve, REC.memset(state[:], 0.0), [], [state.b])
            P.emit(dve, REC.memset(prevp[:], 0.0), [], [prevp.b])
            P.emit(dve, REC.memset(xdtp[:], 0.0), [], [xdtp.b])
            P.emit(act, REC.activation(arow[:, 0, :], rows[:, 1028:1032], AF.Exp), [rows.b], [arow.b])
            P.emit(dve, REC.tensor_scalar(arow[:, 0, :], arow[:, 0, :], -1.0, None, ALU.mult), [arow.b], [arow.b])

            for blk in range(NB):
                t0 = blk * 512
                cqn = cqns[blk % 2]
                P.shift = -SHIFT_M
                if l == 0 or not F_ROPECACHE:
                    P.dma(sp, rint[0:32, 0, :], pos_d[:, t0:t0 + 512], [], [rint.b])
                    P.emit(dve, REC.tensor_copy(rtmp[0:32, 0, :], rint[0:32, 0, :]), [rint.b], [rtmp.b])
                    P.emit(dve, REC.tensor_scalar(rtmp[0:32, 0, :], rtmp[0:32, 0, :], ropec[:, 0:1], None, ALU.mult),
                           [rtmp.b, ropec.b], [rtmp.b])
                    for ti, phase in ((0, PI / 2), (1, 0.0)):
                        P.emit(dve, REC.tensor_scalar(rtmp[0:32, 1, :], rtmp[0:32, 0, :], 1.0 / (2 * PI), phase / (2 * PI) + 0.5, ALU.mult, ALU.add),
                               [rtmp.b], [rtmp.b])
                        P.emit(dve, REC.tensor_copy(rint[0:32, 0, :], rtmp[0:32, 1, :]), [rtmp.b], [rint.b])
                        P.emit(dve, REC.tensor_copy(rtmp[0:32, 1, :], rint[0:32, 0, :]), [rint.b], [rtmp.b])
                        P.emit(dve, REC.tensor_scalar(rtmp[0:32, 1, :], rtmp[0:32, 1, :], -2 * PI, None, ALU.mult), [rtmp.b], [rtmp.b])
                        P.emit(dve, REC.scalar_tensor_tensor(tabs[0:32, ti, :], rtmp[0:32, 0, :], phase, rtmp[0:32, 1, :], ALU.add, ALU.add),
                               [rtmp.b], [tabs.b])
                        P.emit(dve, REC.tensor_scalar(rtmp[0:32, 1, :], tabs[0:32, ti, :], -PI, 2 * PI, ALU.is_lt, ALU.mult), [tabs.b], [rtmp.b])
                        P.emit(dve, REC.tensor_tensor(tabs[0:32, ti, :], tabs[0:32, ti, :], rtmp[0:32, 1, :], ALU.add), [tabs.b, rtmp.b], [tabs.b])
                        P.emit(dve, REC.tensor_scalar(rtmp[0:32, 1, :], tabs[0:32, ti, :], PI, -2 * PI, ALU.is_gt, ALU.mult), [tabs.b], [rtmp.b])
                        P.emit(dve, REC.tensor_tensor(tabs[0:32, ti, :], tabs[0:32, ti, :], rtmp[0:32, 1, :], ALU.add), [tabs.b, rtmp.b], [tabs.b])
                        if ti == 0:
                            P.emit(act, REC.activation(tabs[0:32, 0, :], tabs[0:32, 0, :], AF.Sin), [tabs.b], [tabs.b])
                        else:
                            P.emit(act, REC.activation(tabs[0:32, 1, :], tabs[0:32, 1, :], AF.Sin, scale=ropec[:, 1:2]), [tabs.b, ropec.b], [tabs.b])
                    if F_ROPECACHE:
                        P.dma(sp, tab_d[blk], tabs[0:32, :, :].rearrange("p a b -> p (a b)"), [tabs.b], [b_tab[blk]])
                else:
                    P.dma(sp, tabs[0:32, :, :].rearrange("p a b -> p (a b)"), tab_d[blk], [b_tab[blk]], [tabs.b])
                Ctab = tabs[0:32, 0, :]
                Stab = tabs[0:32, 1, :]

                norm_transpose(src_d, src_buf, t0, l, C_GPRE1)
                if l == 0 and blk == NB - 1:
                    dump("hT0", hT[:, 0, :], [hT.b])

                def inproj(c0, M):
                    ps = PS("mm")
                    for kc in range(8):
                        P.emit(pe, REC.matmul(ps[0:M, :], w_in_sb[:, 8 * c0 + kc * M:8 * c0 + (kc + 1) * M], hT[:, kc, :], start=(kc == 0), stop=(kc == 7)),
                               [bWg[c0], hT.b], [ps.b], sig=(kc == 7))
                    return ps
                for g in range(3):
                    ps = inproj(g * 128, 128)
                    P.emit(act, REC.activation(cq_sb[:, g, :], ps[:, :], AF.Copy), [ps.b], [cq_sb.b])
                    P.emit(act, REC.activation(cq_sq[:, g, :], ps[:, :], AF.Square), [ps.b], [cq_sq.b])
                ps_kr = inproj(384, 32)
                P.emit(dve, REC.tensor_tensor(rtmp[0:32, 0, :], ps_kr[0:32, :], Ctab, ALU.mult), [ps_kr.b, tabs.b], [rtmp.b])
                ps_ks = inproj(416, 32)
                P.emit(dve, REC.tensor_tensor(rtmp[0:32, 1, :], ps_ks[0:32, :], Stab, ALU.mult), [ps_ks.b, tabs.b], [rtmp.b])
                P.emit(dve, REC.tensor_tensor(krot[0:32, 0, :], rtmp[0:32, 0, :], rtmp[0:32, 1, :], ALU.add), [rtmp.b], [krot.b])
                P.emit(dve, REC.tensor_copy(Kcur[64:96, :, :], krot[0:32, 0:1, :].to_broadcast([32, 8, 512])), [krot.b], [Kcur.b])
                for g in range(4):
                    ps = inproj(448 + g * 128, 128)
                    P.emit(act, REC.activation(conv_sb[:, g, :], ps[:, :], AF.Copy), [ps.b], [conv_sb.b])
                for c in range(2):
                    ps = inproj(448 + (4 + c) * 128, 128)
                    P.emit(dve, REC.tensor_tensor(vbuf[:, c, 2:514], conv_sb[:, 2 + c, :], ps[:, :], ALU.mult), [conv_sb.b, ps.b], [vbuf.b])
                    P.emit(dve, REC.tensor_scalar(tmpA[:, :], vbuf[:, c, 0:512], col(l, C_SCW + c * 3 + 0), None, ALU.mult), [vbuf.b, cols.b], [tmpA.b])
                    P.emit(dve, REC.scalar_tensor_tensor(tmpA[:, :], vbuf[:, c, 1:513], col(l, C_SCW + c * 3 + 1), tmpA[:, :], ALU.mult, ALU.add), [vbuf.b, cols.b, tmpA.b], [tmpA.b])
                    P.emit(dve, REC.scalar_tensor_tensor(tmpA[:, :], vbuf[:, c, 2:514], col(l, C_SCW + c * 3 + 2), tmpA[:, :], ALU.mult, ALU.add), [vbuf.b, cols.b, tmpA.b], [tmpA.b])
                    P.emit(dve, REC.tensor_tensor(yT[:, 4 + c, :], tmpA[:, :], conv_sb[:, c, :], ALU.mult), [tmpA.b, conv_sb.b], [yT.b])
                    P.emit(dve, REC.tensor_copy(vbuf[:, c, 0:2], vbuf[:, c, 512:514]), [vbuf.b], [vbuf.b])
                for g in range(2):
                    ps = inproj(448 + 768 + g * 128, 128)
                    P.emit(act, REC.activation(szs[:, g, :], ps[:, :], AF.Silu), [ps.b], [szs.b])
                for c in range(6):
                    ps = inproj(448 + 1024 + c * 128, 128)
                    ub = ubs[c % 2]
                    P.emit(act, REC.activation(ub[:, 0, 3:515], ps[:, :], AF.Copy), [ps.b], [ub.b])
                    P.emit(dve, REC.tensor_copy(ub[:, 0, 0:3], uh[:, c, :]), [uh.b], [ub.b])
                    P.emit(dve, REC.tensor_scalar(tmpA[:, :], ub[:, 0, 0:512], col(l, C_SSW + c * 4 + 0), None, ALU.mult), [ub.b, cols.b], [tmpA.b])
                    for k in range(1, 4):
                        P.emit(dve, REC.scalar_tensor_tensor(tmpA[:, :], ub[:, 0, k:k + 512], col(l, C_SSW + c * 4 + k), tmpA[:, :], ALU.mult, ALU.add),
                               [ub.b, cols.b, tmpA.b], [tmpA.b])
                    if c < 2:
                        P.emit(act, REC.activation(xs_f[:, c, :], tmpA[:, :], AF.Silu, bias=col(l, C_SSB + c)), [tmpA.b, cols.b], [xs_f.b])
                        P.emit(dve, REC.tensor_copy(xs_b[:, c, :], xs_f[:, c, :]), [xs_f.b], [xs_b.b])
                    elif c < 4:
                        P.emit(act, REC.activation(Bt[:, c - 2, :], tmpA[:, :], AF.Silu, bias=col(l, C_SSB + c)), [tmpA.b, cols.b], [Bt.b])
                    else:
                        P.emit(act, REC.activation(Ct[:, c - 4, :], tmpA[:, :], AF.Silu, bias=col(l, C_SSB + c)), [tmpA.b, cols.b], [Ct.b])
                    P.emit(dve, REC.tensor_copy(uh[:, c, :], ub[:, 0, 512:515]), [ub.b], [uh.b])
                ps = PS("aux")
                for tt in range(4):
                    for kc in range(8):
                        P.emit(pe, REC.matmul(ps[:, tt * 4:tt * 4 + 4], hT[:, kc, tt * 128:(tt + 1) * 128], w_dt_sb[:, kc, :],
                                                                          start=(kc == 0), stop=(kc == 7)),
                               [bW, hT.b], [ps.b], sig=(kc == 7))
                dt4 = dtT[:, :, :]
                dtb = rows[:, 1024:1028].unsqueeze(1).to_broadcast([128, 4, 4])
                P.emit(dve, REC.tensor_tensor(dt4, ps[:, 0:16].rearrange("p (a b) -> p a b", a=4), dtb, ALU.add), [ps.b, rows.b], [dtT.b])
                P.emit(act, REC.activation(adt[:, :, :], dt4, AF.Abs), [dtT.b], [adt.b])
                P.emit(act, REC.activation(adt[:, :, :], adt[:, :, :], AF.Exp, scale=-1.0), [adt.b], [adt.b])
                P.emit(act, REC.activation(adt[:, :, :], adt[:, :, :], AF.Ln, bias=1.0), [adt.b], [adt.b])
                P.emit(dve, REC.scalar_tensor_tensor(dt4, dt4, 0.0, adt[:, :, :], ALU.max, ALU.add), [dtT.b, adt.b], [dtT.b])
                P.emit(dve, REC.tensor_tensor(adt[:, :, :], dt4, arow[:, 0:1, :].to_broadcast([128, 4, 4]), ALU.mult), [dtT.b, arow.b], [adt.b])

                for (chs, n, qc) in (((0, 1), 256, C_QN), ((2,), 128, C_KVN)):
                    ps = PS("aux")
                    for i, c in enumerate(chs):
                        P.emit(pe, REC.matmul(ps[:, :], ones_b, cq_sq[:, c, :], start=(i == 0), stop=(i == len(chs) - 1)),
                               [cstb.b, cq_sq.b], [ps.b], sig=(i == len(chs) - 1))
                    P.emit(act, REC.activation(tmpA[:, :], ps[:, :], AF.Ln, scale=1.0 / n, bias=epsc[:, 0:1]), [ps.b, epsc.b], [tmpA.b])
                    P.emit(act, REC.activation(tmpA[:, :], tmpA[:, :], AF.Exp, scale=-0.5), [tmpA.b], [tmpA.b])
                    for i, c in enumerate(chs):
                        P.emit(dve, REC.scalar_tensor_tensor(cqn[:, c, :], cq_sb[:, c, :], col(l, qc + i), tmpA[:, :], ALU.mult, ALU.mult),
                               [cq_sb.b, cols.b, tmpA.b], [cqn.b])
                ckvn = cqn[:, 2, :]
                if l == 0 and blk == NB - 1:
                    for c in range(3):
                        dump("cqn", cqn[:, c, :], [cqn.b], idx=c)
                    dump("dtT", dtT[:, :, :].rearrange("p a b -> p (a b)"), [dtT.b]) if False else None

                for hp in range(4):
                    ps = PS("mm")
                    P.emit(pe, REC.matmul(ps[:, :], w_kn_sb[:, hp * 128:(hp + 1) * 128], ckvn, start=True, stop=True), [bW, cqn.b], [ps.b])
                    P.emit(act, REC.activation(Kcur[0:64, 2 * hp, :], ps[0:64, :], AF.Copy), [ps.b], [Kcur.b])
                    P.emit(dve, REC.tensor_copy(Kcur[0:64, 2 * hp + 1, :], ps[64:128, :]), [ps.b], [Kcur.b])
                for tt in range(4):
                    ps = PS("mm")
                    P.emit(pe, REC.matmul(ps[:, :], cqn[:, 2, tt * 128:(tt + 1) * 128], w_v_sb, start=True, stop=True), [bW, cqn.b], [ps.b])
                    P.emit(act, REC.activation(Vcur[:, :, tt, 0:64], ps[:, :].rearrange("p (h d) -> p h d", h=8), AF.Copy), [ps.b], [Vcur.b])
                if blk < NB - 1:
                    P.dma(sp, kd_d.rearrange("h d s -> d h s")[:, :, t0:t0 + 512], Kcur[0:96, :, :], [Kcur.b], [b_kd])
                    P.dma(sp, vd_d.rearrange("h p j d -> p h (j d)")[:, :, blk * 260:(blk + 1) * 260],
                          Vcur[:, :, :, :].rearrange("p h j d -> p h (j d)"), [Vcur.b], [b_vd])

                P.shift = 0
                for h in range(NH):
                    q = Qh[h % 2]
                    psQ = PS("mm")
                    for c in range(2):
                        P.emit(pe, REC.matmul(psQ[0:96, :], w_q_sb[:, c, h * 96:(h + 1) * 96], cqn[:, c, :], start=(c == 0), stop=(c == 1)),
                               [bW, cqn.b], [psQ.b], sig=(c == 1))
                    psS = PS("mm")
                    for c in range(2):
                        P.emit(pe, REC.matmul(psS[0:32, :], w_qsw_sb[:, c, h * 32:(h + 1) * 32], cqn[:, c, :], start=(c == 0), stop=(c == 1)),
                               [bW, cqn.b], [psS.b], sig=(c == 1))
                    P.emit(act, REC.activation(q[0:64, 0, :], psQ[0:64, :], AF.Copy), [psQ.b], [q.b])
                    P.emit(dve, REC.tensor_tensor(rtmp[0:32, 0, :], psS[0:32, :], Stab, ALU.mult), [psS.b, tabs.b], [rtmp.b])
                    P.emit(dve, REC.tensor_tensor(rtmp[0:32, 1, :], psQ[64:96, :], Ctab, ALU.mult), [psQ.b, tabs.b], [rtmp.b])
                    P.emit(dve, REC.tensor_tensor(q[64:96, 0, :], rtmp[0:32, 0, :], rtmp[0:32, 1, :], ALU.add), [rtmp.b], [q.b])
                    kb = kbuf[h % 2]
                    kbq_h = kbqs[h % 2]
                    vb = vbufs[h % 2]
                    if l == 0 and blk == NB - 1 and h == 0:
                        dump("Q0", q[0:96, 0, :], [q.b], npart=96)
                        dump("K0", Kcur[0:96, 0, :], [Kcur.b], npart=96)
                    if blk > 0:
                        for qi in range(4):
                            lo_, hi_ = qi * 1024, min(t0, (qi + 1) * 1024)
                            if hi_ > lo_:
                                P.dma(sp, kb[0:96, 0, lo_:hi_], kd_d[h, :, lo_:hi_], [b_kd], [kbq_h[qi]])
                        P.dma(sp, vb[:, 0:4 * blk, :], vd_d[h, :, 0:4 * blk, :], [b_vd], [vb.b])
                    psO = PS("acc")
                    nfull = 4 * blk
                    for j in range(nfull + 4):
                        pss = PS("mm")
                        pt = Pt[j % 3]
                        if j < nfull:
                            c0 = 0
                            P.emit(pe, REC.matmul(pss[:, :], kb[0:96, 0, j * 128:(j + 1) * 128], q[0:96, 0, :], start=True, stop=True),
                                   [kbq_h[(j * 128) // 1024], q.b], [pss.b])
                        else:
                            jj = j - nfull
                            c0 = jj * 128
                            P.emit(pe, REC.matmul(pss[:, c0:512], Kcur[0:96, h, c0:c0 + 128], q[0:96, 0, c0:512], start=True, stop=False),
                                   [Kcur.b, q.b], [pss.b], sig=False)
                            P.emit(pe, REC.matmul(pss[:, c0:c0 + 128], ident_b, mneg_b, start=False, stop=True),
                                   [cstb.b], [pss.b])
                        P.emit(act, REC.activation(pt[:, 0, c0:512], pss[:, c0:512], AF.Exp, scale=SCALE), [pss.b], [pt.b])
                        if j < nfull:
                            vl, vlb = vb[:, j, :], vb.b
                        else:
                            vl, vlb = Vcur[:, h, j - nfull, :], Vcur.b
                        P.emit(pe, REC.matmul(psO[0:65, c0:512], vl, pt[:, 0, c0:512], start=(j == 0), stop=(j == nfull + 3)),
                               [vlb, pt.b], [psO.b])
                    P.emit(act, REC.activation(rd[64:65, 0, :], psO[64:65, :], AF.Ln), [psO.b], [rd.b])
                    P.emit(act, REC.activation(rd[64:65, 0, :], rd[64:65, 0, :], AF.Exp, scale=-1.0), [rd.b], [rd.b])
                    psB = PS("mm")
                    P.emit(pe, REC.matmul(psB[0:64, :], ones_f[64:65, 0:64], rd[64:65, 0, :], start=True, stop=True), [cst32.b, rd.b], [psB.b])
                    P.emit(act, REC.activation(osb[0:64, 0, :], psO[0:64, :], AF.Copy), [psO.b], [osb.b])
                    P.emit(dve, REC.tensor_tensor(yT[(h % 2) * 64:(h % 2) * 64 + 64, h // 2, :], osb[0:64, 0, :], psB[0:64, :], ALU.mult),
                           [osb.b, psB.b], [yT.b])

                P.dma(pool, w_out_sb, w_out_d[l].rearrange("(k p) n -> p k n", p=128), [], [Kcur.b, bWo] + kbq)
                psY = [PS("acc"), PS("acc")]
                for tt in range(4):
                    ts = slice(tt * 128, (tt + 1) * 128)
                    P.emit(dve, REC.tensor_copy(prevp[:, 0:4:2, 0:64], state[:, 0:4:2, :]), [state.b], [prevp.b])
                    P.emit(dve, REC.tensor_copy(prevp[:, 1:4:2, 64:128], state[:, 1:4:2, :]), [state.b], [prevp.b])
                    for c in range(2):
                        P.emit(pe, REC.transpose(pst[:, c * 128:(c + 1) * 128], xs_b[:, c, ts], ident_b), [xs_b.b, cstb.b], [pst.b], sig=False)
                    for g in range(2):
                        P.emit(pe, REC.transpose(pst[:, 256 + g * 128:256 + (g + 1) * 128], Bt[:, g, ts], ident_b), [Bt.b, cstb.b], [pst.b], sig=(g == 1))
                    xtok = pst[:, 0:256].rearrange("p (h d) -> p h d", h=4)
                    P.emit(dve, REC.tensor_tensor(xdtp[:, 0:4:2, 0:64], xtok[:, 0:4:2, :], dtT[:, tt, 0:4:2].unsqueeze(2).to_broadcast([128, 2, 64]), ALU.mult),
                           [pst.b, dtT.b], [xdtp.b])
                    P.emit(dve, REC.tensor_tensor(xdtp[:, 1:4:2, 64:128], xtok[:, 1:4:2, :], dtT[:, tt, 1:4:2].unsqueeze(2).to_broadcast([128, 2, 64]), ALU.mult),
                           [pst.b, dtT.b], [xdtp.b])
                    P.emit(act, REC.activation(Btok[:, :, :], pst[:, 256:512].rearrange("p (g n) -> p g n", g=2), AF.Copy), [pst.b], [Btok.b])
                    P.emit(dve, REC.tensor_tensor(adtri[:, :, :], tri_f.unsqueeze(1).to_broadcast([128, 4, 128]),
                                                                 adt[:, tt, :].unsqueeze(2).to_broadcast([128, 4, 128]), ALU.mult), [cst32.b, adt.b], [adtri.b])
                    psR = PS("aux")
                    P.emit(pe, REC.matmul(psR[:, :], ones_f, adtri[:, :, :].rearrange("p h l -> p (h l)"), start=True, stop=True), [cst32.b, adtri.b], [psR.b])
                    psA = PS("aux")
                    P.emit(pe, REC.matmul(psA[:, 0:4], tri_f, adt[:, tt, :], start=True, stop=True), [cst32.b, adt.b], [psA.b])
                    acol = s4[:, 0, :]
                    P.emit(dve, REC.tensor_copy(acol, psA[:, 0:4]), [psA.b], [s4.b])
                    psR3 = psR[:, :].rearrange("p (h l) -> p h l", h=4)
                    P.emit(dve, REC.tensor_tensor(dec[:, :, :], psR3, acol.unsqueeze(2).to_broadcast([128, 4, 128]), ALU.subtract), [psR.b, s4.b], [dec.b])
                    P.emit(dve, REC.tensor_scalar(dec[:, :, :], dec[:, :, :], 0.0, None, ALU.min), [dec.b], [dec.b])
                    P.emit(act, REC.activation(dec[:, :, :], dec[:, :, :], AF.Exp), [dec.b], [dec.b])
                    P.emit(dve, REC.tensor_tensor(dec[:, :, :], dec[:, :, :], tri_f.unsqueeze(1).to_broadcast([128, 4, 128]), ALU.mult), [dec.b, cst32.b], [dec.b])
                    P.emit(act, REC.activation(drow[:, :, :], psR3, AF.Exp), [psR.b], [drow.b])
                    P.emit(dve, REC.tensor_tensor(s4[:, 1, :], psR3[:, :, 127], acol, ALU.subtract), [psR.b, s4.b], [s4.b])
                    P.emit(act, REC.activation(s4[:, 2, :], s4[:, 1, :], AF.Exp), [s4.b], [s4.b])
                    P.emit(act, REC.activation(s4[:, 3, :], psR3[:, :, 127], AF.Exp), [psR.b], [s4.b])
                    P.emit(dve, REC.tensor_tensor(s4[:, 4, :], s4[:, 2, :], dtT[:, tt, :], ALU.mult), [s4.b, dtT.b], [s4.b])
                    P.emit(dve, REC.tensor_tensor(xdtd[:, :, :], xtok, s4[:, 4, :].unsqueeze(2).to_broadcast([128, 4, 64]), ALU.mult), [pst.b, s4.b], [xdtd.b])
                    for g in range(2):
                        P.emit(dve, REC.tensor_tensor(Cs[:, 2 * g:2 * g + 2, :], drow[:, 2 * g:2 * g + 2, :],
                                                                          Ct[:, g:g + 1, ts].to_broadcast([128, 2, 128]), ALU.mult), [drow.b, Ct.b], [Cs.b])
                    psG = PS("mm")
                    for g in range(2):
                        P.emit(pe, REC.matmul(psG[:, g * 128:(g + 1) * 128], Bt[:, g, ts], Ct[:, g, ts], start=True, stop=True), [Bt.b, Ct.b], [psG.b], sig=(g == 1))
                    for g in range(2):
                        P.emit(dve, REC.tensor_tensor(scT[:, 2 * g:2 * g + 2, :], dec[:, 2 * g:2 * g + 2, :],
                                                                             psG[:, g * 128:(g + 1) * 128].unsqueeze(1).to_broadcast([128, 2, 128]), ALU.mult), [dec.b, psG.b], [scT.b])
                    for k in range(2):
                        ops = [(xdtp[:, 2 * k, :], scT[:, 2 * k, :], [xdtp.b, scT.b]), (xdtp[:, 2 * k + 1, :], scT[:, 2 * k + 1, :], [xdtp.b, scT.b]),
                               (prevp[:, 2 * k, :], Cs[:, 2 * k, :], [prevp.b, Cs.b]), (prevp[:, 2 * k + 1, :], Cs[:, 2 * k + 1, :], [prevp.b, Cs.b])]
                        for i, (lh, rh, rb) in enumerate(ops):
                            P.emit(pe, REC.matmul(psY[k][:, ts], lh, rh, start=(i == 0), stop=(i == 3)), rb, [psY[k].b], sig=(i == 3))
                    psSt = PS("aux")
                    for g in range(2):
                        P.emit(pe, REC.matmul(psSt[:, g * 128:(g + 1) * 128], Btok[:, g, :], xdtd[:, 2 * g:2 * g + 2, :].rearrange("p h d -> p (h d)"), start=True, stop=True),
                               [Btok.b, xdtd.b], [psSt.b], sig=(g == 1))
                    P.emit(dve, REC.tensor_tensor(state[:, :, :], state[:, :, :], s4[:, 3, :].unsqueeze(2).to_broadcast([128, 4, 64]), ALU.mult), [state.b, s4.b], [state.b])
                    P.emit(dve, REC.tensor_tensor(state[:, :, :], state[:, :, :], psSt[:, 0:256].rearrange("p (h d) -> p h d", h=4), ALU.add), [state.b, psSt.b], [state.b])
                for k in range(2):
                    P.emit(dve, REC.scalar_tensor_tensor(gat[:, k, :], xs_f[:, k, :], col(l, C_DSK + k), psY[k][:, :], ALU.mult, ALU.add), [xs_f.b, cols.b, psY[k].b], [gat.b])
                    P.emit(dve, REC.tensor_tensor(gat[:, k, :], gat[:, k, :], szs[:, k, :], ALU.mult), [gat.b, szs.b], [gat.b])
                    P.emit(act, REC.activation(gsq[:, k, :], gat[:, k, :], AF.Square), [gat.b], [gsq.b])
                ps = PS("aux")
                for k in range(2):
                    P.emit(pe, REC.matmul(ps[:, :], ones_b, gsq[:, k, :], start=(k == 0), stop=(k == 1)), [cstb.b, gsq.b], [ps.b], sig=(k == 1))
                P.emit(act, REC.activation(tmpA[:, :], ps[:, :], AF.Ln, scale=1.0 / 256, bias=epsc[:, 0:1]), [ps.b, epsc.b], [tmpA.b])
                P.emit(act, REC.activation(tmpA[:, :], tmpA[:, :], AF.Exp, scale=-0.5), [tmpA.b], [tmpA.b])
                for k in range(2):
                    P.emit(dve, REC.scalar_tensor_tensor(yT[:, 6 + k, :], gat[:, k, :], col(l, C_SSN + k), tmpA[:, :], ALU.mult, ALU.mult), [gat.b, cols.b, tmpA.b], [yT.b])

                if l == 0 and blk == NB - 1:
                    for kc in range(8):
                        dump("yT", yT[:, kc, :], [yT.b], idx=kc)
                for tt in range(4):
                    pl = []
                    for half in range(2):
                        ps = PS("mm")
                        for kc in range(8):
                            P.emit(pe, REC.matmul(ps[:, :], yT[:, kc, tt * 128:(tt + 1) * 128], w_out_sb[:, kc, half * 512:(half + 1) * 512],
                                                                                        start=(kc == 0), stop=(kc == 7)), [yT.b, bWo, Kcur.b] + kbq, [ps.b], sig=(kc == 7))
                        pl.append(ps)
                    post_norm_residual(pl, src_d, src_buf, xa_d, b_xa, t0 + tt * 128, tt, nocopy=True)

            bWF = Buf("wF%d" % l)
            P.barrier()
            set_pools({"mm": [0, 1, 2, 3, 4, 5, 6]}, [pst])
            w_up_sb = wreg.t[:, 0:8 * 2 * FF].rearrange("p (j k n) -> p j k n", j=44, k=8)
            w_down_sb = wreg.t[:, 8 * 2 * FF:8 * 2 * FF + NFC * D].rearrange("p (k n) -> p k n", k=NFC)
            assert (8 * 2 * FF + NFC * D) * 2 <= WREG
            bWU = [Buf("wu%d_%d" % (l, j)) for j in range(44)]
            def slot(j_):
                return 2 * (j_ % NFC) + (j_ // NFC)
            for j in range(44):
                reg_buf("wreg", 2048 * slot(j), 2048 * (slot(j) + 1), bWU[j])
            reg_buf("wreg", 2 * 8 * 2 * FF, 2 * (8 * 2 * FF + NFC * D), bWF)
            for i in range(NFC):
                for j in (i, NFC + i):
                    P.dma(pool, w_up_sb[:, slot(j), :, :], w_up_d[l, j].rearrange("p (k n) -> p k n", k=8), [], [bWU[j]])
            wdv = w_down_d[l].rearrange("(k p) n -> p k n", p=128)
            P.dma(pool, w_down_sb[:, 0:11, :], wdv[:, 0:11, :], [], [bWF])
            P.dma(pool, w_down_sb[:, 11:22, :], wdv[:, 11:22, :], [], [bWF])
            P.dma(sp, rows[:, 0:1024], rows_d[l, :, 1024:2048], [], [rows.b])
            gT = Tile(reg2.t[:, 0:NFC * 512].rearrange("p (k n) -> p k n", k=NFC), Buf("gT%d" % l))
            reg_buf("reg2", 0, 2 * NFC * 512, gT.b)
            o2 = NFC * 512
            def carve2(shape):
                nonlocal o2
                n = int(np.prod(shape))
                a = reg2.t[:, o2:o2 + 2 * n].bitcast(F32)
                o2 += 2 * n
                t_ = Tile(a.rearrange("p (a b) -> p a b", a=shape[0]), Buf("r2_%d" % o2))
                reg_buf("reg2", 2 * (o2 - 2 * n), 2 * o2, t_.b)
                return t_
            ug = [carve2([1, 514]) for _ in range(2)]
            uu = [carve2([1, 514]) for _ in range(2)]
            fh = carve2([44, 2])
            tB = carve2([1, 512])
            tC = carve2([1, 512])
            tCs = [Tile(tB.t[:, 0, :], tB.b), Tile(tC.t[:, 0, :], tC.b)]
            assert o2 * 2 <= REG2, o2 * 2
            P.emit(dve, REC.memset(fh[:, :, :], 0.0), [], [fh.b])

            for blk in range(NB):
                t0 = blk * 512
                norm_transpose(xa_d, b_xa, t0, l, C_GPRE2)
                for i in range(NFC):
                    br = []
                    for bi, (ub, cbase) in enumerate(((ug[i % 2], i * 128), (uu[i % 2], FF + i * 128))):
                        ps = PS("mm")
                        for kc in range(8):
                            P.emit(pe, REC.matmul(ps[:, :], w_up_sb[:, slot(i + bi * NFC), kc, :], hT[:, kc, :], start=(kc == 0), stop=(kc == 7)),
                                   [bWU[i + bi * NFC], hT.b], [ps.b], sig=(kc == 7))
                        j = i + bi * NFC
                        br.append(ps)
                        tc_ = tCs[i % 2]
                        if bi == 0:
                            acc_ap, acc_b = tc_[:, :], tc_.b
                        else:
                            acc_ap, acc_b = tmpA[:, :], tmpA.b
                        P.emit(act, REC.activation(ub[:, 0, 2:514], ps[:, :], AF.Copy), [ps.b], [ub.b])
                        P.emit(act, REC.activation(acc_ap, ps[:, :], AF.Identity, scale=col(l, C_FW + j * 3 + 2), bias=col(l, C_FB + j)), [ps.b, cols.b], [acc_b])
                        P.emit(act, REC.activation(ub[:, 0, 0:2], fh[:, j, :], AF.Copy), [fh.b], [ub.b])
                        for k in (0, 1):
                            P.emit(dve, REC.scalar_tensor_tensor(acc_ap, ub[:, 0, k:k + 512], col(l, C_FW + j * 3 + k), acc_ap, ALU.mult, ALU.add),
                                   [ub.b, cols.b, acc_b], [acc_b])
                        P.emit(act, REC.activation(fh[:, j, :], ub[:, 0, 512:514], AF.Copy), [ub.b], [fh.b])
                    tc_ = tCs[i % 2]
                    P.emit(act, REC.activation(tc_[:, :], tc_[:, :], AF.Silu), [tc_.b], [tc_.b])
                    P.emit(dve, REC.tensor_tensor(gT[:, i, :], tc_[:, :], tmpA[:, :], ALU.mult), [tc_.b, tmpA.b], [gT.b])
                for tt in range(4):
                    pl = []
                    for half in range(2):
                        ps = PS("mm")
                        for i in range(NFC):
                            P.emit(pe, REC.matmul(ps[:, :], gT[:, i, tt * 128:(tt + 1) * 128], w_down_sb[:, i, half * 512:(half + 1) * 512],
                                                                                       start=(i == 0), stop=(i == NFC - 1)), [gT.b, bWF], [ps.b], sig=(i == NFC - 1))
                        pl.append(ps)
                    post_norm_residual(pl, xa_d, b_xa, dst_d, dst_buf, t0 + tt * 128, tt)

        P.finalize()
        block = es.enter_context(nc.Block())

        @block.tensor
        def _(e):
            _replay(pe.q, e)

        @block.scalar
        def _(e):
            _replay(act.q, e)

        @block.vector
        def _(e):
            _replay(dve.q, e)

        @block.gpsimd
        def _(e):
            _replay(pool.q, e)

        @block.sync
        def _(e):
            _replay(sp.q, e)
    return nc


def _cols_layer(p, l):
    c = np.zeros((128, NCOLS), np.float32)
    def chunks(v, n):
        return np.asarray(v, np.float32).reshape(n, 128).T
    c[:, C_GPRE1:C_GPRE1 + 8] = chunks(p["norm_mix_pre"][l], 8)
    c[:, C_GPRE2:C_GPRE2 + 8] = chunks(p["norm_ffn_pre"][l], 8)
    c[:, C_QN:C_QN + 2] = chunks(p["mla_q_norm"][l], 2)
    c[:, C_KVN:C_KVN + 1] = chunks(p["mla_kv_norm"][l], 1)
    scw = np.asarray(p["sc_conv_w"][l], np.float32)
    for ch in range(2):
        for k in range(3):
            c[:, C_SCW + ch * 3 + k] = scw[k, ch * 128:(ch + 1) * 128]
    ssw = np.asarray(p["ssd_conv_w"][l], np.float32)
    ssb = np.asarray(p["ssd_conv_b"][l], np.float32)
    for ch in range(6):
        for k in range(4):
            c[:, C_SSW + ch * 4 + k] = ssw[k, ch * 128:(ch + 1) * 128]
        c[:, C_SSB + ch] = ssb[ch * 128:(ch + 1) * 128]
    dsk = np.repeat(np.asarray(p["ssd_d"][l], np.float32), 64)
    c[:, C_DSK:C_DSK + 2] = chunks(dsk, 2)
    c[:, C_SSN:C_SSN + 2] = chunks(p["ssd_norm"][l], 2)
    fw = np.asarray(p["ffn_conv_w"][l], np.float32)
    fb = np.asarray(p["ffn_conv_b"][l], np.float32)
    for j in range(44):
        for k in range(3):
            c[:, C_FW + j * 3 + k] = fw[k, j * 128:(j + 1) * 128]
        c[:, C_FB + j] = fb[j * 128:(j + 1) * 128]
    return c


def prep_shared(p, NL):
    f = lambda a: np.ascontiguousarray(np.asarray(a, np.float32))
    w_in = f(p["w_in"])[:NL]
    sw = np.concatenate([np.arange(16, 32), np.arange(0, 16)])
    kr = w_in[:, :, 384:416]
    w_in_r = np.concatenate([w_in[:, :, 0:384], kr, kr[:, :, sw], w_in[:, :, 416:2208]], axis=2)
    assert w_in_r.shape[2] == NCOLW
    groups = [(0, 128), (128, 128), (256, 128), (384, 32), (416, 32)] + [(448 + g * 128, 128) for g in range(14)]
    w4 = w_in_r.reshape(NL, 8, 128, NCOLW)
    w_in_g = np.concatenate([np.transpose(w4[:, :, :, c0:c0 + M], (0, 2, 1, 3)).reshape(NL, 128, 8 * M) for (c0, M) in groups], axis=2)
    assert w_in_g.shape[2] == 8 * NCOLW
    w_dt = w_in[:, :, 2208:2212]
    wq = f(p["mla_w_q_up"])[:NL]
    wq4 = wq.reshape(NL, 256, 8, 96)
    w_qsw = wq4[:, :, :, 64:96][:, :, :, sw].reshape(NL, 256, 256)
    wkv = f(p["mla_w_kv_up"])[:NL].reshape(NL, 128, 8, 128)
    w_kn = wkv[:, :, :, 0:64].reshape(NL, 128, 512)
    w_v = wkv[:, :, :, 64:128].reshape(NL, 128, 512)
    wu = f(p["ffn_w_up"])[:NL].reshape(NL, 8, 128, 44, 128)
    w_up_g = np.ascontiguousarray(np.transpose(wu, (0, 3, 2, 1, 4)).reshape(NL, 44, 128, 1024))
    cols = np.stack([_cols_layer(p, l) for l in range(NL)], axis=1)
    rows = np.zeros((NL, 128, 2056), np.float32)
    for l in range(NL):
        rows[l, :, 0:1024] = np.asarray(p["norm_mix_post"][l], np.float32)[None, :]
        rows[l, :, 1024:2048] = np.asarray(p["norm_ffn_post"][l], np.float32)[None, :]
        rows[l, :, 2048:2052] = np.asarray(p["ssd_dt_bias"][l], np.float32)[None, :]
        rows[l, :, 2052:2056] = np.asarray(p["ssd_a_log"][l], np.float32)[None, :]
    inv_freq = (1.0 / (10000.0 ** (np.arange(0, 32, 2, dtype=np.float32) / np.float32(32)))).astype(np.float32)
    ropec = np.zeros((32, 2), np.float32)
    ropec[:, 0] = np.concatenate([inv_freq, inv_freq])
    ropec[:, 1] = np.concatenate([-np.ones(16), np.ones(16)])
    tri_ = np.triu(np.ones((128, 128)))
    consts = np.concatenate([np.eye(128), tri_, np.ones((128, 128)), (1.0 - tri_) * -30000.0], axis=1).astype(np.float32)
    return {
        "ropec": ropec, "consts": consts, "cols": np.ascontiguousarray(cols), "rows": rows,
        "w_in_g": np.ascontiguousarray(w_in_g), "w_dt": np.ascontiguousarray(w_dt),
        "w_q": np.ascontiguousarray(wq), "w_qsw": np.ascontiguousarray(w_qsw),
        "w_kn": np.ascontiguousarray(w_kn), "w_v": np.ascontiguousarray(w_v),
        "w_out": f(p["w_out"])[:NL], "w_up_g": w_up_g, "w_down": f(p["ffn_w_down"])[:NL],
    }


def run(inputs, S, NL, ncores, dbg=None):
    shared = prep_shared(inputs, NL)
    x = np.asarray(inputs["x"], np.float32)
    pos = np.asarray(inputs["positions"], np.int32)
    in_maps = []
    for c in range(ncores):
        m = dict(shared)
        m["x"] = np.ascontiguousarray(x[c, :S])
        m["posrep"] = np.ascontiguousarray(np.broadcast_to(pos[c, :S][None, :], (32, S)))
        in_maps.append(m)
    nc = build(S, NL, dbg)
    res = run_bass_kernel_spmd(nc, in_maps, core_ids=list(range(ncores)))
    return res


def kernel(**inputs):
    res = run(inputs, 4096, 4, 8)
    return np.stack([r["y"] for r in res.results], axis=0).astype(np.float32)
```
